# Optimizing a Trainium2 kernel written in Bass

```python
import math
import jax, jax.numpy as jnp
from jax import lax
import numpy as np

D_MODEL = 1024
BATCH = 8
SEQ = 4096
DEPTH = 2

HEAD_DIM = 64
D_ATTN = D_MODEL // 2
N_ATTN_HEADS = D_ATTN // HEAD_DIM
D_CONV = D_MODEL // 4
N_CONV_GROUPS = D_CONV // HEAD_DIM
D_LRU = D_MODEL // 4
N_LRU_HEADS = D_LRU // HEAD_DIM
LRU_BLOCK = D_LRU // N_LRU_HEADS
D_MIX = D_ATTN + D_CONV + D_LRU
D_IN_PROJ = 3 * D_ATTN + 3 * D_CONV + 2 * D_LRU
D_FF = 4 * D_MODEL
MOBA_BLOCK = 256
MOBA_TOPK = 3
Q_CHUNK = 32
SHORT_CONV_W = 3
LRU_CONV_W = 4
LRU_C = 8.0
N_MOD = 6
EPS = 1e-6

kernel_name = "hymba_moba_conv_rglru_hybrid"


def rms_norm(x, g):
    xf = x.astype(jnp.float32)
    y = xf * lax.rsqrt(jnp.mean(xf * xf, axis=-1, keepdims=True) + EPS)
    return (y * g.astype(jnp.float32)).astype(x.dtype)


def causal_depthwise_conv(x, w):
    k_w, ch = w.shape
    return lax.conv_general_dilated(
        x, w[:, None, :].astype(x.dtype), window_strides=(1,),
        padding=((k_w - 1, 0),), dimension_numbers=("NWC", "WIO", "NWC"),
        feature_group_count=ch)


def moba_attention(q, k, v):
    B, H, S, hd = q.shape
    nb = -(-S // MOBA_BLOCK)
    pad = nb * MOBA_BLOCK - S
    kp = jnp.pad(k, ((0, 0), (0, 0), (0, pad), (0, 0)))
    vp = jnp.pad(v, ((0, 0), (0, 0), (0, pad), (0, 0)))
    kb = kp.reshape(B, H, nb, MOBA_BLOCK, hd)
    vb = vp.reshape(B, H, nb, MOBA_BLOCK, hd)
    kmean = jnp.mean(kb.astype(jnp.float32), axis=3)
    topk = min(MOBA_TOPK, nb)
    scale = 1.0 / math.sqrt(hd)
    bi = jnp.arange(B)[:, None, None, None]
    hi = jnp.arange(H)[None, :, None, None]
    blk_ids = jnp.arange(nb)

    def chunk(start):
        qc = lax.dynamic_slice_in_dim(q, start, Q_CHUNK, axis=2)
        own = start // MOBA_BLOCK
        qpos = start + jnp.arange(Q_CHUNK)
        gate = jnp.einsum("bhqd,bhnd->bhqn", qc.astype(jnp.float32), kmean)
        gate = jnp.where(blk_ids < own, gate, -jnp.inf)
        _, idx = lax.top_k(gate, topk)
        valid = idx < own
        ksel = kb[bi, hi, idx]
        vsel = vb[bi, hi, idx]
        s_sel = jnp.einsum("bhqd,bhqjkd->bhqjk", qc, ksel).astype(jnp.float32) * scale
        s_sel = jnp.where(valid[..., None], s_sel, -jnp.inf)
        s_sel = s_sel.reshape(B, H, Q_CHUNK, topk * MOBA_BLOCK)
        kown = lax.dynamic_slice_in_dim(kp, own * MOBA_BLOCK, MOBA_BLOCK, axis=2)
        vown = lax.dynamic_slice_in_dim(vp, own * MOBA_BLOCK, MOBA_BLOCK, axis=2)
        s_own = jnp.einsum("bhqd,bhkd->bhqk", qc, kown).astype(jnp.float32) * scale
        kpos = own * MOBA_BLOCK + jnp.arange(MOBA_BLOCK)
        s_own = jnp.where(kpos[None, :] <= qpos[:, None], s_own, -jnp.inf)
        p = jax.nn.softmax(jnp.concatenate([s_sel, s_own], axis=-1), axis=-1)
        p_sel = p[..., :topk * MOBA_BLOCK].reshape(B, H, Q_CHUNK, topk, MOBA_BLOCK).astype(v.dtype)
        p_own = p[..., topk * MOBA_BLOCK:].astype(v.dtype)
        return (jnp.einsum("bhqjk,bhqjkd->bhqd", p_sel, vsel)
                + jnp.einsum("bhqk,bhkd->bhqd", p_own, vown))

    starts = jnp.arange(S // Q_CHUNK) * Q_CHUNK
    o = lax.map(chunk, starts)
    return o.transpose(1, 0, 3, 2, 4).reshape(B, S, H * hd)


def rg_lru(x, w_a, b_a, w_x, b_x, lam):
    B, S, _ = x.shape
    xh = x.reshape(B, S, N_LRU_HEADS, LRU_BLOCK)
    r = jax.nn.sigmoid(jnp.einsum("bshi,hij->bshj", xh, w_a).reshape(B, S, D_LRU) + b_a)
    i = jax.nn.sigmoid(jnp.einsum("bshi,hij->bshj", xh, w_x).reshape(B, S, D_LRU) + b_x)
    log_a = -LRU_C * r.astype(jnp.float32) * jax.nn.softplus(-lam.astype(jnp.float32))
    a = jnp.exp(log_a)
    u = jnp.sqrt(-jnp.expm1(2.0 * log_a)) * (i * x).astype(jnp.float32)

    def combine(left, right):
        a1, b1 = left
        a2, b2 = right
        return a1 * a2, a2 * b1 + b2

    _, h = lax.associative_scan(combine, (a, u), axis=1)
    return h.astype(x.dtype)


def hybrid_layer(x, mod, ln1_g, ln2_g, w_in, q_norm_g, k_norm_g, sc_w,
                 lru_conv_w, lru_conv_b, lru_wa, lru_ba, lru_wx, lru_bx, lru_lambda,
                 mix_norm_g, w_out, w_up, w_down):
    B, S, _ = x.shape
    shift1, scale1, gate1, shift2, scale2, gate2 = jnp.split(mod, N_MOD, axis=-1)

    h = rms_norm(x, ln1_g) * (1.0 + scale1[:, None, :]) + shift1[:, None, :]
    proj = h @ w_in
    q, k, v, sc_b, sc_c, sc_u, lru_x, lru_gate = jnp.split(
        proj,
        [D_ATTN, 2 * D_ATTN, 3 * D_ATTN,
         3 * D_ATTN + D_CONV, 3 * D_ATTN + 2 * D_CONV, 3 * D_ATTN + 3 * D_CONV,
         3 * D_ATTN + 3 * D_CONV + D_LRU], axis=-1)

    def heads(t):
        return t.reshape(B, S, N_ATTN_HEADS, HEAD_DIM).transpose(0, 2, 1, 3)
    qh = rms_norm(heads(q), q_norm_g)
    kh = rms_norm(heads(k), k_norm_g)
    y_attn = moba_attention(qh, kh, heads(v))

    y_conv = sc_b * causal_depthwise_conv(sc_c * sc_u, sc_w)

    xr = causal_depthwise_conv(lru_x, lru_conv_w) + lru_conv_b
    y_lru = rg_lru(xr, lru_wa, lru_ba, lru_wx, lru_bx, lru_lambda) * jax.nn.gelu(lru_gate)

    y = jnp.concatenate([
        rms_norm(y_attn, mix_norm_g[:D_ATTN]),
        rms_norm(y_conv, mix_norm_g[D_ATTN:D_ATTN + D_CONV]),
        rms_norm(y_lru, mix_norm_g[D_ATTN + D_CONV:]),
    ], axis=-1)
    x = x + gate1[:, None, :] * (y @ w_out)

    h2 = rms_norm(x, ln2_g) * (1.0 + scale2[:, None, :]) + shift2[:, None, :]
    ff = jnp.square(jax.nn.relu(h2 @ w_up)) @ w_down
    return x + gate2[:, None, :] * ff


def setup_inputs(seed: int = 0) -> dict:
    key = jax.random.key(seed)
    ks = jax.random.split(key, 24)
    f32 = jnp.float32
    L = DEPTH

    def nrm(k, shape, scale):
        return jax.random.normal(k, shape, f32) * scale

    a_c = jax.random.uniform(ks[20], (L, D_LRU), f32, 0.9, 0.999)
    a0 = a_c ** (1.0 / LRU_C)
    lru_lambda = jnp.log(a0) - jnp.log1p(-a0)
    return {
        "x": nrm(ks[0], (BATCH, SEQ, D_MODEL), 1.0),
        "c": nrm(ks[1], (BATCH, D_MODEL), 1.0),
        "ln1_g": 1.0 + nrm(ks[2], (L, D_MODEL), 0.02),
        "ln2_g": 1.0 + nrm(ks[3], (L, D_MODEL), 0.02),
        "w_ada": nrm(ks[4], (L, D_MODEL, N_MOD * D_MODEL), D_MODEL ** -0.5),
        "b_ada": nrm(ks[5], (L, N_MOD * D_MODEL), 0.01),
        "w_in": nrm(ks[6], (L, D_MODEL, D_IN_PROJ), D_MODEL ** -0.5),
        "q_norm_g": 1.0 + nrm(ks[7], (L, HEAD_DIM), 0.02),
        "k_norm_g": 1.0 + nrm(ks[8], (L, HEAD_DIM), 0.02),
        "sc_w": nrm(ks[9], (L, SHORT_CONV_W, D_CONV), SHORT_CONV_W ** -0.5),
        "lru_conv_w": nrm(ks[10], (L, LRU_CONV_W, D_LRU), LRU_CONV_W ** -0.5),
        "lru_conv_b": nrm(ks[11], (L, D_LRU), 0.01),
        "lru_wa": nrm(ks[12], (L, N_LRU_HEADS, LRU_BLOCK, LRU_BLOCK), LRU_BLOCK ** -0.5),
        "lru_ba": nrm(ks[13], (L, D_LRU), 0.01),
        "lru_wx": nrm(ks[14], (L, N_LRU_HEADS, LRU_BLOCK, LRU_BLOCK), LRU_BLOCK ** -0.5),
        "lru_bx": nrm(ks[15], (L, D_LRU), 0.01),
        "lru_lambda": lru_lambda,
        "mix_norm_g": 1.0 + nrm(ks[16], (L, D_MIX), 0.02),
        "w_out": nrm(ks[17], (L, D_MIX, D_MODEL), D_MIX ** -0.5),
        "w_up": nrm(ks[18], (L, D_MODEL, D_FF), D_MODEL ** -0.5),
        "w_down": nrm(ks[19], (L, D_FF, D_MODEL), D_FF ** -0.5),
    }


def reference(x, c, ln1_g, ln2_g, w_ada, b_ada, w_in, q_norm_g, k_norm_g, sc_w,
              lru_conv_w, lru_conv_b, lru_wa, lru_ba, lru_wx, lru_bx, lru_lambda,
              mix_norm_g, w_out, w_up, w_down):
    c_act = jax.nn.silu(c)
    for l in range(DEPTH):
        mod = c_act @ w_ada[l] + b_ada[l]
        x = hybrid_layer(x, mod, ln1_g[l], ln2_g[l], w_in[l], q_norm_g[l], k_norm_g[l],
                         sc_w[l], lru_conv_w[l], lru_conv_b[l], lru_wa[l], lru_ba[l],
                         lru_wx[l], lru_bx[l], lru_lambda[l], mix_norm_g[l], w_out[l],
                         w_up[l], w_down[l])
    return x
```

```python
import numpy as np
import ml_dtypes
from contextlib import ExitStack
import concourse.bass as bass
import concourse.mybir as mybir
from concourse.bass_utils import run_bass_kernel_spmd

F32 = mybir.dt.float32
BF16 = mybir.dt.bfloat16
AF = mybir.ActivationFunctionType
ALU = mybir.AluOpType
AX = mybir.AxisListType

S = 4096
D = 1024
NT = 32
NB = 16
DIN = 2816
DFF = 4096
BIG = 30000.0
EPS = 1e-6
NCOLS = 78
ENGS = ("pe", "act", "dve", "pool", "sp")
import os as _osg
FP32_GUARD = _osg.environ.get("FP32_GUARD", "1") == "1"


class Prog:
    def __init__(self, nc, strict=True):
        self.nc = nc
        self.strict = strict
        self.ops = []
        self.last_w = {}
        self.readers = {}
        self.last_dma = {}
        self.keymap = {}
        self.nosched = False
        self.phase = 0

    def add(self, eng, fn, reads=(), writes=(), dma_key=None, n=256, cost=None):
        if cost is None:
            if dma_key is not None:
                cost = 3000.0
            elif eng == "pe":
                cost = 400.0
            elif eng == "act":
                cost = 320.0 + n / 1.4
            elif eng == "dve":
                cost = 250.0 + n / 0.96
            else:
                cost = 300.0 + n / 0.5
        if dma_key is not None:
            cls = "W" if eng == "pool" else "H"
            kk = (cls, dma_key)
            if kk not in self.keymap:
                self.keymap[kk] = (cls, sum(1 for q in self.keymap if q[0] == cls))
            dma_key = self.keymap[kk]
        i = len(self.ops)
        deps = set()
        for t in reads:
            if t in self.last_w:
                deps.add(self.last_w[t])
        for t in writes:
            if t in self.last_w:
                deps.add(self.last_w[t])
            for r in self.readers.get(t, ()):
                deps.add(r)
        if dma_key is not None:
            if dma_key in self.last_dma:
                deps.add(self.last_dma[dma_key])
            self.last_dma[dma_key] = i
        deps.discard(i)
        for t in reads:
            self.readers.setdefault(t, []).append(i)
        for t in writes:
            self.last_w[t] = i
            self.readers[t] = []
        self.ops.append(dict(eng=eng, fn=fn, deps=sorted(deps), dma_key=dma_key,
                             phase=self.phase, barrier=False, cost=float(cost), nosched=self.nosched))
        return i

    def barrier(self):
        self.ops.append(dict(eng=None, fn=None, deps=[], dma_key=None,
                             phase=self.phase, barrier=True))
        self.phase += 1
        self.last_w = {}
        self.readers = {}
        self.last_dma = {}
        self.keymap = {}

    def schedule(self, window=48):
        ops = self.ops
        n = len(ops)
        order = []
        start = 0
        while start < n:
            end = start
            while end < n and not ops[end]["barrier"]:
                end += 1
            ids = list(range(start, end))
            import os as _os2
            sp_ = _os2.environ.get("SCHED_PHASES")
            if ids and sp_ is not None and str(ops[ids[0]]["phase"]) not in sp_.split(","):
                order.extend(ids)
            elif ids:
                fz = set(_os2.environ.get("SCHED_FREEZE", "").split(","))
                if ops[ids[0]].get("nosched"):
                    fz |= {"sp", "pool", "pe"}
                order.extend(self._sched_phase(ids, window, fz))
            if end < n:
                order.append(end)
            start = end + 1
        remap = {old: new for new, old in enumerate(order)}
        newops = []
        for old in order:
            o = ops[old]
            o["deps"] = sorted(remap[d] for d in o["deps"])
            newops.append(o)
        self.ops = newops

    def _sched_phase(self, ids, window, freeze=()):
        ops = self.ops
        self._freeze = set(freeze)
        idset = set(ids)
        queues = {e: [i for i in ids if ops[i]["eng"] == e] for e in ENGS}
        qpos = {e: 0 for e in ENGS}
        scheduled = {}
        eng_free = {e: 0.0 for e in ENGS}
        out = []
        remaining = len(ids)
        taken = set()
        while remaining:
            best = None
            for e in ENGS:
                q = queues[e]
                p = qpos[e]
                while p < len(q) and q[p] in taken:
                    p += 1
                qpos[e] = p
                cnt = 0
                k = p
                win = 1 if e in self._freeze else window
                while k < len(q) and cnt < win:
                    i = q[k]
                    k += 1
                    if i in taken:
                        continue
                    cnt += 1
                    ok = True
                    rt = 0.0
                    for d in ops[i]["deps"]:
                        if d in idset:
                            if d not in scheduled:
                                ok = False
                                break
                            if scheduled[d] > rt:
                                rt = scheduled[d]
                    if not ok:
                        continue
                    st = max(eng_free[e], rt)
                    if best is None or st < best[0] - 1e-9 or (abs(st - best[0]) <= 1e-9 and i < best[1]):
                        best = (st, i, e)
                    if rt <= eng_free[e]:
                        break
            assert best is not None, "scheduler deadlock"
            st, i, e = best
            o = ops[i]
            if o["dma_key"] is not None:
                eng_free[e] = st + 120.0
                scheduled[i] = st + o["cost"]
            else:
                eng_free[e] = st + o["cost"]
                scheduled[i] = st + o["cost"] + 150.0
            taken.add(i)
            out.append(i)
            remaining -= 1
        self.est_ns = getattr(self, "est_ns", 0.0) + max(scheduled.values())
        return out

    def emit(self):
        nc = self.nc
        import os as _os1
        if _os1.environ.get("SCHED", "1") == "1":
            self.schedule()
        ops = self.ops
        nph = self.phase + 1
        strict = self.strict

        def same_eng_free(od, o):
            return od["eng"] == o["eng"] and o["dma_key"] is None and (od["eng"] == "pe" or not strict)

        need = [False] * len(ops)
        for i, o in enumerate(ops):
            if o["barrier"]:
                continue
            for d in o["deps"]:
                od = ops[d]
                if od["dma_key"] is not None:
                    continue
                if same_eng_free(od, o):
                    continue
                need[d] = True
        last_in_phase = {}
        for i, o in enumerate(ops):
            if o["barrier"] or o["dma_key"] is not None:
                continue
            last_in_phase[(o["eng"], o["phase"])] = i
        for i in last_in_phase.values():
            need[i] = True
        cnt = {}
        dcnt = {}
        val = [None] * len(ops)
        dma_keys = []
        for i, o in enumerate(ops):
            if o["barrier"]:
                continue
            if o["dma_key"] is not None:
                k = o["dma_key"]
                if k not in dcnt:
                    dcnt[k] = 0
                    dma_keys.append(k)
                dcnt[k] += 16
                val[i] = dcnt[k]
            elif need[i]:
                k = (o["eng"], o["phase"])
                cnt[k] = cnt.get(k, 0) + 1
                val[i] = cnt[k]
        esem = {}
        for (e, ph) in sorted(cnt.keys(), key=lambda t: (t[1], t[0])):
            esem[(e, ph)] = nc.alloc_semaphore(f"s_{e}_{ph}")
        dsem = {k: nc.alloc_semaphore(f"d_{j}") for j, k in enumerate(dma_keys)}
        self.n_sems = len(esem) + len(dsem)
        final_cnt = dict(cnt)
        per = {e: [] for e in ENGS}
        for i, o in enumerate(ops):
            if o["barrier"]:
                for e in ENGS:
                    per[e].append(i)
            else:
                per[o["eng"]].append(i)
        dma_upto = {}
        run = {}
        for i, o in enumerate(ops):
            if o["barrier"]:
                dma_upto[i] = dict(run)
            elif o["dma_key"] is not None:
                run[o["dma_key"]] = val[i]
        dma_final = dict(run)

        def gen(e):
            def body(eng):
                waited = {}

                def w(sem, name, v):
                    if waited.get(name, 0) >= v:
                        return
                    waited[name] = v
                    eng.wait_ge(sem, v)

                for i in per[e]:
                    o = ops[i]
                    if o["barrier"]:
                        ph = o["phase"]
                        for e2 in ("pe", "act", "dve", "pool"):
                            v = final_cnt.get((e2, ph), 0)
                            if v:
                                w(esem[(e2, ph)], (e2, ph), v)
                        for k, v in dma_upto[i].items():
                            w(dsem[k], k, v)
                        continue
                    for d in o["deps"]:
                        od = ops[d]
                        if od["dma_key"] is not None:
                            w(dsem[od["dma_key"]], od["dma_key"], val[d])
                        else:
                            if same_eng_free(od, o):
                                continue
                            k = (od["eng"], od["phase"])
                            w(esem[k], k, val[d])
                    inst = o["fn"](eng)
                    if o["dma_key"] is not None:
                        inst.then_inc(dsem[o["dma_key"]], 16)
                    elif need[i]:
                        inst.then_inc(esem[(e, o["phase"])], 1)
                if e == "sp":
                    for k, v in dma_final.items():
                        w(dsem[k], k, v)
                    for (e2, ph), v in final_cnt.items():
                        w(esem[(e2, ph)], (e2, ph), v)
            return body

        with nc.Block() as block:
            block.tensor(gen("pe"))
            block.scalar(gen("act"))
            block.vector(gen("dve"))
            block.gpsimd(gen("pool"))
            block.sync(gen("sp"))


class Rot:
    def __init__(self, tensors, name):
        self.t = tensors
        self.name = name
        self.i = 0

    def next(self):
        k = self.i % len(self.t)
        self.i += 1
        return self.t[k], (self.name, k)


def build_program(n_layers=2, phases="ABC", debug=False, nblk=NB):
    nc = bass.Bass("TRN2", target_bir_lowering=False)
    dbg_kind = "ExternalOutput" if debug else "Internal"

    def din(name, shape, dt=F32):
        return nc.dram_tensor(name, list(shape), dt, kind="ExternalInput").ap()

    def dscr(name, shape, dt=F32):
        return nc.dram_tensor(name, list(shape), dt, kind=dbg_kind).ap()

    x_in = din("x", [S, D])
    ccol = din("ccol", [128, 8])
    cols = din("cols", [2, 128, NCOLS])
    bada = din("b_ada", [2, 6 * D])
    w_ada = din("w_ada", [2, D, 6 * D])
    w_in = din("w_in", [2, D, DIN])
    qng = din("q_norm_g", [2, 64])
    kng = din("k_norm_g", [2, 64])
    lru_wa = din("lru_wa", [2, 4, 64, 64])
    lru_wx = din("lru_wx", [2, 4, 64, 64])
    w_out = din("w_out", [2, D, D])
    w_up = din("w_up", [2, D, DFF])
    w_down = din("w_down", [2, DFF, D])
    c_identb = din("c_identb", [128, 128], BF16)
    c_identf = din("c_identf", [128, 128], F32)
    c_oh = din("c_oh", [16, S], BF16)
    c_negmask = din("c_negmask", [1, 256], F32)
    c_caus = din("c_caus", [128, 2, 256], BF16)
    out = nc.dram_tensor("out", [S, D], F32, kind="ExternalOutput").ap()

    gates_scr = dscr("gates_scr", [2, 2, D])
    qT_scr = dscr("qT_scr", [80, 8, S], BF16)
    kT_scr = dscr("kT_scr", [64, 8, S], BF16)
    v_scr = dscr("v_scr", [NT, 128, 520], BF16)
    ycl_scr = dscr("ycl_scr", [512, S], BF16)
    x1_scr = dscr("x1_scr", [S, D])
    xmid_scr = dscr("xmid_scr", [S, D])
    dbg_grstd = dscr("dbg_grstd", [128, NT * 2]) if debug else None

    import os as _os0
    P = Prog(nc, strict=(_os0.environ.get('STRICT', '1') == '1'))
    A = P.add

    with ExitStack() as top:
        def sb(es, name, shape, dt):
            return es.enter_context(nc.sbuf_tensor(name, list(shape), dt))

        def ps(es, name, shape, dt):
            return es.enter_context(nc.psum_tensor(name, list(shape), dt))

        cols_sb = sb(top, "cols_sb", [128, 2, NCOLS], F32)
        eff_sb = sb(top, "eff_sb", [128, 2, 32], F32)
        shb_sb = sb(top, "shb_sb", [128, 2, 2, 8], BF16)
        lruc_sb = sb(top, "lruc_sb", [128, 2, 8], F32)
        identb = sb(top, "identb", [128, 128], BF16)
        identf = sb(top, "identf", [128, 128], F32)
        ones_bf = sb(top, "ones_bf", [128, 128], BF16)
        ones_f = sb(top, "ones_f", [128, 128], F32)
        grstd = sb(top, "grstd", [128, NT, 2], F32)
        epsc = sb(top, "epsc", [128, 1], F32)

        A("sp", lambda e: e.dma_start(out=identb[:], in_=c_identb[:, :]), writes=["identb"], dma_key="c0")
        A("sp", lambda e: e.dma_start(out=identf[:], in_=c_identf[:, :]), writes=["identf"], dma_key="c1")
        A("sp", lambda e: e.dma_start(out=cols_sb[:], in_=cols.rearrange("l p n -> p l n")), writes=["cols"], dma_key="c2")
        A("pool", lambda e: e.memset(ones_bf[:], 1.0), writes=["ones_bf"])
        A("pool", lambda e: e.memset(ones_f[:], 1.0), writes=["ones_f"])
        A("pool", lambda e: e.memset(epsc[:], EPS), writes=["epsc"])
        if debug:
            A("pool", lambda e: e.memset(grstd[:], 0.0), writes=["grstd_init"])

        def rstd_from_ssq(dst, ssq, n, tags_r, tags_w, tmp, tmptag):
            A("dve", lambda e: e.tensor_scalar(out=tmp, in0=ssq, scalar1=1.0 / n, scalar2=EPS, op0=ALU.mult, op1=ALU.add),
              reads=tags_r, writes=[tmptag])
            A("act", lambda e: e.activation(out=tmp, in_=tmp, func=AF.Ln), reads=[tmptag], writes=[tmptag])
            A("act", lambda e: e.activation(out=dst, in_=tmp, func=AF.Exp, scale=-0.5), reads=[tmptag], writes=tags_w)

        with ExitStack() as es:
            cc = sb(es, "cc", [128, 8], F32)
            ce = sb(es, "ce", [128, 8], F32)
            cact = sb(es, "cact", [128, 8], BF16)
            wa = [sb(es, f"wa{i}", [128, 8, 512], BF16) for i in range(2)]
            modc = sb(es, "modc", [128, 32], F32)
            grow = sb(es, "grow", [1, 2, D], F32)
            brow = sb(es, "brow", [1, 2, 2, D], F32)
            lam = sb(es, "lam", [128, 2, 2], F32)
            mod_ps_t = ps(es, "mod_ps", [128, 512], F32)
            mod_ps = mod_ps_t[:, 0:32]
            g_ps = [ps(es, f"g_ps{i}", [128, 512], F32)[0:1, :] for i in range(2)]

            A("sp", lambda e: e.dma_start(out=cc[:], in_=ccol[:, :]), writes=["cc"], dma_key="c3")
            A("act", lambda e: e.activation(out=ce[:], in_=cc[:], func=AF.Exp, scale=-1.0), reads=["cc"], writes=["ce"])
            A("dve", lambda e: e.tensor_scalar(out=ce[:], in0=ce[:], scalar1=1.0, scalar2=None, op0=ALU.add), reads=["ce"], writes=["ce"])
            A("dve", lambda e: e.reciprocal(out=ce[:], in_=ce[:]), reads=["ce"], writes=["ce"])
            A("dve", lambda e: e.tensor_tensor(out=cact[:], in0=ce[:], in1=cc[:], op=ALU.mult), reads=["ce", "cc"], writes=["cact"])
            for l in range(n_layers):
                for g_ in range(2):
                    A("sp", lambda e, l=l, g_=g_: e.dma_start(out=brow[:, l, g_, :], in_=bada[l:l + 1, (2 + 3 * g_) * D:(3 + 3 * g_) * D]),
                      writes=[("brow", l, g_)], dma_key=("c4", l, g_))
                colidx = 0
                for blk in range(12):
                    wt, wtag = wa[blk % 2], ("wa", blk % 2)
                    A("pool", lambda e, wt=wt, l=l, blk=blk: e.dma_start(
                        out=wt[:], in_=w_ada[l, :, blk * 512:(blk + 1) * 512].rearrange("(c p) n -> p c n", p=128)),
                      writes=[wtag], dma_key=("wa", blk % 2))
                    if blk in (4, 5, 10, 11):
                        g = 0 if blk < 6 else 1
                        half = blk % 2
                        gp, gtag = g_ps[half], ("g_ps", half)

                        def mm(e, wt=wt, gp=gp):
                            r = None
                            for c in range(8):
                                r = e.matmul(gp, lhsT=cact[:, c:c + 1], rhs=wt[:, c, :], start=(c == 0), stop=(c == 7))
                            return r
                        A("pe", mm, reads=[wtag, "cact"], writes=[gtag])
                        A("dve", lambda e, gp=gp, l=l, g=g, half=half: e.tensor_tensor(
                            out=grow[:, g, half * 512:(half + 1) * 512], in0=gp, in1=brow[:, l, g, half * 512:(half + 1) * 512], op=ALU.add),
                          reads=[gtag, ("brow", l, g)], writes=[("grow", g, half)])
                        if half == 1:
                            A("sp", lambda e, l=l, g=g: e.dma_start(out=gates_scr[l, g:g + 1, :], in_=grow[:, g, :]),
                              reads=[("grow", g, 0), ("grow", g, 1)], dma_key=("grow", g))
                    else:
                        def mm(e, wt=wt, colidx=colidx):
                            r = None
                            for sub in range(4):
                                for c in range(8):
                                    r = e.matmul(mod_ps[:, colidx + sub:colidx + sub + 1], lhsT=wt[:, c, sub * 128:(sub + 1) * 128],
                                                 rhs=cact[:, c:c + 1], start=(c == 0), stop=(c == 7))
                            return r
                        A("pe", mm, reads=[wtag, "cact", "modc"], writes=["mod_ps"])
                        colidx += 4
                A("dve", lambda e, l=l: e.tensor_tensor(out=modc[:], in0=mod_ps, in1=cols_sb[:, l, 16:48], op=ALU.add),
                  reads=["mod_ps", "cols"], writes=["modc"])
                for k2 in range(2):
                    A("dve", lambda e, l=l, k2=k2: e.scalar_tensor_tensor(
                        out=eff_sb[:, l, 16 * k2:16 * k2 + 8], in0=modc[:, 16 * k2 + 8:16 * k2 + 16], scalar=1.0,
                        in1=cols_sb[:, l, 8 * k2:8 * k2 + 8], op0=ALU.add, op1=ALU.mult),
                      reads=["modc", "cols"], writes=[("eff", l, k2, 0)])
                    A("dve", lambda e, l=l, k2=k2: e.tensor_copy(out=eff_sb[:, l, 16 * k2 + 8:16 * k2 + 16], in_=modc[:, 16 * k2:16 * k2 + 8]),
                      reads=["modc"], writes=[("eff", l, k2, 1)])
                    A("dve", lambda e, l=l, k2=k2: e.tensor_copy(out=shb_sb[:, l, k2, :], in_=modc[:, 16 * k2:16 * k2 + 8]),
                      reads=["modc"], writes=[("shb", l, k2)])
                A("dve", lambda e, l=l: e.tensor_scalar(out=lruc_sb[:, l, 0:4], in0=cols_sb[:, l, 72:76], scalar1=-1.0, scalar2=None, op0=ALU.mult),
                  reads=["cols"], writes=[("lruc", l, 0)])
                A("act", lambda e, l=l: e.activation(out=lam[:, l, :], in_=cols_sb[:, l, 76:78], func=AF.Exp, scale=-1.0),
                  reads=["cols"], writes=[("lam", l)])
                A("act", lambda e, l=l: e.activation(out=lam[:, l, :], in_=lam[:, l, :], func=AF.Ln, bias=1.0),
                  reads=[("lam", l)], writes=[("lam", l)])
                A("dve", lambda e, l=l: e.tensor_scalar(out=lruc_sb[:, l, 4:6], in0=lam[:, l, :], scalar1=-8.0, scalar2=None, op0=ALU.mult),
                  reads=[("lam", l)], writes=[("lruc", l, 1)])
            P.barrier()

        for l in range(n_layers):
            x_src = x_in if l == 0 else xmid_scr
            x_dst = xmid_scr if l < n_layers - 1 else out
            if l >= 2:
                x_dst = out
            if "A" in phases:
                P.nosched = True
                phase_A(nc, P, top, sb, ps, l, x_src, dict(
                    cols_sb=cols_sb, eff_sb=eff_sb, shb_sb=shb_sb, lruc_sb=lruc_sb, identb=identb, identf=identf,
                    ones_bf=ones_bf, ones_f=ones_f, grstd=grstd, w_in=w_in, qng=qng, kng=kng, lru_wa=lru_wa,
                    lru_wx=lru_wx, c_negmask=c_negmask, qT_scr=qT_scr, kT_scr=kT_scr, v_scr=v_scr, ycl_scr=ycl_scr,
                    rstd_from_ssq=rstd_from_ssq, dbg_grstd=dbg_grstd, epsc=epsc), nblk)
                P.barrier()
                P.nosched = False
            TT = dict(cols_sb=cols_sb, eff_sb=eff_sb, shb_sb=shb_sb, identb=identb, identf=identf, ones_bf=ones_bf, ones_f=ones_f,
                      grstd=grstd, w_out=w_out, w_up=w_up, w_down=w_down, gates_scr=gates_scr, c_oh=c_oh, c_caus=c_caus,
                      qT_scr=qT_scr, kT_scr=kT_scr, v_scr=v_scr, ycl_scr=ycl_scr, x1_scr=x1_scr, rstd_from_ssq=rstd_from_ssq)
            if "B" in phases:
                phase_B(nc, P, top, sb, ps, l, x_src, TT, nblk)
                P.barrier()
            if "C" in phases:
                phase_C(nc, P, top, sb, ps, l, x_dst, TT, nblk)
                P.barrier()
        P.emit()
    return nc, P


def phase_A(nc, P, top, sb0, ps0, l, x_src, T, nblk):
    A = P.add
    sb = lambda es, name, shape, dt: sb0(es, f"{name}_A{l}", shape, dt)
    ps = lambda es, name, shape, dt: ps0(es, f"{name}_A{l}", shape, dt)
    cols_sb, eff_sb, shb_sb, lruc_sb = T["cols_sb"], T["eff_sb"], T["shb_sb"], T["lruc_sb"]
    identb, identf, ones_bf, ones_f, grstd, epsc = T["identb"], T["identf"], T["ones_bf"], T["ones_f"], T["grstd"], T["epsc"]

    def rstd2(dst, ssq, n, rtags, wtags, tmp, tmptag):
        A("act", lambda e: e.activation(out=tmp, in_=ssq, func=AF.Ln, scale=1.0 / n, bias=epsc[:, 0:1]), reads=rtags, writes=[tmptag], n=8)
        A("act", lambda e: e.activation(out=dst, in_=tmp, func=AF.Exp, scale=-0.5), reads=[tmptag], writes=wtags, n=8)

    with ExitStack() as es:
        dbl = lambda name, shape, dt: [sb(es, f"{name}{i}", shape, dt) for i in range(2)]
        w_sb = sb(es, "w_in_sb", [128, 8, DIN], BF16)
        brow = sb(es, "b_in_row", [1, 1536], BF16)
        bcol = sb(es, "b_in_col", [128, 10], F32)
        gq = sb(es, "gq", [128, 64], F32)
        gk = sb(es, "gk", [128, 64], F32)
        negm = sb(es, "negm", [128, 16, 16], F32)
        kmeanT = sb(es, "kmeanT", [128, 4, 16], F32)
        wabd = sb(es, "wabd", [128, 2, 2, 128], BF16)
        wtmp = sb(es, "wtmp", [128, 2, 2, 64], F32)
        x_sbs = Rot([sb(es, f"xA{i}", [128, D], F32) for i in range(2)], "xA")
        junkx = sb(es, "junkx", [128, D], BF16)
        junkq = dbl("junkq", [128, 512], F32)
        junkk = dbl("junkk", [128, 512], F32)
        xn = dbl("xn", [128, D], BF16)
        xnT = dbl("xnT", [128, 8, 256], BF16)
        stx = dbl("stx", [128, 4], F32)
        stq = dbl("stq", [128, 3, 8], F32)
        stk = dbl("stk", [128, 3, 8], F32)
        stg = dbl("stg", [128, 4], F32)
        qf = dbl("qf", [128, 512], F32)
        kf = dbl("kf", [128, 512], F32)
        kb = dbl("kb", [128, 512], BF16)
        qaug = dbl("qaug", [128, 8, 80], BF16)
        qfT = dbl("qfT", [128, 4, 128], F32)
        gm = dbl("gm", [128, 8, 16], F32)
        m8 = dbl("m8", [128, 8, 8], F32)
        msk = dbl("msk", [128, 8, 16], F32)
        v_sbs = Rot([sb(es, f"vA{i}", [128, 8, 65], BF16) for i in range(2)], "vA")
        kT_sbs = Rot([sb(es, f"kTA{i}", [64, 8, 128], BF16) for i in range(2)], "kTA")
        qT_sbs = Rot([sb(es, f"qTA{i}", [80, 8, 128], BF16) for i in range(2)], "qTA")
        fmS = dbl("fmS", [128, 10, 256], F32)
        cu = sb(es, "cu", [128, 2, 258], F32)
        lx = sb(es, "lx", [128, 2, 259], F32)
        hb = sb(es, "hb", [128, 2, 257], F32)
        ct = dbl("ct", [128, 256], F32)
        cy = dbl("cy", [128, 256], F32)
        lt = [[sb(es, f"lt{lc}_{k}", [128, 256], F32) for k in range(7)] for lc in range(2)]
        xrb = dbl("xrb", [128, 256], BF16)
        ysq = dbl("ysq", [128, 4, 256], BF16)
        ycl_sbs = Rot([sb(es, f"ycl{i}", [128, 4, 256], BF16) for i in range(2)], "ycl")
        tp_ps = ps(es, "tp_ps", [128, 8, 128], BF16)
        qkv_ps = [ps(es, f"qkv_ps{i}", [128, 512], F32) for i in range(3)]
        fm_pss = Rot([ps(es, f"fm_ps{i}", [128, 512], F32) for i in range(2)], "fm_ps")
        misc_ps = ps(es, "misc_ps", [128, 512], F32)
        sm_ps = ps(es, "sm_ps", [128, 512], F32)
        gate_ps = sm_ps[:, 0:128].rearrange("p (h n) -> p h n", h=8)
        km_ps = sm_ps[:, 128:132]
        ss_ps = sm_ps[:, 136:138]
        bc_ps = sm_ps[:, 144:154]

        for c in range(8):
            A("pool", lambda e, c=c: e.dma_start(out=w_sb[:, c, :], in_=T["w_in"][l, c * 128:(c + 1) * 128, :], max_dma_last_dim=4096),
              writes=[("w_in", c)], dma_key=("w", c), cost=12000)
        A("sp", lambda e: e.dma_start(out=gq[:], in_=T["qng"][l:l + 1, :].partition_broadcast(128)), writes=["gq"], dma_key="a0")
        A("sp", lambda e: e.dma_start(out=gk[:], in_=T["kng"][l:l + 1, :].partition_broadcast(128)), writes=["gk"], dma_key="a1")
        A("sp", lambda e: e.dma_start(out=negm[:].rearrange("p a b -> p (a b)"), in_=T["c_negmask"][0:1, :].partition_broadcast(128)),
          writes=["negm"], dma_key="a2")
        A("dve", lambda e: e.scalar_tensor_tensor(out=gk[:], in0=gk[:], scalar=0.125, in1=gq[:], op0=ALU.mult, op1=ALU.mult),
          reads=["gq", "gk"], writes=["gk"])
        A("pool", lambda e: e.memset(kmeanT[:], 0.0), writes=["kmeanT"])
        A("pool", lambda e: e.memset(cu[:], 0.0), writes=["cu0", "cu1"])
        A("pool", lambda e: e.memset(lx[:], 0.0), writes=["lx0", "lx1"])
        A("pool", lambda e: e.memset(hb[:], 0.0), writes=["hb0", "hb1"])
        for i in range(2):
            A("pool", lambda e, i=i: e.memset(qaug[i][:], 0.0), writes=[f"qaug_b{i}", f"qaug_q{i}"])
        for k, vt in enumerate(v_sbs.t):
            A("pool", lambda e, vt=vt: e.memset(vt[:], 1.0), writes=[("vA", k)])
        A("pool", lambda e: e.memset(wabd[:], 0.0), writes=["wabd"])
        for g, wsrc in enumerate((T["lru_wa"], T["lru_wx"])):
            for hh in range(4):
                ch, hf = hh // 2, hh % 2
                A("sp", lambda e, g=g, wsrc=wsrc, hh=hh, ch=ch, hf=hf: e.dma_start(
                    out=wtmp[hf * 64:(hf + 1) * 64, g, ch, :], in_=wsrc[l, hh, :, :]), writes=[("wtmp", g, hh)], dma_key=("spk", g * 4 + hh))
                A("dve", lambda e, g=g, ch=ch, hf=hf: e.tensor_copy(out=wabd[hf * 64:(hf + 1) * 64, g, ch, hf * 64:(hf + 1) * 64],
                                                                   in_=wtmp[hf * 64:(hf + 1) * 64, g, ch, :]),
                  reads=[("wtmp", g, hh), "wabd"], writes=[("wabd", g, hh)])
        wabd_tags = [("wabd", g, hh) for g in range(2) for hh in range(4)]
        w_tags = [("w_in", c) for c in range(8)]

        for j in range(3):
            def mm(e, j=j):
                r = None
                for c in range(8):
                    r = e.matmul(qkv_ps[j][0:1, :], lhsT=shb_sb[:, l, 0, c:c + 1], rhs=w_sb[:, c, j * 512:(j + 1) * 512],
                                 start=(c == 0), stop=(c == 7))
                return r
            A("pe", mm, reads=w_tags + [("shb", l, 0)], writes=[("qkv_ps", j)])
            A("act", lambda e, j=j: e.copy(out=brow[:, j * 512:(j + 1) * 512], in_=qkv_ps[j][0:1, :]), reads=[("qkv_ps", j)], writes=["brow"], n=512)

        def mm(e):
            r = None
            for fc in range(10):
                for c in range(8):
                    r = e.matmul(bc_ps[:, fc:fc + 1], lhsT=w_sb[:, c, 1536 + fc * 128:1536 + (fc + 1) * 128],
                                 rhs=shb_sb[:, l, 0, c:c + 1], start=(c == 0), stop=(c == 7))
            return r
        A("pe", mm, reads=w_tags + [("shb", l, 0)], writes=["sm_ps"])
        A("dve", lambda e: e.tensor_copy(out=bcol[:], in_=bc_ps), reads=["sm_ps"], writes=["bcol"])
        for c in range(8):
            if c % 2 == 0:
                A("dve", lambda e, c=c: e.tensor_scalar(out=w_sb[:, c, :], in0=w_sb[:, c, :], scalar1=eff_sb[:, l, c:c + 1], scalar2=None, op0=ALU.mult),
                  reads=[("eff", l, 0, 0)], writes=[("w_in", c)], n=DIN)
            else:
                A("act", lambda e, c=c: e.activation(out=w_sb[:, c, :], in_=w_sb[:, c, :], func=AF.Copy, scale=eff_sb[:, l, c:c + 1]),
                  reads=[("eff", l, 0, 0)], writes=[("w_in", c)], n=DIN)

        def h3(ap):
            return ap.rearrange("p (h d) -> p h d", h=8)

        def tiles(st):
            bp = st % 2
            for i in range(2):
                t = 2 * st + i
                xt, xtag = x_sbs.next()
                A("sp", lambda e, xt=xt, t=t: e.dma_start(out=xt[:], in_=x_src[t * 128:(t + 1) * 128, :]), writes=[xtag], dma_key=xtag)
                A("act", lambda e, xt=xt, i=i: e.activation(out=junkx[:], in_=xt[:], func=AF.Square, accum_out=stx[i][:, 0:1]),
                  reads=[xtag], writes=["junkx", f"stx{i}"], n=D)
                rstd2(stx[i][:, 2:3], stx[i][:, 0:1], D, [f"stx{i}"], [f"stxr{i}"], stx[i][:, 1:2], f"stxt{i}")
                A("dve", lambda e, xt=xt, i=i: e.tensor_scalar(out=xn[i][:], in0=xt[:], scalar1=stx[i][:, 2:3], scalar2=None, op0=ALU.mult),
                  reads=[xtag, f"stxr{i}"], writes=[f"xn{i}"], n=D)

                def tr(e, i=i):
                    r = None
                    for c in range(8):
                        r = e.transpose(out=tp_ps[:, c, :], in_=xn[i][:, c * 128:(c + 1) * 128], identity=identb[:])
                    return r
                A("pe", tr, reads=[f"xn{i}", "identb"], writes=["tp_ps"])
                A("act", lambda e, i=i, bp=bp: e.copy(out=xnT[bp][:, :, i * 128:(i + 1) * 128], in_=tp_ps[:]), reads=["tp_ps"], writes=[("xnT", bp, i)], n=D)
                for j in range(3):
                    def mm(e, j=j, i=i, bp=bp):
                        for c in range(8):
                            e.matmul(qkv_ps[j][:], lhsT=xnT[bp][:, c, i * 128:(i + 1) * 128], rhs=w_sb[:, c, j * 512:(j + 1) * 512],
                                     start=(c == 0), stop=False)
                        return e.matmul(qkv_ps[j][:], lhsT=ones_bf[0:1, :], rhs=brow[0:1, j * 512:(j + 1) * 512], start=False, stop=True)
                    A("pe", mm, reads=w_tags + [("xnT", bp, i), "brow", "ones_bf"], writes=[("qkv_ps", j)], cost=2200)
                A("act", lambda e, i=i: e.activation(out=junkq[i][:], in_=qkv_ps[0][:], func=AF.Square), reads=[("qkv_ps", 0)], writes=[f"junkq{i}"], n=512)
                A("dve", lambda e, i=i: e.tensor_reduce(out=stq[i][:, 0, :], in_=h3(junkq[i][:]), axis=AX.X, op=ALU.add),
                  reads=[f"junkq{i}"], writes=[f"stq{i}"], n=512)
                rstd2(stq[i][:, 2, :], stq[i][:, 0, :], 64, [f"stq{i}"], [f"stqr{i}"], stq[i][:, 1, :], f"stqt{i}")
                A("dve", lambda e, i=i: e.tensor_tensor(out=h3(qf[i][:]), in0=h3(qkv_ps[0][:]),
                                                        in1=stq[i][:, 2, :].unsqueeze(2).to_broadcast([128, 8, 64]), op=ALU.mult),
                  reads=[("qkv_ps", 0), f"stqr{i}"], writes=[f"qf{i}"], n=512)
                A("act", lambda e, i=i: e.copy(out=qaug[i][:, :, 0:64], in_=h3(qf[i][:])), reads=[f"qf{i}"], writes=[f"qaug_q{i}"], n=512)
                A("act", lambda e, i=i: e.activation(out=junkk[i][:], in_=qkv_ps[1][:], func=AF.Square), reads=[("qkv_ps", 1)], writes=[f"junkk{i}"], n=512)
                A("dve", lambda e, i=i: e.tensor_reduce(out=stk[i][:, 0, :], in_=h3(junkk[i][:]), axis=AX.X, op=ALU.add),
                  reads=[f"junkk{i}"], writes=[f"stk{i}"], n=512)
                rstd2(stk[i][:, 2, :], stk[i][:, 0, :], 64, [f"stk{i}"], [f"stkr{i}"], stk[i][:, 1, :], f"stkt{i}")
                A("dve", lambda e, i=i: e.tensor_tensor(out=h3(kf[i][:]), in0=h3(qkv_ps[1][:]),
                                                        in1=stk[i][:, 2, :].unsqueeze(2).to_broadcast([128, 8, 64]), op=ALU.mult),
                  reads=[("qkv_ps", 1), f"stkr{i}"], writes=[f"kf{i}"], n=512)
                A("dve", lambda e, i=i: e.tensor_tensor(out=h3(kf[i][:]), in0=h3(kf[i][:]),
                                                        in1=gk[:].unsqueeze(1).to_broadcast([128, 8, 64]), op=ALU.mult),
                  reads=[f"kf{i}", "gk"], writes=[f"kf{i}"], n=512)
                A("act", lambda e, i=i: e.copy(out=kb[i][:], in_=kf[i][:]), reads=[f"kf{i}"], writes=[f"kb{i}"], n=512)
                vt, vtag = v_sbs.next()
                A("act", lambda e, vt=vt: e.copy(out=vt[:, :, 0:64], in_=h3(qkv_ps[2][:])), reads=[("qkv_ps", 2)], writes=[vtag], n=512)
                A("sp", lambda e, vt=vt, t=t: e.dma_start(out=T["v_scr"][t, :, :], in_=vt[:].rearrange("p h d -> p (h d)")),
                  reads=[vtag], writes=[("v_scr", t)], dma_key=("st",) + vtag)
                def mm(e, i=i):
                    r = None
                    for cp in range(4):
                        r = e.matmul(km_ps[:, cp:cp + 1], lhsT=kf[i][:, cp * 128:(cp + 1) * 128], rhs=ones_f[:, 0:1], start=True, stop=True)
                    return r
                A("pe", mm, reads=[f"kf{i}", "ones_f"], writes=["sm_ps"], cost=900)
                if i == 0:
                    A("dve", lambda e, st=st: e.tensor_scalar(out=kmeanT[:, :, st], in0=km_ps, scalar1=1.0 / 256, scalar2=None, op0=ALU.mult),
                      reads=["sm_ps"], writes=["kmeanT"], n=4)
                else:
                    A("dve", lambda e, st=st: e.scalar_tensor_tensor(out=kmeanT[:, :, st], in0=km_ps, scalar=1.0 / 256, in1=kmeanT[:, :, st],
                                                                    op0=ALU.mult, op1=ALU.add),
                      reads=["sm_ps"], writes=["kmeanT"], n=4)
                kTt, kTtag = kT_sbs.next()

                def tr(e, i=i):
                    r = None
                    for h in range(8):
                        r = e.transpose(out=tp_ps[0:64, h, :], in_=kb[i][:, h * 64:(h + 1) * 64], identity=identb[:])
                    return r
                A("pe", tr, reads=[f"kb{i}", "identb"], writes=["tp_ps"])
                A("act", lambda e, kTt=kTt: e.copy(out=kTt[:], in_=tp_ps[0:64, :, :]), reads=["tp_ps"], writes=[kTtag], n=D)
                A("sp", lambda e, kTt=kTt, t=t: e.dma_start(out=T["kT_scr"][:, :, t * 128:(t + 1) * 128], in_=kTt[:]),
                  reads=[kTtag], writes=[("kT_scr", t)], dma_key=("st",) + kTtag)
                if st >= 1:
                    def tr(e, i=i):
                        r = None
                        for cp in range(4):
                            r = e.transpose(out=misc_ps[:, cp * 128:(cp + 1) * 128], in_=qf[i][:, cp * 128:(cp + 1) * 128], identity=identf[:])
                        return r
                    A("pe", tr, reads=[f"qf{i}", "identf"], writes=["misc_ps"], cost=900)
                    A("act", lambda e, i=i: e.copy(out=qfT[i][:].rearrange("p a b -> p (a b)"), in_=misc_ps[:]), reads=["misc_ps"], writes=[f"qfT{i}"], n=512)

                    def mm(e, i=i):
                        r = None
                        for h in range(8):
                            pb = (h % 2) * 64
                            r = e.matmul(gate_ps[:, h, :], lhsT=qfT[i][pb:pb + 64, h // 2, :], rhs=kmeanT[pb:pb + 64, h // 2, :], start=True, stop=True)
                        return r
                    A("pe", mm, reads=[f"qfT{i}", "kmeanT"], writes=["sm_ps"], cost=1500)
                    A("dve", lambda e, st=st, i=i: e.tensor_tensor(out=gm[i][:], in0=gate_ps, in1=negm[:, st:st + 1, :].to_broadcast([128, 8, 16]), op=ALU.add),
                      reads=["sm_ps", "negm"], writes=[f"gm{i}"], n=128)
                    for h in range(8):
                        A("dve", lambda e, h=h, i=i: e.max(out=m8[i][:, h, :], in_=gm[i][:, h, :]), reads=[f"gm{i}"], writes=[(f"m8{i}", h)], n=16)
                    A("dve", lambda e, i=i: e.tensor_tensor(out=msk[i][:], in0=gm[i][:], in1=m8[i][:, :, 2:3].to_broadcast([128, 8, 16]), op=ALU.is_ge),
                      reads=[f"gm{i}"] + [(f"m8{i}", h) for h in range(8)], writes=[f"msk{i}"], n=128)
                    A("dve", lambda e, i=i: e.tensor_scalar(out=qaug[i][:, :, 64:80], in0=msk[i][:], scalar1=BIG, scalar2=-BIG, op0=ALU.mult, op1=ALU.add),
                      reads=[f"msk{i}"], writes=[f"qaug_b{i}"], n=128)
                qTt, qTtag = qT_sbs.next()

                def tr(e, i=i):
                    r = None
                    for h in range(8):
                        r = e.transpose(out=tp_ps[0:80, h, :], in_=qaug[i][:, h, :], identity=identb[:])
                    return r
                A("pe", tr, reads=[f"qaug_q{i}", f"qaug_b{i}", "identb"], writes=["tp_ps"])
                A("act", lambda e, qTt=qTt: e.copy(out=qTt[:], in_=tp_ps[0:80, :, :]), reads=["tp_ps"], writes=[qTtag], n=D)
                A("sp", lambda e, qTt=qTt, t=t: e.dma_start(out=T["qT_scr"][:, :, t * 128:(t + 1) * 128], in_=qTt[:]),
                  reads=[qTtag], writes=[("qT_scr", t)], dma_key=("st",) + qTtag)

        def fm(st):
            bp = st % 2
            F = fmS[bp]
            ycl_t, ycl_tag = ycl_sbs.next()
            for fc in (6, 7, 8, 9, 2, 4, 0, 3, 5, 1):
                bank, btag = fm_pss.next()

                def mm(e, bank=bank, fc=fc, bp=bp):
                    r = None
                    for c in range(8):
                        r = e.matmul(bank[:, 0:256], lhsT=w_sb[:, c, 1536 + fc * 128:1536 + (fc + 1) * 128], rhs=xnT[bp][:, c, :],
                                     start=(c == 0), stop=(c == 7))
                    return r
                A("pe", mm, reads=w_tags + [("xnT", bp, 0), ("xnT", bp, 1)], writes=[btag], cost=1100)
                if fc in (6, 7):
                    lc = fc - 6
                    A("act", lambda e, bank=bank, lc=lc, fc=fc: e.activation(out=lx[:, lc, 3:259], in_=bank[:, 0:256], func=AF.Identity, bias=bcol[:, fc:fc + 1]),
                      reads=[btag, "bcol"], writes=[f"lx{lc}"])
                else:
                    A("act", lambda e, bank=bank, fc=fc, F=F: e.activation(out=F[:, fc, :], in_=bank[:, 0:256], func=AF.Identity, bias=bcol[:, fc:fc + 1]),
                      reads=[btag, "bcol"], writes=[("fmS", bp, fc)])
            for lc in range(2):
                xr, ea, sa, ei, uu, gz, yy = lt[lc]
                tg = lambda nm, lc=lc: f"{nm}{lc}"
                lxb, hbb = lx[:, lc, :], hb[:, lc, :]
                cw = [cols_sb[:, l, 62 + lc * 4 + k:62 + lc * 4 + k + 1] for k in range(4)]
                cb = cols_sb[:, l, 70 + lc:71 + lc]
                nba = lruc_sb[:, l, 0 + lc:1 + lc]
                nbx = lruc_sb[:, l, 2 + lc:3 + lc]
                sp8 = lruc_sb[:, l, 4 + lc:5 + lc]
                G = F[:, 8 + lc, :]
                gtag = ("fmS", bp, 8 + lc)
                A("dve", lambda e, lxb=lxb, cw=cw, cb=cb, xr=xr: e.tensor_scalar(out=xr[:], in0=lxb[:, 0:256], scalar1=cw[0], scalar2=cb, op0=ALU.mult, op1=ALU.add),
                  reads=[tg("lx")], writes=[tg("xr")])
                for k in range(1, 4):
                    A("dve", lambda e, lxb=lxb, cw=cw, k=k, xr=xr: e.scalar_tensor_tensor(out=xr[:], in0=lxb[:, k:k + 256], scalar=cw[k], in1=xr[:],
                                                                                      op0=ALU.mult, op1=ALU.add),
                      reads=[tg("lx"), tg("xr")], writes=[tg("xr")])
                A("dve", lambda e, lxb=lxb: e.tensor_copy(out=lxb[:, 0:3], in_=lxb[:, 256:259]), reads=[tg("lx"), tg("xr")], writes=[tg("lx")], n=3)
                A("act", lambda e, lc=lc, xr=xr: e.copy(out=xrb[lc][:], in_=xr[:]), reads=[tg("xr")], writes=[tg("xrb")])
                rb, rbtag = fm_pss.next()
                A("pe", lambda e, lc=lc, rb=rb: e.matmul(rb[:, 0:256], lhsT=wabd[:, 0, lc, :], rhs=xrb[lc][:], start=True, stop=True),
                  reads=[tg("xrb")] + wabd_tags, writes=[rbtag], cost=200)
                A("act", lambda e, nba=nba, rb=rb, ea=ea: e.activation(out=ea[:], in_=rb[:, 0:256], func=AF.Exp, scale=-1.0, bias=nba),
                  reads=[rbtag], writes=[tg("ea")])
                ib, ibtag = fm_pss.next()
                A("pe", lambda e, lc=lc, ib=ib: e.matmul(ib[:, 0:256], lhsT=wabd[:, 1, lc, :], rhs=xrb[lc][:], start=True, stop=True),
                  reads=[tg("xrb")] + wabd_tags, writes=[ibtag], cost=200)
                A("act", lambda e, nbx=nbx, ib=ib, ei=ei: e.activation(out=ei[:], in_=ib[:, 0:256], func=AF.Exp, scale=-1.0, bias=nbx),
                  reads=[ibtag], writes=[tg("ei")])
                A("act", lambda e, ea=ea: e.activation(out=ea[:], in_=ea[:], func=AF.Ln, bias=1.0), reads=[tg("ea")], writes=[tg("ea")])
                A("act", lambda e, ea=ea: e.activation(out=ea[:], in_=ea[:], func=AF.Exp, scale=-1.0), reads=[tg("ea")], writes=[tg("ea")])
                A("act", lambda e, ea=ea, sp8=sp8: e.activation(out=ea[:], in_=ea[:], func=AF.Exp, scale=sp8), reads=[tg("ea")], writes=[tg("ea")])
                A("act", lambda e, ea=ea, sa=sa: e.activation(out=sa[:], in_=ea[:], func=AF.Square), reads=[tg("ea")], writes=[tg("sa")])
                A("act", lambda e, sa=sa: e.activation(out=sa[:], in_=sa[:], func=AF.Ln, scale=-1.0, bias=1.0), reads=[tg("sa")], writes=[tg("sa")])
                A("act", lambda e, sa=sa: e.activation(out=sa[:], in_=sa[:], func=AF.Exp, scale=0.5), reads=[tg("sa")], writes=[tg("sa")])
                A("act", lambda e, ei=ei: e.activation(out=ei[:], in_=ei[:], func=AF.Ln, bias=1.0), reads=[tg("ei")], writes=[tg("ei")])
                A("act", lambda e, ei=ei: e.activation(out=ei[:], in_=ei[:], func=AF.Exp, scale=-1.0), reads=[tg("ei")], writes=[tg("ei")])
                A("dve", lambda e, ei=ei, xr=xr, uu=uu: e.tensor_tensor(out=uu[:], in0=ei[:], in1=xr[:], op=ALU.mult), reads=[tg("ei"), tg("xr")], writes=[tg("uu")])
                A("dve", lambda e, sa=sa, uu=uu: e.tensor_tensor(out=uu[:], in0=uu[:], in1=sa[:], op=ALU.mult), reads=[tg("uu"), tg("sa")], writes=[tg("uu")])
                A("dve", lambda e, hbb=hbb, ea=ea, uu=uu: e.tensor_tensor_scan(out=hbb[:, 1:257], data0=ea[:], data1=uu[:], initial=hbb[:, 0:1],
                                                                              op0=ALU.mult, op1=ALU.add),
                  reads=[tg("ea"), tg("uu"), tg("hb")], writes=[tg("hbh")], n=512)
                A("act", lambda e, G=G, gz=gz: e.activation(out=gz[:], in_=G, func=AF.Square), reads=[gtag], writes=[tg("gz")])
                A("dve", lambda e, gz=gz: e.tensor_scalar(out=gz[:], in0=gz[:], scalar1=0.044715, scalar2=1.0, op0=ALU.mult, op1=ALU.add),
                  reads=[tg("gz")], writes=[tg("gz")])
                A("dve", lambda e, gz=gz, G=G: e.tensor_tensor(out=gz[:], in0=gz[:], in1=G, op=ALU.mult), reads=[tg("gz"), gtag], writes=[tg("gz")])
                A("act", lambda e, gz=gz: e.activation(out=gz[:], in_=gz[:], func=AF.Exp, scale=-1.5957691216057308), reads=[tg("gz")], writes=[tg("gz")])
                A("act", lambda e, gz=gz: e.activation(out=gz[:], in_=gz[:], func=AF.Ln, bias=1.0), reads=[tg("gz")], writes=[tg("gz")])
                A("act", lambda e, gz=gz: e.activation(out=gz[:], in_=gz[:], func=AF.Exp, scale=-1.0), reads=[tg("gz")], writes=[tg("gz")])
                A("dve", lambda e, gz=gz, G=G: e.tensor_tensor(out=gz[:], in0=gz[:], in1=G, op=ALU.mult), reads=[tg("gz"), gtag], writes=[tg("gz")])
                A("dve", lambda e, hbb=hbb, gz=gz, yy=yy: e.tensor_tensor(out=yy[:], in0=hbb[:, 1:257], in1=gz[:], op=ALU.mult),
                  reads=[tg("hbh"), tg("gz")], writes=[tg("yy")])
                A("dve", lambda e, hbb=hbb: e.tensor_copy(out=hbb[:, 0:1], in_=hbb[:, 256:257]), reads=[tg("hbh"), tg("yy")], writes=[tg("hb")], n=1)
                A("act", lambda e, lc=lc, ycl_t=ycl_t, yy=yy: e.copy(out=ycl_t[:, 2 + lc, :], in_=yy[:]), reads=[tg("yy")], writes=[ycl_tag + (2 + lc,)])
                A("act", lambda e, lc=lc, yy=yy, bp=bp: e.activation(out=ysq[bp][:, 2 + lc, :], in_=yy[:], func=AF.Square), reads=[tg("yy")], writes=[("ysq", bp, 2 + lc)])
            for cc in range(2):
                cub = cu[:, cc, :]
                tg = lambda nm, cc=cc: f"{nm}{cc}"
                w0 = cols_sb[:, l, 56 + cc * 3 + 0:56 + cc * 3 + 1]
                w1 = cols_sb[:, l, 56 + cc * 3 + 1:56 + cc * 3 + 2]
                w2 = cols_sb[:, l, 56 + cc * 3 + 2:56 + cc * 3 + 3]
                Bt, Ct, Ut = ("fmS", bp, 0 + cc), ("fmS", bp, 2 + cc), ("fmS", bp, 4 + cc)
                A("dve", lambda e, cub=cub, cc=cc, F=F: e.tensor_tensor(out=cub[:, 2:258], in0=F[:, 2 + cc, :], in1=F[:, 4 + cc, :], op=ALU.mult),
                  reads=[Ct, Ut], writes=[tg("cu")])
                A("dve", lambda e, cub=cub, w0=w0, cc=cc: e.tensor_scalar(out=ct[cc][:], in0=cub[:, 0:256], scalar1=w0, scalar2=None, op0=ALU.mult),
                  reads=[tg("cu")], writes=[tg("ct")])
                A("dve", lambda e, cub=cub, w1=w1, cc=cc: e.scalar_tensor_tensor(out=ct[cc][:], in0=cub[:, 1:257], scalar=w1, in1=ct[cc][:], op0=ALU.mult, op1=ALU.add),
                  reads=[tg("cu"), tg("ct")], writes=[tg("ct")])
                A("dve", lambda e, cub=cub, w2=w2, cc=cc: e.scalar_tensor_tensor(out=ct[cc][:], in0=cub[:, 2:258], scalar=w2, in1=ct[cc][:], op0=ALU.mult, op1=ALU.add),
                  reads=[tg("cu"), tg("ct")], writes=[tg("ct")])
                A("dve", lambda e, cub=cub: e.tensor_copy(out=cub[:, 0:2], in_=cub[:, 256:258]), reads=[tg("cu"), tg("ct")], writes=[tg("cu")], n=2)
                A("dve", lambda e, cc=cc, F=F: e.tensor_tensor(out=cy[cc][:], in0=F[:, 0 + cc, :], in1=ct[cc][:], op=ALU.mult),
                  reads=[Bt, tg("ct")], writes=[tg("cy")])
                A("act", lambda e, cc=cc, ycl_t=ycl_t: e.copy(out=ycl_t[:, cc, :], in_=cy[cc][:]), reads=[tg("cy")], writes=[ycl_tag + (cc,)])
                A("act", lambda e, cc=cc, bp=bp: e.activation(out=ysq[bp][:, cc, :], in_=cy[cc][:], func=AF.Square), reads=[tg("cy")], writes=[("ysq", bp, cc)])
            for i in range(2):
                t = 2 * st + i

                def mm(e, i=i, bp=bp):
                    r = None
                    for g in range(2):
                        for c2 in range(2):
                            r = e.matmul(ss_ps[:, g:g + 1], lhsT=ysq[bp][:, 2 * g + c2, i * 128:(i + 1) * 128], rhs=ones_bf[:, 0:1],
                                         start=(c2 == 0), stop=(c2 == 1))
                    return r
                A("pe", mm, reads=[("ysq", bp, c) for c in range(4)] + ["ones_bf"], writes=["sm_ps"], cost=500)
                rstd2(grstd[:, t, :], ss_ps, 256, ["sm_ps"], [("grstd", t)], stg[i][:, 0:2], f"stg{i}")
            A("sp", lambda e, ycl_t=ycl_t, st=st: e.dma_start(
                out=T["ycl_scr"].rearrange("(c p) t -> p c t", p=128)[:, :, st * 256:(st + 1) * 256], in_=ycl_t[:]),
              reads=[ycl_tag + (c,) for c in range(4)], writes=[("ycl_scr", st)], dma_key=("st",) + ycl_tag)

        tiles(0)
        for st in range(nblk):
            if st + 1 < nblk:
                tiles(st + 1)
            fm(st)


def phase_B(nc, P, top, sb0, ps0, l, x_src, T, nblk):
    A = P.add
    sb = lambda es, name, shape, dt: sb0(es, f"{name}_B{l}", shape, dt)
    ps = lambda es, name, shape, dt: ps0(es, f"{name}_B{l}", shape, dt)
    cols_sb, identb, ones_bf, ones_f, grstd = T["cols_sb"], T["identb"], T["ones_bf"], T["ones_f"], T["grstd"]
    rstd_from_ssq = T["rstd_from_ssq"]
    with ExitStack() as es:
        kT = sb(es, "kT_all", [80, 8, S], BF16)
        v_all = sb(es, "v_all", [128, NT, 520], BF16)
        wo = sb(es, "w_out_sb", [128, 8, D], BF16)
        g1bc = sb(es, "g1bc", [128, D], F32)
        caus = sb(es, "caus", [128, 2, 256], BF16)
        qT_blks = Rot([sb(es, f"qTb{i}", [80, 8, 256], BF16) for i in range(2)], "qTb")
        ycl_blks = Rot([sb(es, f"yclb{i}", [128, 4, 256], BF16) for i in range(2)], "yclb")
        x_sbs = Rot([sb(es, f"xB{i}", [128, D], F32) for i in range(4)], "xB")
        p_sbs = Rot([sb(es, f"pB{i}", [128, 2, 256], BF16) for i in range(3)], "pB")
        o_sbs = Rot([sb(es, f"oB{i}", [65, 256], F32) for i in range(2)], "oB")
        rdens = Rot([sb(es, f"rdB{i}", [65, 256], F32) for i in range(2)], "rdB")
        yattn = sb(es, "yattn", [128, 4, 256], F32)
        ya_bf = sb(es, "ya_bf", [128, 4, 256], BF16)
        ysq = sb(es, "ysqB", [128, 4, 256], BF16)
        stb = sb(es, "stb", [128, 8], F32)
        s_pss = Rot([ps(es, f"s_ps{i}", [128, 2, 256], F32) for i in range(2)], "s_ps")
        oT_pss = Rot([ps(es, f"oT_ps{i}", [128, 512], F32) for i in range(2)], "oT_ps")
        bc_ps = ps(es, "bc_ps", [128, 512], F32)
        sm_ps = ps(es, "smB_ps", [128, 512], F32)
        op_pss = Rot([ps(es, f"op_ps{i}", [128, 512], F32) for i in range(2)], "op_ps")

        for c in range(8):
            A("pool", lambda e, c=c: e.dma_start(out=wo[:, c, :], in_=T["w_out"][l, c * 128:(c + 1) * 128, :]),
              writes=[("wo", c)], dma_key=("w", c))
        A("sp", lambda e: e.dma_start(out=g1bc[:], in_=T["gates_scr"][l, 0:1, :].partition_broadcast(128)), writes=["g1bc"], dma_key="a0")
        A("sp", lambda e: e.dma_start(out=caus[:], in_=T["c_caus"][:, :, :]), writes=["caus"], dma_key="a1")
        for c in range(8):
            eng = "dve"
            A(eng, lambda e, c=c: e.scalar_tensor_tensor(out=wo[:, c, :], in0=wo[:, c, :], scalar=cols_sb[:, l, 48 + c:49 + c], in1=g1bc[:],
                                                         op0=ALU.mult, op1=ALU.mult),
              reads=["g1bc"], writes=[("wo", c)])
        wo_tags = [("wo", c) for c in range(8)]
        for h in range(8):
            A("sp", lambda e, h=h: e.dma_start(out=kT[64:80, h, :], in_=T["c_oh"][:, :]), writes=[("kToh", h)], dma_key=("spk", h))
        oh_tags = [("kToh", h) for h in range(8)]
        for j in range(nblk):
            A("sp", lambda e, j=j: e.dma_start(out=kT[0:64, :, j * 256:(j + 1) * 256], in_=T["kT_scr"][:, :, j * 256:(j + 1) * 256]),
              writes=[("kT", j)], dma_key=("kT", j % 4))
            A("sp", lambda e, j=j: e.dma_start(out=v_all[:, 2 * j:2 * j + 2, :], in_=T["v_scr"][2 * j:2 * j + 2, :, :].rearrange("t p f -> p t f")),
              writes=[("v", j)], dma_key=("v", j % 4))

        for qb in range(nblk):
            qTb, qtag = qT_blks.next()
            yclb, ytag = ycl_blks.next()
            A("sp", lambda e, qTb=qTb, qb=qb: e.dma_start(out=qTb[:], in_=T["qT_scr"][:, :, qb * 256:(qb + 1) * 256]), writes=[qtag], dma_key=qtag)
            A("sp", lambda e, yclb=yclb, qb=qb: e.dma_start(
                out=yclb[:], in_=T["ycl_scr"].rearrange("(c p) t -> p c t", p=128)[:, :, qb * 256:(qb + 1) * 256]), writes=[ytag], dma_key=ytag)
            xts = []
            for i in range(2):
                t = 2 * qb + i
                xt, xtag = x_sbs.next()
                A("sp", lambda e, xt=xt, t=t: e.dma_start(out=xt[:], in_=x_src[t * 128:(t + 1) * 128, :]), writes=[xtag], dma_key=xtag)
                xts.append((xt, xtag))
            for h in range(8):
                oT, otag = oT_pss.next()
                for j in range(qb + 1):
                    own = (j == qb)
                    sp_, stag = s_pss.next()
                    if not own:
                        def mm(e, sp_=sp_, j=j, h=h, qTb=qTb):
                            r = None
                            for kk in range(2):
                                r = e.matmul(sp_[:, kk, :], lhsT=kT[0:80, h, (2 * j + kk) * 128:(2 * j + kk + 1) * 128], rhs=qTb[0:80, h, :],
                                             start=True, stop=True)
                            return r
                        A("pe", mm, reads=[("kT", j), ("kToh", h), qtag], writes=[stag])
                    else:
                        def mm(e, sp_=sp_, j=j, h=h, qTb=qTb):
                            r = None
                            for kk in range(2):
                                e.matmul(sp_[:, kk, :], lhsT=kT[0:64, h, (2 * j + kk) * 128:(2 * j + kk + 1) * 128], rhs=qTb[0:64, h, :],
                                         start=True, stop=False)
                                r = e.matmul(sp_[:, kk, :], lhsT=identb[:], rhs=caus[:, kk, :], start=False, stop=True)
                            return r
                        A("pe", mm, reads=[("kT", j), qtag, "caus", "identb"], writes=[stag])
                    pt, ptag = p_sbs.next()
                    A("act", lambda e, pt=pt, sp_=sp_: e.activation(out=pt[:], in_=sp_[:], func=AF.Exp), reads=[stag], writes=[ptag])

                    def mm(e, pt=pt, oT=oT, j=j, h=h, qb=qb):
                        r = None
                        for kk in range(2):
                            r = e.matmul(oT[0:65, 0:256], lhsT=v_all[:, 2 * j + kk, h * 65:(h + 1) * 65], rhs=pt[:, kk, :],
                                         start=(j == 0 and kk == 0), stop=(j == qb and kk == 1))
                        return r
                    A("pe", mm, reads=[("v", j), ptag], writes=[otag])
                ot, ottag = o_sbs.next()
                rd, rdtag = rdens.next()
                A("act", lambda e, ot=ot, oT=oT: e.copy(out=ot[:], in_=oT[0:65, 0:256]), reads=[otag], writes=[ottag])
                A("dve", lambda e, ot=ot, rd=rd: e.reciprocal(out=rd[64:65, :], in_=ot[64:65, :]), reads=[ottag], writes=[rdtag])
                A("pe", lambda e, rd=rd: e.matmul(bc_ps[0:64, 0:256], lhsT=ones_f[64:65, 0:64], rhs=rd[64:65, :], start=True, stop=True),
                  reads=[rdtag, "ones_f"], writes=["bc_ps"])
                pb = (h % 2) * 64
                A("dve", lambda e, ot=ot, pb=pb, h=h: e.tensor_tensor(out=yattn[pb:pb + 64, h // 2, :], in0=ot[0:64, :], in1=bc_ps[0:64, 0:256], op=ALU.mult),
                  reads=[ottag, "bc_ps"], writes=[("yattn", h)])
            ya_tags = [("yattn", h) for h in range(8)]
            A("pool", lambda e: e.tensor_copy(out=ya_bf[:], in_=yattn[:]), reads=ya_tags, writes=["ya_bf"])
            A("act", lambda e: e.activation(out=ysq[:], in_=yattn[:], func=AF.Square), reads=ya_tags, writes=["ysqB"])
            for i in range(2):
                t = 2 * qb + i
                xt, xtag = xts[i]

                def mm(e, i=i):
                    r = None
                    for c in range(4):
                        r = e.matmul(sm_ps[:, 0:1], lhsT=ysq[:, c, i * 128:(i + 1) * 128], rhs=ones_bf[:, 0:1], start=(c == 0), stop=(c == 3))
                    return r
                A("pe", mm, reads=["ysqB", "ones_bf"], writes=["smB_ps"])
                rstd_from_ssq(stb[:, 2:3], sm_ps[:, 0:1], 512, ["smB_ps"], ["stbr"], stb[:, 1:2], "stbt")
                for n in range(2):
                    nsl = slice(n * 512, (n + 1) * 512)
                    for g in range(3):
                        op_, optag = op_pss.next()
                        if g == 0:
                            srcs = [(ya_bf, c, c) for c in range(4)]
                            rtags = ["ya_bf"]
                            scal = stb[:, 2:3]
                            stag2 = ["stbr"]
                        else:
                            srcs = [(yclb, 2 * (g - 1) + c2, 4 + 2 * (g - 1) + c2) for c2 in range(2)]
                            rtags = [ytag]
                            scal = grstd[:, t, g - 1:g]
                            stag2 = []

                        def mm(e, op_=op_, srcs=srcs, i=i, nsl=nsl):
                            r = None
                            for k, (src, sc, wc) in enumerate(srcs):
                                r = e.matmul(op_[:], lhsT=src[:, sc, i * 128:(i + 1) * 128], rhs=wo[:, wc, nsl], start=(k == 0), stop=(k == len(srcs) - 1))
                            return r
                        A("pe", mm, reads=rtags + wo_tags, writes=[optag])
                        A("dve", lambda e, op_=op_, xt=xt, scal=scal, nsl=nsl: e.scalar_tensor_tensor(
                            out=xt[:, nsl], in0=op_[:], scalar=scal, in1=xt[:, nsl], op0=ALU.mult, op1=ALU.add),
                          reads=[optag, xtag] + stag2, writes=[xtag])
                A("sp", lambda e, xt=xt, t=t: e.dma_start(out=T["x1_scr"][t * 128:(t + 1) * 128, :], in_=xt[:]),
                  reads=[xtag], writes=[("x1_scr", t)], dma_key=("st",) + xtag)


def phase_C(nc, P, top, sb0, ps0, l, x_dst, T, nblk):
    A = P.add
    sb = lambda es, name, shape, dt: sb0(es, f"{name}_C{l}", shape, dt)
    ps = lambda es, name, shape, dt: ps0(es, f"{name}_C{l}", shape, dt)
    eff_sb, shb_sb, identb = T["eff_sb"], T["shb_sb"], T["identb"]
    rstd_from_ssq = T["rstd_from_ssq"]
    with ExitStack() as es:
        wu = sb(es, "w_up_sb", [128, 8, DFF], BF16)
        wd = sb(es, "w_dn_sb", [128, 32, D], BF16)
        bup = sb(es, "bup", [128, 32], F32)
        x_sbs = Rot([sb(es, f"xC{i}", [128, D], F32) for i in range(2)], "xC")
        junk = sb(es, "junkC", [128, D], BF16)
        xn = sb(es, "xnC", [128, D], BF16)
        xnT = sb(es, "xnTC", [128, 8, 256], BF16)
        hT = sb(es, "hT", [128, 32, 256], BF16)
        rts = Rot([sb(es, f"rt{i}", [128, 256], BF16) for i in range(3)], "rt")
        stc = sb(es, "stc", [128, 8], F32)
        tp_ps = ps(es, "tpC_ps", [128, 8, 128], BF16)
        up_pss = Rot([ps(es, f"up_ps{i}", [128, 512], F32) for i in range(3)], "up_ps")
        dn_pss = Rot([ps(es, f"dn_ps{i}", [128, 512], F32) for i in range(2)], "dn_ps")
        sm_ps = ps(es, "smC_ps", [128, 512], F32)

        for c in range(8):
            A("pool", lambda e, c=c: e.dma_start(out=wu[:, c, :], in_=T["w_up"][l, c * 128:(c + 1) * 128, :], max_dma_last_dim=4096),
              writes=[("wu", c)], dma_key=("w", c))
        for f in range(32):
            A("pool", lambda e, f=f: e.dma_start(out=wd[:, f, :], in_=T["w_down"][l, f * 128:(f + 1) * 128, :]),
              writes=[("wd", f)], dma_key=("wdk", f % 8))
        g2t, g2tag = x_sbs.t[1], ("xC", 1)
        A("sp", lambda e: e.dma_start(out=g2t[:], in_=T["gates_scr"][l, 1:2, :].partition_broadcast(128)), writes=[g2tag], dma_key=g2tag)
        wu_tags = [("wu", c) for c in range(8)]
        wd_tags = [("wd", f) for f in range(32)]

        def mm(e):
            r = None
            for f in range(32):
                for c in range(8):
                    r = e.matmul(sm_ps[:, f:f + 1], lhsT=wu[:, c, f * 128:(f + 1) * 128], rhs=shb_sb[:, l, 1, c:c + 1], start=(c == 0), stop=(c == 7))
            return r
        A("pe", mm, reads=wu_tags, writes=["smC_ps"])
        A("dve", lambda e: e.tensor_copy(out=bup[:], in_=sm_ps[:, 0:32]), reads=["smC_ps"], writes=["bup"])
        for c in range(8):
            A("act", lambda e, c=c: e.activation(out=wu[:, c, :], in_=wu[:, c, :], func=AF.Copy, scale=eff_sb[:, l, 16 + c:17 + c]),
              reads=[], writes=[("wu", c)], n=DFF)
        for f in range(32):
            eng = "pool" if f % 4 == 3 else "dve"
            A(eng, lambda e, f=f: e.tensor_tensor(out=wd[:, f, :], in0=wd[:, f, :], in1=g2t[:], op=ALU.mult), reads=[g2tag], writes=[("wd", f)], n=D)

        for st in range(nblk):
            xts = []
            for i in range(2):
                t = 2 * st + i
                xt, xtag = x_sbs.next()
                xts.append((xt, xtag))
                A("sp", lambda e, xt=xt, t=t: e.dma_start(out=xt[:], in_=T["x1_scr"][t * 128:(t + 1) * 128, :]), writes=[xtag], dma_key=xtag)
                A("act", lambda e, xt=xt: e.activation(out=junk[:], in_=xt[:], func=AF.Square, accum_out=stc[:, 0:1]),
                  reads=[xtag], writes=["junkC", "stc"])
                rstd_from_ssq(stc[:, 2:3], stc[:, 0:1], D, ["stc"], ["stcr"], stc[:, 1:2], "stct")
                A("dve", lambda e, xt=xt: e.tensor_scalar(out=xn[:], in0=xt[:], scalar1=stc[:, 2:3], scalar2=None, op0=ALU.mult),
                  reads=[xtag, "stcr"], writes=["xnC"])

                def tr(e):
                    r = None
                    for c in range(8):
                        r = e.transpose(out=tp_ps[:, c, :], in_=xn[:, c * 128:(c + 1) * 128], identity=identb[:])
                    return r
                A("pe", tr, reads=["xnC", "identb"], writes=["tpC_ps"])
                A("act", lambda e, i=i: e.copy(out=xnT[:, :, i * 128:(i + 1) * 128], in_=tp_ps[:]), reads=["tpC_ps"], writes=[("xnTC", i)])
            for f in range(32):
                up, uptag = up_pss.next()

                def mm(e, up=up, f=f):
                    r = None
                    for c in range(8):
                        r = e.matmul(up[:, 0:256], lhsT=wu[:, c, f * 128:(f + 1) * 128], rhs=xnT[:, c, :], start=(c == 0), stop=(c == 7))
                    return r
                A("pe", mm, reads=wu_tags + [("xnTC", 0), ("xnTC", 1)], writes=[uptag])
                rt, rttag = rts.next()
                A("dve", lambda e, up=up, rt=rt, f=f: e.tensor_scalar(out=rt[:], in0=up[:, 0:256], scalar1=bup[:, f:f + 1], scalar2=0.0, op0=ALU.add, op1=ALU.max),
                  reads=[uptag, "bup"], writes=[rttag])
                A("act", lambda e, rt=rt, f=f: e.activation(out=hT[:, f, :], in_=rt[:], func=AF.Square), reads=[rttag], writes=[("hT", f)])
            hT_tags = [("hT", f) for f in range(32)]
            for i in range(2):
                t = 2 * st + i
                xt, xtag = xts[i]
                for n in range(2):
                    nsl = slice(n * 512, (n + 1) * 512)
                    dn, dntag = dn_pss.next()

                    def mm(e, dn=dn, i=i, nsl=nsl):
                        r = None
                        for f in range(32):
                            r = e.matmul(dn[:], lhsT=hT[:, f, i * 128:(i + 1) * 128], rhs=wd[:, f, nsl], start=(f == 0), stop=(f == 31))
                        return r
                    A("pe", mm, reads=hT_tags + wd_tags, writes=[dntag])
                    A("dve", lambda e, dn=dn, xt=xt, nsl=nsl: e.tensor_tensor(out=xt[:, nsl], in0=dn[:], in1=xt[:, nsl], op=ALU.add),
                      reads=[dntag, xtag], writes=[xtag])
                A("sp", lambda e, xt=xt, t=t: e.dma_start(out=x_dst[t * 128:(t + 1) * 128, :], in_=xt[:]),
                  reads=[xtag], writes=[("x_dst", t)], dma_key=("st",) + xtag)


def _consts():
    bf = ml_dtypes.bfloat16
    identb = np.eye(128, dtype=np.float32).astype(bf)
    identf = np.eye(128, dtype=np.float32)
    oh = np.zeros((16, S), np.float32)
    for j in range(16):
        oh[j, j * 256:(j + 1) * 256] = 1.0
    negmask = np.zeros((16, 16), np.float32)
    for own in range(16):
        negmask[own, own:] = -1e30
    kk = np.arange(128)[:, None]
    qq = np.arange(128)[None, :]
    tri = np.where(kk <= qq, 0.0, -BIG).astype(np.float32)
    caus = np.zeros((128, 2, 256), np.float32)
    caus[:, 0, 0:128] = tri
    caus[:, 1, 0:128] = -BIG
    caus[:, 1, 128:256] = tri
    return dict(c_identb=identb, c_identf=identf, c_oh=oh.astype(bf), c_negmask=negmask.reshape(1, 256),
                c_caus=caus.astype(bf))


def _col(v):
    v = np.asarray(v, np.float32)
    return np.ascontiguousarray(v.reshape(-1, 128).T)


def make_in_maps(inputs, n_cores=8):
    f = lambda k: np.ascontiguousarray(np.asarray(inputs[k], np.float32))
    cols = np.zeros((2, 128, NCOLS), np.float32)
    for l in range(2):
        b = f("b_ada")[l]
        parts = [_col(f("ln1_g")[l]), _col(f("ln2_g")[l]),
                 _col(b[0:1024]), _col(b[1024:2048]), _col(b[3072:4096]), _col(b[4096:5120]),
                 _col(f("mix_norm_g")[l])]
        scw = f("sc_w")[l]
        parts.append(np.concatenate([np.stack([scw[k, cc * 128:(cc + 1) * 128] for k in range(3)], 1) for cc in range(2)], 1))
        lcw = f("lru_conv_w")[l]
        parts.append(np.concatenate([np.stack([lcw[k, cc * 128:(cc + 1) * 128] for k in range(4)], 1) for cc in range(2)], 1))
        parts += [_col(f("lru_conv_b")[l]), _col(f("lru_ba")[l]), _col(f("lru_bx")[l]), _col(f("lru_lambda")[l])]
        cols[l] = np.concatenate(parts, 1)
    shared = dict(cols=cols, b_ada=f("b_ada"), w_ada=f("w_ada"), w_in=f("w_in"), q_norm_g=f("q_norm_g"), k_norm_g=f("k_norm_g"),
                  lru_wa=f("lru_wa"), lru_wx=f("lru_wx"), w_out=f("w_out"), w_up=f("w_up"), w_down=f("w_down"))
    shared.update(_consts())
    x = f("x")
    c = f("c")
    maps = []
    for b in range(n_cores):
        m = dict(shared)
        m["x"] = x[b]
        m["ccol"] = _col(c[b])
        maps.append(m)
    return maps


_NC = None


def kernel(**inputs):
    global _NC
    if _NC is None:
        _NC = build_program()[0]
    maps = make_in_maps(inputs)
    res = run_bass_kernel_spmd(_NC, maps, core_ids=list(range(8)))
    return np.stack([np.asarray(r["out"], np.float32) for r in res.results], 0)
```

```python
import numpy as np
import ml_dtypes
from contextlib import ExitStack
import concourse.bass as bass
import concourse.mybir as mybir
from concourse.bass_utils import run_bass_kernel_spmd

F32 = mybir.dt.float32
BF16 = mybir.dt.bfloat16
AF = mybir.ActivationFunctionType
ALU = mybir.AluOpType
AX = mybir.AxisListType

S = 4096
D = 1024
NT = 32
NB = 16
DIN = 2816
DFF = 4096
BIG = 30000.0
EPS = 1e-6
NCOLS = 78
ENGS = ("pe", "act", "dve", "pool", "sp")
import os as _osg
FP32_GUARD = _osg.environ.get("FP32_GUARD", "1") == "1"


class Prog:
    def __init__(self, nc, strict=True):
        self.nc = nc
        self.strict = strict
        self.ops = []
        self.last_w = {}
        self.readers = {}
        self.last_dma = {}
        self.keymap = {}
        self.nosched = False
        self.phase = 0

    def add(self, eng, fn, reads=(), writes=(), dma_key=None, n=256, cost=None):
        if cost is None:
            if dma_key is not None:
                cost = 3000.0
            elif eng == "pe":
                cost = 400.0
            elif eng == "act":
                cost = 320.0 + n / 1.4
            elif eng == "dve":
                cost = 250.0 + n / 0.96
            else:
                cost = 300.0 + n / 0.5
        if dma_key is not None:
            cls = "W" if eng == "pool" else "H"
            kk = (cls, dma_key)
            if kk not in self.keymap:
                self.keymap[kk] = (cls, sum(1 for q in self.keymap if q[0] == cls))
            dma_key = self.keymap[kk]
        i = len(self.ops)
        deps = set()
        for t in reads:
            if t in self.last_w:
                deps.add(self.last_w[t])
        for t in writes:
            if t in self.last_w:
                deps.add(self.last_w[t])
            for r in self.readers.get(t, ()):
                deps.add(r)
        if dma_key is not None:
            if dma_key in self.last_dma:
                deps.add(self.last_dma[dma_key])
            self.last_dma[dma_key] = i
        deps.discard(i)
        for t in reads:
            self.readers.setdefault(t, []).append(i)
        for t in writes:
            self.last_w[t] = i
            self.readers[t] = []
        self.ops.append(dict(eng=eng, fn=fn, deps=sorted(deps), dma_key=dma_key,
                             phase=self.phase, barrier=False, cost=float(cost), nosched=self.nosched))
        return i

    def barrier(self):
        self.ops.append(dict(eng=None, fn=None, deps=[], dma_key=None,
                             phase=self.phase, barrier=True))
        self.phase += 1
        self.last_w = {}
        self.readers = {}
        self.last_dma = {}
        self.keymap = {}

    def schedule(self, window=48):
        ops = self.ops
        n = len(ops)
        order = []
        start = 0
        while start < n:
            end = start
            while end < n and not ops[end]["barrier"]:
                end += 1
            ids = list(range(start, end))
            import os as _os2
            sp_ = _os2.environ.get("SCHED_PHASES")
            if ids and sp_ is not None and str(ops[ids[0]]["phase"]) not in sp_.split(","):
                order.extend(ids)
            elif ids:
                fz = set(_os2.environ.get("SCHED_FREEZE", "").split(","))
                if ops[ids[0]].get("nosched"):
                    fz |= {"sp", "pool", "pe"}
                order.extend(self._sched_phase(ids, window, fz))
            if end < n:
                order.append(end)
            start = end + 1
        remap = {old: new for new, old in enumerate(order)}
        newops = []
        for old in order:
            o = ops[old]
            o["deps"] = sorted(remap[d] for d in o["deps"])
            newops.append(o)
        self.ops = newops

    def _sched_phase(self, ids, window, freeze=()):
        ops = self.ops
        self._freeze = set(freeze)
        idset = set(ids)
        queues = {e: [i for i in ids if ops[i]["eng"] == e] for e in ENGS}
        qpos = {e: 0 for e in ENGS}
        scheduled = {}
        eng_free = {e: 0.0 for e in ENGS}
        out = []
        remaining = len(ids)
        taken = set()
        while remaining:
            best = None
            for e in ENGS:
                q = queues[e]
                p = qpos[e]
                while p < len(q) and q[p] in taken:
                    p += 1
                qpos[e] = p
                cnt = 0
                k = p
                win = 1 if e in self._freeze else window
                while k < len(q) and cnt < win:
                    i = q[k]
                    k += 1
                    if i in taken:
                        continue
                    cnt += 1
                    ok = True
                    rt = 0.0
                    for d in ops[i]["deps"]:
                        if d in idset:
                            if d not in scheduled:
                                ok = False
                                break
                            if scheduled[d] > rt:
                                rt = scheduled[d]
                    if not ok:
                        continue
                    st = max(eng_free[e], rt)
                    if best is None or st < best[0] - 1e-9 or (abs(st - best[0]) <= 1e-9 and i < best[1]):
                        best = (st, i, e)
                    if rt <= eng_free[e]:
                        break
            assert best is not None, "scheduler deadlock"
            st, i, e = best
            o = ops[i]
            if o["dma_key"] is not None:
                eng_free[e] = st + 120.0
                scheduled[i] = st + o["cost"]
            else:
                eng_free[e] = st + o["cost"]
                scheduled[i] = st + o["cost"] + 150.0
            taken.add(i)
            out.append(i)
            remaining -= 1
        self.est_ns = getattr(self, "est_ns", 0.0) + max(scheduled.values())
        return out

    def emit(self):
        nc = self.nc
        import os as _os1
        if _os1.environ.get("SCHED", "1") == "1":
            self.schedule()
        ops = self.ops
        nph = self.phase + 1
        strict = self.strict

        def same_eng_free(od, o):
            return od["eng"] == o["eng"] and o["dma_key"] is None and (od["eng"] == "pe" or not strict)

        need = [False] * len(ops)
        for i, o in enumerate(ops):
            if o["barrier"]:
                continue
            for d in o["deps"]:
                od = ops[d]
                if od["dma_key"] is not None:
                    continue
                if same_eng_free(od, o):
                    continue
                need[d] = True
        last_in_phase = {}
        for i, o in enumerate(ops):
            if o["barrier"] or o["dma_key"] is not None:
                continue
            last_in_phase[(o["eng"], o["phase"])] = i
        for i in last_in_phase.values():
            need[i] = True
        cnt = {}
        dcnt = {}
        val = [None] * len(ops)
        dma_keys = []
        for i, o in enumerate(ops):
            if o["barrier"]:
                continue
            if o["dma_key"] is not None:
                k = o["dma_key"]
                if k not in dcnt:
                    dcnt[k] = 0
                    dma_keys.append(k)
                dcnt[k] += 16
                val[i] = dcnt[k]
            elif need[i]:
                k = (o["eng"], o["phase"])
                cnt[k] = cnt.get(k, 0) + 1
                val[i] = cnt[k]
        esem = {}
        for (e, ph) in sorted(cnt.keys(), key=lambda t: (t[1], t[0])):
            esem[(e, ph)] = nc.alloc_semaphore(f"s_{e}_{ph}")
        dsem = {k: nc.alloc_semaphore(f"d_{j}") for j, k in enumerate(dma_keys)}
        self.n_sems = len(esem) + len(dsem)
        final_cnt = dict(cnt)
        per = {e: [] for e in ENGS}
        for i, o in enumerate(ops):
            if o["barrier"]:
                for e in ENGS:
                    per[e].append(i)
            else:
                per[o["eng"]].append(i)
        dma_upto = {}
        run = {}
        for i, o in enumerate(ops):
            if o["barrier"]:
                dma_upto[i] = dict(run)
            elif o["dma_key"] is not None:
                run[o["dma_key"]] = val[i]
        dma_final = dict(run)

        def gen(e):
            def body(eng):
                waited = {}

                def w(sem, name, v):
                    if waited.get(name, 0) >= v:
                        return
                    waited[name] = v
                    eng.wait_ge(sem, v)

                for i in per[e]:
                    o = ops[i]
                    if o["barrier"]:
                        ph = o["phase"]
                        for e2 in ("pe", "act", "dve", "pool"):
                            v = final_cnt.get((e2, ph), 0)
                            if v:
                                w(esem[(e2, ph)], (e2, ph), v)
                        for k, v in dma_upto[i].items():
                            w(dsem[k], k, v)
                        continue
                    for d in o["deps"]:
                        od = ops[d]
                        if od["dma_key"] is not None:
                            w(dsem[od["dma_key"]], od["dma_key"], val[d])
                        else:
                            if same_eng_free(od, o):
                                continue
                            k = (od["eng"], od["phase"])
                            w(esem[k], k, val[d])
                    inst = o["fn"](eng)
                    if o["dma_key"] is not None:
                        inst.then_inc(dsem[o["dma_key"]], 16)
                    elif need[i]:
                        inst.then_inc(esem[(e, o["phase"])], 1)
                if e == "sp":
                    for k, v in dma_final.items():
                        w(dsem[k], k, v)
                    for (e2, ph), v in final_cnt.items():
                        w(esem[(e2, ph)], (e2, ph), v)
            return body

        with nc.Block() as block:
            block.tensor(gen("pe"))
            block.scalar(gen("act"))
            block.vector(gen("dve"))
            block.gpsimd(gen("pool"))
            block.sync(gen("sp"))


class Rot:
    def __init__(self, tensors, name):
        self.t = tensors
        self.name = name
        self.i = 0

    def next(self):
        k = self.i % len(self.t)
        self.i += 1
        return self.t[k], (self.name, k)


def build_program(n_layers=2, phases="ABC", debug=False, nblk=NB):
    nc = bass.Bass("TRN2", target_bir_lowering=False)
    dbg_kind = "ExternalOutput" if debug else "Internal"

    def din(name, shape, dt=F32):
        return nc.dram_tensor(name, list(shape), dt, kind="ExternalInput").ap()

    def dscr(name, shape, dt=F32):
        return nc.dram_tensor(name, list(shape), dt, kind=dbg_kind).ap()

    x_in = din("x", [S, D])
    ccol = din("ccol", [128, 8])
    cols = din("cols", [2, 128, NCOLS])
    bada = din("b_ada", [2, 6 * D])
    w_ada = din("w_ada", [2, D, 6 * D])
    w_in = din("w_in", [2, D, DIN])
    qng = din("q_norm_g", [2, 64])
    kng = din("k_norm_g", [2, 64])
    lru_wa = din("lru_wa", [2, 4, 64, 64])
    lru_wx = din("lru_wx", [2, 4, 64, 64])
    w_out = din("w_out", [2, D, D])
    w_up = din("w_up", [2, D, DFF])
    w_down = din("w_down", [2, DFF, D])
    c_identb = din("c_identb", [128, 128], BF16)
    c_identf = din("c_identf", [128, 128], F32)
    c_oh = din("c_oh", [16, S], BF16)
    c_negmask = din("c_negmask", [1, 256], F32)
    c_caus = din("c_caus", [128, 2, 256], BF16)
    out = nc.dram_tensor("out", [S, D], F32, kind="ExternalOutput").ap()

    gates_scr = dscr("gates_scr", [2, 2, D])
    qT_scr = dscr("qT_scr", [80, 8, S], BF16)
    kT_scr = dscr("kT_scr", [64, 8, S], BF16)
    v_scr = dscr("v_scr", [NT, 128, 520], BF16)
    ycl_scr = dscr("ycl_scr", [512, S], BF16)
    x1_scr = dscr("x1_scr", [S, D])
    xmid_scr = dscr("xmid_scr", [S, D])
    dbg_grstd = dscr("dbg_grstd", [128, NT * 2]) if debug else None

    import os as _os0
    P = Prog(nc, strict=(_os0.environ.get('STRICT', '1') == '1'))
    A = P.add

    with ExitStack() as top:
        def sb(es, name, shape, dt):
            return es.enter_context(nc.sbuf_tensor(name, list(shape), dt))

        def ps(es, name, shape, dt):
            return es.enter_context(nc.psum_tensor(name, list(shape), dt))

        cols_sb = sb(top, "cols_sb", [128, 2, NCOLS], F32)
        eff_sb = sb(top, "eff_sb", [128, 2, 32], F32)
        shb_sb = sb(top, "shb_sb", [128, 2, 2, 8], BF16)
        lruc_sb = sb(top, "lruc_sb", [128, 2, 8], F32)
        identb = sb(top, "identb", [128, 128], BF16)
        identf = sb(top, "identf", [128, 128], F32)
        ones_bf = sb(top, "ones_bf", [128, 128], BF16)
        ones_f = sb(top, "ones_f", [128, 128], F32)
        grstd = sb(top, "grstd", [128, NT, 2], F32)
        epsc = sb(top, "epsc", [128, 1], F32)

        A("sp", lambda e: e.dma_start(out=identb[:], in_=c_identb[:, :]), writes=["identb"], dma_key="c0")
        A("sp", lambda e: e.dma_start(out=identf[:], in_=c_identf[:, :]), writes=["identf"], dma_key="c1")
        A("sp", lambda e: e.dma_start(out=cols_sb[:], in_=cols.rearrange("l p n -> p l n")), writes=["cols"], dma_key="c2")
        A("pool", lambda e: e.memset(ones_bf[:], 1.0), writes=["ones_bf"])
        A("pool", lambda e: e.memset(ones_f[:], 1.0), writes=["ones_f"])
        A("pool", lambda e: e.memset(epsc[:], EPS), writes=["epsc"])
        if debug:
            A("pool", lambda e: e.memset(grstd[:], 0.0), writes=["grstd_init"])

        def rstd_from_ssq(dst, ssq, n, tags_r, tags_w, tmp, tmptag):
            A("dve", lambda e: e.tensor_scalar(out=tmp, in0=ssq, scalar1=1.0 / n, scalar2=EPS, op0=ALU.mult, op1=ALU.add),
              reads=tags_r, writes=[tmptag])
            A("act", lambda e: e.activation(out=tmp, in_=tmp, func=AF.Ln), reads=[tmptag], writes=[tmptag])
            A("act", lambda e: e.activation(out=dst, in_=tmp, func=AF.Exp, scale=-0.5), reads=[tmptag], writes=tags_w)

        with ExitStack() as es:
            cc = sb(es, "cc", [128, 8], F32)
            ce = sb(es, "ce", [128, 8], F32)
            cact = sb(es, "cact", [128, 8], BF16)
            wa = [sb(es, f"wa{i}", [128, 8, 512], BF16) for i in range(2)]
            modc = sb(es, "modc", [128, 32], F32)
            grow = sb(es, "grow", [1, 2, D], F32)
            brow = sb(es, "brow", [1, 2, 2, D], F32)
            lam = sb(es, "lam", [128, 2, 2], F32)
            mod_ps_t = ps(es, "mod_ps", [128, 512], F32)
            mod_ps = mod_ps_t[:, 0:32]
            g_ps = [ps(es, f"g_ps{i}", [128, 512], F32)[0:1, :] for i in range(2)]

            A("sp", lambda e: e.dma_start(out=cc[:], in_=ccol[:, :]), writes=["cc"], dma_key="c3")
            A("act", lambda e: e.activation(out=ce[:], in_=cc[:], func=AF.Exp, scale=-1.0), reads=["cc"], writes=["ce"])
            A("dve", lambda e: e.tensor_scalar(out=ce[:], in0=ce[:], scalar1=1.0, scalar2=None, op0=ALU.add), reads=["ce"], writes=["ce"])
            A("dve", lambda e: e.reciprocal(out=ce[:], in_=ce[:]), reads=["ce"], writes=["ce"])
            A("dve", lambda e: e.tensor_tensor(out=cact[:], in0=ce[:], in1=cc[:], op=ALU.mult), reads=["ce", "cc"], writes=["cact"])
            for l in range(n_layers):
                for g_ in range(2):
                    A("sp", lambda e, l=l, g_=g_: e.dma_start(out=brow[:, l, g_, :], in_=bada[l:l + 1, (2 + 3 * g_) * D:(3 + 3 * g_) * D]),
                      writes=[("brow", l, g_)], dma_key=("c4", l, g_))
                colidx = 0
                for blk in range(12):
                    wt, wtag = wa[blk % 2], ("wa", blk % 2)
                    A("pool", lambda e, wt=wt, l=l, blk=blk: e.dma_start(
                        out=wt[:], in_=w_ada[l, :, blk * 512:(blk + 1) * 512].rearrange("(c p) n -> p c n", p=128)),
                      writes=[wtag], dma_key=("wa", blk % 2))
                    if blk in (4, 5, 10, 11):
                        g = 0 if blk < 6 else 1
                        half = blk % 2
                        gp, gtag = g_ps[half], ("g_ps", half)

                        def mm(e, wt=wt, gp=gp):
                            r = None
                            for c in range(8):
                                r = e.matmul(gp, lhsT=cact[:, c:c + 1], rhs=wt[:, c, :], start=(c == 0), stop=(c == 7))
                            return r
                        A("pe", mm, reads=[wtag, "cact"], writes=[gtag])
                        A("dve", lambda e, gp=gp, l=l, g=g, half=half: e.tensor_tensor(
                            out=grow[:, g, half * 512:(half + 1) * 512], in0=gp, in1=brow[:, l, g, half * 512:(half + 1) * 512], op=ALU.add),
                          reads=[gtag, ("brow", l, g)], writes=[("grow", g, half)])
                        if half == 1:
                            A("sp", lambda e, l=l, g=g: e.dma_start(out=gates_scr[l, g:g + 1, :], in_=grow[:, g, :]),
                              reads=[("grow", g, 0), ("grow", g, 1)], dma_key=("grow", g))
                    else:
                        def mm(e, wt=wt, colidx=colidx):
                            r = None
                            for sub in range(4):
                                for c in range(8):
                                    r = e.matmul(mod_ps[:, colidx + sub:colidx + sub + 1], lhsT=wt[:, c, sub * 128:(sub + 1) * 128],
                                                 rhs=cact[:, c:c + 1], start=(c == 0), stop=(c == 7))
                            return r
                        A("pe", mm, reads=[wtag, "cact", "modc"], writes=["mod_ps"])
                        colidx += 4
                A("dve", lambda e, l=l: e.tensor_tensor(out=modc[:], in0=mod_ps, in1=cols_sb[:, l, 16:48], op=ALU.add),
                  reads=["mod_ps", "cols"], writes=["modc"])
                for k2 in range(2):
                    A("dve", lambda e, l=l, k2=k2: e.scalar_tensor_tensor(
                        out=eff_sb[:, l, 16 * k2:16 * k2 + 8], in0=modc[:, 16 * k2 + 8:16 * k2 + 16], scalar=1.0,
                        in1=cols_sb[:, l, 8 * k2:8 * k2 + 8], op0=ALU.add, op1=ALU.mult),
                      reads=["modc", "cols"], writes=[("eff", l, k2, 0)])
                    A("dve", lambda e, l=l, k2=k2: e.tensor_copy(out=eff_sb[:, l, 16 * k2 + 8:16 * k2 + 16], in_=modc[:, 16 * k2:16 * k2 + 8]),
                      reads=["modc"], writes=[("eff", l, k2, 1)])
                    A("dve", lambda e, l=l, k2=k2: e.tensor_copy(out=shb_sb[:, l, k2, :], in_=modc[:, 16 * k2:16 * k2 + 8]),
                      reads=["modc"], writes=[("shb", l, k2)])
                A("dve", lambda e, l=l: e.tensor_scalar(out=lruc_sb[:, l, 0:4], in0=cols_sb[:, l, 72:76], scalar1=-1.0, scalar2=None, op0=ALU.mult),
                  reads=["cols"], writes=[("lruc", l, 0)])
                A("act", lambda e, l=l: e.activation(out=lam[:, l, :], in_=cols_sb[:, l, 76:78], func=AF.Exp, scale=-1.0),
                  reads=["cols"], writes=[("lam", l)])
                A("act", lambda e, l=l: e.activation(out=lam[:, l, :], in_=lam[:, l, :], func=AF.Ln, bias=1.0),
                  reads=[("lam", l)], writes=[("lam", l)])
                A("dve", lambda e, l=l: e.tensor_scalar(out=lruc_sb[:, l, 4:6], in0=lam[:, l, :], scalar1=-8.0, scalar2=None, op0=ALU.mult),
                  reads=[("lam", l)], writes=[("lruc", l, 1)])
            P.barrier()

        for l in range(n_layers):
            x_src = x_in if l == 0 else xmid_scr
            x_dst = xmid_scr if l < n_layers - 1 else out
            if l >= 2:
                x_dst = out
            if "A" in phases:
                P.nosched = True
                phase_A(nc, P, top, sb, ps, l, x_src, dict(
                    cols_sb=cols_sb, eff_sb=eff_sb, shb_sb=shb_sb, lruc_sb=lruc_sb, identb=identb, identf=identf,
                    ones_bf=ones_bf, ones_f=ones_f, grstd=grstd, w_in=w_in, qng=qng, kng=kng, lru_wa=lru_wa,
                    lru_wx=lru_wx, c_negmask=c_negmask, qT_scr=qT_scr, kT_scr=kT_scr, v_scr=v_scr, ycl_scr=ycl_scr,
                    rstd_from_ssq=rstd_from_ssq, dbg_grstd=dbg_grstd, epsc=epsc), nblk)
                P.barrier()
                P.nosched = False
            TT = dict(cols_sb=cols_sb, eff_sb=eff_sb, shb_sb=shb_sb, identb=identb, identf=identf, ones_bf=ones_bf, ones_f=ones_f,
                      grstd=grstd, w_out=w_out, w_up=w_up, w_down=w_down, gates_scr=gates_scr, c_oh=c_oh, c_caus=c_caus,
                      qT_scr=qT_scr, kT_scr=kT_scr, v_scr=v_scr, ycl_scr=ycl_scr, x1_scr=x1_scr, rstd_from_ssq=rstd_from_ssq)
            if "B" in phases:
                phase_B(nc, P, top, sb, ps, l, x_src, TT, nblk)
                P.barrier()
            if "C" in phases:
                phase_C(nc, P, top, sb, ps, l, x_dst, TT, nblk)
                P.barrier()
        P.emit()
    return nc, P


def phase_A(nc, P, top, sb0, ps0, l, x_src, T, nblk):
    A = P.add
    sb = lambda es, name, shape, dt: sb0(es, f"{name}_A{l}", shape, dt)
    ps = lambda es, name, shape, dt: ps0(es, f"{name}_A{l}", shape, dt)
    cols_sb, eff_sb, shb_sb, lruc_sb = T["cols_sb"], T["eff_sb"], T["shb_sb"], T["lruc_sb"]
    identb, identf, ones_bf, ones_f, grstd, epsc = T["identb"], T["identf"], T["ones_bf"], T["ones_f"], T["grstd"], T["epsc"]

    def rstd2(dst, ssq, n, rtags, wtags, tmp, tmptag):
        A("act", lambda e: e.activation(out=tmp, in_=ssq, func=AF.Ln, scale=1.0 / n, bias=epsc[:, 0:1]), reads=rtags, writes=[tmptag], n=8)
        A("act", lambda e: e.activation(out=dst, in_=tmp, func=AF.Exp, scale=-0.5), reads=[tmptag], writes=wtags, n=8)

    with ExitStack() as es:
        dbl = lambda name, shape, dt: [sb(es, f"{name}{i}", shape, dt) for i in range(2)]
        w_sb = sb(es, "w_in_sb", [128, 8, DIN], BF16)
        brow = sb(es, "b_in_row", [1, 1536], BF16)
        bcol = sb(es, "b_in_col", [128, 10], F32)
        gq = sb(es, "gq", [128, 64], F32)
        gk = sb(es, "gk", [128, 64], F32)
        negm = sb(es, "negm", [128, 16, 16], F32)
        kmeanT = sb(es, "kmeanT", [128, 4, 16], F32)
        wabd = sb(es, "wabd", [128, 2, 2, 128], BF16)
        wtmp = sb(es, "wtmp", [128, 2, 2, 64], F32)
        x_sbs = Rot([sb(es, f"xA{i}", [128, D], F32) for i in range(2)], "xA")
        junkx = sb(es, "junkx", [128, D], BF16)
        junkq = dbl("junkq", [128, 512], BF16)
        junkk = dbl("junkk", [128, 512], BF16)
        qraw = dbl("qraw", [128, 512], F32)
        kraw = dbl("kraw", [128, 512], F32)
        xn = dbl("xn", [128, D], BF16)
        xnT = dbl("xnT", [128, 8, 256], BF16)
        stx = dbl("stx", [128, 4], F32)
        stq = dbl("stq", [128, 3, 8], F32)
        stk = dbl("stk", [128, 3, 8], F32)
        stg = dbl("stg", [128, 4], F32)
        qf = dbl("qf", [128, 512], F32)
        kf = dbl("kf", [128, 512], F32)
        kb = dbl("kb", [128, 512], BF16)
        qaug = dbl("qaug", [128, 8, 80], BF16)
        qfT = dbl("qfT", [128, 4, 128], F32)
        gm = dbl("gm", [128, 8, 16], F32)
        m8 = dbl("m8", [128, 8, 8], F32)
        msk = dbl("msk", [128, 8, 16], F32)
        v_sbs = Rot([sb(es, f"vA{i}", [128, 8, 65], BF16) for i in range(2)], "vA")
        kT_sbs = Rot([sb(es, f"kTA{i}", [64, 8, 128], BF16) for i in range(2)], "kTA")
        qT_sbs = Rot([sb(es, f"qTA{i}", [80, 8, 128], BF16) for i in range(2)], "qTA")
        fmS = [sb(es, "fmS0", [128, 10, 256], F32)] * 2
        cu = sb(es, "cu", [128, 2, 258], F32)
        lx = sb(es, "lx", [128, 2, 259], F32)
        hb = sb(es, "hb", [128, 2, 257], F32)
        ct = dbl("ct", [128, 256], F32)
        cy = dbl("cy", [128, 256], F32)
        lt = [[sb(es, f"lt{lc}_{k}", [128, 256], F32) for k in range(7)] for lc in range(2)]
        xrb = dbl("xrb", [128, 256], BF16)
        ysq = dbl("ysq", [128, 4, 256], BF16)
        ycl_sbs = Rot([sb(es, f"ycl{i}", [128, 4, 256], BF16) for i in range(2)], "ycl")
        tp_ps = ps(es, "tp_ps", [128, 8, 128], BF16)
        qkv_ps = [ps(es, f"qkv_ps{i}", [128, 512], F32) for i in range(3)]
        fm_pss = Rot([ps(es, f"fm_ps{i}", [128, 512], F32) for i in range(2)], "fm_ps")
        misc_ps = ps(es, "misc_ps", [128, 512], F32)
        sm_ps = ps(es, "sm_ps", [128, 512], F32)
        gate_ps = sm_ps[:, 0:128].rearrange("p (h n) -> p h n", h=8)
        km_ps = sm_ps[:, 128:132]
        ss_ps = sm_ps[:, 136:138]
        bc_ps = sm_ps[:, 144:154]

        for c in range(8):
            A("pool", lambda e, c=c: e.dma_start(out=w_sb[:, c, :], in_=T["w_in"][l, c * 128:(c + 1) * 128, :], max_dma_last_dim=4096),
              writes=[("w_in", c)], dma_key=("w", c), cost=12000)
        A("sp", lambda e: e.dma_start(out=gq[:], in_=T["qng"][l:l + 1, :].partition_broadcast(128)), writes=["gq"], dma_key="a0")
        A("sp", lambda e: e.dma_start(out=gk[:], in_=T["kng"][l:l + 1, :].partition_broadcast(128)), writes=["gk"], dma_key="a1")
        A("sp", lambda e: e.dma_start(out=negm[:].rearrange("p a b -> p (a b)"), in_=T["c_negmask"][0:1, :].partition_broadcast(128)),
          writes=["negm"], dma_key="a2")
        A("dve", lambda e: e.scalar_tensor_tensor(out=gk[:], in0=gk[:], scalar=0.125, in1=gq[:], op0=ALU.mult, op1=ALU.mult),
          reads=["gq", "gk"], writes=["gk"])
        A("pool", lambda e: e.memset(kmeanT[:], 0.0), writes=["kmeanT"])
        A("pool", lambda e: e.memset(cu[:], 0.0), writes=["cu0", "cu1"])
        A("pool", lambda e: e.memset(lx[:], 0.0), writes=["lx0", "lx1"])
        A("pool", lambda e: e.memset(hb[:], 0.0), writes=["hb0", "hb1"])
        for i in range(2):
            A("pool", lambda e, i=i: e.memset(qaug[i][:], 0.0), writes=[f"qaug_b{i}", f"qaug_q{i}"])
        for k, vt in enumerate(v_sbs.t):
            A("pool", lambda e, vt=vt: e.memset(vt[:], 1.0), writes=[("vA", k)])
        A("pool", lambda e: e.memset(wabd[:], 0.0), writes=["wabd"])
        for g, wsrc in enumerate((T["lru_wa"], T["lru_wx"])):
            for hh in range(4):
                ch, hf = hh // 2, hh % 2
                A("sp", lambda e, g=g, wsrc=wsrc, hh=hh, ch=ch, hf=hf: e.dma_start(
                    out=wtmp[hf * 64:(hf + 1) * 64, g, ch, :], in_=wsrc[l, hh, :, :]), writes=[("wtmp", g, hh)], dma_key=("spk", g * 4 + hh))
                A("dve", lambda e, g=g, ch=ch, hf=hf: e.tensor_copy(out=wabd[hf * 64:(hf + 1) * 64, g, ch, hf * 64:(hf + 1) * 64],
                                                                   in_=wtmp[hf * 64:(hf + 1) * 64, g, ch, :]),
                  reads=[("wtmp", g, hh), "wabd"], writes=[("wabd", g, hh)])
        wabd_tags = [("wabd", g, hh) for g in range(2) for hh in range(4)]
        w_tags = [("w_in", c) for c in range(8)]

        for j in range(3):
            def mm(e, j=j):
                r = None
                for c in range(8):
                    r = e.matmul(qkv_ps[j][0:1, :], lhsT=shb_sb[:, l, 0, c:c + 1], rhs=w_sb[:, c, j * 512:(j + 1) * 512],
                                 start=(c == 0), stop=(c == 7))
                return r
            A("pe", mm, reads=w_tags + [("shb", l, 0)], writes=[("qkv_ps", j)])
            A("act", lambda e, j=j: e.copy(out=brow[:, j * 512:(j + 1) * 512], in_=qkv_ps[j][0:1, :]), reads=[("qkv_ps", j)], writes=["brow"], n=512)

        def mm(e):
            r = None
            for fc in range(10):
                for c in range(8):
                    r = e.matmul(bc_ps[:, fc:fc + 1], lhsT=w_sb[:, c, 1536 + fc * 128:1536 + (fc + 1) * 128],
                                 rhs=shb_sb[:, l, 0, c:c + 1], start=(c == 0), stop=(c == 7))
            return r
        A("pe", mm, reads=w_tags + [("shb", l, 0)], writes=["sm_ps"])
        A("dve", lambda e: e.tensor_copy(out=bcol[:], in_=bc_ps), reads=["sm_ps"], writes=["bcol"])
        for c in range(8):
            if c % 2 == 0:
                A("dve", lambda e, c=c: e.tensor_scalar(out=w_sb[:, c, :], in0=w_sb[:, c, :], scalar1=eff_sb[:, l, c:c + 1], scalar2=None, op0=ALU.mult),
                  reads=[("eff", l, 0, 0)], writes=[("w_in", c)], n=DIN)
            else:
                A("act", lambda e, c=c: e.activation(out=w_sb[:, c, :], in_=w_sb[:, c, :], func=AF.Copy, scale=eff_sb[:, l, c:c + 1]),
                  reads=[("eff", l, 0, 0)], writes=[("w_in", c)], n=DIN)

        def h3(ap):
            return ap.rearrange("p (h d) -> p h d", h=8)

        def tiles(st):
            bp = st % 2
            ctx = [dict(), dict()]

            def s0(i):
                t = 2 * st + i
                xt, xtag = x_sbs.next()
                ctx[i].update(t=t, xt=xt, xtag=xtag)
                A("sp", lambda e, xt=xt, t=t: e.dma_start(out=xt[:], in_=x_src[t * 128:(t + 1) * 128, :]), writes=[xtag], dma_key=xtag)
                A("act", lambda e, xt=xt, i=i: e.activation(out=junkx[:], in_=xt[:], func=AF.Square, accum_out=stx[i][:, 0:1]),
                  reads=[xtag], writes=["junkx", f"stx{i}"], n=D)
                rstd2(stx[i][:, 2:3], stx[i][:, 0:1], D, [f"stx{i}"], [f"stxr{i}"], stx[i][:, 1:2], f"stxt{i}")
                A("dve", lambda e, xt=xt, i=i: e.tensor_scalar(out=xn[i][:], in0=xt[:], scalar1=stx[i][:, 2:3], scalar2=None, op0=ALU.mult),
                  reads=[xtag, f"stxr{i}"], writes=[f"xn{i}"], n=D)

            def s1(i):
                def tr(e, i=i):
                    r = None
                    for c in range(8):
                        r = e.transpose(out=tp_ps[:, c, :], in_=xn[i][:, c * 128:(c + 1) * 128], identity=identb[:])
                    return r
                A("pe", tr, reads=[f"xn{i}", "identb"], writes=["tp_ps"])
                A("act", lambda e, i=i, bp=bp: e.copy(out=xnT[bp][:, :, i * 128:(i + 1) * 128], in_=tp_ps[:]), reads=["tp_ps"], writes=[("xnT", bp, i)], n=D)

            def s2(i):
                t = ctx[i]["t"]
                for j in range(3):
                    def mm(e, j=j, i=i, bp=bp):
                        for c in range(8):
                            e.matmul(qkv_ps[j][:], lhsT=xnT[bp][:, c, i * 128:(i + 1) * 128], rhs=w_sb[:, c, j * 512:(j + 1) * 512],
                                     start=(c == 0), stop=False)
                        return e.matmul(qkv_ps[j][:], lhsT=ones_bf[0:1, :], rhs=brow[0:1, j * 512:(j + 1) * 512], start=False, stop=True)
                    A("pe", mm, reads=w_tags + [("xnT", bp, i), "brow", "ones_bf"], writes=[("qkv_ps", j)], cost=2200)
                A("act", lambda e, i=i: e.copy(out=qraw[i][:], in_=qkv_ps[0][:]), reads=[("qkv_ps", 0)], writes=[f"qraw{i}"], n=512)
                A("act", lambda e, i=i: e.copy(out=kraw[i][:], in_=qkv_ps[1][:]), reads=[("qkv_ps", 1)], writes=[f"kraw{i}"], n=512)
                vt, vtag = v_sbs.next()
                A("act", lambda e, vt=vt: e.copy(out=vt[:, :, 0:64], in_=h3(qkv_ps[2][:])), reads=[("qkv_ps", 2)], writes=[vtag], n=512)
                A("sp", lambda e, vt=vt, t=t: e.dma_start(out=T["v_scr"][t, :, :], in_=vt[:].rearrange("p h d -> p (h d)")),
                  reads=[vtag], writes=[("v_scr", t)], dma_key=("st",) + vtag)

            def s3(i):
                A("act", lambda e, i=i: e.activation(out=junkq[i][:], in_=qraw[i][:], func=AF.Square), reads=[f"qraw{i}"], writes=[f"junkq{i}"], n=512)
                A("dve", lambda e, i=i: e.tensor_reduce(out=stq[i][:, 0, :], in_=h3(junkq[i][:]), axis=AX.X, op=ALU.add),
                  reads=[f"junkq{i}"], writes=[f"stq{i}"], n=512)
                rstd2(stq[i][:, 2, :], stq[i][:, 0, :], 64, [f"stq{i}"], [f"stqr{i}"], stq[i][:, 1, :], f"stqt{i}")
                A("dve", lambda e, i=i: e.tensor_tensor(out=h3(qf[i][:]), in0=h3(qraw[i][:]),
                                                        in1=stq[i][:, 2, :].unsqueeze(2).to_broadcast([128, 8, 64]), op=ALU.mult),
                  reads=[f"qraw{i}", f"stqr{i}"], writes=[f"qf{i}"], n=512)
                A("act", lambda e, i=i: e.copy(out=qaug[i][:, :, 0:64], in_=h3(qf[i][:])), reads=[f"qf{i}"], writes=[f"qaug_q{i}"], n=512)
                A("act", lambda e, i=i: e.activation(out=junkk[i][:], in_=kraw[i][:], func=AF.Square), reads=[f"kraw{i}"], writes=[f"junkk{i}"], n=512)
                A("dve", lambda e, i=i: e.tensor_reduce(out=stk[i][:, 0, :], in_=h3(junkk[i][:]), axis=AX.X, op=ALU.add),
                  reads=[f"junkk{i}"], writes=[f"stk{i}"], n=512)
                rstd2(stk[i][:, 2, :], stk[i][:, 0, :], 64, [f"stk{i}"], [f"stkr{i}"], stk[i][:, 1, :], f"stkt{i}")
                A("dve", lambda e, i=i: e.tensor_tensor(out=h3(kf[i][:]), in0=h3(kraw[i][:]),
                                                        in1=stk[i][:, 2, :].unsqueeze(2).to_broadcast([128, 8, 64]), op=ALU.mult),
                  reads=[f"kraw{i}", f"stkr{i}"], writes=[f"kf{i}"], n=512)
                A("dve", lambda e, i=i: e.tensor_tensor(out=h3(kf[i][:]), in0=h3(kf[i][:]),
                                                        in1=gk[:].unsqueeze(1).to_broadcast([128, 8, 64]), op=ALU.mult),
                  reads=[f"kf{i}", "gk"], writes=[f"kf{i}"], n=512)
                A("act", lambda e, i=i: e.copy(out=kb[i][:], in_=kf[i][:]), reads=[f"kf{i}"], writes=[f"kb{i}"], n=512)

            def s4(i):
                def mm(e, i=i):
                    r = None
                    for cp in range(4):
                        r = e.matmul(km_ps[:, cp:cp + 1], lhsT=kf[i][:, cp * 128:(cp + 1) * 128], rhs=ones_f[:, 0:1], start=True, stop=True)
                    return r
                A("pe", mm, reads=[f"kf{i}", "ones_f"], writes=["sm_ps"], cost=900)
                if i == 0:
                    A("dve", lambda e: e.tensor_scalar(out=kmeanT[:, :, st], in0=km_ps, scalar1=1.0 / 256, scalar2=None, op0=ALU.mult),
                      reads=["sm_ps"], writes=["kmeanT"], n=4)
                else:
                    A("dve", lambda e: e.scalar_tensor_tensor(out=kmeanT[:, :, st], in0=km_ps, scalar=1.0 / 256, in1=kmeanT[:, :, st],
                                                              op0=ALU.mult, op1=ALU.add),
                      reads=["sm_ps"], writes=["kmeanT"], n=4)

            def s5(i):
                t = ctx[i]["t"]
                kTt, kTtag = kT_sbs.next()

                def tr(e, i=i):
                    r = None
                    for h in range(8):
                        r = e.transpose(out=tp_ps[0:64, h, :], in_=kb[i][:, h * 64:(h + 1) * 64], identity=identb[:])
                    return r
                A("pe", tr, reads=[f"kb{i}", "identb"], writes=["tp_ps"])
                A("act", lambda e, kTt=kTt: e.copy(out=kTt[:], in_=tp_ps[0:64, :, :]), reads=["tp_ps"], writes=[kTtag], n=D)
                A("sp", lambda e, kTt=kTt, t=t: e.dma_start(out=T["kT_scr"][:, :, t * 128:(t + 1) * 128], in_=kTt[:]),
                  reads=[kTtag], writes=[("kT_scr", t)], dma_key=("st",) + kTtag)

            def s6(i):
                if st < 1:
                    return

                def tr(e, i=i):
                    r = None
                    for cp in range(4):
                        r = e.transpose(out=misc_ps[:, cp * 128:(cp + 1) * 128], in_=qf[i][:, cp * 128:(cp + 1) * 128], identity=identf[:])
                    return r
                A("pe", tr, reads=[f"qf{i}", "identf"], writes=["misc_ps"], cost=900)
                A("act", lambda e, i=i: e.copy(out=qfT[i][:].rearrange("p a b -> p (a b)"), in_=misc_ps[:]), reads=["misc_ps"], writes=[f"qfT{i}"], n=512)

            def s7(i):
                if st < 1:
                    return

                def mm(e, i=i):
                    r = None
                    for h in range(8):
                        pb = (h % 2) * 64
                        r = e.matmul(gate_ps[:, h, :], lhsT=qfT[i][pb:pb + 64, h // 2, :], rhs=kmeanT[pb:pb + 64, h // 2, :], start=True, stop=True)
                    return r
                A("pe", mm, reads=[f"qfT{i}", "kmeanT"], writes=["sm_ps"], cost=1500)
                A("dve", lambda e, i=i: e.tensor_tensor(out=gm[i][:], in0=gate_ps, in1=negm[:, st:st + 1, :].to_broadcast([128, 8, 16]), op=ALU.add),
                  reads=["sm_ps", "negm"], writes=[f"gm{i}"], n=128)
                for h in range(8):
                    A("dve", lambda e, h=h, i=i: e.max(out=m8[i][:, h, :], in_=gm[i][:, h, :]), reads=[f"gm{i}"], writes=[(f"m8{i}", h)], n=16)
                A("dve", lambda e, i=i: e.tensor_tensor(out=msk[i][:], in0=gm[i][:], in1=m8[i][:, :, 2:3].to_broadcast([128, 8, 16]), op=ALU.is_ge),
                  reads=[f"gm{i}"] + [(f"m8{i}", h) for h in range(8)], writes=[f"msk{i}"], n=128)
                A("dve", lambda e, i=i: e.tensor_scalar(out=qaug[i][:, :, 64:80], in0=msk[i][:], scalar1=BIG, scalar2=-BIG, op0=ALU.mult, op1=ALU.add),
                  reads=[f"msk{i}"], writes=[f"qaug_b{i}"], n=128)

            def s8(i):
                t = ctx[i]["t"]
                qTt, qTtag = qT_sbs.next()

                def tr(e, i=i):
                    r = None
                    for h in range(8):
                        r = e.transpose(out=tp_ps[0:80, h, :], in_=qaug[i][:, h, :], identity=identb[:])
                    return r
                A("pe", tr, reads=[f"qaug_q{i}", f"qaug_b{i}", "identb"], writes=["tp_ps"])
                A("act", lambda e, qTt=qTt: e.copy(out=qTt[:], in_=tp_ps[0:80, :, :]), reads=["tp_ps"], writes=[qTtag], n=D)
                A("sp", lambda e, qTt=qTt, t=t: e.dma_start(out=T["qT_scr"][:, :, t * 128:(t + 1) * 128], in_=qTt[:]),
                  reads=[qTtag], writes=[("qT_scr", t)], dma_key=("st",) + qTtag)

            import os as _os7
            if _os7.environ.get("TILE_SEQ", "1") == "1":
                for i in range(2):
                    for stage in (s0, s1, s2, s3, s4, s5, s6, s7, s8):
                        stage(i)
            else:
                for stage in (s0, s1, s2, s3, s4, s5, s6, s7, s8):
                    for i in range(2):
                        stage(i)

        def fm(st):
            bp = st % 2
            F = fmS[bp]
            ycl_t, ycl_tag = ycl_sbs.next()
            for fc in (6, 7, 8, 9, 2, 4, 0, 3, 5, 1):
                bank, btag = fm_pss.next()

                def mm(e, bank=bank, fc=fc, bp=bp):
                    r = None
                    for c in range(8):
                        r = e.matmul(bank[:, 0:256], lhsT=w_sb[:, c, 1536 + fc * 128:1536 + (fc + 1) * 128], rhs=xnT[bp][:, c, :],
                                     start=(c == 0), stop=(c == 7))
                    return r
                A("pe", mm, reads=w_tags + [("xnT", bp, 0), ("xnT", bp, 1)], writes=[btag], cost=1100)
                if fc in (6, 7):
                    lc = fc - 6
                    A("act", lambda e, bank=bank, lc=lc, fc=fc: e.activation(out=lx[:, lc, 3:259], in_=bank[:, 0:256], func=AF.Identity, bias=bcol[:, fc:fc + 1]),
                      reads=[btag, "bcol"], writes=[f"lx{lc}"])
                else:
                    A("act", lambda e, bank=bank, fc=fc, F=F: e.activation(out=F[:, fc, :], in_=bank[:, 0:256], func=AF.Identity, bias=bcol[:, fc:fc + 1]),
                      reads=[btag, "bcol"], writes=[("fmS", fc)])
            for lc in range(2):
                xr, ea, sa, ei, uu, gz, yy = lt[lc]
                tg = lambda nm, lc=lc: f"{nm}{lc}"
                lxb, hbb = lx[:, lc, :], hb[:, lc, :]
                cw = [cols_sb[:, l, 62 + lc * 4 + k:62 + lc * 4 + k + 1] for k in range(4)]
                cb = cols_sb[:, l, 70 + lc:71 + lc]
                nba = lruc_sb[:, l, 0 + lc:1 + lc]
                nbx = lruc_sb[:, l, 2 + lc:3 + lc]
                sp8 = lruc_sb[:, l, 4 + lc:5 + lc]
                G = F[:, 8 + lc, :]
                gtag = ("fmS", 8 + lc)
                A("dve", lambda e, lxb=lxb, cw=cw, cb=cb, xr=xr: e.tensor_scalar(out=xr[:], in0=lxb[:, 0:256], scalar1=cw[0], scalar2=cb, op0=ALU.mult, op1=ALU.add),
                  reads=[tg("lx")], writes=[tg("xr")])
                for k in range(1, 4):
                    A("dve", lambda e, lxb=lxb, cw=cw, k=k, xr=xr: e.scalar_tensor_tensor(out=xr[:], in0=lxb[:, k:k + 256], scalar=cw[k], in1=xr[:],
                                                                                      op0=ALU.mult, op1=ALU.add),
                      reads=[tg("lx"), tg("xr")], writes=[tg("xr")])
                A("dve", lambda e, lxb=lxb: e.tensor_copy(out=lxb[:, 0:3], in_=lxb[:, 256:259]), reads=[tg("lx"), tg("xr")], writes=[tg("lx")], n=3)
                A("act", lambda e, lc=lc, xr=xr: e.copy(out=xrb[lc][:], in_=xr[:]), reads=[tg("xr")], writes=[tg("xrb")])
                rb, rbtag = fm_pss.next()
                A("pe", lambda e, lc=lc, rb=rb: e.matmul(rb[:, 0:256], lhsT=wabd[:, 0, lc, :], rhs=xrb[lc][:], start=True, stop=True),
                  reads=[tg("xrb")] + wabd_tags, writes=[rbtag], cost=200)
                A("act", lambda e, nba=nba, rb=rb, ea=ea: e.activation(out=ea[:], in_=rb[:, 0:256], func=AF.Exp, scale=-1.0, bias=nba),
                  reads=[rbtag], writes=[tg("ea")])
                ib, ibtag = fm_pss.next()
                A("pe", lambda e, lc=lc, ib=ib: e.matmul(ib[:, 0:256], lhsT=wabd[:, 1, lc, :], rhs=xrb[lc][:], start=True, stop=True),
                  reads=[tg("xrb")] + wabd_tags, writes=[ibtag], cost=200)
                A("act", lambda e, nbx=nbx, ib=ib, ei=ei: e.activation(out=ei[:], in_=ib[:, 0:256], func=AF.Exp, scale=-1.0, bias=nbx),
                  reads=[ibtag], writes=[tg("ei")])
                A("act", lambda e, ea=ea: e.activation(out=ea[:], in_=ea[:], func=AF.Ln, bias=1.0), reads=[tg("ea")], writes=[tg("ea")])
                A("act", lambda e, ea=ea: e.activation(out=ea[:], in_=ea[:], func=AF.Exp, scale=-1.0), reads=[tg("ea")], writes=[tg("ea")])
                A("act", lambda e, ea=ea, sp8=sp8: e.activation(out=ea[:], in_=ea[:], func=AF.Exp, scale=sp8), reads=[tg("ea")], writes=[tg("ea")])
                A("act", lambda e, ea=ea, sa=sa: e.activation(out=sa[:], in_=ea[:], func=AF.Square), reads=[tg("ea")], writes=[tg("sa")])
                A("act", lambda e, sa=sa: e.activation(out=sa[:], in_=sa[:], func=AF.Ln, scale=-1.0, bias=1.0), reads=[tg("sa")], writes=[tg("sa")])
                A("act", lambda e, sa=sa: e.activation(out=sa[:], in_=sa[:], func=AF.Exp, scale=0.5), reads=[tg("sa")], writes=[tg("sa")])
                A("act", lambda e, ei=ei: e.activation(out=ei[:], in_=ei[:], func=AF.Ln, bias=1.0), reads=[tg("ei")], writes=[tg("ei")])
                A("act", lambda e, ei=ei: e.activation(out=ei[:], in_=ei[:], func=AF.Exp, scale=-1.0), reads=[tg("ei")], writes=[tg("ei")])
                A("dve", lambda e, ei=ei, xr=xr, uu=uu: e.tensor_tensor(out=uu[:], in0=ei[:], in1=xr[:], op=ALU.mult), reads=[tg("ei"), tg("xr")], writes=[tg("uu")])
                A("dve", lambda e, sa=sa, uu=uu: e.tensor_tensor(out=uu[:], in0=uu[:], in1=sa[:], op=ALU.mult), reads=[tg("uu"), tg("sa")], writes=[tg("uu")])
                A("dve", lambda e, hbb=hbb, ea=ea, uu=uu: e.tensor_tensor_scan(out=hbb[:, 1:257], data0=ea[:], data1=uu[:], initial=hbb[:, 0:1],
                                                                              op0=ALU.mult, op1=ALU.add),
                  reads=[tg("ea"), tg("uu"), tg("hb")], writes=[tg("hbh")], n=512)
                A("act", lambda e, G=G, gz=gz: e.activation(out=gz[:], in_=G, func=AF.Square), reads=[gtag], writes=[tg("gz")])
                A("dve", lambda e, gz=gz: e.tensor_scalar(out=gz[:], in0=gz[:], scalar1=0.044715, scalar2=1.0, op0=ALU.mult, op1=ALU.add),
                  reads=[tg("gz")], writes=[tg("gz")])
                A("dve", lambda e, gz=gz, G=G: e.tensor_tensor(out=gz[:], in0=gz[:], in1=G, op=ALU.mult), reads=[tg("gz"), gtag], writes=[tg("gz")])
                A("act", lambda e, gz=gz: e.activation(out=gz[:], in_=gz[:], func=AF.Exp, scale=-1.5957691216057308), reads=[tg("gz")], writes=[tg("gz")])
                A("act", lambda e, gz=gz: e.activation(out=gz[:], in_=gz[:], func=AF.Ln, bias=1.0), reads=[tg("gz")], writes=[tg("gz")])
                A("act", lambda e, gz=gz: e.activation(out=gz[:], in_=gz[:], func=AF.Exp, scale=-1.0), reads=[tg("gz")], writes=[tg("gz")])
                A("dve", lambda e, gz=gz, G=G: e.tensor_tensor(out=gz[:], in0=gz[:], in1=G, op=ALU.mult), reads=[tg("gz"), gtag], writes=[tg("gz")])
                A("dve", lambda e, hbb=hbb, gz=gz, yy=yy: e.tensor_tensor(out=yy[:], in0=hbb[:, 1:257], in1=gz[:], op=ALU.mult),
                  reads=[tg("hbh"), tg("gz")], writes=[tg("yy")])
                A("dve", lambda e, hbb=hbb: e.tensor_copy(out=hbb[:, 0:1], in_=hbb[:, 256:257]), reads=[tg("hbh"), tg("yy")], writes=[tg("hb")], n=1)
                A("act", lambda e, lc=lc, ycl_t=ycl_t, yy=yy: e.copy(out=ycl_t[:, 2 + lc, :], in_=yy[:]), reads=[tg("yy")], writes=[ycl_tag + (2 + lc,)])
                A("act", lambda e, lc=lc, yy=yy, bp=bp: e.activation(out=ysq[bp][:, 2 + lc, :], in_=yy[:], func=AF.Square), reads=[tg("yy")], writes=[("ysq", bp, 2 + lc)])
            for cc in range(2):
                cub = cu[:, cc, :]
                tg = lambda nm, cc=cc: f"{nm}{cc}"
                w0 = cols_sb[:, l, 56 + cc * 3 + 0:56 + cc * 3 + 1]
                w1 = cols_sb[:, l, 56 + cc * 3 + 1:56 + cc * 3 + 2]
                w2 = cols_sb[:, l, 56 + cc * 3 + 2:56 + cc * 3 + 3]
                Bt, Ct, Ut = ("fmS", 0 + cc), ("fmS", 2 + cc), ("fmS", 4 + cc)
                A("dve", lambda e, cub=cub, cc=cc, F=F: e.tensor_tensor(out=cub[:, 2:258], in0=F[:, 2 + cc, :], in1=F[:, 4 + cc, :], op=ALU.mult),
                  reads=[Ct, Ut], writes=[tg("cu")])
                A("dve", lambda e, cub=cub, w0=w0, cc=cc: e.tensor_scalar(out=ct[cc][:], in0=cub[:, 0:256], scalar1=w0, scalar2=None, op0=ALU.mult),
                  reads=[tg("cu")], writes=[tg("ct")])
                A("dve", lambda e, cub=cub, w1=w1, cc=cc: e.scalar_tensor_tensor(out=ct[cc][:], in0=cub[:, 1:257], scalar=w1, in1=ct[cc][:], op0=ALU.mult, op1=ALU.add),
                  reads=[tg("cu"), tg("ct")], writes=[tg("ct")])
                A("dve", lambda e, cub=cub, w2=w2, cc=cc: e.scalar_tensor_tensor(out=ct[cc][:], in0=cub[:, 2:258], scalar=w2, in1=ct[cc][:], op0=ALU.mult, op1=ALU.add),
                  reads=[tg("cu"), tg("ct")], writes=[tg("ct")])
                A("dve", lambda e, cub=cub: e.tensor_copy(out=cub[:, 0:2], in_=cub[:, 256:258]), reads=[tg("cu"), tg("ct")], writes=[tg("cu")], n=2)
                A("dve", lambda e, cc=cc, F=F: e.tensor_tensor(out=cy[cc][:], in0=F[:, 0 + cc, :], in1=ct[cc][:], op=ALU.mult),
                  reads=[Bt, tg("ct")], writes=[tg("cy")])
                A("act", lambda e, cc=cc, ycl_t=ycl_t: e.copy(out=ycl_t[:, cc, :], in_=cy[cc][:]), reads=[tg("cy")], writes=[ycl_tag + (cc,)])
                A("act", lambda e, cc=cc, bp=bp: e.activation(out=ysq[bp][:, cc, :], in_=cy[cc][:], func=AF.Square), reads=[tg("cy")], writes=[("ysq", bp, cc)])
            for i in range(2):
                t = 2 * st + i

                def mm(e, i=i, bp=bp):
                    r = None
                    for g in range(2):
                        for c2 in range(2):
                            r = e.matmul(ss_ps[:, g:g + 1], lhsT=ysq[bp][:, 2 * g + c2, i * 128:(i + 1) * 128], rhs=ones_bf[:, 0:1],
                                         start=(c2 == 0), stop=(c2 == 1))
                    return r
                A("pe", mm, reads=[("ysq", bp, c) for c in range(4)] + ["ones_bf"], writes=["sm_ps"], cost=500)
                rstd2(grstd[:, t, :], ss_ps, 256, ["sm_ps"], [("grstd", t)], stg[i][:, 0:2], f"stg{i}")
            A("sp", lambda e, ycl_t=ycl_t, st=st: e.dma_start(
                out=T["ycl_scr"].rearrange("(c p) t -> p c t", p=128)[:, :, st * 256:(st + 1) * 256], in_=ycl_t[:]),
              reads=[ycl_tag + (c,) for c in range(4)], writes=[("ycl_scr", st)], dma_key=("st",) + ycl_tag)

        tiles(0)
        for st in range(nblk):
            if st + 1 < nblk:
                tiles(st + 1)
            fm(st)


def phase_B(nc, P, top, sb0, ps0, l, x_src, T, nblk):
    A = P.add
    sb = lambda es, name, shape, dt: sb0(es, f"{name}_B{l}", shape, dt)
    ps = lambda es, name, shape, dt: ps0(es, f"{name}_B{l}", shape, dt)
    cols_sb, identb, ones_bf, ones_f, grstd = T["cols_sb"], T["identb"], T["ones_bf"], T["ones_f"], T["grstd"]
    rstd_from_ssq = T["rstd_from_ssq"]
    with ExitStack() as es:
        kT = sb(es, "kT_all", [128, 8, S], BF16)
        v_all = sb(es, "v_all", [128, NT, 520], BF16)
        wo = sb(es, "w_out_sb", [128, 8, D], BF16)
        g1bc = sb(es, "g1bc", [128, D], F32)
        caus = sb(es, "caus", [128, 2, 256], BF16)
        qT_blks = Rot([sb(es, f"qTb{i}", [128, 8, 256], BF16) for i in range(2)], "qTb")
        ycl_blks = Rot([sb(es, f"yclb{i}", [128, 4, 256], BF16) for i in range(2)], "yclb")
        x_sbs = Rot([sb(es, f"xB{i}", [128, D], F32) for i in range(4)], "xB")
        p_sbs = Rot([sb(es, f"pB{i}", [128, 2, 256], BF16) for i in range(3)], "pB")
        o_sbs = Rot([sb(es, f"oB{i}", [65, 256], F32) for i in range(2)], "oB")
        rdens = Rot([sb(es, f"rdB{i}", [65, 256], F32) for i in range(2)], "rdB")
        yattn = sb(es, "yattn", [128, 4, 256], F32)
        ya_bf = sb(es, "ya_bf", [128, 4, 256], BF16)
        ysq = sb(es, "ysqB", [128, 4, 256], BF16)
        stb = sb(es, "stb", [128, 8], F32)
        s_pss = Rot([ps(es, f"s_ps{i}", [128, 2, 256], F32) for i in range(2)], "s_ps")
        oT_pss = Rot([ps(es, f"oT_ps{i}", [128, 512], F32) for i in range(2)], "oT_ps")
        bc_ps = ps(es, "bc_ps", [128, 512], F32)
        sm_ps = ps(es, "smB_ps", [128, 512], F32)
        op_pss = Rot([ps(es, f"op_ps{i}", [128, 512], F32) for i in range(2)], "op_ps")

        for c in range(8):
            A("pool", lambda e, c=c: e.dma_start(out=wo[:, c, :], in_=T["w_out"][l, c * 128:(c + 1) * 128, :]),
              writes=[("wo", c)], dma_key=("w", c))
        A("sp", lambda e: e.dma_start(out=g1bc[:], in_=T["gates_scr"][l, 0:1, :].partition_broadcast(128)), writes=["g1bc"], dma_key="a0")
        A("sp", lambda e: e.dma_start(out=caus[:], in_=T["c_caus"][:, :, :]), writes=["caus"], dma_key="a1")
        for c in range(8):
            eng = "dve"
            A(eng, lambda e, c=c: e.scalar_tensor_tensor(out=wo[:, c, :], in0=wo[:, c, :], scalar=cols_sb[:, l, 48 + c:49 + c], in1=g1bc[:],
                                                         op0=ALU.mult, op1=ALU.mult),
              reads=["g1bc"], writes=[("wo", c)])
        wo_tags = [("wo", c) for c in range(8)]
        for h in range(8):
            A("pool", lambda e, h=h: e.memset(kT[64:128, h, :], 0.0), writes=[("kToh", h)], n=4096)
        for k_, qt_ in enumerate(qT_blks.t):
            A("pool", lambda e, qt_=qt_: e.memset(qt_[:], 0.0), writes=[("qTb", k_)], n=2048)
        for h in range(8):
            A("sp", lambda e, h=h: e.dma_start(out=kT[64:80, h, :], in_=T["c_oh"][:, :]), writes=[("kToh", h)], dma_key=("spk", h))
        oh_tags = [("kToh", h) for h in range(8)]
        for j in range(nblk):
            A("sp", lambda e, j=j: e.dma_start(out=kT[0:64, :, j * 256:(j + 1) * 256], in_=T["kT_scr"][:, :, j * 256:(j + 1) * 256]),
              writes=[("kT", j)], dma_key=("kT", j % 4))
            A("sp", lambda e, j=j: e.dma_start(out=v_all[:, 2 * j:2 * j + 2, :], in_=T["v_scr"][2 * j:2 * j + 2, :, :].rearrange("t p f -> p t f")),
              writes=[("v", j)], dma_key=("v", j % 4))

        for qb in range(nblk):
            qTb, qtag = qT_blks.next()
            yclb, ytag = ycl_blks.next()
            A("sp", lambda e, qTb=qTb, qb=qb: e.dma_start(out=qTb[0:80, :, :], in_=T["qT_scr"][:, :, qb * 256:(qb + 1) * 256]), writes=[qtag], dma_key=qtag)
            A("sp", lambda e, yclb=yclb, qb=qb: e.dma_start(
                out=yclb[:], in_=T["ycl_scr"].rearrange("(c p) t -> p c t", p=128)[:, :, qb * 256:(qb + 1) * 256]), writes=[ytag], dma_key=ytag)
            xts = []
            for i in range(2):
                t = 2 * qb + i
                xt, xtag = x_sbs.next()
                A("sp", lambda e, xt=xt, t=t: e.dma_start(out=xt[:], in_=x_src[t * 128:(t + 1) * 128, :]), writes=[xtag], dma_key=xtag)
                xts.append((xt, xtag))
            for h in range(8):
                oT, otag = oT_pss.next()
                for j in range(qb + 1):
                    own = (j == qb)
                    sp_, stag = s_pss.next()
                    if not own:
                        def mm(e, sp_=sp_, j=j, h=h, qTb=qTb):
                            r = None
                            for kk in range(2):
                                r = e.matmul(sp_[:, kk, :], lhsT=kT[:, h, (2 * j + kk) * 128:(2 * j + kk + 1) * 128], rhs=qTb[:, h, :],
                                             start=True, stop=True)
                            return r
                        A("pe", mm, reads=[("kT", j), ("kToh", h), qtag], writes=[stag])
                    else:
                        def mm(e, sp_=sp_, j=j, h=h, qTb=qTb):
                            r = None
                            for kk in range(2):
                                e.matmul(sp_[:, kk, :], lhsT=kT[0:64, h, (2 * j + kk) * 128:(2 * j + kk + 1) * 128], rhs=qTb[0:64, h, :],
                                         start=True, stop=False)
                                r = e.matmul(sp_[:, kk, :], lhsT=identb[:], rhs=caus[:, kk, :], start=False, stop=True)
                            return r
                        A("pe", mm, reads=[("kT", j), qtag, "caus", "identb"], writes=[stag])
                    pt, ptag = p_sbs.next()
                    A("act", lambda e, pt=pt, sp_=sp_: e.activation(out=pt[:], in_=sp_[:], func=AF.Exp), reads=[stag], writes=[ptag])

                    def mm(e, pt=pt, oT=oT, j=j, h=h, qb=qb):
                        r = None
                        for kk in range(2):
                            r = e.matmul(oT[0:65, 0:256], lhsT=v_all[:, 2 * j + kk, h * 65:(h + 1) * 65], rhs=pt[:, kk, :],
                                         start=(j == 0 and kk == 0), stop=(j == qb and kk == 1))
                        return r
                    A("pe", mm, reads=[("v", j), ptag], writes=[otag])
                ot, ottag = o_sbs.next()
                rd, rdtag = rdens.next()
                A("act", lambda e, ot=ot, oT=oT: e.copy(out=ot[:], in_=oT[0:65, 0:256]), reads=[otag], writes=[ottag])
                A("dve", lambda e, ot=ot, rd=rd: e.reciprocal(out=rd[64:65, :], in_=ot[64:65, :]), reads=[ottag], writes=[rdtag])
                A("pe", lambda e, rd=rd: e.matmul(bc_ps[0:64, 0:256], lhsT=ones_f[64:65, 0:64], rhs=rd[64:65, :], start=True, stop=True),
                  reads=[rdtag, "ones_f"], writes=["bc_ps"])
                pb = (h % 2) * 64
                A("dve", lambda e, ot=ot, pb=pb, h=h: e.tensor_tensor(out=yattn[pb:pb + 64, h // 2, :], in0=ot[0:64, :], in1=bc_ps[0:64, 0:256], op=ALU.mult),
                  reads=[ottag, "bc_ps"], writes=[("yattn", h)])
            ya_tags = [("yattn", h) for h in range(8)]
            A("pool", lambda e: e.tensor_copy(out=ya_bf[:], in_=yattn[:]), reads=ya_tags, writes=["ya_bf"])
            A("act", lambda e: e.activation(out=ysq[:], in_=yattn[:], func=AF.Square), reads=ya_tags, writes=["ysqB"])
            for i in range(2):
                t = 2 * qb + i
                xt, xtag = xts[i]

                def mm(e, i=i):
                    r = None
                    for c in range(4):
                        r = e.matmul(sm_ps[:, 0:1], lhsT=ysq[:, c, i * 128:(i + 1) * 128], rhs=ones_bf[:, 0:1], start=(c == 0), stop=(c == 3))
                    return r
                A("pe", mm, reads=["ysqB", "ones_bf"], writes=["smB_ps"])
                rstd_from_ssq(stb[:, 2:3], sm_ps[:, 0:1], 512, ["smB_ps"], ["stbr"], stb[:, 1:2], "stbt")
                for n in range(2):
                    nsl = slice(n * 512, (n + 1) * 512)
                    for g in range(3):
                        op_, optag = op_pss.next()
                        if g == 0:
                            srcs = [(ya_bf, c, c) for c in range(4)]
                            rtags = ["ya_bf"]
                            scal = stb[:, 2:3]
                            stag2 = ["stbr"]
                        else:
                            srcs = [(yclb, 2 * (g - 1) + c2, 4 + 2 * (g - 1) + c2) for c2 in range(2)]
                            rtags = [ytag]
                            scal = grstd[:, t, g - 1:g]
                            stag2 = []

                        def mm(e, op_=op_, srcs=srcs, i=i, nsl=nsl):
                            r = None
                            for k, (src, sc, wc) in enumerate(srcs):
                                r = e.matmul(op_[:], lhsT=src[:, sc, i * 128:(i + 1) * 128], rhs=wo[:, wc, nsl], start=(k == 0), stop=(k == len(srcs) - 1))
                            return r
                        A("pe", mm, reads=rtags + wo_tags, writes=[optag])
                        A("dve", lambda e, op_=op_, xt=xt, scal=scal, nsl=nsl: e.scalar_tensor_tensor(
                            out=xt[:, nsl], in0=op_[:], scalar=scal, in1=xt[:, nsl], op0=ALU.mult, op1=ALU.add),
                          reads=[optag, xtag] + stag2, writes=[xtag])
                A("sp", lambda e, xt=xt, t=t: e.dma_start(out=T["x1_scr"][t * 128:(t + 1) * 128, :], in_=xt[:]),
                  reads=[xtag], writes=[("x1_scr", t)], dma_key=("st",) + xtag)


def phase_C(nc, P, top, sb0, ps0, l, x_dst, T, nblk):
    A = P.add
    sb = lambda es, name, shape, dt: sb0(es, f"{name}_C{l}", shape, dt)
    ps = lambda es, name, shape, dt: ps0(es, f"{name}_C{l}", shape, dt)
    eff_sb, shb_sb, identb = T["eff_sb"], T["shb_sb"], T["identb"]
    rstd_from_ssq = T["rstd_from_ssq"]
    with ExitStack() as es:
        wu = sb(es, "w_up_sb", [128, 8, DFF], BF16)
        wd = sb(es, "w_dn_sb", [128, 32, D], BF16)
        bup = sb(es, "bup", [128, 32], F32)
        x_sbs = Rot([sb(es, f"xC{i}", [128, D], F32) for i in range(2)], "xC")
        junk = sb(es, "junkC", [128, D], BF16)
        xn = sb(es, "xnC", [128, D], BF16)
        xnT = sb(es, "xnTC", [128, 8, 256], BF16)
        hT = sb(es, "hT", [128, 32, 256], BF16)
        rts = Rot([sb(es, f"rt{i}", [128, 256], BF16) for i in range(3)], "rt")
        stc = sb(es, "stc", [128, 8], F32)
        tp_ps = ps(es, "tpC_ps", [128, 8, 128], BF16)
        up_pss = Rot([ps(es, f"up_ps{i}", [128, 512], F32) for i in range(3)], "up_ps")
        dn_pss = Rot([ps(es, f"dn_ps{i}", [128, 512], F32) for i in range(2)], "dn_ps")
        sm_ps = ps(es, "smC_ps", [128, 512], F32)

        for c in range(8):
            A("pool", lambda e, c=c: e.dma_start(out=wu[:, c, :], in_=T["w_up"][l, c * 128:(c + 1) * 128, :], max_dma_last_dim=4096),
              writes=[("wu", c)], dma_key=("w", c))
        for f in range(32):
            A("pool", lambda e, f=f: e.dma_start(out=wd[:, f, :], in_=T["w_down"][l, f * 128:(f + 1) * 128, :]),
              writes=[("wd", f)], dma_key=("wdk", f % 8))
        g2t, g2tag = x_sbs.t[1], ("xC", 1)
        A("sp", lambda e: e.dma_start(out=g2t[:], in_=T["gates_scr"][l, 1:2, :].partition_broadcast(128)), writes=[g2tag], dma_key=g2tag)
        wu_tags = [("wu", c) for c in range(8)]
        wd_tags = [("wd", f) for f in range(32)]

        def mm(e):
            r = None
            for f in range(32):
                for c in range(8):
                    r = e.matmul(sm_ps[:, f:f + 1], lhsT=wu[:, c, f * 128:(f + 1) * 128], rhs=shb_sb[:, l, 1, c:c + 1], start=(c == 0), stop=(c == 7))
            return r
        A("pe", mm, reads=wu_tags, writes=["smC_ps"])
        A("dve", lambda e: e.tensor_copy(out=bup[:], in_=sm_ps[:, 0:32]), reads=["smC_ps"], writes=["bup"])
        for c in range(8):
            A("act", lambda e, c=c: e.activation(out=wu[:, c, :], in_=wu[:, c, :], func=AF.Copy, scale=eff_sb[:, l, 16 + c:17 + c]),
              reads=[], writes=[("wu", c)], n=DFF)
        for f in range(32):
            eng = "pool" if f % 4 == 3 else "dve"
            A(eng, lambda e, f=f: e.tensor_tensor(out=wd[:, f, :], in0=wd[:, f, :], in1=g2t[:], op=ALU.mult), reads=[g2tag], writes=[("wd", f)], n=D)

        for st in range(nblk):
            xts = []
            for i in range(2):
                t = 2 * st + i
                xt, xtag = x_sbs.next()
                xts.append((xt, xtag))
                A("sp", lambda e, xt=xt, t=t: e.dma_start(out=xt[:], in_=T["x1_scr"][t * 128:(t + 1) * 128, :]), writes=[xtag], dma_key=xtag)
                A("act", lambda e, xt=xt: e.activation(out=junk[:], in_=xt[:], func=AF.Square, accum_out=stc[:, 0:1]),
                  reads=[xtag], writes=["junkC", "stc"])
                rstd_from_ssq(stc[:, 2:3], stc[:, 0:1], D, ["stc"], ["stcr"], stc[:, 1:2], "stct")
                A("dve", lambda e, xt=xt: e.tensor_scalar(out=xn[:], in0=xt[:], scalar1=stc[:, 2:3], scalar2=None, op0=ALU.mult),
                  reads=[xtag, "stcr"], writes=["xnC"])

                def tr(e):
                    r = None
                    for c in range(8):
                        r = e.transpose(out=tp_ps[:, c, :], in_=xn[:, c * 128:(c + 1) * 128], identity=identb[:])
                    return r
                A("pe", tr, reads=["xnC", "identb"], writes=["tpC_ps"])
                A("act", lambda e, i=i: e.copy(out=xnT[:, :, i * 128:(i + 1) * 128], in_=tp_ps[:]), reads=["tpC_ps"], writes=[("xnTC", i)])
            for f in range(32):
                up, uptag = up_pss.next()

                def mm(e, up=up, f=f):
                    r = None
                    for c in range(8):
                        r = e.matmul(up[:, 0:256], lhsT=wu[:, c, f * 128:(f + 1) * 128], rhs=xnT[:, c, :], start=(c == 0), stop=(c == 7))
                    return r
                A("pe", mm, reads=wu_tags + [("xnTC", 0), ("xnTC", 1)], writes=[uptag])
                rt, rttag = rts.next()
                A("dve", lambda e, up=up, rt=rt, f=f: e.tensor_scalar(out=rt[:], in0=up[:, 0:256], scalar1=bup[:, f:f + 1], scalar2=0.0, op0=ALU.add, op1=ALU.max),
                  reads=[uptag, "bup"], writes=[rttag])
                A("act", lambda e, rt=rt, f=f: e.activation(out=hT[:, f, :], in_=rt[:], func=AF.Square), reads=[rttag], writes=[("hT", f)])
            hT_tags = [("hT", f) for f in range(32)]
            for i in range(2):
                t = 2 * st + i
                xt, xtag = xts[i]
                for n in range(2):
                    nsl = slice(n * 512, (n + 1) * 512)
                    dn, dntag = dn_pss.next()

                    def mm(e, dn=dn, i=i, nsl=nsl):
                        r = None
                        for f in range(32):
                            r = e.matmul(dn[:], lhsT=hT[:, f, i * 128:(i + 1) * 128], rhs=wd[:, f, nsl], start=(f == 0), stop=(f == 31))
                        return r
                    A("pe", mm, reads=hT_tags + wd_tags, writes=[dntag])
                    A("dve", lambda e, dn=dn, xt=xt, nsl=nsl: e.tensor_tensor(out=xt[:, nsl], in0=dn[:], in1=xt[:, nsl], op=ALU.add),
                      reads=[dntag, xtag], writes=[xtag])
                A("sp", lambda e, xt=xt, t=t: e.dma_start(out=x_dst[t * 128:(t + 1) * 128, :], in_=xt[:]),
                  reads=[xtag], writes=[("x_dst", t)], dma_key=("st",) + xtag)


def _consts():
    bf = ml_dtypes.bfloat16
    identb = np.eye(128, dtype=np.float32).astype(bf)
    identf = np.eye(128, dtype=np.float32)
    oh = np.zeros((16, S), np.float32)
    for j in range(16):
        oh[j, j * 256:(j + 1) * 256] = 1.0
    negmask = np.zeros((16, 16), np.float32)
    for own in range(16):
        negmask[own, own:] = -1e30
    kk = np.arange(128)[:, None]
    qq = np.arange(128)[None, :]
    tri = np.where(kk <= qq, 0.0, -BIG).astype(np.float32)
    caus = np.zeros((128, 2, 256), np.float32)
    caus[:, 0, 0:128] = tri
    caus[:, 1, 0:128] = -BIG
    caus[:, 1, 128:256] = tri
    return dict(c_identb=identb, c_identf=identf, c_oh=oh.astype(bf), c_negmask=negmask.reshape(1, 256),
                c_caus=caus.astype(bf))


def _col(v):
    v = np.asarray(v, np.float32)
    return np.ascontiguousarray(v.reshape(-1, 128).T)


def make_in_maps(inputs, n_cores=8):
    f = lambda k: np.ascontiguousarray(np.asarray(inputs[k], np.float32))
    cols = np.zeros((2, 128, NCOLS), np.float32)
    for l in range(2):
        b = f("b_ada")[l]
        parts = [_col(f("ln1_g")[l]), _col(f("ln2_g")[l]),
                 _col(b[0:1024]), _col(b[1024:2048]), _col(b[3072:4096]), _col(b[4096:5120]),
                 _col(f("mix_norm_g")[l])]
        scw = f("sc_w")[l]
        parts.append(np.concatenate([np.stack([scw[k, cc * 128:(cc + 1) * 128] for k in range(3)], 1) for cc in range(2)], 1))
        lcw = f("lru_conv_w")[l]
        parts.append(np.concatenate([np.stack([lcw[k, cc * 128:(cc + 1) * 128] for k in range(4)], 1) for cc in range(2)], 1))
        parts += [_col(f("lru_conv_b")[l]), _col(f("lru_ba")[l]), _col(f("lru_bx")[l]), _col(f("lru_lambda")[l])]
        cols[l] = np.concatenate(parts, 1)
    shared = dict(cols=cols, b_ada=f("b_ada"), w_ada=f("w_ada"), w_in=f("w_in"), q_norm_g=f("q_norm_g"), k_norm_g=f("k_norm_g"),
                  lru_wa=f("lru_wa"), lru_wx=f("lru_wx"), w_out=f("w_out"), w_up=f("w_up"), w_down=f("w_down"))
    shared.update(_consts())
    x = f("x")
    c = f("c")
    maps = []
    for b in range(n_cores):
        m = dict(shared)
        m["x"] = x[b]
        m["ccol"] = _col(c[b])
        maps.append(m)
    return maps


_NC = None


def kernel(**inputs):
    global _NC
    if _NC is None:
        _NC = build_program()[0]
    maps = make_in_maps(inputs)
    res = run_bass_kernel_spmd(_NC, maps, core_ids=list(range(8)))
    return np.stack([np.asarray(r["out"], np.float32) for r in res.results], 0)
```

```python
import numpy as np
import ml_dtypes
from contextlib import ExitStack
import concourse.bass as bass
import concourse.mybir as mybir
from concourse.bass_utils import run_bass_kernel_spmd

F32 = mybir.dt.float32
BF16 = mybir.dt.bfloat16
AF = mybir.ActivationFunctionType
ALU = mybir.AluOpType
AX = mybir.AxisListType

S = 4096
D = 1024
NT = 32
NB = 16
DIN = 2816
DFF = 4096
BIG = 30000.0
EPS = 1e-6
NCOLS = 78
ENGS = ("pe", "act", "dve", "pool", "sp")
import os as _osg
FP32_GUARD = _osg.environ.get("FP32_GUARD", "1") == "1"


class Prog:
    def __init__(self, nc, strict=True):
        self.nc = nc
        self.strict = strict
        self.ops = []
        self.last_w = {}
        self.readers = {}
        self.last_dma = {}
        self.keymap = {}
        self.nosched = False
        self.phase = 0

    def add(self, eng, fn, reads=(), writes=(), dma_key=None, n=256, cost=None):
        if cost is None:
            if dma_key is not None:
                cost = 3000.0
            elif eng == "pe":
                cost = 400.0
            elif eng == "act":
                cost = 320.0 + n / 1.4
            elif eng == "dve":
                cost = 250.0 + n / 0.96
            else:
                cost = 300.0 + n / 0.5
        if dma_key is not None:
            cls = "W" if eng == "pool" else "H"
            kk = (cls, dma_key)
            if kk not in self.keymap:
                self.keymap[kk] = (cls, sum(1 for q in self.keymap if q[0] == cls))
            dma_key = self.keymap[kk]
        i = len(self.ops)
        deps = set()
        for t in reads:
            if t in self.last_w:
                deps.add(self.last_w[t])
        for t in writes:
            if t in self.last_w:
                deps.add(self.last_w[t])
            for r in self.readers.get(t, ()):
                deps.add(r)
        if dma_key is not None:
            if dma_key in self.last_dma:
                deps.add(self.last_dma[dma_key])
            self.last_dma[dma_key] = i
        deps.discard(i)
        for t in reads:
            self.readers.setdefault(t, []).append(i)
        for t in writes:
            self.last_w[t] = i
            self.readers[t] = []
        self.ops.append(dict(eng=eng, fn=fn, deps=sorted(deps), dma_key=dma_key,
                             phase=self.phase, barrier=False, cost=float(cost), nosched=self.nosched))
        return i

    def barrier(self):
        self.ops.append(dict(eng=None, fn=None, deps=[], dma_key=None,
                             phase=self.phase, barrier=True))
        self.phase += 1
        self.last_w = {}
        self.readers = {}
        self.last_dma = {}
        self.keymap = {}

    def schedule(self, window=48):
        ops = self.ops
        n = len(ops)
        order = []
        start = 0
        while start < n:
            end = start
            while end < n and not ops[end]["barrier"]:
                end += 1
            ids = list(range(start, end))
            import os as _os2
            sp_ = _os2.environ.get("SCHED_PHASES")
            if ids and sp_ is not None and str(ops[ids[0]]["phase"]) not in sp_.split(","):
                order.extend(ids)
            elif ids:
                fz = set(_os2.environ.get("SCHED_FREEZE", "").split(","))
                if ops[ids[0]].get("nosched"):
                    fz |= set(_os2.environ.get("PHASEA_FREEZE", "").split(","))
                order.extend(self._sched_phase(ids, window, fz))
            if end < n:
                order.append(end)
            start = end + 1
        remap = {old: new for new, old in enumerate(order)}
        newops = []
        for old in order:
            o = ops[old]
            o["deps"] = sorted(remap[d] for d in o["deps"])
            newops.append(o)
        self.ops = newops

    def _sched_phase(self, ids, window, freeze=()):
        ops = self.ops
        self._freeze = set(freeze)
        idset = set(ids)
        queues = {e: [i for i in ids if ops[i]["eng"] == e] for e in ENGS}
        qpos = {e: 0 for e in ENGS}
        scheduled = {}
        eng_free = {e: 0.0 for e in ENGS}
        out = []
        remaining = len(ids)
        taken = set()
        while remaining:
            best = None
            for e in ENGS:
                q = queues[e]
                p = qpos[e]
                while p < len(q) and q[p] in taken:
                    p += 1
                qpos[e] = p
                cnt = 0
                k = p
                win = 1 if e in self._freeze else window
                while k < len(q) and cnt < win:
                    i = q[k]
                    k += 1
                    if i in taken:
                        continue
                    cnt += 1
                    ok = True
                    rt = 0.0
                    for d in ops[i]["deps"]:
                        if d in idset:
                            if d not in scheduled:
                                ok = False
                                break
                            if scheduled[d] > rt:
                                rt = scheduled[d]
                    if not ok:
                        continue
                    st = max(eng_free[e], rt)
                    if best is None or st < best[0] - 1e-9 or (abs(st - best[0]) <= 1e-9 and i < best[1]):
                        best = (st, i, e)
                    if rt <= eng_free[e]:
                        break
            assert best is not None, "scheduler deadlock"
            st, i, e = best
            o = ops[i]
            if o["dma_key"] is not None:
                eng_free[e] = st + 120.0
                scheduled[i] = st + o["cost"]
            else:
                eng_free[e] = st + o["cost"]
                scheduled[i] = st + o["cost"] + 150.0
            taken.add(i)
            out.append(i)
            remaining -= 1
        self.est_ns = getattr(self, "est_ns", 0.0) + max(scheduled.values())
        return out

    def emit(self):
        nc = self.nc
        import os as _os1
        if _os1.environ.get("SCHED", "1") == "1":
            self.schedule()
        ops = self.ops
        nph = self.phase + 1
        strict = self.strict

        def same_eng_free(od, o):
            return od["eng"] == o["eng"] and o["dma_key"] is None and (od["eng"] == "pe" or not strict)

        need = [False] * len(ops)
        for i, o in enumerate(ops):
            if o["barrier"]:
                continue
            for d in o["deps"]:
                od = ops[d]
                if od["dma_key"] is not None:
                    continue
                if same_eng_free(od, o):
                    continue
                need[d] = True
        last_in_phase = {}
        for i, o in enumerate(ops):
            if o["barrier"] or o["dma_key"] is not None:
                continue
            last_in_phase[(o["eng"], o["phase"])] = i
        for i in last_in_phase.values():
            need[i] = True
        cnt = {}
        dcnt = {}
        val = [None] * len(ops)
        dma_keys = []
        for i, o in enumerate(ops):
            if o["barrier"]:
                continue
            if o["dma_key"] is not None:
                k = o["dma_key"]
                if k not in dcnt:
                    dcnt[k] = 0
                    dma_keys.append(k)
                dcnt[k] += 16
                val[i] = dcnt[k]
            elif need[i]:
                k = (o["eng"], o["phase"])
                cnt[k] = cnt.get(k, 0) + 1
                val[i] = cnt[k]
        esem = {}
        for (e, ph) in sorted(cnt.keys(), key=lambda t: (t[1], t[0])):
            esem[(e, ph)] = nc.alloc_semaphore(f"s_{e}_{ph}")
        dsem = {k: nc.alloc_semaphore(f"d_{j}") for j, k in enumerate(dma_keys)}
        self.n_sems = len(esem) + len(dsem)
        final_cnt = dict(cnt)
        per = {e: [] for e in ENGS}
        for i, o in enumerate(ops):
            if o["barrier"]:
                for e in ENGS:
                    per[e].append(i)
            else:
                per[o["eng"]].append(i)
        dma_upto = {}
        run = {}
        for i, o in enumerate(ops):
            if o["barrier"]:
                dma_upto[i] = dict(run)
            elif o["dma_key"] is not None:
                run[o["dma_key"]] = val[i]
        dma_final = dict(run)

        def gen(e):
            def body(eng):
                waited = {}

                def w(sem, name, v):
                    if waited.get(name, 0) >= v:
                        return
                    waited[name] = v
                    eng.wait_ge(sem, v)

                for i in per[e]:
                    o = ops[i]
                    if o["barrier"]:
                        ph = o["phase"]
                        for e2 in ("pe", "act", "dve", "pool"):
                            v = final_cnt.get((e2, ph), 0)
                            if v:
                                w(esem[(e2, ph)], (e2, ph), v)
                        for k, v in dma_upto[i].items():
                            w(dsem[k], k, v)
                        continue
                    for d in o["deps"]:
                        od = ops[d]
                        if od["dma_key"] is not None:
                            w(dsem[od["dma_key"]], od["dma_key"], val[d])
                        else:
                            if same_eng_free(od, o):
                                continue
                            k = (od["eng"], od["phase"])
                            w(esem[k], k, val[d])
                    inst = o["fn"](eng)
                    if o["dma_key"] is not None:
                        inst.then_inc(dsem[o["dma_key"]], 16)
                    elif need[i]:
                        inst.then_inc(esem[(e, o["phase"])], 1)
                if e == "sp":
                    for k, v in dma_final.items():
                        w(dsem[k], k, v)
                    for (e2, ph), v in final_cnt.items():
                        w(esem[(e2, ph)], (e2, ph), v)
            return body

        with nc.Block() as block:
            block.tensor(gen("pe"))
            block.scalar(gen("act"))
            block.vector(gen("dve"))
            block.gpsimd(gen("pool"))
            block.sync(gen("sp"))


class Rot:
    def __init__(self, tensors, name):
        self.t = tensors
        self.name = name
        self.i = 0

    def next(self):
        k = self.i % len(self.t)
        self.i += 1
        return self.t[k], (self.name, k)


def build_program(n_layers=2, phases="ABC", debug=False, nblk=NB):
    nc = bass.Bass("TRN2", target_bir_lowering=False)
    dbg_kind = "ExternalOutput" if debug else "Internal"

    def din(name, shape, dt=F32):
        return nc.dram_tensor(name, list(shape), dt, kind="ExternalInput").ap()

    def dscr(name, shape, dt=F32):
        return nc.dram_tensor(name, list(shape), dt, kind=dbg_kind).ap()

    x_in = din("x", [S, D])
    ccol = din("ccol", [128, 8])
    cols = din("cols", [2, 128, NCOLS])
    bada = din("b_ada", [2, 6 * D])
    w_ada = din("w_ada", [2, D, 6 * D])
    w_in = din("w_in", [2, D, DIN])
    qng = din("q_norm_g", [2, 64])
    kng = din("k_norm_g", [2, 64])
    lru_wa = din("lru_wa", [2, 4, 64, 64])
    lru_wx = din("lru_wx", [2, 4, 64, 64])
    w_out = din("w_out", [2, D, D])
    w_up = din("w_up", [2, D, DFF])
    w_down = din("w_down", [2, DFF, D])
    c_identb = din("c_identb", [128, 128], BF16)
    c_identf = din("c_identf", [128, 128], F32)
    c_oh = din("c_oh", [16, S], BF16)
    c_negmask = din("c_negmask", [1, 256], F32)
    c_caus = din("c_caus", [128, 2, 256], BF16)
    out = nc.dram_tensor("out", [S, D], F32, kind="ExternalOutput").ap()

    gates_scr = dscr("gates_scr", [2, 2, D])
    qT_scr = dscr("qT_scr", [80, 8, S], BF16)
    kT_scr = dscr("kT_scr", [64, 8, S], BF16)
    v_scr = dscr("v_scr", [NT, 128, 1024], BF16)
    ycl_scr = dscr("ycl_scr", [512, S], BF16)
    x1_scr = dscr("x1_scr", [S, D])
    xmid_scr = dscr("xmid_scr", [S, D])
    dbg_grstd = dscr("dbg_grstd", [128, NT * 2]) if debug else None

    import os as _os0
    P = Prog(nc, strict=(_os0.environ.get('STRICT', '1') == '1'))
    A = P.add

    with ExitStack() as top:
        def sb(es, name, shape, dt):
            return es.enter_context(nc.sbuf_tensor(name, list(shape), dt))

        def ps(es, name, shape, dt):
            return es.enter_context(nc.psum_tensor(name, list(shape), dt))

        cols_sb = sb(top, "cols_sb", [128, 2, NCOLS], F32)
        eff_sb = sb(top, "eff_sb", [128, 2, 32], F32)
        shb_sb = sb(top, "shb_sb", [128, 2, 2, 8], BF16)
        lruc_sb = sb(top, "lruc_sb", [128, 2, 8], F32)
        identb = sb(top, "identb", [128, 128], BF16)
        identf = sb(top, "identf", [128, 128], F32)
        ones_bf = sb(top, "ones_bf", [128, 128], BF16)
        ones_f = sb(top, "ones_f", [128, 128], F32)
        grstd = sb(top, "grstd", [128, NT, 2], F32)
        epsc = sb(top, "epsc", [128, 1], F32)

        A("sp", lambda e: e.dma_start(out=identb[:], in_=c_identb[:, :]), writes=["identb"], dma_key="c0")
        A("sp", lambda e: e.dma_start(out=identf[:], in_=c_identf[:, :]), writes=["identf"], dma_key="c1")
        A("sp", lambda e: e.dma_start(out=cols_sb[:], in_=cols.rearrange("l p n -> p l n")), writes=["cols"], dma_key="c2")
        A("pool", lambda e: e.memset(ones_bf[:], 1.0), writes=["ones_bf"])
        A("pool", lambda e: e.memset(ones_f[:], 1.0), writes=["ones_f"])
        A("pool", lambda e: e.memset(epsc[:], EPS), writes=["epsc"])
        if debug:
            A("pool", lambda e: e.memset(grstd[:], 0.0), writes=["grstd_init"])

        def rstd_from_ssq(dst, ssq, n, tags_r, tags_w, tmp, tmptag):
            A("dve", lambda e: e.tensor_scalar(out=tmp, in0=ssq, scalar1=1.0 / n, scalar2=EPS, op0=ALU.mult, op1=ALU.add),
              reads=tags_r, writes=[tmptag])
            A("act", lambda e: e.activation(out=tmp, in_=tmp, func=AF.Ln), reads=[tmptag], writes=[tmptag])
            A("act", lambda e: e.activation(out=dst, in_=tmp, func=AF.Exp, scale=-0.5), reads=[tmptag], writes=tags_w)

        with ExitStack() as es:
            cc = sb(es, "cc", [128, 8], F32)
            ce = sb(es, "ce", [128, 8], F32)
            cact = sb(es, "cact", [128, 8], BF16)
            wa = [sb(es, f"wa{i}", [128, 8, 512], BF16) for i in range(2)]
            modc = sb(es, "modc", [128, 32], F32)
            grow = sb(es, "grow", [1, 2, D], F32)
            brow = sb(es, "brow", [1, 2, 2, D], F32)
            lam = sb(es, "lam", [128, 2, 2], F32)
            mod_ps_t = ps(es, "mod_ps", [128, 512], F32)
            mod_ps = mod_ps_t[:, 0:32]
            g_ps = [ps(es, f"g_ps{i}", [128, 512], F32)[0:1, :] for i in range(2)]

            A("sp", lambda e: e.dma_start(out=cc[:], in_=ccol[:, :]), writes=["cc"], dma_key="c3")
            A("act", lambda e: e.activation(out=ce[:], in_=cc[:], func=AF.Exp, scale=-1.0), reads=["cc"], writes=["ce"])
            A("dve", lambda e: e.tensor_scalar(out=ce[:], in0=ce[:], scalar1=1.0, scalar2=None, op0=ALU.add), reads=["ce"], writes=["ce"])
            A("dve", lambda e: e.reciprocal(out=ce[:], in_=ce[:]), reads=["ce"], writes=["ce"])
            A("dve", lambda e: e.tensor_tensor(out=cact[:], in0=ce[:], in1=cc[:], op=ALU.mult), reads=["ce", "cc"], writes=["cact"])
            for l in range(n_layers):
                for g_ in range(2):
                    A("sp", lambda e, l=l, g_=g_: e.dma_start(out=brow[:, l, g_, :], in_=bada[l:l + 1, (2 + 3 * g_) * D:(3 + 3 * g_) * D]),
                      writes=[("brow", l, g_)], dma_key=("c4", l, g_))
                colidx = 0
                for blk in range(12):
                    wt, wtag = wa[blk % 2], ("wa", blk % 2)
                    A("pool", lambda e, wt=wt, l=l, blk=blk: e.dma_start(
                        out=wt[:], in_=w_ada[l, :, blk * 512:(blk + 1) * 512].rearrange("(c p) n -> p c n", p=128)),
                      writes=[wtag], dma_key=("wa", blk % 2))
                    if blk in (4, 5, 10, 11):
                        g = 0 if blk < 6 else 1
                        half = blk % 2
                        gp, gtag = g_ps[half], ("g_ps", half)

                        def mm(e, wt=wt, gp=gp):
                            r = None
                            for c in range(8):
                                r = e.matmul(gp, lhsT=cact[:, c:c + 1], rhs=wt[:, c, :], start=(c == 0), stop=(c == 7))
                            return r
                        A("pe", mm, reads=[wtag, "cact"], writes=[gtag])
                        A("dve", lambda e, gp=gp, l=l, g=g, half=half: e.tensor_tensor(
                            out=grow[:, g, half * 512:(half + 1) * 512], in0=gp, in1=brow[:, l, g, half * 512:(half + 1) * 512], op=ALU.add),
                          reads=[gtag, ("brow", l, g)], writes=[("grow", g, half)])
                        if half == 1:
                            A("sp", lambda e, l=l, g=g: e.dma_start(out=gates_scr[l, g:g + 1, :], in_=grow[:, g, :]),
                              reads=[("grow", g, 0), ("grow", g, 1)], dma_key=("grow", g))
                    else:
                        def mm(e, wt=wt, colidx=colidx):
                            r = None
                            for sub in range(4):
                                for c in range(8):
                                    r = e.matmul(mod_ps[:, colidx + sub:colidx + sub + 1], lhsT=wt[:, c, sub * 128:(sub + 1) * 128],
                                                 rhs=cact[:, c:c + 1], start=(c == 0), stop=(c == 7))
                            return r
                        A("pe", mm, reads=[wtag, "cact", "modc"], writes=["mod_ps"])
                        colidx += 4
                A("dve", lambda e, l=l: e.tensor_tensor(out=modc[:], in0=mod_ps, in1=cols_sb[:, l, 16:48], op=ALU.add),
                  reads=["mod_ps", "cols"], writes=["modc"])
                for k2 in range(2):
                    A("dve", lambda e, l=l, k2=k2: e.scalar_tensor_tensor(
                        out=eff_sb[:, l, 16 * k2:16 * k2 + 8], in0=modc[:, 16 * k2 + 8:16 * k2 + 16], scalar=1.0,
                        in1=cols_sb[:, l, 8 * k2:8 * k2 + 8], op0=ALU.add, op1=ALU.mult),
                      reads=["modc", "cols"], writes=[("eff", l, k2, 0)])
                    A("dve", lambda e, l=l, k2=k2: e.tensor_copy(out=eff_sb[:, l, 16 * k2 + 8:16 * k2 + 16], in_=modc[:, 16 * k2:16 * k2 + 8]),
                      reads=["modc"], writes=[("eff", l, k2, 1)])
                    A("dve", lambda e, l=l, k2=k2: e.tensor_copy(out=shb_sb[:, l, k2, :], in_=modc[:, 16 * k2:16 * k2 + 8]),
                      reads=["modc"], writes=[("shb", l, k2)])
                A("dve", lambda e, l=l: e.tensor_scalar(out=lruc_sb[:, l, 0:4], in0=cols_sb[:, l, 72:76], scalar1=-1.0, scalar2=None, op0=ALU.mult),
                  reads=["cols"], writes=[("lruc", l, 0)])
                A("act", lambda e, l=l: e.activation(out=lam[:, l, :], in_=cols_sb[:, l, 76:78], func=AF.Exp, scale=-1.0),
                  reads=["cols"], writes=[("lam", l)])
                A("act", lambda e, l=l: e.activation(out=lam[:, l, :], in_=lam[:, l, :], func=AF.Ln, bias=1.0),
                  reads=[("lam", l)], writes=[("lam", l)])
                A("dve", lambda e, l=l: e.tensor_scalar(out=lruc_sb[:, l, 4:6], in0=lam[:, l, :], scalar1=-8.0, scalar2=None, op0=ALU.mult),
                  reads=[("lam", l)], writes=[("lruc", l, 1)])
            P.barrier()

        for l in range(n_layers):
            x_src = x_in if l == 0 else xmid_scr
            x_dst = xmid_scr if l < n_layers - 1 else out
            if l >= 2:
                x_dst = out
            if "A" in phases:
                P.nosched = True
                phase_A(nc, P, top, sb, ps, l, x_src, dict(
                    cols_sb=cols_sb, eff_sb=eff_sb, shb_sb=shb_sb, lruc_sb=lruc_sb, identb=identb, identf=identf,
                    ones_bf=ones_bf, ones_f=ones_f, grstd=grstd, w_in=w_in, qng=qng, kng=kng, lru_wa=lru_wa,
                    lru_wx=lru_wx, c_negmask=c_negmask, qT_scr=qT_scr, kT_scr=kT_scr, v_scr=v_scr, ycl_scr=ycl_scr,
                    rstd_from_ssq=rstd_from_ssq, dbg_grstd=dbg_grstd, epsc=epsc), nblk)
                P.barrier()
                P.nosched = False
            TT = dict(cols_sb=cols_sb, eff_sb=eff_sb, shb_sb=shb_sb, identb=identb, identf=identf, ones_bf=ones_bf, ones_f=ones_f,
                      grstd=grstd, w_out=w_out, w_up=w_up, w_down=w_down, gates_scr=gates_scr, c_oh=c_oh, c_caus=c_caus,
                      qT_scr=qT_scr, kT_scr=kT_scr, v_scr=v_scr, ycl_scr=ycl_scr, x1_scr=x1_scr, rstd_from_ssq=rstd_from_ssq)
            if "B" in phases:
                phase_B(nc, P, top, sb, ps, l, x_src, TT, nblk)
                P.barrier()
            if "C" in phases:
                phase_C(nc, P, top, sb, ps, l, x_dst, TT, nblk)
                P.barrier()
        P.emit()
    return nc, P


def phase_A(nc, P, top, sb0, ps0, l, x_src, T, nblk):
    A = P.add
    sb = lambda es, name, shape, dt: sb0(es, f"{name}_A{l}", shape, dt)
    ps = lambda es, name, shape, dt: ps0(es, f"{name}_A{l}", shape, dt)
    cols_sb, eff_sb, shb_sb, lruc_sb = T["cols_sb"], T["eff_sb"], T["shb_sb"], T["lruc_sb"]
    identb, identf, ones_bf, ones_f, grstd, epsc = T["identb"], T["identf"], T["ones_bf"], T["ones_f"], T["grstd"], T["epsc"]

    def rstd2(dst, ssq, n, rtags, wtags, tmp, tmptag):
        A("act", lambda e: e.activation(out=tmp, in_=ssq, func=AF.Ln, scale=1.0 / n, bias=epsc[:, 0:1]), reads=rtags, writes=[tmptag], n=8)
        A("act", lambda e: e.activation(out=dst, in_=tmp, func=AF.Exp, scale=-0.5), reads=[tmptag], writes=wtags, n=8)

    with ExitStack() as es:
        dbl = lambda name, shape, dt: [sb(es, f"{name}{i}", shape, dt) for i in range(2)]
        w_sb = sb(es, "w_in_sb", [128, 8, DIN], BF16)
        brow = sb(es, "b_in_row", [1, 1536], BF16)
        bcol = sb(es, "b_in_col", [128, 10], F32)
        gq = sb(es, "gq", [128, 64], F32)
        gk = sb(es, "gk", [128, 64], F32)
        negm = sb(es, "negm", [128, 16, 16], F32)
        kmeanT = sb(es, "kmeanT", [64, 8, 16], F32)
        km_hi = sb(es, "km_hi", [64, 8, 16], BF16)
        km_lo = sb(es, "km_lo", [64, 8, 16], BF16)
        km_tmp = sb(es, "km_tmp", [64, 8], F32)
        wabd = sb(es, "wabd", [128, 2, 2, 128], BF16)
        wtmp = sb(es, "wtmp", [128, 2, 2, 64], F32)
        x_sbs = Rot([sb(es, f"xA{i}", [128, D], F32) for i in range(2)], "xA")
        junkx = sb(es, "junkx", [128, D], BF16)
        junkq = dbl("junkq", [128, 512], BF16)
        junkk = dbl("junkk", [128, 512], BF16)
        qraw = dbl("qraw", [128, 512], F32)
        kraw = dbl("kraw", [128, 512], F32)
        xn = dbl("xn", [128, D], BF16)
        xnT = dbl("xnT", [128, 8, 256], BF16)
        stx = dbl("stx", [128, 4], F32)
        stq = dbl("stq", [128, 3, 8], F32)
        stk = dbl("stk", [128, 3, 8], F32)
        stg = dbl("stg", [128, 4], F32)
        kf = dbl("kf", [128, 512], F32)
        kb = dbl("kb", [128, 512], BF16)
        qaug = dbl("qaug", [128, 8, 80], BF16)
        qT8 = dbl("qT8", [64, 8, 128], BF16)
        gm = dbl("gm", [128, 8, 16], F32)
        m8 = dbl("m8", [128, 8, 8], F32)
        msk = dbl("msk", [128, 8, 16], F32)
        v_sbs = Rot([sb(es, f"vA{i}", [128, 8, 128], BF16) for i in range(2)], "vA")
        kT_sbs = Rot([sb(es, f"kTA{i}", [64, 8, 128], BF16) for i in range(2)], "kTA")
        qT_sbs = Rot([sb(es, f"qTA{i}", [80, 8, 128], BF16) for i in range(2)], "qTA")
        fmS = [sb(es, "fmS0", [128, 10, 256], F32)] * 2
        cu = sb(es, "cu", [128, 2, 258], F32)
        lx = sb(es, "lx", [128, 2, 259], F32)
        hb = sb(es, "hb", [128, 2, 257], F32)
        ct = dbl("ct", [128, 256], F32)
        cy = dbl("cy", [128, 256], F32)
        lt = [[sb(es, f"lt{lc}_{k}", [128, 256], F32) for k in range(7)] for lc in range(2)]
        xrb = dbl("xrb", [128, 256], BF16)
        ysq = dbl("ysq", [128, 4, 256], BF16)
        ycl_sbs = Rot([sb(es, f"ycl{i}", [128, 4, 256], BF16) for i in range(2)], "ycl")
        tp_ps = ps(es, "tp_ps", [128, 8, 128], BF16)
        qkv_ps = [ps(es, f"qkv_ps{i}", [128, 512], F32) for i in range(3)]
        fm_pss = Rot([ps(es, f"fm_ps{i}", [128, 512], F32) for i in range(2)], "fm_ps")
        sm_ps = ps(es, "sm_ps", [128, 512], F32)
        gate_ps = sm_ps[:, 0:128].rearrange("p (h n) -> p h n", h=8)
        km_ps = sm_ps[0:64, 128:136]
        ss_ps = sm_ps[:, 136:138]
        bc_ps = sm_ps[:, 144:154]

        for c in range(8):
            A("pool", lambda e, c=c: e.dma_start(out=w_sb[:, c, :], in_=T["w_in"][l, c * 128:(c + 1) * 128, :], max_dma_last_dim=4096),
              writes=[("w_in", c)], dma_key=("w", c), cost=12000)
        A("sp", lambda e: e.dma_start(out=gq[:], in_=T["qng"][l:l + 1, :].partition_broadcast(128)), writes=["gq"], dma_key="a0")
        A("sp", lambda e: e.dma_start(out=gk[:], in_=T["kng"][l:l + 1, :].partition_broadcast(128)), writes=["gk"], dma_key="a1")
        A("sp", lambda e: e.dma_start(out=negm[:].rearrange("p a b -> p (a b)"), in_=T["c_negmask"][0:1, :].partition_broadcast(128)),
          writes=["negm"], dma_key="a2")
        A("dve", lambda e: e.scalar_tensor_tensor(out=gk[:], in0=gk[:], scalar=0.125, in1=gq[:], op0=ALU.mult, op1=ALU.mult),
          reads=["gq", "gk"], writes=["gk"])
        A("pool", lambda e: e.memset(kmeanT[:], 0.0), writes=["kmeanT"])
        A("pool", lambda e: e.memset(km_hi[:], 0.0), writes=["km_hi"])
        A("pool", lambda e: e.memset(km_lo[:], 0.0), writes=["km_lo"])
        A("pool", lambda e: e.memset(cu[:], 0.0), writes=["cu0", "cu1"])
        A("pool", lambda e: e.memset(lx[:], 0.0), writes=["lx0", "lx1"])
        A("pool", lambda e: e.memset(hb[:], 0.0), writes=["hb0", "hb1"])
        for i in range(2):
            A("pool", lambda e, i=i: e.memset(qaug[i][:], 0.0), writes=[f"qaug_b{i}", f"qaug_q{i}"])
        for k, vt in enumerate(v_sbs.t):
            A("pool", lambda e, vt=vt: e.memset(vt[:], 1.0), writes=[("vA", k)])
        A("pool", lambda e: e.memset(wabd[:], 0.0), writes=["wabd"])
        for g, wsrc in enumerate((T["lru_wa"], T["lru_wx"])):
            for hh in range(4):
                ch, hf = hh // 2, hh % 2
                A("sp", lambda e, g=g, wsrc=wsrc, hh=hh, ch=ch, hf=hf: e.dma_start(
                    out=wtmp[hf * 64:(hf + 1) * 64, g, ch, :], in_=wsrc[l, hh, :, :]), writes=[("wtmp", g, hh)], dma_key=("spk", g * 4 + hh))
                A("dve", lambda e, g=g, ch=ch, hf=hf: e.tensor_copy(out=wabd[hf * 64:(hf + 1) * 64, g, ch, hf * 64:(hf + 1) * 64],
                                                                   in_=wtmp[hf * 64:(hf + 1) * 64, g, ch, :]),
                  reads=[("wtmp", g, hh), "wabd"], writes=[("wabd", g, hh)])
        wabd_tags = [("wabd", g, hh) for g in range(2) for hh in range(4)]
        w_tags = [("w_in", c) for c in range(8)]

        for j in range(3):
            def mm(e, j=j):
                r = None
                for c in range(8):
                    r = e.matmul(qkv_ps[j][0:1, :], lhsT=shb_sb[:, l, 0, c:c + 1], rhs=w_sb[:, c, j * 512:(j + 1) * 512],
                                 start=(c == 0), stop=(c == 7))
                return r
            A("pe", mm, reads=w_tags + [("shb", l, 0)], writes=[("qkv_ps", j)])
            A("act", lambda e, j=j: e.copy(out=brow[:, j * 512:(j + 1) * 512], in_=qkv_ps[j][0:1, :]), reads=[("qkv_ps", j)], writes=["brow"], n=512)

        def mm(e):
            r = None
            for fc in range(10):
                for c in range(8):
                    r = e.matmul(bc_ps[:, fc:fc + 1], lhsT=w_sb[:, c, 1536 + fc * 128:1536 + (fc + 1) * 128],
                                 rhs=shb_sb[:, l, 0, c:c + 1], start=(c == 0), stop=(c == 7))
            return r
        A("pe", mm, reads=w_tags + [("shb", l, 0)], writes=["sm_ps"])
        A("dve", lambda e: e.tensor_copy(out=bcol[:], in_=bc_ps), reads=["sm_ps"], writes=["bcol"])
        for c in range(8):
            if c % 2 == 0:
                A("dve", lambda e, c=c: e.tensor_scalar(out=w_sb[:, c, :], in0=w_sb[:, c, :], scalar1=eff_sb[:, l, c:c + 1], scalar2=None, op0=ALU.mult),
                  reads=[("eff", l, 0, 0)], writes=[("w_in", c)], n=DIN)
            else:
                A("act", lambda e, c=c: e.activation(out=w_sb[:, c, :], in_=w_sb[:, c, :], func=AF.Copy, scale=eff_sb[:, l, c:c + 1]),
                  reads=[("eff", l, 0, 0)], writes=[("w_in", c)], n=DIN)

        def h3(ap):
            return ap.rearrange("p (h d) -> p h d", h=8)

        def tiles(st):
            bp = st % 2
            ctx = [dict(), dict()]

            def s0(i):
                t = 2 * st + i
                xt, xtag = x_sbs.next()
                ctx[i].update(t=t, xt=xt, xtag=xtag)
                A("sp", lambda e, xt=xt, t=t: e.dma_start(out=xt[:], in_=x_src[t * 128:(t + 1) * 128, :]), writes=[xtag], dma_key=xtag)
                A("act", lambda e, xt=xt, i=i: e.activation(out=junkx[:], in_=xt[:], func=AF.Square, accum_out=stx[i][:, 0:1]),
                  reads=[xtag], writes=["junkx", f"stx{i}"], n=D)
                rstd2(stx[i][:, 2:3], stx[i][:, 0:1], D, [f"stx{i}"], [f"stxr{i}"], stx[i][:, 1:2], f"stxt{i}")
                A("dve", lambda e, xt=xt, i=i: e.tensor_scalar(out=xn[i][:], in0=xt[:], scalar1=stx[i][:, 2:3], scalar2=None, op0=ALU.mult),
                  reads=[xtag, f"stxr{i}"], writes=[f"xn{i}"], n=D)

            def s1(i):
                def tr(e, i=i):
                    r = None
                    for c in range(8):
                        r = e.transpose(out=tp_ps[:, c, :], in_=xn[i][:, c * 128:(c + 1) * 128], identity=identb[:])
                    return r
                A("pe", tr, reads=[f"xn{i}", "identb"], writes=["tp_ps"])
                A("act", lambda e, i=i, bp=bp: e.copy(out=xnT[bp][:, :, i * 128:(i + 1) * 128], in_=tp_ps[:]), reads=["tp_ps"], writes=[("xnT", bp, i)], n=D)

            def s2(i):
                t = ctx[i]["t"]
                for j in range(3):
                    def mm(e, j=j, i=i, bp=bp):
                        for c in range(8):
                            e.matmul(qkv_ps[j][:], lhsT=xnT[bp][:, c, i * 128:(i + 1) * 128], rhs=w_sb[:, c, j * 512:(j + 1) * 512],
                                     start=(c == 0), stop=False)
                        return e.matmul(qkv_ps[j][:], lhsT=ones_bf[0:1, :], rhs=brow[0:1, j * 512:(j + 1) * 512], start=False, stop=True)
                    A("pe", mm, reads=w_tags + [("xnT", bp, i), "brow", "ones_bf"], writes=[("qkv_ps", j)], cost=2200)
                A("act", lambda e, i=i: e.copy(out=qraw[i][:], in_=qkv_ps[0][:]), reads=[("qkv_ps", 0)], writes=[f"qraw{i}"], n=512)
                A("act", lambda e, i=i: e.copy(out=kraw[i][:], in_=qkv_ps[1][:]), reads=[("qkv_ps", 1)], writes=[f"kraw{i}"], n=512)
                vt, vtag = v_sbs.next()
                A("act", lambda e, vt=vt: e.copy(out=vt[:, :, 0:64], in_=h3(qkv_ps[2][:])), reads=[("qkv_ps", 2)], writes=[vtag], n=512)
                A("sp", lambda e, vt=vt, t=t: e.dma_start(out=T["v_scr"][t, :, :], in_=vt[:].rearrange("p h d -> p (h d)")),
                  reads=[vtag], writes=[("v_scr", t)], dma_key=("st",) + vtag)

            def s3(i):
                A("act", lambda e, i=i: e.activation(out=junkq[i][:], in_=qraw[i][:], func=AF.Square), reads=[f"qraw{i}"], writes=[f"junkq{i}"], n=512)
                A("dve", lambda e, i=i: e.tensor_reduce(out=stq[i][:, 0, :], in_=h3(junkq[i][:]), axis=AX.X, op=ALU.add),
                  reads=[f"junkq{i}"], writes=[f"stq{i}"], n=512)
                rstd2(stq[i][:, 2, :], stq[i][:, 0, :], 64, [f"stq{i}"], [f"stqr{i}"], stq[i][:, 1, :], f"stqt{i}")
                A("dve", lambda e, i=i: e.tensor_tensor(out=qaug[i][:, :, 0:64], in0=h3(qraw[i][:]),
                                                        in1=stq[i][:, 2, :].unsqueeze(2).to_broadcast([128, 8, 64]), op=ALU.mult),
                  reads=[f"qraw{i}", f"stqr{i}"], writes=[f"qaug_q{i}"], n=512)
                A("act", lambda e, i=i: e.activation(out=junkk[i][:], in_=kraw[i][:], func=AF.Square), reads=[f"kraw{i}"], writes=[f"junkk{i}"], n=512)
                A("dve", lambda e, i=i: e.tensor_reduce(out=stk[i][:, 0, :], in_=h3(junkk[i][:]), axis=AX.X, op=ALU.add),
                  reads=[f"junkk{i}"], writes=[f"stk{i}"], n=512)
                rstd2(stk[i][:, 2, :], stk[i][:, 0, :], 64, [f"stk{i}"], [f"stkr{i}"], stk[i][:, 1, :], f"stkt{i}")
                A("dve", lambda e, i=i: e.tensor_tensor(out=h3(kf[i][:]), in0=h3(kraw[i][:]),
                                                        in1=stk[i][:, 2, :].unsqueeze(2).to_broadcast([128, 8, 64]), op=ALU.mult),
                  reads=[f"kraw{i}", f"stkr{i}"], writes=[f"kf{i}"], n=512)
                A("dve", lambda e, i=i: e.tensor_tensor(out=h3(kb[i][:]), in0=h3(kf[i][:]),
                                                        in1=gk[:].unsqueeze(1).to_broadcast([128, 8, 64]), op=ALU.mult),
                  reads=[f"kf{i}", "gk"], writes=[f"kb{i}"], n=512)

            def s4(i):
                def mm(e, i=i):
                    r = None
                    for h in range(8):
                        r = e.matmul(km_ps[:, h:h + 1], lhsT=kb[i][:, h * 64:(h + 1) * 64], rhs=ones_bf[:, 0:1], start=True, stop=True)
                    return r
                A("pe", mm, reads=[f"kb{i}", "ones_bf"], writes=["sm_ps"], cost=600)
                if i == 0:
                    A("dve", lambda e: e.tensor_scalar(out=kmeanT[:, :, st], in0=km_ps, scalar1=1.0 / 256, scalar2=None, op0=ALU.mult),
                      reads=["sm_ps"], writes=["kmeanT"], n=8)
                else:
                    A("dve", lambda e: e.scalar_tensor_tensor(out=kmeanT[:, :, st], in0=km_ps, scalar=1.0 / 256, in1=kmeanT[:, :, st],
                                                              op0=ALU.mult, op1=ALU.add),
                      reads=["sm_ps"], writes=["kmeanT"], n=8)
                    A("dve", lambda e: e.tensor_copy(out=km_hi[:, :, st], in_=kmeanT[:, :, st]), reads=["kmeanT"], writes=["km_hi"], n=8)
                    A("dve", lambda e: e.tensor_tensor(out=km_tmp[:], in0=kmeanT[:, :, st], in1=km_hi[:, :, st], op=ALU.subtract),
                      reads=["kmeanT", "km_hi"], writes=["km_tmp"], n=8)
                    A("dve", lambda e: e.tensor_copy(out=km_lo[:, :, st], in_=km_tmp[:]), reads=["km_tmp"], writes=["km_lo"], n=8)

            def s5(i):
                t = ctx[i]["t"]
                kTt, kTtag = kT_sbs.next()

                def tr(e, i=i):
                    r = None
                    for h in range(8):
                        r = e.transpose(out=tp_ps[0:64, h, :], in_=kb[i][:, h * 64:(h + 1) * 64], identity=identb[:])
                    return r
                A("pe", tr, reads=[f"kb{i}", "identb"], writes=["tp_ps"])
                A("act", lambda e, kTt=kTt: e.copy(out=kTt[:], in_=tp_ps[0:64, :, :]), reads=["tp_ps"], writes=[kTtag], n=D)
                A("sp", lambda e, kTt=kTt, t=t: e.dma_start(out=T["kT_scr"][:, :, t * 128:(t + 1) * 128], in_=kTt[:]),
                  reads=[kTtag], writes=[("kT_scr", t)], dma_key=("st",) + kTtag)

            def s6(i):
                if st < 1:
                    return

                def tr(e, i=i):
                    r = None
                    for h in range(8):
                        r = e.transpose(out=tp_ps[0:64, h, :], in_=qaug[i][:, h, 0:64], identity=identb[:])
                    return r
                A("pe", tr, reads=[f"qaug_q{i}", "identb"], writes=["tp_ps"])
                A("act", lambda e, i=i: e.copy(out=qT8[i][:], in_=tp_ps[0:64, :, :]), reads=["tp_ps"], writes=[f"qT8{i}"], n=D)

            def s7(i):
                if st < 1:
                    return

                def mm(e, i=i):
                    r = None
                    for h in range(8):
                        e.matmul(gate_ps[:, h, :], lhsT=qT8[i][:, h, :], rhs=km_hi[:, h, :], start=True, stop=False)
                        r = e.matmul(gate_ps[:, h, :], lhsT=qT8[i][:, h, :], rhs=km_lo[:, h, :], start=False, stop=True)
                    return r
                A("pe", mm, reads=[f"qT8{i}", "km_hi", "km_lo"], writes=["sm_ps"], cost=1000)
                A("dve", lambda e, i=i: e.tensor_tensor(out=gm[i][:], in0=gate_ps, in1=negm[:, st:st + 1, :].to_broadcast([128, 8, 16]), op=ALU.add),
                  reads=["sm_ps", "negm"], writes=[f"gm{i}"], n=128)
                for h in range(8):
                    A("dve", lambda e, h=h, i=i: e.max(out=m8[i][:, h, :], in_=gm[i][:, h, :]), reads=[f"gm{i}"], writes=[(f"m8{i}", h)], n=16)
                A("dve", lambda e, i=i: e.tensor_tensor(out=msk[i][:], in0=gm[i][:], in1=m8[i][:, :, 2:3].to_broadcast([128, 8, 16]), op=ALU.is_ge),
                  reads=[f"gm{i}"] + [(f"m8{i}", h) for h in range(8)], writes=[f"msk{i}"], n=128)
                A("dve", lambda e, i=i: e.tensor_scalar(out=qaug[i][:, :, 64:80], in0=msk[i][:], scalar1=BIG, scalar2=-BIG, op0=ALU.mult, op1=ALU.add),
                  reads=[f"msk{i}"], writes=[f"qaug_b{i}"], n=128)

            def s8(i):
                t = ctx[i]["t"]
                qTt, qTtag = qT_sbs.next()

                def tr(e, i=i):
                    r = None
                    for h in range(8):
                        r = e.transpose(out=tp_ps[0:80, h, :], in_=qaug[i][:, h, :], identity=identb[:])
                    return r
                A("pe", tr, reads=[f"qaug_q{i}", f"qaug_b{i}", "identb"], writes=["tp_ps"])
                A("act", lambda e, qTt=qTt: e.copy(out=qTt[:], in_=tp_ps[0:80, :, :]), reads=["tp_ps"], writes=[qTtag], n=D)
                A("sp", lambda e, qTt=qTt, t=t: e.dma_start(out=T["qT_scr"][:, :, t * 128:(t + 1) * 128], in_=qTt[:]),
                  reads=[qTtag], writes=[("qT_scr", t)], dma_key=("st",) + qTtag)

            import os as _os7
            stages = (s0, s1, s2, s3, s4, s5, s6, s7, s8)
            tmode = _os7.environ.get("TILE_SEQ", "0")
            if tmode.startswith("g"):
                k_int = int(tmode[1:])
                for si in range(k_int):
                    for i in range(2):
                        stages[si](i)
                for i in range(2):
                    for si in range(k_int, 9):
                        stages[si](i)
            elif tmode == "1":
                for i in range(2):
                    for stage in stages:
                        stage(i)
            else:
                for stage in (s0, s1, s2, s3, s4, s5, s6, s7, s8):
                    for i in range(2):
                        stage(i)

        def fm(st):
            bp = st % 2
            F = fmS[bp]
            ycl_t, ycl_tag = ycl_sbs.next()
            for fc in (6, 7, 8, 9, 2, 4, 0, 3, 5, 1):
                bank, btag = fm_pss.next()

                def mm(e, bank=bank, fc=fc, bp=bp):
                    r = None
                    for c in range(8):
                        r = e.matmul(bank[:, 0:256], lhsT=w_sb[:, c, 1536 + fc * 128:1536 + (fc + 1) * 128], rhs=xnT[bp][:, c, :],
                                     start=(c == 0), stop=(c == 7))
                    return r
                A("pe", mm, reads=w_tags + [("xnT", bp, 0), ("xnT", bp, 1)], writes=[btag], cost=1100)
                if fc in (6, 7):
                    lc = fc - 6
                    A("act", lambda e, bank=bank, lc=lc, fc=fc: e.activation(out=lx[:, lc, 3:259], in_=bank[:, 0:256], func=AF.Identity, bias=bcol[:, fc:fc + 1]),
                      reads=[btag, "bcol"], writes=[f"lx{lc}"])
                else:
                    A("dve", lambda e, bank=bank, fc=fc, F=F: e.tensor_scalar(out=F[:, fc, :], in0=bank[:, 0:256], scalar1=bcol[:, fc:fc + 1], scalar2=None, op0=ALU.add),
                      reads=[btag, "bcol"], writes=[("fmS", fc)])
            for lc in range(2):
                xr, ea, sa, ei, uu, gz, yy = lt[lc]
                tg = lambda nm, lc=lc: f"{nm}{lc}"
                lxb, hbb = lx[:, lc, :], hb[:, lc, :]
                cw = [cols_sb[:, l, 62 + lc * 4 + k:62 + lc * 4 + k + 1] for k in range(4)]
                cb = cols_sb[:, l, 70 + lc:71 + lc]
                nba = lruc_sb[:, l, 0 + lc:1 + lc]
                nbx = lruc_sb[:, l, 2 + lc:3 + lc]
                sp8 = lruc_sb[:, l, 4 + lc:5 + lc]
                G = F[:, 8 + lc, :]
                gtag = ("fmS", 8 + lc)
                A("dve", lambda e, lxb=lxb, cw=cw, cb=cb, xr=xr: e.tensor_scalar(out=xr[:], in0=lxb[:, 0:256], scalar1=cw[0], scalar2=cb, op0=ALU.mult, op1=ALU.add),
                  reads=[tg("lx")], writes=[tg("xr")])
                for k in range(1, 4):
                    A("dve", lambda e, lxb=lxb, cw=cw, k=k, xr=xr: e.scalar_tensor_tensor(out=xr[:], in0=lxb[:, k:k + 256], scalar=cw[k], in1=xr[:],
                                                                                      op0=ALU.mult, op1=ALU.add),
                      reads=[tg("lx"), tg("xr")], writes=[tg("xr")])
                A("dve", lambda e, lxb=lxb: e.tensor_copy(out=lxb[:, 0:3], in_=lxb[:, 256:259]), reads=[tg("lx"), tg("xr")], writes=[tg("lx")], n=3)
                A("pool", lambda e, lc=lc, xr=xr: e.tensor_copy(out=xrb[lc][:], in_=xr[:]), reads=[tg("xr")], writes=[tg("xrb")], cost=1500)
                rb, rbtag = fm_pss.next()
                A("pe", lambda e, lc=lc, rb=rb: e.matmul(rb[:, 0:256], lhsT=wabd[:, 0, lc, :], rhs=xrb[lc][:], start=True, stop=True),
                  reads=[tg("xrb")] + wabd_tags, writes=[rbtag], cost=200)
                A("act", lambda e, nba=nba, rb=rb, ea=ea: e.activation(out=ea[:], in_=rb[:, 0:256], func=AF.Exp, scale=-1.0, bias=nba),
                  reads=[rbtag], writes=[tg("ea")])
                ib, ibtag = fm_pss.next()
                A("pe", lambda e, lc=lc, ib=ib: e.matmul(ib[:, 0:256], lhsT=wabd[:, 1, lc, :], rhs=xrb[lc][:], start=True, stop=True),
                  reads=[tg("xrb")] + wabd_tags, writes=[ibtag], cost=200)
                A("act", lambda e, nbx=nbx, ib=ib, ei=ei: e.activation(out=ei[:], in_=ib[:, 0:256], func=AF.Exp, scale=-1.0, bias=nbx),
                  reads=[ibtag], writes=[tg("ei")])
                A("act", lambda e, ea=ea: e.activation(out=ea[:], in_=ea[:], func=AF.Ln, bias=1.0), reads=[tg("ea")], writes=[tg("ea")])
                A("act", lambda e, ea=ea: e.activation(out=ea[:], in_=ea[:], func=AF.Exp, scale=-1.0), reads=[tg("ea")], writes=[tg("ea")])
                A("act", lambda e, ea=ea, sp8=sp8: e.activation(out=ea[:], in_=ea[:], func=AF.Exp, scale=sp8), reads=[tg("ea")], writes=[tg("ea")])
                A("act", lambda e, ea=ea, sa=sa: e.activation(out=sa[:], in_=ea[:], func=AF.Square), reads=[tg("ea")], writes=[tg("sa")])
                A("act", lambda e, sa=sa: e.activation(out=sa[:], in_=sa[:], func=AF.Ln, scale=-1.0, bias=1.0), reads=[tg("sa")], writes=[tg("sa")])
                A("act", lambda e, sa=sa: e.activation(out=sa[:], in_=sa[:], func=AF.Exp, scale=0.5), reads=[tg("sa")], writes=[tg("sa")])
                A("act", lambda e, ei=ei: e.activation(out=ei[:], in_=ei[:], func=AF.Ln, bias=1.0), reads=[tg("ei")], writes=[tg("ei")])
                A("act", lambda e, ei=ei: e.activation(out=ei[:], in_=ei[:], func=AF.Exp, scale=-1.0), reads=[tg("ei")], writes=[tg("ei")])
                A("dve", lambda e, ei=ei, xr=xr, uu=uu: e.tensor_tensor(out=uu[:], in0=ei[:], in1=xr[:], op=ALU.mult), reads=[tg("ei"), tg("xr")], writes=[tg("uu")])
                A("dve", lambda e, sa=sa, uu=uu: e.tensor_tensor(out=uu[:], in0=uu[:], in1=sa[:], op=ALU.mult), reads=[tg("uu"), tg("sa")], writes=[tg("uu")])
                A("dve", lambda e, hbb=hbb, ea=ea, uu=uu: e.tensor_tensor_scan(out=hbb[:, 1:257], data0=ea[:], data1=uu[:], initial=hbb[:, 0:1],
                                                                              op0=ALU.mult, op1=ALU.add),
                  reads=[tg("ea"), tg("uu"), tg("hb")], writes=[tg("hbh")], n=512)
                A("act", lambda e, G=G, gz=gz: e.activation(out=gz[:], in_=G, func=AF.Square), reads=[gtag], writes=[tg("gz")])
                A("dve", lambda e, gz=gz: e.tensor_scalar(out=gz[:], in0=gz[:], scalar1=0.044715, scalar2=1.0, op0=ALU.mult, op1=ALU.add),
                  reads=[tg("gz")], writes=[tg("gz")])
                A("dve", lambda e, gz=gz, G=G: e.tensor_tensor(out=gz[:], in0=gz[:], in1=G, op=ALU.mult), reads=[tg("gz"), gtag], writes=[tg("gz")])
                A("act", lambda e, gz=gz: e.activation(out=gz[:], in_=gz[:], func=AF.Exp, scale=-1.5957691216057308), reads=[tg("gz")], writes=[tg("gz")])
                A("act", lambda e, gz=gz: e.activation(out=gz[:], in_=gz[:], func=AF.Ln, bias=1.0), reads=[tg("gz")], writes=[tg("gz")])
                A("act", lambda e, gz=gz: e.activation(out=gz[:], in_=gz[:], func=AF.Exp, scale=-1.0), reads=[tg("gz")], writes=[tg("gz")])
                A("dve", lambda e, gz=gz, G=G: e.tensor_tensor(out=gz[:], in0=gz[:], in1=G, op=ALU.mult), reads=[tg("gz"), gtag], writes=[tg("gz")])
                A("dve", lambda e, hbb=hbb, gz=gz, yy=yy: e.tensor_tensor(out=yy[:], in0=hbb[:, 1:257], in1=gz[:], op=ALU.mult),
                  reads=[tg("hbh"), tg("gz")], writes=[tg("yy")])
                A("dve", lambda e, hbb=hbb: e.tensor_copy(out=hbb[:, 0:1], in_=hbb[:, 256:257]), reads=[tg("hbh"), tg("yy")], writes=[tg("hb")], n=1)
                A("pool", lambda e, lc=lc, ycl_t=ycl_t, yy=yy: e.tensor_copy(out=ycl_t[:, 2 + lc, :], in_=yy[:]), reads=[tg("yy")], writes=[ycl_tag + (2 + lc,)], cost=1500)
                A("pool", lambda e, lc=lc, yy=yy, bp=bp: e.tensor_tensor(out=ysq[bp][:, 2 + lc, :], in0=yy[:], in1=yy[:], op=ALU.mult), reads=[tg("yy")], writes=[("ysq", bp, 2 + lc)], cost=1000)
            for cc in range(2):
                cub = cu[:, cc, :]
                tg = lambda nm, cc=cc: f"{nm}{cc}"
                w0 = cols_sb[:, l, 56 + cc * 3 + 0:56 + cc * 3 + 1]
                w1 = cols_sb[:, l, 56 + cc * 3 + 1:56 + cc * 3 + 2]
                w2 = cols_sb[:, l, 56 + cc * 3 + 2:56 + cc * 3 + 3]
                Bt, Ct, Ut = ("fmS", 0 + cc), ("fmS", 2 + cc), ("fmS", 4 + cc)
                A("dve", lambda e, cub=cub, cc=cc, F=F: e.tensor_tensor(out=cub[:, 2:258], in0=F[:, 2 + cc, :], in1=F[:, 4 + cc, :], op=ALU.mult),
                  reads=[Ct, Ut], writes=[tg("cu")])
                A("dve", lambda e, cub=cub, w0=w0, cc=cc: e.tensor_scalar(out=ct[cc][:], in0=cub[:, 0:256], scalar1=w0, scalar2=None, op0=ALU.mult),
                  reads=[tg("cu")], writes=[tg("ct")])
                A("dve", lambda e, cub=cub, w1=w1, cc=cc: e.scalar_tensor_tensor(out=ct[cc][:], in0=cub[:, 1:257], scalar=w1, in1=ct[cc][:], op0=ALU.mult, op1=ALU.add),
                  reads=[tg("cu"), tg("ct")], writes=[tg("ct")])
                A("dve", lambda e, cub=cub, w2=w2, cc=cc: e.scalar_tensor_tensor(out=ct[cc][:], in0=cub[:, 2:258], scalar=w2, in1=ct[cc][:], op0=ALU.mult, op1=ALU.add),
                  reads=[tg("cu"), tg("ct")], writes=[tg("ct")])
                A("dve", lambda e, cub=cub: e.tensor_copy(out=cub[:, 0:2], in_=cub[:, 256:258]), reads=[tg("cu"), tg("ct")], writes=[tg("cu")], n=2)
                A("dve", lambda e, cc=cc, F=F: e.tensor_tensor(out=cy[cc][:], in0=F[:, 0 + cc, :], in1=ct[cc][:], op=ALU.mult),
                  reads=[Bt, tg("ct")], writes=[tg("cy")])
                A("pool", lambda e, cc=cc, ycl_t=ycl_t: e.tensor_copy(out=ycl_t[:, cc, :], in_=cy[cc][:]), reads=[tg("cy")], writes=[ycl_tag + (cc,)], cost=1500)
                A("pool", lambda e, cc=cc, bp=bp: e.tensor_tensor(out=ysq[bp][:, cc, :], in0=cy[cc][:], in1=cy[cc][:], op=ALU.mult), reads=[tg("cy")], writes=[("ysq", bp, cc)], cost=1000)
            for i in range(2):
                t = 2 * st + i

                def mm(e, i=i, bp=bp):
                    r = None
                    for g in range(2):
                        for c2 in range(2):
                            r = e.matmul(ss_ps[:, g:g + 1], lhsT=ysq[bp][:, 2 * g + c2, i * 128:(i + 1) * 128], rhs=ones_bf[:, 0:1],
                                         start=(c2 == 0), stop=(c2 == 1))
                    return r
                A("pe", mm, reads=[("ysq", bp, c) for c in range(4)] + ["ones_bf"], writes=["sm_ps"], cost=500)
                rstd2(grstd[:, t, :], ss_ps, 256, ["sm_ps"], [("grstd", t)], stg[i][:, 0:2], f"stg{i}")
            A("sp", lambda e, ycl_t=ycl_t, st=st: e.dma_start(
                out=T["ycl_scr"].rearrange("(c p) t -> p c t", p=128)[:, :, st * 256:(st + 1) * 256], in_=ycl_t[:]),
              reads=[ycl_tag + (c,) for c in range(4)], writes=[("ycl_scr", st)], dma_key=("st",) + ycl_tag)

        tiles(0)
        for st in range(nblk):
            if st + 1 < nblk:
                tiles(st + 1)
            fm(st)


def phase_B(nc, P, top, sb0, ps0, l, x_src, T, nblk):
    A = P.add
    sb = lambda es, name, shape, dt: sb0(es, f"{name}_B{l}", shape, dt)
    ps = lambda es, name, shape, dt: ps0(es, f"{name}_B{l}", shape, dt)
    cols_sb, identb, ones_bf, ones_f, grstd = T["cols_sb"], T["identb"], T["ones_bf"], T["ones_f"], T["grstd"]
    rstd_from_ssq = T["rstd_from_ssq"]
    with ExitStack() as es:
        kT = sb(es, "kT_all", [128, 8, S], BF16)
        v_all = sb(es, "v_all", [128, NT, 1024], BF16)
        wo = sb(es, "w_out_sb", [128, 8, D], BF16)
        caus = sb(es, "caus", [128, 2, 256], BF16)
        qT_blks = Rot([sb(es, f"qTb{i}", [128, 8, 256], BF16) for i in range(2)], "qTb")
        ycl_blks = Rot([sb(es, f"yclb{i}", [128, 4, 256], BF16) for i in range(2)], "yclb")
        x_sbs = Rot([sb(es, f"xB{i}", [128, D], F32) for i in range(3)], "xB")
        g1bc = x_sbs.t[2]
        p_sbs = Rot([sb(es, f"pB{i}", [128, 2, 256], BF16) for i in range(4)], "pB")
        rdens = Rot([sb(es, f"rdB{i}", [64, 256], F32) for i in range(2)], "rdB")
        yattn = sb(es, "yattn", [128, 4, 256], F32)
        ya_bf = sb(es, "ya_bf", [128, 4, 256], BF16)
        ysq = sb(es, "ysqB", [128, 4, 256], BF16)
        stb = sb(es, "stb", [128, 8], F32)
        s_pss = Rot([ps(es, f"s_ps{i}", [128, 2, 256], F32) for i in range(3)], "s_ps")
        oT_pss = Rot([ps(es, f"oT_ps{i}", [128, 512], F32) for i in range(2)], "oT_ps")
        sm_ps = ps(es, "smB_ps", [128, 512], F32)
        op_pss = Rot([ps(es, f"op_ps{i}", [128, 512], F32) for i in range(2)], "op_ps")

        for c in range(8):
            A("pool", lambda e, c=c: e.dma_start(out=wo[:, c, :], in_=T["w_out"][l, c * 128:(c + 1) * 128, :]),
              writes=[("wo", c)], dma_key=("w", c))
        A("sp", lambda e: e.dma_start(out=g1bc[:], in_=T["gates_scr"][l, 0:1, :].partition_broadcast(128)), writes=[("xB", 2)], dma_key=("xB", 2))
        A("sp", lambda e: e.dma_start(out=caus[:], in_=T["c_caus"][:, :, :]), writes=["caus"], dma_key="a1")
        for c in range(8):
            eng = "dve"
            A(eng, lambda e, c=c: e.scalar_tensor_tensor(out=wo[:, c, :], in0=wo[:, c, :], scalar=cols_sb[:, l, 48 + c:49 + c], in1=g1bc[:],
                                                         op0=ALU.mult, op1=ALU.mult),
              reads=[("xB", 2)], writes=[("wo", c)], n=D)
        wo_tags = [("wo", c) for c in range(8)]
        for h in range(8):
            A("pool", lambda e, h=h: e.memset(kT[64:128, h, :], 0.0), writes=[("kToh", h)], n=4096)
        for k_, qt_ in enumerate(qT_blks.t):
            A("pool", lambda e, qt_=qt_: e.memset(qt_[:], 0.0), writes=[("qTb", k_)], n=2048)
        for h in range(8):
            A("sp", lambda e, h=h: e.dma_start(out=kT[64:80, h, :], in_=T["c_oh"][:, :]), writes=[("kToh", h)], dma_key=("spk", h))
        oh_tags = [("kToh", h) for h in range(8)]
        for j in range(nblk):
            A("sp", lambda e, j=j: e.dma_start(out=kT[0:64, :, j * 256:(j + 1) * 256], in_=T["kT_scr"][:, :, j * 256:(j + 1) * 256]),
              writes=[("kT", j)], dma_key=("kT", j % 4))
            A("sp", lambda e, j=j: e.dma_start(out=v_all[:, 2 * j:2 * j + 2, :], in_=T["v_scr"][2 * j:2 * j + 2, :, :].rearrange("t p f -> p t f")),
              writes=[("v", j)], dma_key=("v", j % 4))

        for qb in range(nblk):
            qTb, qtag = qT_blks.next()
            yclb, ytag = ycl_blks.next()
            A("sp", lambda e, qTb=qTb, qb=qb: e.dma_start(out=qTb[0:80, :, :], in_=T["qT_scr"][:, :, qb * 256:(qb + 1) * 256]), writes=[qtag], dma_key=qtag)
            A("sp", lambda e, yclb=yclb, qb=qb: e.dma_start(
                out=yclb[:], in_=T["ycl_scr"].rearrange("(c p) t -> p c t", p=128)[:, :, qb * 256:(qb + 1) * 256]), writes=[ytag], dma_key=ytag)
            xts = []
            for i in range(2):
                t = 2 * qb + i
                xt, xtag = x_sbs.next()
                A("sp", lambda e, xt=xt, t=t: e.dma_start(out=xt[:], in_=x_src[t * 128:(t + 1) * 128, :]), writes=[xtag], dma_key=xtag)
                xts.append((xt, xtag))
            for h in range(8):
                oT, otag = oT_pss.next()
                for j in range(qb + 1):
                    own = (j == qb)
                    sp_, stag = s_pss.next()
                    if not own:
                        def mm(e, sp_=sp_, j=j, h=h, qTb=qTb):
                            r = None
                            for kk in range(2):
                                r = e.matmul(sp_[:, kk, :], lhsT=kT[:, h, (2 * j + kk) * 128:(2 * j + kk + 1) * 128], rhs=qTb[:, h, :],
                                             start=True, stop=True)
                            return r
                        A("pe", mm, reads=[("kT", j), ("kToh", h), qtag], writes=[stag])
                    else:
                        def mm(e, sp_=sp_, j=j, h=h, qTb=qTb):
                            r = None
                            for kk in range(2):
                                e.matmul(sp_[:, kk, :], lhsT=kT[0:64, h, (2 * j + kk) * 128:(2 * j + kk + 1) * 128], rhs=qTb[0:64, h, :],
                                         start=True, stop=False)
                                r = e.matmul(sp_[:, kk, :], lhsT=identb[:], rhs=caus[:, kk, :], start=False, stop=True)
                            return r
                        A("pe", mm, reads=[("kT", j), qtag, "caus", "identb"], writes=[stag])
                    pt, ptag = p_sbs.next()
                    A("act", lambda e, pt=pt, sp_=sp_: e.activation(out=pt[:], in_=sp_[:], func=AF.Exp), reads=[stag], writes=[ptag], cost=600)

                    def mm(e, pt=pt, oT=oT, j=j, h=h, qb=qb):
                        r = None
                        for kk in range(2):
                            r = e.matmul(oT[:, 0:256], lhsT=v_all[:, 2 * j + kk, h * 128:(h + 1) * 128], rhs=pt[:, kk, :],
                                         start=(j == 0 and kk == 0), stop=(j == qb and kk == 1))
                        return r
                    A("pe", mm, reads=[("v", j), ptag], writes=[otag], cost=280)
                rd, rdtag = rdens.next()
                A("dve", lambda e, rd=rd, oT=oT: e.reciprocal(out=rd[:], in_=oT[64:128, 0:256]), reads=[otag], writes=[rdtag], cost=1900)
                pb = (h % 2) * 64
                A("dve", lambda e, rd=rd, oT=oT, pb=pb, h=h: e.tensor_tensor(out=yattn[pb:pb + 64, h // 2, :], in0=oT[0:64, 0:256], in1=rd[:], op=ALU.mult),
                  reads=[otag, rdtag], writes=[("yattn", h)])
            ya_tags = [("yattn", h) for h in range(8)]
            A("pool", lambda e: e.tensor_copy(out=ya_bf[:], in_=yattn[:]), reads=ya_tags, writes=["ya_bf"])
            A("act", lambda e: e.activation(out=ysq[:], in_=yattn[:], func=AF.Square), reads=ya_tags, writes=["ysqB"])
            for i in range(2):
                t = 2 * qb + i
                xt, xtag = xts[i]

                def mm(e, i=i):
                    r = None
                    for c in range(4):
                        r = e.matmul(sm_ps[:, 0:1], lhsT=ysq[:, c, i * 128:(i + 1) * 128], rhs=ones_bf[:, 0:1], start=(c == 0), stop=(c == 3))
                    return r
                A("pe", mm, reads=["ysqB", "ones_bf"], writes=["smB_ps"])
                rstd_from_ssq(stb[:, 2:3], sm_ps[:, 0:1], 512, ["smB_ps"], ["stbr"], stb[:, 1:2], "stbt")
                for n in range(2):
                    nsl = slice(n * 512, (n + 1) * 512)
                    for g in range(3):
                        op_, optag = op_pss.next()
                        if g == 0:
                            srcs = [(ya_bf, c, c) for c in range(4)]
                            rtags = ["ya_bf"]
                            scal = stb[:, 2:3]
                            stag2 = ["stbr"]
                        else:
                            srcs = [(yclb, 2 * (g - 1) + c2, 4 + 2 * (g - 1) + c2) for c2 in range(2)]
                            rtags = [ytag]
                            scal = grstd[:, t, g - 1:g]
                            stag2 = []

                        def mm(e, op_=op_, srcs=srcs, i=i, nsl=nsl):
                            r = None
                            for k, (src, sc, wc) in enumerate(srcs):
                                r = e.matmul(op_[:], lhsT=src[:, sc, i * 128:(i + 1) * 128], rhs=wo[:, wc, nsl], start=(k == 0), stop=(k == len(srcs) - 1))
                            return r
                        A("pe", mm, reads=rtags + wo_tags, writes=[optag])
                        A("dve", lambda e, op_=op_, xt=xt, scal=scal, nsl=nsl: e.scalar_tensor_tensor(
                            out=xt[:, nsl], in0=op_[:], scalar=scal, in1=xt[:, nsl], op0=ALU.mult, op1=ALU.add),
                          reads=[optag, xtag] + stag2, writes=[xtag])
                A("sp", lambda e, xt=xt, t=t: e.dma_start(out=T["x1_scr"][t * 128:(t + 1) * 128, :], in_=xt[:]),
                  reads=[xtag], writes=[("x1_scr", t)], dma_key=("st",) + xtag)


def phase_C(nc, P, top, sb0, ps0, l, x_dst, T, nblk):
    A = P.add
    sb = lambda es, name, shape, dt: sb0(es, f"{name}_C{l}", shape, dt)
    ps = lambda es, name, shape, dt: ps0(es, f"{name}_C{l}", shape, dt)
    eff_sb, shb_sb, identb = T["eff_sb"], T["shb_sb"], T["identb"]
    rstd_from_ssq = T["rstd_from_ssq"]
    with ExitStack() as es:
        wu = sb(es, "w_up_sb", [128, 8, DFF], BF16)
        wd = sb(es, "w_dn_sb", [128, 32, D], BF16)
        bup = sb(es, "bup", [128, 32], F32)
        x_sbs = Rot([sb(es, f"xC{i}", [128, D], F32) for i in range(3)], "xC")
        junk = sb(es, "junkC", [128, D], BF16)
        xn2 = [sb(es, f"xnC{i}", [128, D], BF16) for i in range(2)]
        xnT2 = [sb(es, f"xnTC{i}", [128, 8, 256], BF16) for i in range(2)]
        hT2 = [sb(es, f"hT{i}", [128, 32, 256], BF16) for i in range(2)]
        rts = Rot([sb(es, f"rt{i}", [128, 256], BF16) for i in range(3)], "rt")
        stc = sb(es, "stc", [128, 8], F32)
        tp_ps = ps(es, "tpC_ps", [128, 8, 128], BF16)
        up_pss = Rot([ps(es, f"up_ps{i}", [128, 512], F32) for i in range(3)], "up_ps")
        dn_pss = Rot([ps(es, f"dn_ps{i}", [128, 512], F32) for i in range(2)], "dn_ps")
        sm_ps = ps(es, "smC_ps", [128, 512], F32)

        for c in range(8):
            A("pool", lambda e, c=c: e.dma_start(out=wu[:, c, :], in_=T["w_up"][l, c * 128:(c + 1) * 128, :], max_dma_last_dim=4096),
              writes=[("wu", c)], dma_key=("w", c))
        for f in range(32):
            A("pool", lambda e, f=f: e.dma_start(out=wd[:, f, :], in_=T["w_down"][l, f * 128:(f + 1) * 128, :]),
              writes=[("wd", f)], dma_key=("wdk", f % 8))
        g2t, g2tag = x_sbs.t[1], ("xC", 1)
        A("sp", lambda e: e.dma_start(out=g2t[:], in_=T["gates_scr"][l, 1:2, :].partition_broadcast(128)), writes=[g2tag], dma_key=g2tag)
        wu_tags = [("wu", c) for c in range(8)]
        wd_tags = [("wd", f) for f in range(32)]

        def mm(e):
            r = None
            for f in range(32):
                for c in range(8):
                    r = e.matmul(sm_ps[:, f:f + 1], lhsT=wu[:, c, f * 128:(f + 1) * 128], rhs=shb_sb[:, l, 1, c:c + 1], start=(c == 0), stop=(c == 7))
            return r
        A("pe", mm, reads=wu_tags, writes=["smC_ps"])
        A("dve", lambda e: e.tensor_copy(out=bup[:], in_=sm_ps[:, 0:32]), reads=["smC_ps"], writes=["bup"])
        for c in range(8):
            A("act", lambda e, c=c: e.activation(out=wu[:, c, :], in_=wu[:, c, :], func=AF.Copy, scale=eff_sb[:, l, 16 + c:17 + c]),
              reads=[], writes=[("wu", c)], n=DFF)
        for f in range(32):
            eng = "pool" if f % 4 == 3 else "dve"
            A(eng, lambda e, f=f: e.tensor_tensor(out=wd[:, f, :], in0=wd[:, f, :], in1=g2t[:], op=ALU.mult), reads=[g2tag], writes=[("wd", f)], n=D)

        for st in range(nblk):
            xts = []
            bp = st % 2
            xnT, hT = xnT2[bp], hT2[bp]
            for i in range(2):
                xn = xn2[i]
                t = 2 * st + i
                xt, xtag = x_sbs.next()
                xts.append((xt, xtag))
                A("sp", lambda e, xt=xt, t=t: e.dma_start(out=xt[:], in_=T["x1_scr"][t * 128:(t + 1) * 128, :]), writes=[xtag], dma_key=xtag)
                A("act", lambda e, xt=xt: e.activation(out=junk[:], in_=xt[:], func=AF.Square, accum_out=stc[:, 0:1]),
                  reads=[xtag], writes=["junkC", "stc"])
                rstd_from_ssq(stc[:, 2:3], stc[:, 0:1], D, ["stc"], ["stcr"], stc[:, 1:2], "stct")
                A("dve", lambda e, xt=xt, xn=xn: e.tensor_scalar(out=xn[:], in0=xt[:], scalar1=stc[:, 2:3], scalar2=None, op0=ALU.mult),
                  reads=[xtag, "stcr"], writes=[f"xnC{i}"], n=D)

                def tr(e, xn=xn):
                    r = None
                    for c in range(8):
                        r = e.transpose(out=tp_ps[:, c, :], in_=xn[:, c * 128:(c + 1) * 128], identity=identb[:])
                    return r
                A("pe", tr, reads=[f"xnC{i}", "identb"], writes=["tpC_ps"])
                A("act", lambda e, i=i, xnT=xnT: e.copy(out=xnT[:, :, i * 128:(i + 1) * 128], in_=tp_ps[:]), reads=["tpC_ps"], writes=[("xnTC", bp, i)], n=D)
            for f in range(32):
                up, uptag = up_pss.next()

                def mm(e, up=up, f=f, xnT=xnT):
                    r = None
                    for c in range(8):
                        r = e.matmul(up[:, 0:256], lhsT=wu[:, c, f * 128:(f + 1) * 128], rhs=xnT[:, c, :], start=(c == 0), stop=(c == 7))
                    return r
                A("pe", mm, reads=wu_tags + [("xnTC", bp, 0), ("xnTC", bp, 1)], writes=[uptag], cost=1000)
                rt, rttag = rts.next()
                A("dve", lambda e, up=up, rt=rt, f=f: e.tensor_scalar(out=rt[:], in0=up[:, 0:256], scalar1=bup[:, f:f + 1], scalar2=0.0, op0=ALU.add, op1=ALU.max),
                  reads=[uptag, "bup"], writes=[rttag])
                A("act", lambda e, rt=rt, f=f, hT=hT: e.activation(out=hT[:, f, :], in_=rt[:], func=AF.Square), reads=[rttag], writes=[("hT", bp, f)])
            hT_tags = [("hT", bp, f) for f in range(32)]
            for i in range(2):
                t = 2 * st + i
                xt, xtag = xts[i]
                for n in range(2):
                    nsl = slice(n * 512, (n + 1) * 512)
                    dn, dntag = dn_pss.next()

                    def mm(e, dn=dn, i=i, nsl=nsl, hT=hT):
                        r = None
                        for f in range(32):
                            r = e.matmul(dn[:], lhsT=hT[:, f, i * 128:(i + 1) * 128], rhs=wd[:, f, nsl], start=(f == 0), stop=(f == 31))
                        return r
                    A("pe", mm, reads=hT_tags + wd_tags, writes=[dntag], cost=7200)
                    A("dve", lambda e, dn=dn, xt=xt, nsl=nsl: e.tensor_tensor(out=xt[:, nsl], in0=dn[:], in1=xt[:, nsl], op=ALU.add),
                      reads=[dntag, xtag], writes=[xtag])
                A("sp", lambda e, xt=xt, t=t: e.dma_start(out=x_dst[t * 128:(t + 1) * 128, :], in_=xt[:]),
                  reads=[xtag], writes=[("x_dst", t)], dma_key=("st",) + xtag)


def _consts():
    bf = ml_dtypes.bfloat16
    identb = np.eye(128, dtype=np.float32).astype(bf)
    identf = np.eye(128, dtype=np.float32)
    oh = np.zeros((16, S), np.float32)
    for j in range(16):
        oh[j, j * 256:(j + 1) * 256] = 1.0
    negmask = np.zeros((16, 16), np.float32)
    for own in range(16):
        negmask[own, own:] = -1e30
    kk = np.arange(128)[:, None]
    qq = np.arange(128)[None, :]
    tri = np.where(kk <= qq, 0.0, -BIG).astype(np.float32)
    caus = np.zeros((128, 2, 256), np.float32)
    caus[:, 0, 0:128] = tri
    caus[:, 1, 0:128] = -BIG
    caus[:, 1, 128:256] = tri
    return dict(c_identb=identb, c_identf=identf, c_oh=oh.astype(bf), c_negmask=negmask.reshape(1, 256),
                c_caus=caus.astype(bf))


def _col(v):
    v = np.asarray(v, np.float32)
    return np.ascontiguousarray(v.reshape(-1, 128).T)


def make_in_maps(inputs, n_cores=8):
    f = lambda k: np.ascontiguousarray(np.asarray(inputs[k], np.float32))
    cols = np.zeros((2, 128, NCOLS), np.float32)
    for l in range(2):
        b = f("b_ada")[l]
        parts = [_col(f("ln1_g")[l]), _col(f("ln2_g")[l]),
                 _col(b[0:1024]), _col(b[1024:2048]), _col(b[3072:4096]), _col(b[4096:5120]),
                 _col(f("mix_norm_g")[l])]
        scw = f("sc_w")[l]
        parts.append(np.concatenate([np.stack([scw[k, cc * 128:(cc + 1) * 128] for k in range(3)], 1) for cc in range(2)], 1))
        lcw = f("lru_conv_w")[l]
        parts.append(np.concatenate([np.stack([lcw[k, cc * 128:(cc + 1) * 128] for k in range(4)], 1) for cc in range(2)], 1))
        parts += [_col(f("lru_conv_b")[l]), _col(f("lru_ba")[l]), _col(f("lru_bx")[l]), _col(f("lru_lambda")[l])]
        cols[l] = np.concatenate(parts, 1)
    shared = dict(cols=cols, b_ada=f("b_ada"), w_ada=f("w_ada"), w_in=f("w_in"), q_norm_g=f("q_norm_g"), k_norm_g=f("k_norm_g"),
                  lru_wa=f("lru_wa"), lru_wx=f("lru_wx"), w_out=f("w_out"), w_up=f("w_up"), w_down=f("w_down"))
    shared.update(_consts())
    x = f("x")
    c = f("c")
    maps = []
    for b in range(n_cores):
        m = dict(shared)
        m["x"] = x[b]
        m["ccol"] = _col(c[b])
        maps.append(m)
    return maps


_NC = None


def kernel(**inputs):
    global _NC
    if _NC is None:
        _NC = build_program()[0]
    maps = make_in_maps(inputs)
    res = run_bass_kernel_spmd(_NC, maps, core_ids=list(range(8)))
    return np.stack([np.asarray(r["out"], np.float32) for r in res.results], 0)
```

```python
import numpy as np
import ml_dtypes
from contextlib import ExitStack
import concourse.bass as bass
import concourse.mybir as mybir
from concourse.bass_utils import run_bass_kernel_spmd

F32 = mybir.dt.float32
BF16 = mybir.dt.bfloat16
AF = mybir.ActivationFunctionType
ALU = mybir.AluOpType
AX = mybir.AxisListType

S = 4096
D = 1024
NT = 32
NB = 16
DIN = 2816
DFF = 4096
BIG = 30000.0
EPS = 1e-6
NCOLS = 78
ENGS = ("pe", "act", "dve", "pool", "sp")
import os as _osg
FP32_GUARD = _osg.environ.get("FP32_GUARD", "1") == "1"


class Prog:
    def __init__(self, nc, strict=True):
        self.nc = nc
        self.strict = strict
        self.ops = []
        self.last_w = {}
        self.readers = {}
        self.last_dma = {}
        self.keymap = {}
        self.nosched = False
        self.phase = 0

    def add(self, eng, fn, reads=(), writes=(), dma_key=None, n=256, cost=None):
        if cost is None:
            if dma_key is not None:
                cost = 3000.0
            elif eng == "pe":
                cost = 400.0
            elif eng == "act":
                cost = 320.0 + n / 1.4
            elif eng == "dve":
                cost = 250.0 + n / 0.96
            else:
                cost = 300.0 + n / 0.5
        if dma_key is not None:
            cls = "W" if eng == "pool" else "H"
            kk = (cls, dma_key)
            if kk not in self.keymap:
                self.keymap[kk] = (cls, sum(1 for q in self.keymap if q[0] == cls))
            dma_key = self.keymap[kk]
        i = len(self.ops)
        deps = set()
        for t in reads:
            if t in self.last_w:
                deps.add(self.last_w[t])
        for t in writes:
            if t in self.last_w:
                deps.add(self.last_w[t])
            for r in self.readers.get(t, ()):
                deps.add(r)
        if dma_key is not None:
            if dma_key in self.last_dma:
                deps.add(self.last_dma[dma_key])
            self.last_dma[dma_key] = i
        deps.discard(i)
        for t in reads:
            self.readers.setdefault(t, []).append(i)
        for t in writes:
            self.last_w[t] = i
            self.readers[t] = []
        self.ops.append(dict(eng=eng, fn=fn, deps=sorted(deps), dma_key=dma_key,
                             phase=self.phase, barrier=False, cost=float(cost), nosched=self.nosched))
        return i

    def barrier(self):
        self.ops.append(dict(eng=None, fn=None, deps=[], dma_key=None,
                             phase=self.phase, barrier=True))
        self.phase += 1
        self.last_w = {}
        self.readers = {}
        self.last_dma = {}
        self.keymap = {}

    def schedule(self, window=48):
        ops = self.ops
        n = len(ops)
        order = []
        start = 0
        while start < n:
            end = start
            while end < n and not ops[end]["barrier"]:
                end += 1
            ids = list(range(start, end))
            import os as _os2
            sp_ = _os2.environ.get("SCHED_PHASES")
            if ids and sp_ is not None and str(ops[ids[0]]["phase"]) not in sp_.split(","):
                order.extend(ids)
            elif ids:
                fz = set(_os2.environ.get("SCHED_FREEZE", "").split(","))
                if ops[ids[0]].get("nosched"):
                    fz |= set(_os2.environ.get("PHASEA_FREEZE", "").split(","))
                order.extend(self._sched_phase(ids, window, fz))
            if end < n:
                order.append(end)
            start = end + 1
        remap = {old: new for new, old in enumerate(order)}
        newops = []
        for old in order:
            o = ops[old]
            o["deps"] = sorted(remap[d] for d in o["deps"])
            newops.append(o)
        self.ops = newops

    def _sched_phase(self, ids, window, freeze=()):
        ops = self.ops
        self._freeze = set(freeze)
        idset = set(ids)
        queues = {e: [i for i in ids if ops[i]["eng"] == e] for e in ENGS}
        qpos = {e: 0 for e in ENGS}
        scheduled = {}
        eng_free = {e: 0.0 for e in ENGS}
        out = []
        remaining = len(ids)
        taken = set()
        while remaining:
            best = None
            for e in ENGS:
                q = queues[e]
                p = qpos[e]
                while p < len(q) and q[p] in taken:
                    p += 1
                qpos[e] = p
                cnt = 0
                k = p
                win = 1 if e in self._freeze else window
                while k < len(q) and cnt < win:
                    i = q[k]
                    k += 1
                    if i in taken:
                        continue
                    cnt += 1
                    ok = True
                    rt = 0.0
                    for d in ops[i]["deps"]:
                        if d in idset:
                            if d not in scheduled:
                                ok = False
                                break
                            if scheduled[d] > rt:
                                rt = scheduled[d]
                    if not ok:
                        continue
                    st = max(eng_free[e], rt)
                    if best is None or st < best[0] - 1e-9 or (abs(st - best[0]) <= 1e-9 and i < best[1]):
                        best = (st, i, e)
                    if rt <= eng_free[e]:
                        break
            assert best is not None, "scheduler deadlock"
            st, i, e = best
            o = ops[i]
            if o["dma_key"] is not None:
                eng_free[e] = st + 120.0
                scheduled[i] = st + o["cost"]
            else:
                eng_free[e] = st + o["cost"]
                scheduled[i] = st + o["cost"] + 150.0
            taken.add(i)
            out.append(i)
            remaining -= 1
        self.est_ns = getattr(self, "est_ns", 0.0) + max(scheduled.values())
        return out

    def emit(self):
        nc = self.nc
        import os as _os1
        if _os1.environ.get("SCHED", "1") == "1":
            self.schedule()
        ops = self.ops
        nph = self.phase + 1
        strict = self.strict

        def same_eng_free(od, o):
            return od["eng"] == o["eng"] and o["dma_key"] is None and (od["eng"] == "pe" or not strict)

        need = [False] * len(ops)
        for i, o in enumerate(ops):
            if o["barrier"]:
                continue
            for d in o["deps"]:
                od = ops[d]
                if od["dma_key"] is not None:
                    continue
                if same_eng_free(od, o):
                    continue
                need[d] = True
        last_in_phase = {}
        for i, o in enumerate(ops):
            if o["barrier"] or o["dma_key"] is not None:
                continue
            last_in_phase[(o["eng"], o["phase"])] = i
        for i in last_in_phase.values():
            need[i] = True
        cnt = {}
        dcnt = {}
        val = [None] * len(ops)
        dma_keys = []
        for i, o in enumerate(ops):
            if o["barrier"]:
                continue
            if o["dma_key"] is not None:
                k = o["dma_key"]
                if k not in dcnt:
                    dcnt[k] = 0
                    dma_keys.append(k)
                dcnt[k] += 16
                val[i] = dcnt[k]
            elif need[i]:
                k = (o["eng"], o["phase"])
                cnt[k] = cnt.get(k, 0) + 1
                val[i] = cnt[k]
        esem = {}
        for (e, ph) in sorted(cnt.keys(), key=lambda t: (t[1], t[0])):
            esem[(e, ph)] = nc.alloc_semaphore(f"s_{e}_{ph}")
        dsem = {k: nc.alloc_semaphore(f"d_{j}") for j, k in enumerate(dma_keys)}
        self.n_sems = len(esem) + len(dsem)
        final_cnt = dict(cnt)
        per = {e: [] for e in ENGS}
        for i, o in enumerate(ops):
            if o["barrier"]:
                for e in ENGS:
                    per[e].append(i)
            else:
                per[o["eng"]].append(i)
        dma_upto = {}
        run = {}
        for i, o in enumerate(ops):
            if o["barrier"]:
                dma_upto[i] = dict(run)
            elif o["dma_key"] is not None:
                run[o["dma_key"]] = val[i]
        dma_final = dict(run)

        def gen(e):
            def body(eng):
                waited = {}

                def w(sem, name, v):
                    if waited.get(name, 0) >= v:
                        return
                    waited[name] = v
                    eng.wait_ge(sem, v)

                for i in per[e]:
                    o = ops[i]
                    if o["barrier"]:
                        ph = o["phase"]
                        for e2 in ("pe", "act", "dve", "pool"):
                            v = final_cnt.get((e2, ph), 0)
                            if v:
                                w(esem[(e2, ph)], (e2, ph), v)
                        for k, v in dma_upto[i].items():
                            w(dsem[k], k, v)
                        continue
                    for d in o["deps"]:
                        od = ops[d]
                        if od["dma_key"] is not None:
                            w(dsem[od["dma_key"]], od["dma_key"], val[d])
                        else:
                            if same_eng_free(od, o):
                                continue
                            k = (od["eng"], od["phase"])
                            w(esem[k], k, val[d])
                    inst = o["fn"](eng)
                    if o["dma_key"] is not None:
                        inst.then_inc(dsem[o["dma_key"]], 16)
                    elif need[i]:
                        inst.then_inc(esem[(e, o["phase"])], 1)
                if e == "sp":
                    for k, v in dma_final.items():
                        w(dsem[k], k, v)
                    for (e2, ph), v in final_cnt.items():
                        w(esem[(e2, ph)], (e2, ph), v)
            return body

        with nc.Block() as block:
            block.tensor(gen("pe"))
            block.scalar(gen("act"))
            block.vector(gen("dve"))
            block.gpsimd(gen("pool"))
            block.sync(gen("sp"))


class Rot:
    def __init__(self, tensors, name):
        self.t = tensors
        self.name = name
        self.i = 0

    def next(self):
        k = self.i % len(self.t)
        self.i += 1
        return self.t[k], (self.name, k)


def build_program(n_layers=2, phases="ABC", debug=False, nblk=NB):
    nc = bass.Bass("TRN2", target_bir_lowering=False)
    dbg_kind = "ExternalOutput" if debug else "Internal"

    def din(name, shape, dt=F32):
        return nc.dram_tensor(name, list(shape), dt, kind="ExternalInput").ap()

    def dscr(name, shape, dt=F32):
        return nc.dram_tensor(name, list(shape), dt, kind=dbg_kind).ap()

    x_in = din("x", [S, D])
    ccol = din("ccol", [128, 8])
    cols = din("cols", [2, 128, NCOLS])
    bada = din("b_ada", [2, 6 * D])
    w_ada = din("w_ada", [2, D, 6 * D])
    w_in = din("w_in", [2, D, DIN])
    qng = din("q_norm_g", [2, 64])
    kng = din("k_norm_g", [2, 64])
    lru_wa = din("lru_wa", [2, 4, 64, 64])
    lru_wx = din("lru_wx", [2, 4, 64, 64])
    w_out = din("w_out", [2, D, D])
    w_up = din("w_up", [2, D, DFF])
    w_down = din("w_down", [2, DFF, D])
    c_identb = din("c_identb", [128, 128], BF16)
    c_identf = din("c_identf", [128, 128], F32)
    c_oh = din("c_oh", [16, S], BF16)
    c_negmask = din("c_negmask", [1, 256], F32)
    c_caus = din("c_caus", [128, 2, 256], BF16)
    out = nc.dram_tensor("out", [S, D], F32, kind="ExternalOutput").ap()

    gates_scr = dscr("gates_scr", [2, 2, D])
    qT_scr = dscr("qT_scr", [80, 8, S], BF16)
    kT_scr = dscr("kT_scr", [64, 8, S], BF16)
    v_scr = dscr("v_scr", [NT, 128, 1024], BF16)
    ycl_scr = dscr("ycl_scr", [512, S], BF16)
    x1_scr = dscr("x1_scr", [S, D])
    xmid_scr = dscr("xmid_scr", [S, D])
    dbg_grstd = dscr("dbg_grstd", [128, NT * 2]) if debug else None

    import os as _os0
    P = Prog(nc, strict=(_os0.environ.get('STRICT', '1') == '1'))
    A = P.add

    with ExitStack() as top:
        def sb(es, name, shape, dt):
            return es.enter_context(nc.sbuf_tensor(name, list(shape), dt))

        def ps(es, name, shape, dt):
            return es.enter_context(nc.psum_tensor(name, list(shape), dt))

        cols_sb = sb(top, "cols_sb", [128, 2, NCOLS], F32)
        eff_sb = sb(top, "eff_sb", [128, 2, 32], F32)
        shb_sb = sb(top, "shb_sb", [128, 2, 2, 8], BF16)
        lruc_sb = sb(top, "lruc_sb", [128, 2, 8], F32)
        identb = sb(top, "identb", [128, 128], BF16)
        identf = sb(top, "identf", [128, 128], F32)
        ones_bf = sb(top, "ones_bf", [128, 128], BF16)
        ones_f = sb(top, "ones_f", [128, 128], F32)
        grstd = sb(top, "grstd", [128, NT, 2], F32)
        epsc = sb(top, "epsc", [128, 1], F32)

        A("sp", lambda e: e.dma_start(out=identb[:], in_=c_identb[:, :]), writes=["identb"], dma_key="c0")
        A("sp", lambda e: e.dma_start(out=identf[:], in_=c_identf[:, :]), writes=["identf"], dma_key="c1")
        A("sp", lambda e: e.dma_start(out=cols_sb[:], in_=cols.rearrange("l p n -> p l n")), writes=["cols"], dma_key="c2")
        A("pool", lambda e: e.memset(ones_bf[:], 1.0), writes=["ones_bf"])
        A("pool", lambda e: e.memset(ones_f[:], 1.0), writes=["ones_f"])
        A("pool", lambda e: e.memset(epsc[:], EPS), writes=["epsc"])
        if debug:
            A("pool", lambda e: e.memset(grstd[:], 0.0), writes=["grstd_init"])

        def rstd_from_ssq(dst, ssq, n, tags_r, tags_w, tmp, tmptag):
            A("dve", lambda e: e.tensor_scalar(out=tmp, in0=ssq, scalar1=1.0 / n, scalar2=EPS, op0=ALU.mult, op1=ALU.add),
              reads=tags_r, writes=[tmptag])
            A("act", lambda e: e.activation(out=tmp, in_=tmp, func=AF.Ln), reads=[tmptag], writes=[tmptag])
            A("act", lambda e: e.activation(out=dst, in_=tmp, func=AF.Exp, scale=-0.5), reads=[tmptag], writes=tags_w)

        with ExitStack() as es:
            cc = sb(es, "cc", [128, 8], F32)
            ce = sb(es, "ce", [128, 8], F32)
            cact = sb(es, "cact", [128, 8], BF16)
            wa = [sb(es, f"wa{i}", [128, 8, 512], BF16) for i in range(2)]
            modc = sb(es, "modc", [128, 32], F32)
            grow = sb(es, "grow", [1, 2, D], F32)
            brow = sb(es, "brow", [1, 2, 2, D], F32)
            lam = sb(es, "lam", [128, 2, 2], F32)
            mod_ps_t = ps(es, "mod_ps", [128, 512], F32)
            mod_ps = mod_ps_t[:, 0:32]
            g_ps = [ps(es, f"g_ps{i}", [128, 512], F32)[0:1, :] for i in range(2)]

            A("sp", lambda e: e.dma_start(out=cc[:], in_=ccol[:, :]), writes=["cc"], dma_key="c3")
            A("act", lambda e: e.activation(out=ce[:], in_=cc[:], func=AF.Exp, scale=-1.0), reads=["cc"], writes=["ce"])
            A("dve", lambda e: e.tensor_scalar(out=ce[:], in0=ce[:], scalar1=1.0, scalar2=None, op0=ALU.add), reads=["ce"], writes=["ce"])
            A("dve", lambda e: e.reciprocal(out=ce[:], in_=ce[:]), reads=["ce"], writes=["ce"])
            A("dve", lambda e: e.tensor_tensor(out=cact[:], in0=ce[:], in1=cc[:], op=ALU.mult), reads=["ce", "cc"], writes=["cact"])
            for l in range(n_layers):
                for g_ in range(2):
                    A("sp", lambda e, l=l, g_=g_: e.dma_start(out=brow[:, l, g_, :], in_=bada[l:l + 1, (2 + 3 * g_) * D:(3 + 3 * g_) * D]),
                      writes=[("brow", l, g_)], dma_key=("c4", l, g_))
                colidx = 0
                for blk in range(12):
                    wt, wtag = wa[blk % 2], ("wa", blk % 2)
                    A("pool", lambda e, wt=wt, l=l, blk=blk: e.dma_start(
                        out=wt[:], in_=w_ada[l, :, blk * 512:(blk + 1) * 512].rearrange("(c p) n -> p c n", p=128)),
                      writes=[wtag], dma_key=("wa", blk % 2))
                    if blk in (4, 5, 10, 11):
                        g = 0 if blk < 6 else 1
                        half = blk % 2
                        gp, gtag = g_ps[half], ("g_ps", half)

                        def mm(e, wt=wt, gp=gp):
                            r = None
                            for c in range(8):
                                r = e.matmul(gp, lhsT=cact[:, c:c + 1], rhs=wt[:, c, :], start=(c == 0), stop=(c == 7))
                            return r
                        A("pe", mm, reads=[wtag, "cact"], writes=[gtag])
                        A("dve", lambda e, gp=gp, l=l, g=g, half=half: e.tensor_tensor(
                            out=grow[:, g, half * 512:(half + 1) * 512], in0=gp, in1=brow[:, l, g, half * 512:(half + 1) * 512], op=ALU.add),
                          reads=[gtag, ("brow", l, g)], writes=[("grow", g, half)])
                        if half == 1:
                            A("sp", lambda e, l=l, g=g: e.dma_start(out=gates_scr[l, g:g + 1, :], in_=grow[:, g, :]),
                              reads=[("grow", g, 0), ("grow", g, 1)], dma_key=("grow", g))
                    else:
                        def mm(e, wt=wt, colidx=colidx):
                            r = None
                            for sub in range(4):
                                for c in range(8):
                                    r = e.matmul(mod_ps[:, colidx + sub:colidx + sub + 1], lhsT=wt[:, c, sub * 128:(sub + 1) * 128],
                                                 rhs=cact[:, c:c + 1], start=(c == 0), stop=(c == 7))
                            return r
                        A("pe", mm, reads=[wtag, "cact", "modc"], writes=["mod_ps"])
                        colidx += 4
                A("dve", lambda e, l=l: e.tensor_tensor(out=modc[:], in0=mod_ps, in1=cols_sb[:, l, 16:48], op=ALU.add),
                  reads=["mod_ps", "cols"], writes=["modc"])
                for k2 in range(2):
                    A("dve", lambda e, l=l, k2=k2: e.scalar_tensor_tensor(
                        out=eff_sb[:, l, 16 * k2:16 * k2 + 8], in0=modc[:, 16 * k2 + 8:16 * k2 + 16], scalar=1.0,
                        in1=cols_sb[:, l, 8 * k2:8 * k2 + 8], op0=ALU.add, op1=ALU.mult),
                      reads=["modc", "cols"], writes=[("eff", l, k2, 0)])
                    A("dve", lambda e, l=l, k2=k2: e.tensor_copy(out=eff_sb[:, l, 16 * k2 + 8:16 * k2 + 16], in_=modc[:, 16 * k2:16 * k2 + 8]),
                      reads=["modc"], writes=[("eff", l, k2, 1)])
                    A("dve", lambda e, l=l, k2=k2: e.tensor_copy(out=shb_sb[:, l, k2, :], in_=modc[:, 16 * k2:16 * k2 + 8]),
                      reads=["modc"], writes=[("shb", l, k2)])
                A("dve", lambda e, l=l: e.tensor_scalar(out=lruc_sb[:, l, 0:4], in0=cols_sb[:, l, 72:76], scalar1=-1.0, scalar2=None, op0=ALU.mult),
                  reads=["cols"], writes=[("lruc", l, 0)])
                A("act", lambda e, l=l: e.activation(out=lam[:, l, :], in_=cols_sb[:, l, 76:78], func=AF.Exp, scale=-1.0),
                  reads=["cols"], writes=[("lam", l)])
                A("act", lambda e, l=l: e.activation(out=lam[:, l, :], in_=lam[:, l, :], func=AF.Ln, bias=1.0),
                  reads=[("lam", l)], writes=[("lam", l)])
                A("dve", lambda e, l=l: e.tensor_scalar(out=lruc_sb[:, l, 4:6], in0=lam[:, l, :], scalar1=-8.0, scalar2=None, op0=ALU.mult),
                  reads=[("lam", l)], writes=[("lruc", l, 1)])
            P.barrier()

        for l in range(n_layers):
            x_src = x_in if l == 0 else xmid_scr
            x_dst = xmid_scr if l < n_layers - 1 else out
            if l >= 2:
                x_dst = out
            if "A" in phases:
                P.nosched = True
                phase_A(nc, P, top, sb, ps, l, x_src, dict(
                    cols_sb=cols_sb, eff_sb=eff_sb, shb_sb=shb_sb, lruc_sb=lruc_sb, identb=identb, identf=identf,
                    ones_bf=ones_bf, ones_f=ones_f, grstd=grstd, w_in=w_in, qng=qng, kng=kng, lru_wa=lru_wa,
                    lru_wx=lru_wx, c_negmask=c_negmask, qT_scr=qT_scr, kT_scr=kT_scr, v_scr=v_scr, ycl_scr=ycl_scr,
                    rstd_from_ssq=rstd_from_ssq, dbg_grstd=dbg_grstd, epsc=epsc), nblk)
                P.barrier()
                P.nosched = False
            TT = dict(cols_sb=cols_sb, eff_sb=eff_sb, shb_sb=shb_sb, identb=identb, identf=identf, ones_bf=ones_bf, ones_f=ones_f,
                      grstd=grstd, w_out=w_out, w_up=w_up, w_down=w_down, gates_scr=gates_scr, c_oh=c_oh, c_caus=c_caus,
                      qT_scr=qT_scr, kT_scr=kT_scr, v_scr=v_scr, ycl_scr=ycl_scr, x1_scr=x1_scr, rstd_from_ssq=rstd_from_ssq)
            if "B" in phases:
                phase_B(nc, P, top, sb, ps, l, x_src, TT, nblk)
                P.barrier()
            if "C" in phases:
                phase_C(nc, P, top, sb, ps, l, x_dst, TT, nblk)
                P.barrier()
        P.emit()
    return nc, P


def phase_A(nc, P, top, sb0, ps0, l, x_src, T, nblk):
    A = P.add
    sb = lambda es, name, shape, dt: sb0(es, f"{name}_A{l}", shape, dt)
    ps = lambda es, name, shape, dt: ps0(es, f"{name}_A{l}", shape, dt)
    cols_sb, eff_sb, shb_sb, lruc_sb = T["cols_sb"], T["eff_sb"], T["shb_sb"], T["lruc_sb"]
    identb, identf, ones_bf, ones_f, grstd, epsc = T["identb"], T["identf"], T["ones_bf"], T["ones_f"], T["grstd"], T["epsc"]

    def rstd2(dst, ssq, n, rtags, wtags, tmp, tmptag):
        A("act", lambda e: e.activation(out=tmp, in_=ssq, func=AF.Ln, scale=1.0 / n, bias=epsc[:, 0:1]), reads=rtags, writes=[tmptag], n=8)
        A("act", lambda e: e.activation(out=dst, in_=tmp, func=AF.Exp, scale=-0.5), reads=[tmptag], writes=wtags, n=8)

    with ExitStack() as es:
        dbl = lambda name, shape, dt: [sb(es, f"{name}{i}", shape, dt) for i in range(2)]
        w_sb = sb(es, "w_in_sb", [128, 8, DIN], BF16)
        brow = sb(es, "b_in_row", [1, 1536], BF16)
        bcol = sb(es, "b_in_col", [128, 10], F32)
        gq = sb(es, "gq", [128, 64], F32)
        gk = sb(es, "gk", [128, 64], F32)
        negm = sb(es, "negm", [128, 16, 16], F32)
        kmeanT = sb(es, "kmeanT", [64, 8, 16], F32)
        km_hi = sb(es, "km_hi", [64, 8, 16], BF16)
        km_lo = sb(es, "km_lo", [64, 8, 16], BF16)
        km_tmp = sb(es, "km_tmp", [64, 8], F32)
        wabd = sb(es, "wabd", [128, 2, 2, 128], BF16)
        wtmp = sb(es, "wtmp", [128, 2, 2, 64], F32)
        x_sbs = Rot([sb(es, f"xA{i}", [128, D], F32) for i in range(2)], "xA")
        junkx = sb(es, "junkx", [128, D], BF16)
        junkq = dbl("junkq", [128, 512], BF16)
        junkk = dbl("junkk", [128, 512], BF16)
        qraw = dbl("qraw", [128, 512], F32)
        kraw = dbl("kraw", [128, 512], F32)
        xn = dbl("xn", [128, D], BF16)
        xnT = dbl("xnT", [128, 8, 256], BF16)
        stx = dbl("stx", [128, 4], F32)
        stq = dbl("stq", [128, 3, 8], F32)
        stk = dbl("stk", [128, 3, 8], F32)
        stg = dbl("stg", [128, 4], F32)
        kf = dbl("kf", [128, 512], F32)
        kb = dbl("kb", [128, 512], BF16)
        qaug = dbl("qaug", [128, 8, 80], BF16)
        qT8 = dbl("qT8", [64, 8, 128], BF16)
        gm = dbl("gm", [128, 8, 16], F32)
        m8 = dbl("m8", [128, 8, 8], F32)
        msk = dbl("msk", [128, 8, 16], F32)
        v_sbs = Rot([sb(es, f"vA{i}", [128, 8, 128], BF16) for i in range(2)], "vA")
        kT_sbs = Rot([sb(es, f"kTA{i}", [64, 8, 128], BF16) for i in range(2)], "kTA")
        qT_sbs = Rot([sb(es, f"qTA{i}", [80, 8, 128], BF16) for i in range(2)], "qTA")
        fmS = [sb(es, "fmS0", [128, 10, 256], F32)] * 2
        cu = sb(es, "cu", [128, 2, 258], F32)
        lx = sb(es, "lx", [128, 2, 259], F32)
        hb = sb(es, "hb", [128, 2, 257], F32)
        ct = dbl("ct", [128, 256], F32)
        cy = dbl("cy", [128, 256], F32)
        lt = [[sb(es, f"lt{lc}_{k}", [128, 256], F32) for k in range(7)] for lc in range(2)]
        xrb = dbl("xrb", [128, 256], BF16)
        ysq = dbl("ysq", [128, 4, 256], BF16)
        ycl_sbs = Rot([sb(es, f"ycl{i}", [128, 4, 256], BF16) for i in range(2)], "ycl")
        tp_ps = ps(es, "tp_ps", [128, 8, 128], BF16)
        qkv_ps = [ps(es, f"qkv_ps{i}", [128, 512], F32) for i in range(3)]
        fm_pss = Rot([ps(es, f"fm_ps{i}", [128, 512], F32) for i in range(2)], "fm_ps")
        sm_ps = ps(es, "sm_ps", [128, 512], F32)
        gate_ps = sm_ps[:, 0:128].rearrange("p (h n) -> p h n", h=8)
        km_ps = sm_ps[0:64, 128:136]
        ss_ps = sm_ps[:, 136:138]
        bc_ps = sm_ps[:, 144:154]

        for c in range(8):
            A("pool", lambda e, c=c: e.dma_start(out=w_sb[:, c, :], in_=T["w_in"][l, c * 128:(c + 1) * 128, :], max_dma_last_dim=4096),
              writes=[("w_in", c)], dma_key=("w", c), cost=12000)
        A("sp", lambda e: e.dma_start(out=gq[:], in_=T["qng"][l:l + 1, :].partition_broadcast(128)), writes=["gq"], dma_key="a0")
        A("sp", lambda e: e.dma_start(out=gk[:], in_=T["kng"][l:l + 1, :].partition_broadcast(128)), writes=["gk"], dma_key="a1")
        A("sp", lambda e: e.dma_start(out=negm[:].rearrange("p a b -> p (a b)"), in_=T["c_negmask"][0:1, :].partition_broadcast(128)),
          writes=["negm"], dma_key="a2")
        A("dve", lambda e: e.scalar_tensor_tensor(out=gk[:], in0=gk[:], scalar=0.125, in1=gq[:], op0=ALU.mult, op1=ALU.mult),
          reads=["gq", "gk"], writes=["gk"])
        A("pool", lambda e: e.memset(kmeanT[:], 0.0), writes=["kmeanT"])
        A("pool", lambda e: e.memset(km_hi[:], 0.0), writes=["km_hi"])
        A("pool", lambda e: e.memset(km_lo[:], 0.0), writes=["km_lo"])
        A("pool", lambda e: e.memset(cu[:], 0.0), writes=["cu0", "cu1"])
        A("pool", lambda e: e.memset(lx[:], 0.0), writes=["lx0", "lx1"])
        A("pool", lambda e: e.memset(hb[:], 0.0), writes=["hb0", "hb1"])
        for i in range(2):
            A("pool", lambda e, i=i: e.memset(qaug[i][:], 0.0), writes=[f"qaug_b{i}", f"qaug_q{i}"])
        for k, vt in enumerate(v_sbs.t):
            A("pool", lambda e, vt=vt: e.memset(vt[:], 1.0), writes=[("vA", k)])
        A("pool", lambda e: e.memset(wabd[:], 0.0), writes=["wabd"])
        for g, wsrc in enumerate((T["lru_wa"], T["lru_wx"])):
            for hh in range(4):
                ch, hf = hh // 2, hh % 2
                A("sp", lambda e, g=g, wsrc=wsrc, hh=hh, ch=ch, hf=hf: e.dma_start(
                    out=wtmp[hf * 64:(hf + 1) * 64, g, ch, :], in_=wsrc[l, hh, :, :]), writes=[("wtmp", g, hh)], dma_key=("spk", g * 4 + hh))
                A("dve", lambda e, g=g, ch=ch, hf=hf: e.tensor_copy(out=wabd[hf * 64:(hf + 1) * 64, g, ch, hf * 64:(hf + 1) * 64],
                                                                   in_=wtmp[hf * 64:(hf + 1) * 64, g, ch, :]),
                  reads=[("wtmp", g, hh), "wabd"], writes=[("wabd", g, hh)])
        wabd_tags = [("wabd", g, hh) for g in range(2) for hh in range(4)]
        w_tags = [("w_in", c) for c in range(8)]

        for j in range(3):
            def mm(e, j=j):
                r = None
                for c in range(8):
                    r = e.matmul(qkv_ps[j][0:1, :], lhsT=shb_sb[:, l, 0, c:c + 1], rhs=w_sb[:, c, j * 512:(j + 1) * 512],
                                 start=(c == 0), stop=(c == 7))
                return r
            A("pe", mm, reads=w_tags + [("shb", l, 0)], writes=[("qkv_ps", j)])
            A("act", lambda e, j=j: e.copy(out=brow[:, j * 512:(j + 1) * 512], in_=qkv_ps[j][0:1, :]), reads=[("qkv_ps", j)], writes=["brow"], n=512)

        def mm(e):
            r = None
            for fc in range(10):
                for c in range(8):
                    r = e.matmul(bc_ps[:, fc:fc + 1], lhsT=w_sb[:, c, 1536 + fc * 128:1536 + (fc + 1) * 128],
                                 rhs=shb_sb[:, l, 0, c:c + 1], start=(c == 0), stop=(c == 7))
            return r
        A("pe", mm, reads=w_tags + [("shb", l, 0)], writes=["sm_ps"])
        A("dve", lambda e: e.tensor_copy(out=bcol[:], in_=bc_ps), reads=["sm_ps"], writes=["bcol"])
        for c in range(8):
            if c % 2 == 0:
                A("dve", lambda e, c=c: e.tensor_scalar(out=w_sb[:, c, :], in0=w_sb[:, c, :], scalar1=eff_sb[:, l, c:c + 1], scalar2=None, op0=ALU.mult),
                  reads=[("eff", l, 0, 0)], writes=[("w_in", c)], n=DIN)
            else:
                A("act", lambda e, c=c: e.activation(out=w_sb[:, c, :], in_=w_sb[:, c, :], func=AF.Copy, scale=eff_sb[:, l, c:c + 1]),
                  reads=[("eff", l, 0, 0)], writes=[("w_in", c)], n=DIN)

        def h3(ap):
            return ap.rearrange("p (h d) -> p h d", h=8)

        def tiles(st):
            bp = st % 2
            ctx = [dict(), dict()]

            def s0(i):
                t = 2 * st + i
                xt, xtag = x_sbs.next()
                ctx[i].update(t=t, xt=xt, xtag=xtag)
                A("sp", lambda e, xt=xt, t=t: e.dma_start(out=xt[:], in_=x_src[t * 128:(t + 1) * 128, :]), writes=[xtag], dma_key=xtag)
                A("act", lambda e, xt=xt, i=i: e.activation(out=junkx[:], in_=xt[:], func=AF.Square, accum_out=stx[i][:, 0:1]),
                  reads=[xtag], writes=["junkx", f"stx{i}"], n=D)
                rstd2(stx[i][:, 2:3], stx[i][:, 0:1], D, [f"stx{i}"], [f"stxr{i}"], stx[i][:, 1:2], f"stxt{i}")
                A("dve", lambda e, xt=xt, i=i: e.tensor_scalar(out=xn[i][:], in0=xt[:], scalar1=stx[i][:, 2:3], scalar2=None, op0=ALU.mult),
                  reads=[xtag, f"stxr{i}"], writes=[f"xn{i}"], n=D)

            def s1(i):
                def tr(e, i=i):
                    r = None
                    for c in range(8):
                        r = e.transpose(out=tp_ps[:, c, :], in_=xn[i][:, c * 128:(c + 1) * 128], identity=identb[:])
                    return r
                A("pe", tr, reads=[f"xn{i}", "identb"], writes=["tp_ps"])
                A("act", lambda e, i=i, bp=bp: e.copy(out=xnT[bp][:, :, i * 128:(i + 1) * 128], in_=tp_ps[:]), reads=["tp_ps"], writes=[("xnT", bp, i)], n=D)

            def s2(i):
                t = ctx[i]["t"]
                for j in range(3):
                    def mm(e, j=j, i=i, bp=bp):
                        for c in range(8):
                            e.matmul(qkv_ps[j][:], lhsT=xnT[bp][:, c, i * 128:(i + 1) * 128], rhs=w_sb[:, c, j * 512:(j + 1) * 512],
                                     start=(c == 0), stop=False)
                        return e.matmul(qkv_ps[j][:], lhsT=ones_bf[0:1, :], rhs=brow[0:1, j * 512:(j + 1) * 512], start=False, stop=True)
                    A("pe", mm, reads=w_tags + [("xnT", bp, i), "brow", "ones_bf"], writes=[("qkv_ps", j)], cost=2200)
                A("act", lambda e, i=i: e.copy(out=qraw[i][:], in_=qkv_ps[0][:]), reads=[("qkv_ps", 0)], writes=[f"qraw{i}"], n=512)
                A("act", lambda e, i=i: e.copy(out=kraw[i][:], in_=qkv_ps[1][:]), reads=[("qkv_ps", 1)], writes=[f"kraw{i}"], n=512)
                vt, vtag = v_sbs.next()
                A("act", lambda e, vt=vt: e.copy(out=vt[:, :, 0:64], in_=h3(qkv_ps[2][:])), reads=[("qkv_ps", 2)], writes=[vtag], n=512)
                A("sp", lambda e, vt=vt, t=t: e.dma_start(out=T["v_scr"][t, :, :], in_=vt[:].rearrange("p h d -> p (h d)")),
                  reads=[vtag], writes=[("v_scr", t)], dma_key=("st",) + vtag)

            def s3(i):
                A("act", lambda e, i=i: e.activation(out=junkq[i][:], in_=qraw[i][:], func=AF.Square), reads=[f"qraw{i}"], writes=[f"junkq{i}"], n=512)
                A("dve", lambda e, i=i: e.tensor_reduce(out=stq[i][:, 0, :], in_=h3(junkq[i][:]), axis=AX.X, op=ALU.add),
                  reads=[f"junkq{i}"], writes=[f"stq{i}"], n=512)
                rstd2(stq[i][:, 2, :], stq[i][:, 0, :], 64, [f"stq{i}"], [f"stqr{i}"], stq[i][:, 1, :], f"stqt{i}")
                A("dve", lambda e, i=i: e.tensor_tensor(out=qaug[i][:, :, 0:64], in0=h3(qraw[i][:]),
                                                        in1=stq[i][:, 2, :].unsqueeze(2).to_broadcast([128, 8, 64]), op=ALU.mult),
                  reads=[f"qraw{i}", f"stqr{i}"], writes=[f"qaug_q{i}"], n=512)
                A("act", lambda e, i=i: e.activation(out=junkk[i][:], in_=kraw[i][:], func=AF.Square), reads=[f"kraw{i}"], writes=[f"junkk{i}"], n=512)
                A("dve", lambda e, i=i: e.tensor_reduce(out=stk[i][:, 0, :], in_=h3(junkk[i][:]), axis=AX.X, op=ALU.add),
                  reads=[f"junkk{i}"], writes=[f"stk{i}"], n=512)
                rstd2(stk[i][:, 2, :], stk[i][:, 0, :], 64, [f"stk{i}"], [f"stkr{i}"], stk[i][:, 1, :], f"stkt{i}")
                A("dve", lambda e, i=i: e.tensor_tensor(out=h3(kf[i][:]), in0=h3(kraw[i][:]),
                                                        in1=stk[i][:, 2, :].unsqueeze(2).to_broadcast([128, 8, 64]), op=ALU.mult),
                  reads=[f"kraw{i}", f"stkr{i}"], writes=[f"kf{i}"], n=512)
                A("dve", lambda e, i=i: e.tensor_tensor(out=h3(kb[i][:]), in0=h3(kf[i][:]),
                                                        in1=gk[:].unsqueeze(1).to_broadcast([128, 8, 64]), op=ALU.mult),
                  reads=[f"kf{i}", "gk"], writes=[f"kb{i}"], n=512)

            def s4(i):
                def mm(e, i=i):
                    r = None
                    for h in range(8):
                        r = e.matmul(km_ps[:, h:h + 1], lhsT=kb[i][:, h * 64:(h + 1) * 64], rhs=ones_bf[:, 0:1], start=True, stop=True)
                    return r
                A("pe", mm, reads=[f"kb{i}", "ones_bf"], writes=["sm_ps"], cost=600)
                if i == 0:
                    A("dve", lambda e: e.tensor_scalar(out=kmeanT[:, :, st], in0=km_ps, scalar1=1.0 / 256, scalar2=None, op0=ALU.mult),
                      reads=["sm_ps"], writes=["kmeanT"], n=8)
                else:
                    A("dve", lambda e: e.scalar_tensor_tensor(out=kmeanT[:, :, st], in0=km_ps, scalar=1.0 / 256, in1=kmeanT[:, :, st],
                                                              op0=ALU.mult, op1=ALU.add),
                      reads=["sm_ps"], writes=["kmeanT"], n=8)
                    A("dve", lambda e: e.tensor_copy(out=km_hi[:, :, st], in_=kmeanT[:, :, st]), reads=["kmeanT"], writes=["km_hi"], n=8)
                    A("dve", lambda e: e.tensor_tensor(out=km_tmp[:], in0=kmeanT[:, :, st], in1=km_hi[:, :, st], op=ALU.subtract),
                      reads=["kmeanT", "km_hi"], writes=["km_tmp"], n=8)
                    A("dve", lambda e: e.tensor_copy(out=km_lo[:, :, st], in_=km_tmp[:]), reads=["km_tmp"], writes=["km_lo"], n=8)

            def s5(i):
                t = ctx[i]["t"]
                kTt, kTtag = kT_sbs.next()

                def tr(e, i=i):
                    r = None
                    for h in range(8):
                        r = e.transpose(out=tp_ps[0:64, h, :], in_=kb[i][:, h * 64:(h + 1) * 64], identity=identb[:])
                    return r
                A("pe", tr, reads=[f"kb{i}", "identb"], writes=["tp_ps"])
                A("act", lambda e, kTt=kTt: e.copy(out=kTt[:], in_=tp_ps[0:64, :, :]), reads=["tp_ps"], writes=[kTtag], n=D)
                A("sp", lambda e, kTt=kTt, t=t: e.dma_start(out=T["kT_scr"][:, :, t * 128:(t + 1) * 128], in_=kTt[:]),
                  reads=[kTtag], writes=[("kT_scr", t)], dma_key=("st",) + kTtag)

            def s6(i):
                if st < 1:
                    return

                def tr(e, i=i):
                    r = None
                    for h in range(8):
                        r = e.transpose(out=tp_ps[0:64, h, :], in_=qaug[i][:, h, 0:64], identity=identb[:])
                    return r
                A("pe", tr, reads=[f"qaug_q{i}", "identb"], writes=["tp_ps"])
                A("act", lambda e, i=i: e.copy(out=qT8[i][:], in_=tp_ps[0:64, :, :]), reads=["tp_ps"], writes=[f"qT8{i}"], n=D)

            def s7(i):
                if st < 1:
                    return

                def mm(e, i=i):
                    r = None
                    for h in range(8):
                        e.matmul(gate_ps[:, h, :], lhsT=qT8[i][:, h, :], rhs=km_hi[:, h, :], start=True, stop=False)
                        r = e.matmul(gate_ps[:, h, :], lhsT=qT8[i][:, h, :], rhs=km_lo[:, h, :], start=False, stop=True)
                    return r
                A("pe", mm, reads=[f"qT8{i}", "km_hi", "km_lo"], writes=["sm_ps"], cost=1000)
                A("dve", lambda e, i=i: e.tensor_tensor(out=gm[i][:], in0=gate_ps, in1=negm[:, st:st + 1, :].to_broadcast([128, 8, 16]), op=ALU.add),
                  reads=["sm_ps", "negm"], writes=[f"gm{i}"], n=128)
                for h in range(8):
                    A("dve", lambda e, h=h, i=i: e.max(out=m8[i][:, h, :], in_=gm[i][:, h, :]), reads=[f"gm{i}"], writes=[(f"m8{i}", h)], n=16)
                A("dve", lambda e, i=i: e.tensor_tensor(out=msk[i][:], in0=gm[i][:], in1=m8[i][:, :, 2:3].to_broadcast([128, 8, 16]), op=ALU.is_ge),
                  reads=[f"gm{i}"] + [(f"m8{i}", h) for h in range(8)], writes=[f"msk{i}"], n=128)
                A("dve", lambda e, i=i: e.tensor_scalar(out=qaug[i][:, :, 64:80], in0=msk[i][:], scalar1=BIG, scalar2=-BIG, op0=ALU.mult, op1=ALU.add),
                  reads=[f"msk{i}"], writes=[f"qaug_b{i}"], n=128)

            def s8(i):
                t = ctx[i]["t"]
                qTt, qTtag = qT_sbs.next()

                def tr(e, i=i):
                    r = None
                    for h in range(8):
                        r = e.transpose(out=tp_ps[0:80, h, :], in_=qaug[i][:, h, :], identity=identb[:])
                    return r
                A("pe", tr, reads=[f"qaug_q{i}", f"qaug_b{i}", "identb"], writes=["tp_ps"])
                A("act", lambda e, qTt=qTt: e.copy(out=qTt[:], in_=tp_ps[0:80, :, :]), reads=["tp_ps"], writes=[qTtag], n=D)
                A("sp", lambda e, qTt=qTt, t=t: e.dma_start(out=T["qT_scr"][:, :, t * 128:(t + 1) * 128], in_=qTt[:]),
                  reads=[qTtag], writes=[("qT_scr", t)], dma_key=("st",) + qTtag)

            import os as _os7
            stages = (s0, s1, s2, s3, s4, s5, s6, s7, s8)
            tmode = _os7.environ.get("TILE_SEQ", "0")
            if tmode.startswith("g"):
                k_int = int(tmode[1:])
                for si in range(k_int):
                    for i in range(2):
                        stages[si](i)
                for i in range(2):
                    for si in range(k_int, 9):
                        stages[si](i)
            elif tmode == "1":
                for i in range(2):
                    for stage in stages:
                        stage(i)
            else:
                for stage in (s0, s1, s2, s3, s4, s5, s6, s7, s8):
                    for i in range(2):
                        stage(i)

        def fm(st):
            bp = st % 2
            F = fmS[bp]
            ycl_t, ycl_tag = ycl_sbs.next()
            for fc in (6, 7, 8, 9, 2, 4, 0, 3, 5, 1):
                bank, btag = fm_pss.next()

                def mm(e, bank=bank, fc=fc, bp=bp):
                    r = None
                    for c in range(8):
                        r = e.matmul(bank[:, 0:256], lhsT=w_sb[:, c, 1536 + fc * 128:1536 + (fc + 1) * 128], rhs=xnT[bp][:, c, :],
                                     start=(c == 0), stop=(c == 7))
                    return r
                A("pe", mm, reads=w_tags + [("xnT", bp, 0), ("xnT", bp, 1)], writes=[btag], cost=1100)
                if fc in (6, 7):
                    lc = fc - 6
                    A("act", lambda e, bank=bank, lc=lc, fc=fc: e.activation(out=lx[:, lc, 3:259], in_=bank[:, 0:256], func=AF.Identity, bias=bcol[:, fc:fc + 1]),
                      reads=[btag, "bcol"], writes=[f"lx{lc}"])
                else:
                    A("dve", lambda e, bank=bank, fc=fc, F=F: e.tensor_scalar(out=F[:, fc, :], in0=bank[:, 0:256], scalar1=bcol[:, fc:fc + 1], scalar2=None, op0=ALU.add),
                      reads=[btag, "bcol"], writes=[("fmS", fc)])
            for lc in range(2):
                xr, ea, sa, ei, uu, gz, yy = lt[lc]
                tg = lambda nm, lc=lc: f"{nm}{lc}"
                lxb, hbb = lx[:, lc, :], hb[:, lc, :]
                cw = [cols_sb[:, l, 62 + lc * 4 + k:62 + lc * 4 + k + 1] for k in range(4)]
                cb = cols_sb[:, l, 70 + lc:71 + lc]
                nba = lruc_sb[:, l, 0 + lc:1 + lc]
                nbx = lruc_sb[:, l, 2 + lc:3 + lc]
                sp8 = lruc_sb[:, l, 4 + lc:5 + lc]
                G = F[:, 8 + lc, :]
                gtag = ("fmS", 8 + lc)
                A("dve", lambda e, lxb=lxb, cw=cw, cb=cb, xr=xr: e.tensor_scalar(out=xr[:], in0=lxb[:, 0:256], scalar1=cw[0], scalar2=cb, op0=ALU.mult, op1=ALU.add),
                  reads=[tg("lx")], writes=[tg("xr")])
                for k in range(1, 4):
                    A("dve", lambda e, lxb=lxb, cw=cw, k=k, xr=xr: e.scalar_tensor_tensor(out=xr[:], in0=lxb[:, k:k + 256], scalar=cw[k], in1=xr[:],
                                                                                      op0=ALU.mult, op1=ALU.add),
                      reads=[tg("lx"), tg("xr")], writes=[tg("xr")])
                A("dve", lambda e, lxb=lxb: e.tensor_copy(out=lxb[:, 0:3], in_=lxb[:, 256:259]), reads=[tg("lx"), tg("xr")], writes=[tg("lx")], n=3)
                A("pool", lambda e, lc=lc, xr=xr: e.tensor_copy(out=xrb[lc][:], in_=xr[:]), reads=[tg("xr")], writes=[tg("xrb")], cost=1500)
                rb, rbtag = fm_pss.next()
                A("pe", lambda e, lc=lc, rb=rb: e.matmul(rb[:, 0:256], lhsT=wabd[:, 0, lc, :], rhs=xrb[lc][:], start=True, stop=True),
                  reads=[tg("xrb")] + wabd_tags, writes=[rbtag], cost=200)
                A("act", lambda e, nba=nba, rb=rb, ea=ea: e.activation(out=ea[:], in_=rb[:, 0:256], func=AF.Exp, scale=-1.0, bias=nba),
                  reads=[rbtag], writes=[tg("ea")])
                ib, ibtag = fm_pss.next()
                A("pe", lambda e, lc=lc, ib=ib: e.matmul(ib[:, 0:256], lhsT=wabd[:, 1, lc, :], rhs=xrb[lc][:], start=True, stop=True),
                  reads=[tg("xrb")] + wabd_tags, writes=[ibtag], cost=200)
                A("act", lambda e, nbx=nbx, ib=ib, ei=ei: e.activation(out=ei[:], in_=ib[:, 0:256], func=AF.Exp, scale=-1.0, bias=nbx),
                  reads=[ibtag], writes=[tg("ei")])
                A("act", lambda e, ea=ea: e.activation(out=ea[:], in_=ea[:], func=AF.Ln, bias=1.0), reads=[tg("ea")], writes=[tg("ea")])
                A("act", lambda e, ea=ea: e.activation(out=ea[:], in_=ea[:], func=AF.Exp, scale=-1.0), reads=[tg("ea")], writes=[tg("ea")])
                A("act", lambda e, ea=ea, sp8=sp8: e.activation(out=ea[:], in_=ea[:], func=AF.Exp, scale=sp8), reads=[tg("ea")], writes=[tg("ea")])
                A("act", lambda e, ea=ea, sa=sa: e.activation(out=sa[:], in_=ea[:], func=AF.Square), reads=[tg("ea")], writes=[tg("sa")])
                A("act", lambda e, sa=sa: e.activation(out=sa[:], in_=sa[:], func=AF.Ln, scale=-1.0, bias=1.0), reads=[tg("sa")], writes=[tg("sa")])
                A("act", lambda e, sa=sa: e.activation(out=sa[:], in_=sa[:], func=AF.Exp, scale=0.5), reads=[tg("sa")], writes=[tg("sa")])
                A("act", lambda e, ei=ei: e.activation(out=ei[:], in_=ei[:], func=AF.Ln, bias=1.0), reads=[tg("ei")], writes=[tg("ei")])
                A("act", lambda e, ei=ei: e.activation(out=ei[:], in_=ei[:], func=AF.Exp, scale=-1.0), reads=[tg("ei")], writes=[tg("ei")])
                A("dve", lambda e, ei=ei, xr=xr, uu=uu: e.tensor_tensor(out=uu[:], in0=ei[:], in1=xr[:], op=ALU.mult), reads=[tg("ei"), tg("xr")], writes=[tg("uu")])
                A("dve", lambda e, sa=sa, uu=uu: e.tensor_tensor(out=uu[:], in0=uu[:], in1=sa[:], op=ALU.mult), reads=[tg("uu"), tg("sa")], writes=[tg("uu")])
                A("dve", lambda e, hbb=hbb, ea=ea, uu=uu: e.tensor_tensor_scan(out=hbb[:, 1:257], data0=ea[:], data1=uu[:], initial=hbb[:, 0:1],
                                                                              op0=ALU.mult, op1=ALU.add),
                  reads=[tg("ea"), tg("uu"), tg("hb")], writes=[tg("hbh")], n=512)
                A("act", lambda e, G=G, gz=gz: e.activation(out=gz[:], in_=G, func=AF.Square), reads=[gtag], writes=[tg("gz")])
                A("dve", lambda e, gz=gz: e.tensor_scalar(out=gz[:], in0=gz[:], scalar1=0.044715, scalar2=1.0, op0=ALU.mult, op1=ALU.add),
                  reads=[tg("gz")], writes=[tg("gz")])
                A("dve", lambda e, gz=gz, G=G: e.tensor_tensor(out=gz[:], in0=gz[:], in1=G, op=ALU.mult), reads=[tg("gz"), gtag], writes=[tg("gz")])
                A("act", lambda e, gz=gz: e.activation(out=gz[:], in_=gz[:], func=AF.Exp, scale=-1.5957691216057308), reads=[tg("gz")], writes=[tg("gz")])
                A("act", lambda e, gz=gz: e.activation(out=gz[:], in_=gz[:], func=AF.Ln, bias=1.0), reads=[tg("gz")], writes=[tg("gz")])
                A("act", lambda e, gz=gz: e.activation(out=gz[:], in_=gz[:], func=AF.Exp, scale=-1.0), reads=[tg("gz")], writes=[tg("gz")])
                A("dve", lambda e, gz=gz, G=G: e.tensor_tensor(out=gz[:], in0=gz[:], in1=G, op=ALU.mult), reads=[tg("gz"), gtag], writes=[tg("gz")])
                A("dve", lambda e, hbb=hbb, gz=gz, yy=yy: e.tensor_tensor(out=yy[:], in0=hbb[:, 1:257], in1=gz[:], op=ALU.mult),
                  reads=[tg("hbh"), tg("gz")], writes=[tg("yy")])
                A("dve", lambda e, hbb=hbb: e.tensor_copy(out=hbb[:, 0:1], in_=hbb[:, 256:257]), reads=[tg("hbh"), tg("yy")], writes=[tg("hb")], n=1)
                A("pool", lambda e, lc=lc, ycl_t=ycl_t, yy=yy: e.tensor_copy(out=ycl_t[:, 2 + lc, :], in_=yy[:]), reads=[tg("yy")], writes=[ycl_tag + (2 + lc,)], cost=1500)
                A("pool", lambda e, lc=lc, yy=yy, bp=bp: e.tensor_tensor(out=ysq[bp][:, 2 + lc, :], in0=yy[:], in1=yy[:], op=ALU.mult), reads=[tg("yy")], writes=[("ysq", bp, 2 + lc)], cost=1000)
            for cc in range(2):
                cub = cu[:, cc, :]
                tg = lambda nm, cc=cc: f"{nm}{cc}"
                w0 = cols_sb[:, l, 56 + cc * 3 + 0:56 + cc * 3 + 1]
                w1 = cols_sb[:, l, 56 + cc * 3 + 1:56 + cc * 3 + 2]
                w2 = cols_sb[:, l, 56 + cc * 3 + 2:56 + cc * 3 + 3]
                Bt, Ct, Ut = ("fmS", 0 + cc), ("fmS", 2 + cc), ("fmS", 4 + cc)
                A("dve", lambda e, cub=cub, cc=cc, F=F: e.tensor_tensor(out=cub[:, 2:258], in0=F[:, 2 + cc, :], in1=F[:, 4 + cc, :], op=ALU.mult),
                  reads=[Ct, Ut], writes=[tg("cu")])
                A("dve", lambda e, cub=cub, w0=w0, cc=cc: e.tensor_scalar(out=ct[cc][:], in0=cub[:, 0:256], scalar1=w0, scalar2=None, op0=ALU.mult),
                  reads=[tg("cu")], writes=[tg("ct")])
                A("dve", lambda e, cub=cub, w1=w1, cc=cc: e.scalar_tensor_tensor(out=ct[cc][:], in0=cub[:, 1:257], scalar=w1, in1=ct[cc][:], op0=ALU.mult, op1=ALU.add),
                  reads=[tg("cu"), tg("ct")], writes=[tg("ct")])
                A("dve", lambda e, cub=cub, w2=w2, cc=cc: e.scalar_tensor_tensor(out=ct[cc][:], in0=cub[:, 2:258], scalar=w2, in1=ct[cc][:], op0=ALU.mult, op1=ALU.add),
                  reads=[tg("cu"), tg("ct")], writes=[tg("ct")])
                A("dve", lambda e, cub=cub: e.tensor_copy(out=cub[:, 0:2], in_=cub[:, 256:258]), reads=[tg("cu"), tg("ct")], writes=[tg("cu")], n=2)
                A("dve", lambda e, cc=cc, F=F: e.tensor_tensor(out=cy[cc][:], in0=F[:, 0 + cc, :], in1=ct[cc][:], op=ALU.mult),
                  reads=[Bt, tg("ct")], writes=[tg("cy")])
                A("pool", lambda e, cc=cc, ycl_t=ycl_t: e.tensor_copy(out=ycl_t[:, cc, :], in_=cy[cc][:]), reads=[tg("cy")], writes=[ycl_tag + (cc,)], cost=1500)
                A("pool", lambda e, cc=cc, bp=bp: e.tensor_tensor(out=ysq[bp][:, cc, :], in0=cy[cc][:], in1=cy[cc][:], op=ALU.mult), reads=[tg("cy")], writes=[("ysq", bp, cc)], cost=1000)
            for i in range(2):
                t = 2 * st + i

                def mm(e, i=i, bp=bp):
                    r = None
                    for g in range(2):
                        for c2 in range(2):
                            r = e.matmul(ss_ps[:, g:g + 1], lhsT=ysq[bp][:, 2 * g + c2, i * 128:(i + 1) * 128], rhs=ones_bf[:, 0:1],
                                         start=(c2 == 0), stop=(c2 == 1))
                    return r
                A("pe", mm, reads=[("ysq", bp, c) for c in range(4)] + ["ones_bf"], writes=["sm_ps"], cost=500)
                rstd2(grstd[:, t, :], ss_ps, 256, ["sm_ps"], [("grstd", t)], stg[i][:, 0:2], f"stg{i}")
            A("sp", lambda e, ycl_t=ycl_t, st=st: e.dma_start(
                out=T["ycl_scr"].rearrange("(c p) t -> p c t", p=128)[:, :, st * 256:(st + 1) * 256], in_=ycl_t[:]),
              reads=[ycl_tag + (c,) for c in range(4)], writes=[("ycl_scr", st)], dma_key=("st",) + ycl_tag)

        tiles(0)
        for st in range(nblk):
            if st + 1 < nblk:
                tiles(st + 1)
            fm(st)


def phase_B(nc, P, top, sb0, ps0, l, x_src, T, nblk):
    A = P.add
    sb = lambda es, name, shape, dt: sb0(es, f"{name}_B{l}", shape, dt)
    ps = lambda es, name, shape, dt: ps0(es, f"{name}_B{l}", shape, dt)
    cols_sb, identb, ones_bf, ones_f, grstd = T["cols_sb"], T["identb"], T["ones_bf"], T["ones_f"], T["grstd"]
    rstd_from_ssq = T["rstd_from_ssq"]
    with ExitStack() as es:
        kT = sb(es, "kT_all", [128, 8, S], BF16)
        v_all = sb(es, "v_all", [128, NT, 1024], BF16)
        wo = sb(es, "w_out_sb", [128, 8, D], BF16)
        caus = sb(es, "caus", [128, 2, 256], BF16)
        qT_blks = Rot([sb(es, f"qTb{i}", [128, 8, 256], BF16) for i in range(2)], "qTb")
        ycl_blks = Rot([sb(es, f"yclb{i}", [128, 4, 256], BF16) for i in range(2)], "yclb")
        x_sbs = Rot([sb(es, f"xB{i}", [128, D], F32) for i in range(3)], "xB")
        g1bc = x_sbs.t[2]
        p_sbs = Rot([sb(es, f"pB{i}", [128, 2, 256], BF16) for i in range(4)], "pB")
        rdens = Rot([sb(es, f"rdB{i}", [64, 256], F32) for i in range(2)], "rdB")
        yattn = sb(es, "yattn", [128, 4, 256], F32)
        ya_bf = sb(es, "ya_bf", [128, 4, 256], BF16)
        ysq = sb(es, "ysqB", [128, 4, 256], BF16)
        stb = sb(es, "stb", [128, 8], F32)
        s_pss = Rot([ps(es, f"s_ps{i}", [128, 2, 256], F32) for i in range(3)], "s_ps")
        oT_pss = Rot([ps(es, f"oT_ps{i}", [128, 512], F32) for i in range(2)], "oT_ps")
        sm_ps = ps(es, "smB_ps", [128, 512], F32)
        op_pss = Rot([ps(es, f"op_ps{i}", [128, 512], F32) for i in range(2)], "op_ps")

        for c in range(8):
            A("pool", lambda e, c=c: e.dma_start(out=wo[:, c, :], in_=T["w_out"][l, c * 128:(c + 1) * 128, :]),
              writes=[("wo", c)], dma_key=("w", c))
        A("sp", lambda e: e.dma_start(out=g1bc[:], in_=T["gates_scr"][l, 0:1, :].partition_broadcast(128)), writes=[("xB", 2)], dma_key=("xB", 2))
        A("sp", lambda e: e.dma_start(out=caus[:], in_=T["c_caus"][:, :, :]), writes=["caus"], dma_key="a1")
        for c in range(8):
            eng = "dve"
            A(eng, lambda e, c=c: e.scalar_tensor_tensor(out=wo[:, c, :], in0=wo[:, c, :], scalar=cols_sb[:, l, 48 + c:49 + c], in1=g1bc[:],
                                                         op0=ALU.mult, op1=ALU.mult),
              reads=[("xB", 2)], writes=[("wo", c)], n=D)
        wo_tags = [("wo", c) for c in range(8)]
        for h in range(8):
            A("dve", lambda e, h=h: e.memset(kT[64:128, h, :], 0.0), writes=[("kToh", h)], n=2048)
        for k_, qt_ in enumerate(qT_blks.t):
            A("dve", lambda e, qt_=qt_: e.memset(qt_[:], 0.0), writes=[("qTb", k_)], n=1024)
        for h in range(8):
            A("sp", lambda e, h=h: e.dma_start(out=kT[64:80, h, :], in_=T["c_oh"][:, :]), writes=[("kToh", h)], dma_key=("spk", h))
        oh_tags = [("kToh", h) for h in range(8)]
        for j in range(nblk):
            A("sp", lambda e, j=j: e.dma_start(out=kT[0:64, :, j * 256:(j + 1) * 256], in_=T["kT_scr"][:, :, j * 256:(j + 1) * 256]),
              writes=[("kT", j)], dma_key=("kT", j % 4))
            A("sp", lambda e, j=j: e.dma_start(out=v_all[:, 2 * j:2 * j + 2, :], in_=T["v_scr"][2 * j:2 * j + 2, :, :].rearrange("t p f -> p t f")),
              writes=[("v", j)], dma_key=("v", j % 4))

        for qb in range(nblk):
            qTb, qtag = qT_blks.next()
            yclb, ytag = ycl_blks.next()
            A("sp", lambda e, qTb=qTb, qb=qb: e.dma_start(out=qTb[0:80, :, :], in_=T["qT_scr"][:, :, qb * 256:(qb + 1) * 256]), writes=[qtag], dma_key=qtag)
            A("sp", lambda e, yclb=yclb, qb=qb: e.dma_start(
                out=yclb[:], in_=T["ycl_scr"].rearrange("(c p) t -> p c t", p=128)[:, :, qb * 256:(qb + 1) * 256]), writes=[ytag], dma_key=ytag)
            xts = []
            for i in range(2):
                t = 2 * qb + i
                xt, xtag = x_sbs.next()
                A("sp", lambda e, xt=xt, t=t: e.dma_start(out=xt[:], in_=x_src[t * 128:(t + 1) * 128, :]), writes=[xtag], dma_key=xtag)
                xts.append((xt, xtag))
            for h in range(8):
                oT, otag = oT_pss.next()
                for j in range(qb + 1):
                    own = (j == qb)
                    sp_, stag = s_pss.next()
                    if not own:
                        def mm(e, sp_=sp_, j=j, h=h, qTb=qTb):
                            r = None
                            for kk in range(2):
                                r = e.matmul(sp_[:, kk, :], lhsT=kT[:, h, (2 * j + kk) * 128:(2 * j + kk + 1) * 128], rhs=qTb[:, h, :],
                                             start=True, stop=True)
                            return r
                        A("pe", mm, reads=[("kT", j), ("kToh", h), qtag], writes=[stag])
                    else:
                        def mm(e, sp_=sp_, j=j, h=h, qTb=qTb):
                            r = None
                            for kk in range(2):
                                e.matmul(sp_[:, kk, :], lhsT=kT[0:64, h, (2 * j + kk) * 128:(2 * j + kk + 1) * 128], rhs=qTb[0:64, h, :],
                                         start=True, stop=False)
                                r = e.matmul(sp_[:, kk, :], lhsT=identb[:], rhs=caus[:, kk, :], start=False, stop=True)
                            return r
                        A("pe", mm, reads=[("kT", j), qtag, "caus", "identb"], writes=[stag])
                    pt, ptag = p_sbs.next()
                    A("act", lambda e, pt=pt, sp_=sp_: e.activation(out=pt[:], in_=sp_[:], func=AF.Exp), reads=[stag], writes=[ptag], cost=600)

                    def mm(e, pt=pt, oT=oT, j=j, h=h, qb=qb):
                        r = None
                        for kk in range(2):
                            r = e.matmul(oT[:, 0:256], lhsT=v_all[:, 2 * j + kk, h * 128:(h + 1) * 128], rhs=pt[:, kk, :],
                                         start=(j == 0 and kk == 0), stop=(j == qb and kk == 1))
                        return r
                    A("pe", mm, reads=[("v", j), ptag], writes=[otag], cost=280)
                rd, rdtag = rdens.next()
                A("dve", lambda e, rd=rd, oT=oT: e.reciprocal(out=rd[:], in_=oT[64:128, 0:256]), reads=[otag], writes=[rdtag], cost=1900)
                pb = (h % 2) * 64
                A("dve", lambda e, rd=rd, oT=oT, pb=pb, h=h: e.tensor_tensor(out=yattn[pb:pb + 64, h // 2, :], in0=oT[0:64, 0:256], in1=rd[:], op=ALU.mult),
                  reads=[otag, rdtag], writes=[("yattn", h)])
            ya_tags = [("yattn", h) for h in range(8)]
            A("pool", lambda e: e.tensor_copy(out=ya_bf[:], in_=yattn[:]), reads=ya_tags, writes=["ya_bf"])
            A("act", lambda e: e.activation(out=ysq[:], in_=yattn[:], func=AF.Square), reads=ya_tags, writes=["ysqB"])
            for i in range(2):
                t = 2 * qb + i
                xt, xtag = xts[i]

                def mm(e, i=i):
                    r = None
                    for c in range(4):
                        r = e.matmul(sm_ps[:, 0:1], lhsT=ysq[:, c, i * 128:(i + 1) * 128], rhs=ones_bf[:, 0:1], start=(c == 0), stop=(c == 3))
                    return r
                A("pe", mm, reads=["ysqB", "ones_bf"], writes=["smB_ps"])
                rstd_from_ssq(stb[:, 2:3], sm_ps[:, 0:1], 512, ["smB_ps"], ["stbr"], stb[:, 1:2], "stbt")
                for n in range(2):
                    nsl = slice(n * 512, (n + 1) * 512)
                    for g in range(3):
                        op_, optag = op_pss.next()
                        if g == 0:
                            srcs = [(ya_bf, c, c) for c in range(4)]
                            rtags = ["ya_bf"]
                            scal = stb[:, 2:3]
                            stag2 = ["stbr"]
                        else:
                            srcs = [(yclb, 2 * (g - 1) + c2, 4 + 2 * (g - 1) + c2) for c2 in range(2)]
                            rtags = [ytag]
                            scal = grstd[:, t, g - 1:g]
                            stag2 = []

                        def mm(e, op_=op_, srcs=srcs, i=i, nsl=nsl):
                            r = None
                            for k, (src, sc, wc) in enumerate(srcs):
                                r = e.matmul(op_[:], lhsT=src[:, sc, i * 128:(i + 1) * 128], rhs=wo[:, wc, nsl], start=(k == 0), stop=(k == len(srcs) - 1))
                            return r
                        A("pe", mm, reads=rtags + wo_tags, writes=[optag])
                        A("dve", lambda e, op_=op_, xt=xt, scal=scal, nsl=nsl: e.scalar_tensor_tensor(
                            out=xt[:, nsl], in0=op_[:], scalar=scal, in1=xt[:, nsl], op0=ALU.mult, op1=ALU.add),
                          reads=[optag, xtag] + stag2, writes=[xtag])
                A("sp", lambda e, xt=xt, t=t: e.dma_start(out=T["x1_scr"][t * 128:(t + 1) * 128, :], in_=xt[:]),
                  reads=[xtag], writes=[("x1_scr", t)], dma_key=("st",) + xtag)


def phase_C(nc, P, top, sb0, ps0, l, x_dst, T, nblk):
    A = P.add
    sb = lambda es, name, shape, dt: sb0(es, f"{name}_C{l}", shape, dt)
    ps = lambda es, name, shape, dt: ps0(es, f"{name}_C{l}", shape, dt)
    eff_sb, shb_sb, identb = T["eff_sb"], T["shb_sb"], T["identb"]
    rstd_from_ssq = T["rstd_from_ssq"]
    with ExitStack() as es:
        wu = sb(es, "w_up_sb", [128, 8, DFF], BF16)
        wd = sb(es, "w_dn_sb", [128, 32, D], BF16)
        bup = sb(es, "bup", [128, 32], F32)
        x_sbs = Rot([sb(es, f"xC{i}", [128, D], F32) for i in range(3)], "xC")
        junk = sb(es, "junkC", [128, D], BF16)
        xn2 = [sb(es, f"xnC{i}", [128, D], BF16) for i in range(2)]
        xnT2 = [sb(es, f"xnTC{i}", [128, 8, 256], BF16) for i in range(2)]
        hT2 = [sb(es, f"hT{i}", [128, 32, 256], BF16) for i in range(2)]
        rts = Rot([sb(es, f"rt{i}", [128, 256], BF16) for i in range(3)], "rt")
        stc = sb(es, "stc", [128, 8], F32)
        tp_ps = ps(es, "tpC_ps", [128, 8, 128], BF16)
        up_pss = Rot([ps(es, f"up_ps{i}", [128, 512], F32) for i in range(3)], "up_ps")
        dn_pss = Rot([ps(es, f"dn_ps{i}", [128, 512], F32) for i in range(2)], "dn_ps")
        sm_ps = ps(es, "smC_ps", [128, 512], F32)

        for c in range(8):
            A("pool", lambda e, c=c: e.dma_start(out=wu[:, c, :], in_=T["w_up"][l, c * 128:(c + 1) * 128, :], max_dma_last_dim=4096),
              writes=[("wu", c)], dma_key=("w", c))
        for f in range(32):
            A("pool", lambda e, f=f: e.dma_start(out=wd[:, f, :], in_=T["w_down"][l, f * 128:(f + 1) * 128, :]),
              writes=[("wd", f)], dma_key=("wdk", f % 8))
        g2t, g2tag = x_sbs.t[1], ("xC", 1)
        A("sp", lambda e: e.dma_start(out=g2t[:], in_=T["gates_scr"][l, 1:2, :].partition_broadcast(128)), writes=[g2tag], dma_key=g2tag)
        wu_tags = [("wu", c) for c in range(8)]
        wd_tags = [("wd", f) for f in range(32)]

        def mm(e):
            r = None
            for f in range(32):
                for c in range(8):
                    r = e.matmul(sm_ps[:, f:f + 1], lhsT=wu[:, c, f * 128:(f + 1) * 128], rhs=shb_sb[:, l, 1, c:c + 1], start=(c == 0), stop=(c == 7))
            return r
        A("pe", mm, reads=wu_tags, writes=["smC_ps"])
        A("dve", lambda e: e.tensor_copy(out=bup[:], in_=sm_ps[:, 0:32]), reads=["smC_ps"], writes=["bup"])
        for c in range(8):
            A("act", lambda e, c=c: e.activation(out=wu[:, c, :], in_=wu[:, c, :], func=AF.Copy, scale=eff_sb[:, l, 16 + c:17 + c]),
              reads=[], writes=[("wu", c)], n=DFF)
        for f in range(32):
            eng = "pool" if f % 4 == 3 else "dve"
            A(eng, lambda e, f=f: e.tensor_tensor(out=wd[:, f, :], in0=wd[:, f, :], in1=g2t[:], op=ALU.mult), reads=[g2tag], writes=[("wd", f)], n=D)

        for st in range(nblk):
            xts = []
            bp = st % 2
            xnT, hT = xnT2[bp], hT2[bp]
            for i in range(2):
                xn = xn2[i]
                t = 2 * st + i
                xt, xtag = x_sbs.next()
                xts.append((xt, xtag))
                A("sp", lambda e, xt=xt, t=t: e.dma_start(out=xt[:], in_=T["x1_scr"][t * 128:(t + 1) * 128, :]), writes=[xtag], dma_key=xtag)
                A("act", lambda e, xt=xt: e.activation(out=junk[:], in_=xt[:], func=AF.Square, accum_out=stc[:, 0:1]),
                  reads=[xtag], writes=["junkC", "stc"])
                rstd_from_ssq(stc[:, 2:3], stc[:, 0:1], D, ["stc"], ["stcr"], stc[:, 1:2], "stct")
                A("dve", lambda e, xt=xt, xn=xn: e.tensor_scalar(out=xn[:], in0=xt[:], scalar1=stc[:, 2:3], scalar2=None, op0=ALU.mult),
                  reads=[xtag, "stcr"], writes=[f"xnC{i}"], n=D)

                def tr(e, xn=xn):
                    r = None
                    for c in range(8):
                        r = e.transpose(out=tp_ps[:, c, :], in_=xn[:, c * 128:(c + 1) * 128], identity=identb[:])
                    return r
                A("pe", tr, reads=[f"xnC{i}", "identb"], writes=["tpC_ps"])
                A("act", lambda e, i=i, xnT=xnT: e.copy(out=xnT[:, :, i * 128:(i + 1) * 128], in_=tp_ps[:]), reads=["tpC_ps"], writes=[("xnTC", bp, i)], n=D)
            for f in range(32):
                up, uptag = up_pss.next()

                def mm(e, up=up, f=f, xnT=xnT):
                    r = None
                    for c in range(8):
                        r = e.matmul(up[:, 0:256], lhsT=wu[:, c, f * 128:(f + 1) * 128], rhs=xnT[:, c, :], start=(c == 0), stop=(c == 7))
                    return r
                A("pe", mm, reads=wu_tags + [("xnTC", bp, 0), ("xnTC", bp, 1)], writes=[uptag], cost=1000)
                rt, rttag = rts.next()
                A("dve", lambda e, up=up, rt=rt, f=f: e.tensor_scalar(out=rt[:], in0=up[:, 0:256], scalar1=bup[:, f:f + 1], scalar2=0.0, op0=ALU.add, op1=ALU.max),
                  reads=[uptag, "bup"], writes=[rttag])
                A("act", lambda e, rt=rt, f=f, hT=hT: e.activation(out=hT[:, f, :], in_=rt[:], func=AF.Square), reads=[rttag], writes=[("hT", bp, f)])
            hT_tags = [("hT", bp, f) for f in range(32)]
            for i in range(2):
                t = 2 * st + i
                xt, xtag = xts[i]
                for n in range(2):
                    nsl = slice(n * 512, (n + 1) * 512)
                    dn, dntag = dn_pss.next()

                    def mm(e, dn=dn, i=i, nsl=nsl, hT=hT):
                        r = None
                        for f in range(32):
                            r = e.matmul(dn[:], lhsT=hT[:, f, i * 128:(i + 1) * 128], rhs=wd[:, f, nsl], start=(f == 0), stop=(f == 31))
                        return r
                    A("pe", mm, reads=hT_tags + wd_tags, writes=[dntag], cost=7200)
                    A("dve", lambda e, dn=dn, xt=xt, nsl=nsl: e.tensor_tensor(out=xt[:, nsl], in0=dn[:], in1=xt[:, nsl], op=ALU.add),
                      reads=[dntag, xtag], writes=[xtag])
                A("sp", lambda e, xt=xt, t=t: e.dma_start(out=x_dst[t * 128:(t + 1) * 128, :], in_=xt[:]),
                  reads=[xtag], writes=[("x_dst", t)], dma_key=("st",) + xtag)


def _consts():
    bf = ml_dtypes.bfloat16
    identb = np.eye(128, dtype=np.float32).astype(bf)
    identf = np.eye(128, dtype=np.float32)
    oh = np.zeros((16, S), np.float32)
    for j in range(16):
        oh[j, j * 256:(j + 1) * 256] = 1.0
    negmask = np.zeros((16, 16), np.float32)
    for own in range(16):
        negmask[own, own:] = -1e30
    kk = np.arange(128)[:, None]
    qq = np.arange(128)[None, :]
    tri = np.where(kk <= qq, 0.0, -BIG).astype(np.float32)
    caus = np.zeros((128, 2, 256), np.float32)
    caus[:, 0, 0:128] = tri
    caus[:, 1, 0:128] = -BIG
    caus[:, 1, 128:256] = tri
    return dict(c_identb=identb, c_identf=identf, c_oh=oh.astype(bf), c_negmask=negmask.reshape(1, 256),
                c_caus=caus.astype(bf))


def _col(v):
    v = np.asarray(v, np.float32)
    return np.ascontiguousarray(v.reshape(-1, 128).T)


def make_in_maps(inputs, n_cores=8):
    f = lambda k: np.ascontiguousarray(np.asarray(inputs[k], np.float32))
    cols = np.zeros((2, 128, NCOLS), np.float32)
    for l in range(2):
        b = f("b_ada")[l]
        parts = [_col(f("ln1_g")[l]), _col(f("ln2_g")[l]),
                 _col(b[0:1024]), _col(b[1024:2048]), _col(b[3072:4096]), _col(b[4096:5120]),
                 _col(f("mix_norm_g")[l])]
        scw = f("sc_w")[l]
        parts.append(np.concatenate([np.stack([scw[k, cc * 128:(cc + 1) * 128] for k in range(3)], 1) for cc in range(2)], 1))
        lcw = f("lru_conv_w")[l]
        parts.append(np.concatenate([np.stack([lcw[k, cc * 128:(cc + 1) * 128] for k in range(4)], 1) for cc in range(2)], 1))
        parts += [_col(f("lru_conv_b")[l]), _col(f("lru_ba")[l]), _col(f("lru_bx")[l]), _col(f("lru_lambda")[l])]
        cols[l] = np.concatenate(parts, 1)
    shared = dict(cols=cols, b_ada=f("b_ada"), w_ada=f("w_ada"), w_in=f("w_in"), q_norm_g=f("q_norm_g"), k_norm_g=f("k_norm_g"),
                  lru_wa=f("lru_wa"), lru_wx=f("lru_wx"), w_out=f("w_out"), w_up=f("w_up"), w_down=f("w_down"))
    shared.update(_consts())
    x = f("x")
    c = f("c")
    maps = []
    for b in range(n_cores):
        m = dict(shared)
        m["x"] = x[b]
        m["ccol"] = _col(c[b])
        maps.append(m)
    return maps


_NC = None


def kernel(**inputs):
    global _NC
    if _NC is None:
        _NC = build_program()[0]
    maps = make_in_maps(inputs)
    res = run_bass_kernel_spmd(_NC, maps, core_ids=list(range(8)))
    return np.stack([np.asarray(r["out"], np.float32) for r in res.results], 0)
```

```python
import numpy as np
import ml_dtypes
from contextlib import ExitStack
import concourse.bass as bass
import concourse.mybir as mybir
from concourse.bass_utils import run_bass_kernel_spmd

F32 = mybir.dt.float32
BF16 = mybir.dt.bfloat16
AF = mybir.ActivationFunctionType
ALU = mybir.AluOpType
AX = mybir.AxisListType

S = 4096
D = 1024
NT = 32
NB = 16
DIN = 2816
DFF = 4096
BIG = 30000.0
EPS = 1e-6
NCOLS = 78
ENGS = ("pe", "act", "dve", "pool", "sp")
import os as _osg
FP32_GUARD = _osg.environ.get("FP32_GUARD", "1") == "1"


class Prog:
    def __init__(self, nc, strict=True):
        self.nc = nc
        self.strict = strict
        self.ops = []
        self.last_w = {}
        self.readers = {}
        self.last_dma = {}
        self.keymap = {}
        self.nosched = False
        self.phase = 0

    def add(self, eng, fn, reads=(), writes=(), dma_key=None, n=256, cost=None):
        if cost is None:
            if dma_key is not None:
                cost = 3000.0
            elif eng == "pe":
                cost = 400.0
            elif eng == "act":
                cost = 320.0 + n / 1.4
            elif eng == "dve":
                cost = 250.0 + n / 0.96
            else:
                cost = 300.0 + n / 0.5
        if dma_key is not None:
            cls = "W" if eng == "pool" else "H"
            kk = (cls, dma_key)
            if kk not in self.keymap:
                self.keymap[kk] = (cls, sum(1 for q in self.keymap if q[0] == cls))
            dma_key = self.keymap[kk]
        i = len(self.ops)
        deps = set()
        for t in reads:
            if t in self.last_w:
                deps.add(self.last_w[t])
        for t in writes:
            if t in self.last_w:
                deps.add(self.last_w[t])
            for r in self.readers.get(t, ()):
                deps.add(r)
        if dma_key is not None:
            if dma_key in self.last_dma:
                deps.add(self.last_dma[dma_key])
            self.last_dma[dma_key] = i
        deps.discard(i)
        for t in reads:
            self.readers.setdefault(t, []).append(i)
        for t in writes:
            self.last_w[t] = i
            self.readers[t] = []
        self.ops.append(dict(eng=eng, fn=fn, deps=sorted(deps), dma_key=dma_key,
                             phase=self.phase, barrier=False, cost=float(cost), nosched=self.nosched))
        return i

    def barrier(self):
        self.ops.append(dict(eng=None, fn=None, deps=[], dma_key=None,
                             phase=self.phase, barrier=True))
        self.phase += 1
        self.last_w = {}
        self.readers = {}
        self.last_dma = {}
        self.keymap = {}

    def schedule(self, window=48):
        ops = self.ops
        n = len(ops)
        order = []
        start = 0
        while start < n:
            end = start
            while end < n and not ops[end]["barrier"]:
                end += 1
            ids = list(range(start, end))
            import os as _os2
            sp_ = _os2.environ.get("SCHED_PHASES")
            if ids and sp_ is not None and str(ops[ids[0]]["phase"]) not in sp_.split(","):
                order.extend(ids)
            elif ids:
                fz = set(_os2.environ.get("SCHED_FREEZE", "").split(","))
                if ops[ids[0]].get("nosched"):
                    fz |= set(_os2.environ.get("PHASEA_FREEZE", "").split(","))
                order.extend(self._sched_phase(ids, window, fz))
            if end < n:
                order.append(end)
            start = end + 1
        remap = {old: new for new, old in enumerate(order)}
        newops = []
        for old in order:
            o = ops[old]
            o["deps"] = sorted(remap[d] for d in o["deps"])
            newops.append(o)
        self.ops = newops

    def _sched_phase(self, ids, window, freeze=()):
        ops = self.ops
        self._freeze = set(freeze)
        idset = set(ids)
        queues = {e: [i for i in ids if ops[i]["eng"] == e] for e in ENGS}
        qpos = {e: 0 for e in ENGS}
        scheduled = {}
        eng_free = {e: 0.0 for e in ENGS}
        out = []
        remaining = len(ids)
        taken = set()
        while remaining:
            best = None
            for e in ENGS:
                q = queues[e]
                p = qpos[e]
                while p < len(q) and q[p] in taken:
                    p += 1
                qpos[e] = p
                cnt = 0
                k = p
                win = 1 if e in self._freeze else window
                while k < len(q) and cnt < win:
                    i = q[k]
                    k += 1
                    if i in taken:
                        continue
                    cnt += 1
                    ok = True
                    rt = 0.0
                    for d in ops[i]["deps"]:
                        if d in idset:
                            if d not in scheduled:
                                ok = False
                                break
                            if scheduled[d] > rt:
                                rt = scheduled[d]
                    if not ok:
                        continue
                    st = max(eng_free[e], rt)
                    if best is None or st < best[0] - 1e-9 or (abs(st - best[0]) <= 1e-9 and i < best[1]):
                        best = (st, i, e)
                    if rt <= eng_free[e]:
                        break
            assert best is not None, "scheduler deadlock"
            st, i, e = best
            o = ops[i]
            if o["dma_key"] is not None:
                eng_free[e] = st + 120.0
                scheduled[i] = st + o["cost"]
            else:
                eng_free[e] = st + o["cost"]
                scheduled[i] = st + o["cost"] + 150.0
            taken.add(i)
            out.append(i)
            remaining -= 1
        self.est_ns = getattr(self, "est_ns", 0.0) + max(scheduled.values())
        self.est_phase = getattr(self, "est_phase", []) + [(ops[ids[0]]["phase"], max(scheduled.values()), dict(eng_free))]
        return out

    def emit(self):
        nc = self.nc
        import os as _os1
        if _os1.environ.get("SCHED", "1") == "1":
            self.schedule()
        ops = self.ops
        nph = self.phase + 1
        strict = self.strict

        def same_eng_free(od, o):
            return od["eng"] == o["eng"] and o["dma_key"] is None and (od["eng"] == "pe" or not strict)

        need = [False] * len(ops)
        for i, o in enumerate(ops):
            if o["barrier"]:
                continue
            for d in o["deps"]:
                od = ops[d]
                if od["dma_key"] is not None:
                    continue
                if same_eng_free(od, o):
                    continue
                need[d] = True
        last_in_phase = {}
        for i, o in enumerate(ops):
            if o["barrier"] or o["dma_key"] is not None:
                continue
            last_in_phase[(o["eng"], o["phase"])] = i
        for i in last_in_phase.values():
            need[i] = True
        cnt = {}
        dcnt = {}
        val = [None] * len(ops)
        dma_keys = []
        for i, o in enumerate(ops):
            if o["barrier"]:
                continue
            if o["dma_key"] is not None:
                k = o["dma_key"]
                if k not in dcnt:
                    dcnt[k] = 0
                    dma_keys.append(k)
                dcnt[k] += 16
                val[i] = dcnt[k]
            elif need[i]:
                k = (o["eng"], o["phase"])
                cnt[k] = cnt.get(k, 0) + 1
                val[i] = cnt[k]
        esem = {}
        for (e, ph) in sorted(cnt.keys(), key=lambda t: (t[1], t[0])):
            esem[(e, ph)] = nc.alloc_semaphore(f"s_{e}_{ph}")
        dsem = {k: nc.alloc_semaphore(f"d_{j}") for j, k in enumerate(dma_keys)}
        self.n_sems = len(esem) + len(dsem)
        final_cnt = dict(cnt)
        per = {e: [] for e in ENGS}
        for i, o in enumerate(ops):
            if o["barrier"]:
                for e in ENGS:
                    per[e].append(i)
            else:
                per[o["eng"]].append(i)
        dma_upto = {}
        run = {}
        for i, o in enumerate(ops):
            if o["barrier"]:
                dma_upto[i] = dict(run)
            elif o["dma_key"] is not None:
                run[o["dma_key"]] = val[i]
        dma_final = dict(run)

        def gen(e):
            def body(eng):
                waited = {}

                def w(sem, name, v):
                    if waited.get(name, 0) >= v:
                        return
                    waited[name] = v
                    eng.wait_ge(sem, v)

                for i in per[e]:
                    o = ops[i]
                    if o["barrier"]:
                        ph = o["phase"]
                        for e2 in ("pe", "act", "dve", "pool"):
                            v = final_cnt.get((e2, ph), 0)
                            if v:
                                w(esem[(e2, ph)], (e2, ph), v)
                        for k, v in dma_upto[i].items():
                            w(dsem[k], k, v)
                        continue
                    for d in o["deps"]:
                        od = ops[d]
                        if od["dma_key"] is not None:
                            w(dsem[od["dma_key"]], od["dma_key"], val[d])
                        else:
                            if same_eng_free(od, o):
                                continue
                            k = (od["eng"], od["phase"])
                            w(esem[k], k, val[d])
                    inst = o["fn"](eng)
                    if o["dma_key"] is not None:
                        inst.then_inc(dsem[o["dma_key"]], 16)
                    elif need[i]:
                        inst.then_inc(esem[(e, o["phase"])], 1)
                if e == "sp":
                    for k, v in dma_final.items():
                        w(dsem[k], k, v)
                    for (e2, ph), v in final_cnt.items():
                        w(esem[(e2, ph)], (e2, ph), v)
            return body

        with nc.Block() as block:
            block.tensor(gen("pe"))
            block.scalar(gen("act"))
            block.vector(gen("dve"))
            block.gpsimd(gen("pool"))
            block.sync(gen("sp"))


class Rot:
    def __init__(self, tensors, name):
        self.t = tensors
        self.name = name
        self.i = 0

    def next(self):
        k = self.i % len(self.t)
        self.i += 1
        return self.t[k], (self.name, k)


def build_program(n_layers=2, phases="ABC", debug=False, nblk=NB):
    nc = bass.Bass("TRN2", target_bir_lowering=False)
    dbg_kind = "ExternalOutput" if debug else "Internal"

    def din(name, shape, dt=F32):
        return nc.dram_tensor(name, list(shape), dt, kind="ExternalInput").ap()

    def dscr(name, shape, dt=F32):
        return nc.dram_tensor(name, list(shape), dt, kind=dbg_kind).ap()

    x_in = din("x", [S, D])
    ccol = din("ccol", [128, 8])
    cols = din("cols", [2, 128, NCOLS])
    bada = din("b_ada", [2, 6 * D])
    w_ada = din("w_ada", [2, D, 6 * D])
    w_in = din("w_in", [2, D, DIN])
    qng = din("q_norm_g", [2, 64])
    kng = din("k_norm_g", [2, 64])
    lru_wa = din("lru_wa", [2, 4, 64, 64])
    lru_wx = din("lru_wx", [2, 4, 64, 64])
    w_out = din("w_out", [2, D, D])
    w_up = din("w_up", [2, D, DFF])
    w_down = din("w_down", [2, DFF, D])
    c_identb = din("c_identb", [128, 128], BF16)
    c_identf = din("c_identf", [128, 128], F32)
    c_oh = din("c_oh", [16, S], BF16)
    c_negmask = din("c_negmask", [1, 256], F32)
    c_caus = din("c_caus", [128, 2, 256], BF16)
    out = nc.dram_tensor("out", [S, D], F32, kind="ExternalOutput").ap()

    gates_scr = dscr("gates_scr", [2, 2, D])
    qT_scr = dscr("qT_scr", [80, 8, S], BF16)
    kT_scr = dscr("kT_scr", [64, 8, S], BF16)
    v_scr = dscr("v_scr", [NT, 128, 1024], BF16)
    ycl_scr = dscr("ycl_scr", [512, S], BF16)
    x1_scr = dscr("x1_scr", [S, D])
    xmid_scr = dscr("xmid_scr", [S, D])
    dbg_grstd = dscr("dbg_grstd", [128, NT * 2]) if debug else None

    import os as _os0
    P = Prog(nc, strict=(_os0.environ.get('STRICT', '1') == '1'))
    A = P.add

    with ExitStack() as top:
        def sb(es, name, shape, dt):
            return es.enter_context(nc.sbuf_tensor(name, list(shape), dt))

        def ps(es, name, shape, dt):
            return es.enter_context(nc.psum_tensor(name, list(shape), dt))

        cols_sb = sb(top, "cols_sb", [128, 2, NCOLS], F32)
        eff_sb = sb(top, "eff_sb", [128, 2, 32], F32)
        shb_sb = sb(top, "shb_sb", [128, 2, 2, 8], BF16)
        lruc_sb = sb(top, "lruc_sb", [128, 2, 8], F32)
        identb = sb(top, "identb", [128, 128], BF16)
        identf = sb(top, "identf", [128, 128], F32)
        ones_bf = sb(top, "ones_bf", [128, 128], BF16)
        ones_f = sb(top, "ones_f", [128, 128], F32)
        grstd = sb(top, "grstd", [128, NT, 2], F32)
        epsc = sb(top, "epsc", [128, 1], F32)

        A("sp", lambda e: e.dma_start(out=identb[:], in_=c_identb[:, :]), writes=["identb"], dma_key="c0")
        A("sp", lambda e: e.dma_start(out=identf[:], in_=c_identf[:, :]), writes=["identf"], dma_key="c1")
        A("sp", lambda e: e.dma_start(out=cols_sb[:], in_=cols.rearrange("l p n -> p l n")), writes=["cols"], dma_key="c2")
        A("pool", lambda e: e.memset(ones_bf[:], 1.0), writes=["ones_bf"])
        A("pool", lambda e: e.memset(ones_f[:], 1.0), writes=["ones_f"])
        A("pool", lambda e: e.memset(epsc[:], EPS), writes=["epsc"])
        if debug:
            A("pool", lambda e: e.memset(grstd[:], 0.0), writes=["grstd_init"])

        def rstd_from_ssq(dst, ssq, n, tags_r, tags_w, tmp, tmptag):
            A("dve", lambda e: e.tensor_scalar(out=tmp, in0=ssq, scalar1=1.0 / n, scalar2=EPS, op0=ALU.mult, op1=ALU.add),
              reads=tags_r, writes=[tmptag])
            A("act", lambda e: e.activation(out=tmp, in_=tmp, func=AF.Ln), reads=[tmptag], writes=[tmptag])
            A("act", lambda e: e.activation(out=dst, in_=tmp, func=AF.Exp, scale=-0.5), reads=[tmptag], writes=tags_w)

        cact = sb(top, "cact", [128, 8], BF16)

        def make_mod_gen(sbf, psf, l, BW=256):
            wa = [sbf(f"wa{i}", [128, 8, BW], BF16) for i in range(2)]
            modc = sbf("modc", [128, 32], F32)
            grow = sbf("grow", [1, 2, D], F32)
            brow = sbf("brow", [1, 2, D], F32)
            lam = sbf("lam", [128, 2], F32)
            bank = psf("mod_ps", [128, 512], F32)
            mod_ps = bank[:, 0:32]
            g_ps = bank[0:1, 256:256 + BW]
            per = D // BW

            def gen():
                for g_ in range(2):
                    A("sp", lambda e, g_=g_: e.dma_start(out=brow[:, g_, :], in_=bada[l:l + 1, (2 + 3 * g_) * D:(3 + 3 * g_) * D]),
                      writes=[("brow", g_)], dma_key=("c4", g_))
                for blk in range(6 * per):
                    sec, off = blk // per, (blk % per) * BW
                    wt, wtag = wa[blk % 2], ("wa", blk % 2)
                    A("pool", lambda e, wt=wt, blk=blk: e.dma_start(
                        out=wt[:], in_=w_ada[l, :, blk * BW:(blk + 1) * BW].rearrange("(c p) n -> p c n", p=128)),
                      writes=[wtag], dma_key=("wa", blk % 2), cost=6000)
                    if sec in (2, 5):
                        g = 0 if sec == 2 else 1

                        def mm(e, wt=wt):
                            r = None
                            for c in range(8):
                                r = e.matmul(g_ps, lhsT=cact[:, c:c + 1], rhs=wt[:, c, :], start=(c == 0), stop=(c == 7))
                            return r
                        A("pe", mm, reads=[wtag, "cact"], writes=["modbank"], cost=1000)
                        A("dve", lambda e, g=g, off=off: e.tensor_tensor(
                            out=grow[:, g, off:off + BW], in0=g_ps, in1=brow[:, g, off:off + BW], op=ALU.add),
                          reads=["modbank", ("brow", g)], writes=[("grow", g, blk % per)])
                        if blk % per == per - 1:
                            A("sp", lambda e, g=g: e.dma_start(out=gates_scr[l, g:g + 1, :], in_=grow[:, g, :]),
                              reads=[("grow", g, q) for q in range(per)], dma_key=("grow", g))
                    else:
                        base = {0: 0, 1: 8, 3: 16, 4: 24}[sec] + off // 128

                        def mm(e, wt=wt, base=base):
                            r = None
                            for sub in range(BW // 128):
                                for c in range(8):
                                    r = e.matmul(mod_ps[:, base + sub:base + sub + 1], lhsT=wt[:, c, sub * 128:(sub + 1) * 128],
                                                 rhs=cact[:, c:c + 1], start=(c == 0), stop=(c == 7))
                            return r
                        A("pe", mm, reads=[wtag, "cact"], writes=["modbank"], cost=1000)
                    yield
                A("dve", lambda e: e.tensor_tensor(out=modc[:], in0=mod_ps, in1=cols_sb[:, l, 16:48], op=ALU.add),
                  reads=["modbank", "cols"], writes=["modc"])
                for k2 in range(2):
                    A("dve", lambda e, k2=k2: e.scalar_tensor_tensor(
                        out=eff_sb[:, l, 16 * k2:16 * k2 + 8], in0=modc[:, 16 * k2 + 8:16 * k2 + 16], scalar=1.0,
                        in1=cols_sb[:, l, 8 * k2:8 * k2 + 8], op0=ALU.add, op1=ALU.mult),
                      reads=["modc", "cols"], writes=[("eff", l, k2, 0)])
                    A("dve", lambda e, k2=k2: e.tensor_copy(out=eff_sb[:, l, 16 * k2 + 8:16 * k2 + 16], in_=modc[:, 16 * k2:16 * k2 + 8]),
                      reads=["modc"], writes=[("eff", l, k2, 1)])
                    A("dve", lambda e, k2=k2: e.tensor_copy(out=shb_sb[:, l, k2, :], in_=modc[:, 16 * k2:16 * k2 + 8]),
                      reads=["modc"], writes=[("shb", l, k2)])
                A("dve", lambda e: e.tensor_scalar(out=lruc_sb[:, l, 0:4], in0=cols_sb[:, l, 72:76], scalar1=-1.0, scalar2=None, op0=ALU.mult),
                  reads=["cols"], writes=[("lruc", l, 0)])
                A("act", lambda e: e.activation(out=lam[:], in_=cols_sb[:, l, 76:78], func=AF.Exp, scale=-1.0),
                  reads=["cols"], writes=[("lam", l)])
                A("act", lambda e: e.activation(out=lam[:], in_=lam[:], func=AF.Ln, bias=1.0),
                  reads=[("lam", l)], writes=[("lam", l)])
                A("dve", lambda e: e.tensor_scalar(out=lruc_sb[:, l, 4:6], in0=lam[:], scalar1=-8.0, scalar2=None, op0=ALU.mult),
                  reads=[("lam", l)], writes=[("lruc", l, 1)])
                yield
            return gen()

        with ExitStack() as es:
            cc = sb(es, "cc", [128, 8], F32)
            ce = sb(es, "ce", [128, 8], F32)
            A("sp", lambda e: e.dma_start(out=cc[:], in_=ccol[:, :]), writes=["cc"], dma_key="c3")
            A("act", lambda e: e.activation(out=ce[:], in_=cc[:], func=AF.Exp, scale=-1.0), reads=["cc"], writes=["ce"])
            A("dve", lambda e: e.tensor_scalar(out=ce[:], in0=ce[:], scalar1=1.0, scalar2=None, op0=ALU.add), reads=["ce"], writes=["ce"])
            A("dve", lambda e: e.reciprocal(out=ce[:], in_=ce[:]), reads=["ce"], writes=["ce"])
            A("dve", lambda e: e.tensor_tensor(out=cact[:], in0=ce[:], in1=cc[:], op=ALU.mult), reads=["ce", "cc"], writes=["cact"])
            overlap_mod = (n_layers == 2 and "A" in phases)
            for l0 in range(1 if overlap_mod else n_layers):
                for _ in make_mod_gen(lambda n_, s_, d_: sb(es, f"{n_}_m{l0}", s_, d_), lambda n_, s_, d_: ps(es, f"{n_}_m{l0}", s_, d_), l0):
                    pass
            P.barrier()
        mod_factory = (lambda sbf, psf: make_mod_gen(sbf, psf, 1)) if overlap_mod else None

        for l in range(n_layers):
            x_src = x_in if l == 0 else xmid_scr
            x_dst = xmid_scr if l < n_layers - 1 else out
            if l >= 2:
                x_dst = out
            if "A" in phases:
                P.nosched = True
                phase_A(nc, P, top, sb, ps, l, x_src, dict(
                    cols_sb=cols_sb, eff_sb=eff_sb, shb_sb=shb_sb, lruc_sb=lruc_sb, identb=identb, identf=identf,
                    ones_bf=ones_bf, ones_f=ones_f, grstd=grstd, w_in=w_in, qng=qng, kng=kng, lru_wa=lru_wa,
                    lru_wx=lru_wx, c_negmask=c_negmask, qT_scr=qT_scr, kT_scr=kT_scr, v_scr=v_scr, ycl_scr=ycl_scr,
                    rstd_from_ssq=rstd_from_ssq, dbg_grstd=dbg_grstd, epsc=epsc, mod_factory=(mod_factory if l == 0 else None)), nblk)
                P.barrier()
                P.nosched = False
            TT = dict(cols_sb=cols_sb, eff_sb=eff_sb, shb_sb=shb_sb, identb=identb, identf=identf, ones_bf=ones_bf, ones_f=ones_f,
                      grstd=grstd, w_out=w_out, w_up=w_up, w_down=w_down, gates_scr=gates_scr, c_oh=c_oh, c_caus=c_caus,
                      qT_scr=qT_scr, kT_scr=kT_scr, v_scr=v_scr, ycl_scr=ycl_scr, x1_scr=x1_scr, rstd_from_ssq=rstd_from_ssq)
            if "B" in phases:
                phase_B(nc, P, top, sb, ps, l, x_src, TT, nblk)
                P.barrier()
            if "C" in phases:
                phase_C(nc, P, top, sb, ps, l, x_dst, TT, nblk)
                P.barrier()
        P.emit()
    return nc, P


def phase_A(nc, P, top, sb0, ps0, l, x_src, T, nblk):
    A = P.add
    sb = lambda es, name, shape, dt: sb0(es, f"{name}_A{l}", shape, dt)
    ps = lambda es, name, shape, dt: ps0(es, f"{name}_A{l}", shape, dt)
    cols_sb, eff_sb, shb_sb, lruc_sb = T["cols_sb"], T["eff_sb"], T["shb_sb"], T["lruc_sb"]
    identb, identf, ones_bf, ones_f, grstd, epsc = T["identb"], T["identf"], T["ones_bf"], T["ones_f"], T["grstd"], T["epsc"]

    def rstd2(dst, ssq, n, rtags, wtags, tmp, tmptag):
        A("act", lambda e: e.activation(out=tmp, in_=ssq, func=AF.Ln, scale=1.0 / n, bias=epsc[:, 0:1]), reads=rtags, writes=[tmptag], n=8)
        A("act", lambda e: e.activation(out=dst, in_=tmp, func=AF.Exp, scale=-0.5), reads=[tmptag], writes=wtags, n=8)

    with ExitStack() as es:
        dbl = lambda name, shape, dt: [sb(es, f"{name}{i}", shape, dt) for i in range(2)]
        w_sb = sb(es, "w_in_sb", [128, 8, DIN], BF16)
        brow = sb(es, "b_in_row", [1, 1536], BF16)
        bcol = sb(es, "b_in_col", [128, 10], F32)
        gq = sb(es, "gq", [128, 64], F32)
        gk = sb(es, "gk", [128, 64], F32)
        negm = sb(es, "negm", [128, 16, 16], F32)
        kmeanT = sb(es, "kmeanT", [64, 8, 16], F32)
        km_hi = sb(es, "km_hi", [64, 8, 16], BF16)
        km_lo = sb(es, "km_lo", [64, 8, 16], BF16)
        km_tmp = sb(es, "km_tmp", [64, 8], F32)
        wabd = sb(es, "wabd", [128, 2, 2, 128], BF16)
        wtmp = sb(es, "wtmp", [128, 2, 2, 64], F32)
        x_sbs = Rot([sb(es, f"xA{i}", [128, D], F32) for i in range(2)], "xA")
        junkx = sb(es, "junkx", [128, D], BF16)
        junkq = dbl("junkq", [128, 512], BF16)
        junkk = dbl("junkk", [128, 512], BF16)
        qraw = dbl("qraw", [128, 512], F32)
        kraw = dbl("kraw", [128, 512], F32)
        xn = dbl("xn", [128, D], BF16)
        xnT = dbl("xnT", [128, 8, 256], BF16)
        stx = dbl("stx", [128, 4], F32)
        stq = dbl("stq", [128, 3, 8], F32)
        stk = dbl("stk", [128, 3, 8], F32)
        stg = dbl("stg", [128, 4], F32)
        kf = dbl("kf", [128, 512], F32)
        kb = dbl("kb", [128, 512], BF16)
        qaug = dbl("qaug", [128, 8, 80], BF16)
        qT8 = dbl("qT8", [64, 8, 128], BF16)
        gm = dbl("gm", [128, 8, 16], F32)
        m8 = dbl("m8", [128, 8, 8], F32)
        msk = dbl("msk", [128, 8, 16], F32)
        v_sbs = Rot([sb(es, f"vA{i}", [128, 8, 128], BF16) for i in range(2)], "vA")
        kT_sbs = Rot([sb(es, f"kTA{i}", [64, 8, 128], BF16) for i in range(2)], "kTA")
        qT_sbs = Rot([sb(es, f"qTA{i}", [80, 8, 128], BF16) for i in range(2)], "qTA")
        fmS = [sb(es, "fmS0", [128, 10, 256], F32)] * 2
        cu = sb(es, "cu", [128, 2, 258], F32)
        lx = sb(es, "lx", [128, 2, 259], F32)
        hb = sb(es, "hb", [128, 2, 257], F32)
        ct = dbl("ct", [128, 256], F32)
        cy = dbl("cy", [128, 256], F32)
        lt = [[sb(es, f"lt{lc}_{k}", [128, 256], F32) for k in range(7)] for lc in range(2)]
        xrb = dbl("xrb", [128, 256], BF16)
        ysq = dbl("ysq", [128, 4, 256], BF16)
        ycl_sbs = Rot([sb(es, f"ycl{i}", [128, 4, 256], BF16) for i in range(2)], "ycl")
        tp_ps = ps(es, "tp_ps", [128, 8, 128], BF16)
        qkv_ps = [ps(es, f"qkv_ps{i}", [128, 512], F32) for i in range(3)]
        fm_pss = Rot([ps(es, f"fm_ps{i}", [128, 512], F32) for i in range(2)], "fm_ps")
        sm_ps = ps(es, "sm_ps", [128, 512], F32)
        gate_ps = sm_ps[:, 0:128].rearrange("p (h n) -> p h n", h=8)
        km_ps = sm_ps[0:64, 128:136]
        ss_ps = sm_ps[:, 136:138]
        bc_ps = sm_ps[:, 144:154]

        for c in range(8):
            A("pool", lambda e, c=c: e.dma_start(out=w_sb[:, c, :], in_=T["w_in"][l, c * 128:(c + 1) * 128, :], max_dma_last_dim=4096),
              writes=[("w_in", c)], dma_key=("w", c), cost=12000)
        A("sp", lambda e: e.dma_start(out=gq[:], in_=T["qng"][l:l + 1, :].partition_broadcast(128)), writes=["gq"], dma_key="a0")
        A("sp", lambda e: e.dma_start(out=gk[:], in_=T["kng"][l:l + 1, :].partition_broadcast(128)), writes=["gk"], dma_key="a1")
        A("sp", lambda e: e.dma_start(out=negm[:].rearrange("p a b -> p (a b)"), in_=T["c_negmask"][0:1, :].partition_broadcast(128)),
          writes=["negm"], dma_key="a2")
        A("dve", lambda e: e.scalar_tensor_tensor(out=gk[:], in0=gk[:], scalar=0.125, in1=gq[:], op0=ALU.mult, op1=ALU.mult),
          reads=["gq", "gk"], writes=["gk"])
        A("pool", lambda e: e.memset(kmeanT[:], 0.0), writes=["kmeanT"])
        A("pool", lambda e: e.memset(km_hi[:], 0.0), writes=["km_hi"])
        A("pool", lambda e: e.memset(km_lo[:], 0.0), writes=["km_lo"])
        A("pool", lambda e: e.memset(cu[:], 0.0), writes=["cu0", "cu1"])
        A("pool", lambda e: e.memset(lx[:], 0.0), writes=["lx0", "lx1"])
        A("pool", lambda e: e.memset(hb[:], 0.0), writes=["hb0", "hb1"])
        for i in range(2):
            A("pool", lambda e, i=i: e.memset(qaug[i][:], 0.0), writes=[f"qaug_b{i}", f"qaug_q{i}"])
        for k, vt in enumerate(v_sbs.t):
            A("pool", lambda e, vt=vt: e.memset(vt[:], 1.0), writes=[("vA", k)])
        A("pool", lambda e: e.memset(wabd[:], 0.0), writes=["wabd"])
        for g, wsrc in enumerate((T["lru_wa"], T["lru_wx"])):
            for hh in range(4):
                ch, hf = hh // 2, hh % 2
                A("sp", lambda e, g=g, wsrc=wsrc, hh=hh, ch=ch, hf=hf: e.dma_start(
                    out=wtmp[hf * 64:(hf + 1) * 64, g, ch, :], in_=wsrc[l, hh, :, :]), writes=[("wtmp", g, hh)], dma_key=("spk", g * 4 + hh))
                A("dve", lambda e, g=g, ch=ch, hf=hf: e.tensor_copy(out=wabd[hf * 64:(hf + 1) * 64, g, ch, hf * 64:(hf + 1) * 64],
                                                                   in_=wtmp[hf * 64:(hf + 1) * 64, g, ch, :]),
                  reads=[("wtmp", g, hh), "wabd"], writes=[("wabd", g, hh)])
        wabd_tags = [("wabd", g, hh) for g in range(2) for hh in range(4)]
        w_tags = [("w_in", c) for c in range(8)]

        for j in range(3):
            def mm(e, j=j):
                r = None
                for c in range(8):
                    r = e.matmul(qkv_ps[j][0:1, :], lhsT=shb_sb[:, l, 0, c:c + 1], rhs=w_sb[:, c, j * 512:(j + 1) * 512],
                                 start=(c == 0), stop=(c == 7))
                return r
            A("pe", mm, reads=w_tags + [("shb", l, 0)], writes=[("qkv_ps", j)])
            A("act", lambda e, j=j: e.copy(out=brow[:, j * 512:(j + 1) * 512], in_=qkv_ps[j][0:1, :]), reads=[("qkv_ps", j)], writes=["brow"], n=512)

        def mm(e):
            r = None
            for fc in range(10):
                for c in range(8):
                    r = e.matmul(bc_ps[:, fc:fc + 1], lhsT=w_sb[:, c, 1536 + fc * 128:1536 + (fc + 1) * 128],
                                 rhs=shb_sb[:, l, 0, c:c + 1], start=(c == 0), stop=(c == 7))
            return r
        A("pe", mm, reads=w_tags + [("shb", l, 0)], writes=["sm_ps"])
        A("dve", lambda e: e.tensor_copy(out=bcol[:], in_=bc_ps), reads=["sm_ps"], writes=["bcol"])
        for c in range(8):
            if c % 2 == 0:
                A("dve", lambda e, c=c: e.tensor_scalar(out=w_sb[:, c, :], in0=w_sb[:, c, :], scalar1=eff_sb[:, l, c:c + 1], scalar2=None, op0=ALU.mult),
                  reads=[("eff", l, 0, 0)], writes=[("w_in", c)], n=DIN)
            else:
                A("act", lambda e, c=c: e.activation(out=w_sb[:, c, :], in_=w_sb[:, c, :], func=AF.Copy, scale=eff_sb[:, l, c:c + 1]),
                  reads=[("eff", l, 0, 0)], writes=[("w_in", c)], n=DIN)

        def h3(ap):
            return ap.rearrange("p (h d) -> p h d", h=8)

        def tiles(st):
            bp = st % 2
            ctx = [dict(), dict()]

            def s0(i):
                t = 2 * st + i
                xt, xtag = x_sbs.next()
                ctx[i].update(t=t, xt=xt, xtag=xtag)
                A("sp", lambda e, xt=xt, t=t: e.dma_start(out=xt[:], in_=x_src[t * 128:(t + 1) * 128, :]), writes=[xtag], dma_key=xtag)
                A("act", lambda e, xt=xt, i=i: e.activation(out=junkx[:], in_=xt[:], func=AF.Square, accum_out=stx[i][:, 0:1]),
                  reads=[xtag], writes=["junkx", f"stx{i}"], n=D)
                rstd2(stx[i][:, 2:3], stx[i][:, 0:1], D, [f"stx{i}"], [f"stxr{i}"], stx[i][:, 1:2], f"stxt{i}")
                A("dve", lambda e, xt=xt, i=i: e.tensor_scalar(out=xn[i][:], in0=xt[:], scalar1=stx[i][:, 2:3], scalar2=None, op0=ALU.mult),
                  reads=[xtag, f"stxr{i}"], writes=[f"xn{i}"], n=D)

            def s1(i):
                def tr(e, i=i):
                    r = None
                    for c in range(8):
                        r = e.transpose(out=tp_ps[:, c, :], in_=xn[i][:, c * 128:(c + 1) * 128], identity=identb[:])
                    return r
                A("pe", tr, reads=[f"xn{i}", "identb"], writes=["tp_ps"])
                A("act", lambda e, i=i, bp=bp: e.copy(out=xnT[bp][:, :, i * 128:(i + 1) * 128], in_=tp_ps[:]), reads=["tp_ps"], writes=[("xnT", bp, i)], n=D)

            def s2(i):
                t = ctx[i]["t"]
                for j in range(3):
                    def mm(e, j=j, i=i, bp=bp):
                        for c in range(8):
                            e.matmul(qkv_ps[j][:], lhsT=xnT[bp][:, c, i * 128:(i + 1) * 128], rhs=w_sb[:, c, j * 512:(j + 1) * 512],
                                     start=(c == 0), stop=False)
                        return e.matmul(qkv_ps[j][:], lhsT=ones_bf[0:1, :], rhs=brow[0:1, j * 512:(j + 1) * 512], start=False, stop=True)
                    A("pe", mm, reads=w_tags + [("xnT", bp, i), "brow", "ones_bf"], writes=[("qkv_ps", j)], cost=2200)
                A("act", lambda e, i=i: e.copy(out=qraw[i][:], in_=qkv_ps[0][:]), reads=[("qkv_ps", 0)], writes=[f"qraw{i}"], n=512)
                A("act", lambda e, i=i: e.copy(out=kraw[i][:], in_=qkv_ps[1][:]), reads=[("qkv_ps", 1)], writes=[f"kraw{i}"], n=512)
                vt, vtag = v_sbs.next()
                A("act", lambda e, vt=vt: e.copy(out=vt[:, :, 0:64], in_=h3(qkv_ps[2][:])), reads=[("qkv_ps", 2)], writes=[vtag], n=512)
                A("sp", lambda e, vt=vt, t=t: e.dma_start(out=T["v_scr"][t, :, :], in_=vt[:].rearrange("p h d -> p (h d)")),
                  reads=[vtag], writes=[("v_scr", t)], dma_key=("st",) + vtag)

            def s3(i):
                A("act", lambda e, i=i: e.activation(out=junkq[i][:], in_=qraw[i][:], func=AF.Square), reads=[f"qraw{i}"], writes=[f"junkq{i}"], n=512)
                A("dve", lambda e, i=i: e.tensor_reduce(out=stq[i][:, 0, :], in_=h3(junkq[i][:]), axis=AX.X, op=ALU.add),
                  reads=[f"junkq{i}"], writes=[f"stq{i}"], n=512)
                rstd2(stq[i][:, 2, :], stq[i][:, 0, :], 64, [f"stq{i}"], [f"stqr{i}"], stq[i][:, 1, :], f"stqt{i}")
                A("dve", lambda e, i=i: e.tensor_tensor(out=qaug[i][:, :, 0:64], in0=h3(qraw[i][:]),
                                                        in1=stq[i][:, 2, :].unsqueeze(2).to_broadcast([128, 8, 64]), op=ALU.mult),
                  reads=[f"qraw{i}", f"stqr{i}"], writes=[f"qaug_q{i}"], n=512)
                A("act", lambda e, i=i: e.activation(out=junkk[i][:], in_=kraw[i][:], func=AF.Square), reads=[f"kraw{i}"], writes=[f"junkk{i}"], n=512)
                A("dve", lambda e, i=i: e.tensor_reduce(out=stk[i][:, 0, :], in_=h3(junkk[i][:]), axis=AX.X, op=ALU.add),
                  reads=[f"junkk{i}"], writes=[f"stk{i}"], n=512)
                rstd2(stk[i][:, 2, :], stk[i][:, 0, :], 64, [f"stk{i}"], [f"stkr{i}"], stk[i][:, 1, :], f"stkt{i}")
                A("dve", lambda e, i=i: e.tensor_tensor(out=h3(kf[i][:]), in0=h3(kraw[i][:]),
                                                        in1=stk[i][:, 2, :].unsqueeze(2).to_broadcast([128, 8, 64]), op=ALU.mult),
                  reads=[f"kraw{i}", f"stkr{i}"], writes=[f"kf{i}"], n=512)
                A("dve", lambda e, i=i: e.tensor_tensor(out=h3(kb[i][:]), in0=h3(kf[i][:]),
                                                        in1=gk[:].unsqueeze(1).to_broadcast([128, 8, 64]), op=ALU.mult),
                  reads=[f"kf{i}", "gk"], writes=[f"kb{i}"], n=512)

            def s4(i):
                def mm(e, i=i):
                    r = None
                    for h in range(8):
                        r = e.matmul(km_ps[:, h:h + 1], lhsT=kb[i][:, h * 64:(h + 1) * 64], rhs=ones_bf[:, 0:1], start=True, stop=True)
                    return r
                A("pe", mm, reads=[f"kb{i}", "ones_bf"], writes=["sm_ps"], cost=600)
                if i == 0:
                    A("dve", lambda e: e.tensor_scalar(out=kmeanT[:, :, st], in0=km_ps, scalar1=1.0 / 256, scalar2=None, op0=ALU.mult),
                      reads=["sm_ps"], writes=["kmeanT"], n=8)
                else:
                    A("dve", lambda e: e.scalar_tensor_tensor(out=kmeanT[:, :, st], in0=km_ps, scalar=1.0 / 256, in1=kmeanT[:, :, st],
                                                              op0=ALU.mult, op1=ALU.add),
                      reads=["sm_ps"], writes=["kmeanT"], n=8)
                    A("dve", lambda e: e.tensor_copy(out=km_hi[:, :, st], in_=kmeanT[:, :, st]), reads=["kmeanT"], writes=["km_hi"], n=8)
                    A("dve", lambda e: e.tensor_tensor(out=km_tmp[:], in0=kmeanT[:, :, st], in1=km_hi[:, :, st], op=ALU.subtract),
                      reads=["kmeanT", "km_hi"], writes=["km_tmp"], n=8)
                    A("dve", lambda e: e.tensor_copy(out=km_lo[:, :, st], in_=km_tmp[:]), reads=["km_tmp"], writes=["km_lo"], n=8)

            def s5(i):
                t = ctx[i]["t"]
                kTt, kTtag = kT_sbs.next()

                def tr(e, i=i):
                    r = None
                    for h in range(8):
                        r = e.transpose(out=tp_ps[0:64, h, :], in_=kb[i][:, h * 64:(h + 1) * 64], identity=identb[:])
                    return r
                A("pe", tr, reads=[f"kb{i}", "identb"], writes=["tp_ps"])
                A("act", lambda e, kTt=kTt: e.copy(out=kTt[:], in_=tp_ps[0:64, :, :]), reads=["tp_ps"], writes=[kTtag], n=D)
                A("sp", lambda e, kTt=kTt, t=t: e.dma_start(out=T["kT_scr"][:, :, t * 128:(t + 1) * 128], in_=kTt[:]),
                  reads=[kTtag], writes=[("kT_scr", t)], dma_key=("st",) + kTtag)

            def s6(i):
                if st < 1:
                    return

                def tr(e, i=i):
                    r = None
                    for h in range(8):
                        r = e.transpose(out=tp_ps[0:64, h, :], in_=qaug[i][:, h, 0:64], identity=identb[:])
                    return r
                A("pe", tr, reads=[f"qaug_q{i}", "identb"], writes=["tp_ps"])
                A("act", lambda e, i=i: e.copy(out=qT8[i][:], in_=tp_ps[0:64, :, :]), reads=["tp_ps"], writes=[f"qT8{i}"], n=D)

            def s7(i):
                if st < 1:
                    return

                def mm(e, i=i):
                    r = None
                    for h in range(8):
                        e.matmul(gate_ps[:, h, :], lhsT=qT8[i][:, h, :], rhs=km_hi[:, h, :], start=True, stop=False)
                        r = e.matmul(gate_ps[:, h, :], lhsT=qT8[i][:, h, :], rhs=km_lo[:, h, :], start=False, stop=True)
                    return r
                A("pe", mm, reads=[f"qT8{i}", "km_hi", "km_lo"], writes=["sm_ps"], cost=1000)
                A("dve", lambda e, i=i: e.tensor_tensor(out=gm[i][:], in0=gate_ps, in1=negm[:, st:st + 1, :].to_broadcast([128, 8, 16]), op=ALU.add),
                  reads=["sm_ps", "negm"], writes=[f"gm{i}"], n=128)
                for h in range(8):
                    A("dve", lambda e, h=h, i=i: e.max(out=m8[i][:, h, :], in_=gm[i][:, h, :]), reads=[f"gm{i}"], writes=[(f"m8{i}", h)], n=16)
                A("dve", lambda e, i=i: e.tensor_tensor(out=msk[i][:], in0=gm[i][:], in1=m8[i][:, :, 2:3].to_broadcast([128, 8, 16]), op=ALU.is_ge),
                  reads=[f"gm{i}"] + [(f"m8{i}", h) for h in range(8)], writes=[f"msk{i}"], n=128)
                A("dve", lambda e, i=i: e.tensor_scalar(out=qaug[i][:, :, 64:80], in0=msk[i][:], scalar1=BIG, scalar2=-BIG, op0=ALU.mult, op1=ALU.add),
                  reads=[f"msk{i}"], writes=[f"qaug_b{i}"], n=128)

            def s8(i):
                t = ctx[i]["t"]
                qTt, qTtag = qT_sbs.next()

                def tr(e, i=i):
                    r = None
                    for h in range(8):
                        r = e.transpose(out=tp_ps[0:80, h, :], in_=qaug[i][:, h, :], identity=identb[:])
                    return r
                A("pe", tr, reads=[f"qaug_q{i}", f"qaug_b{i}", "identb"], writes=["tp_ps"])
                A("act", lambda e, qTt=qTt: e.copy(out=qTt[:], in_=tp_ps[0:80, :, :]), reads=["tp_ps"], writes=[qTtag], n=D)
                A("sp", lambda e, qTt=qTt, t=t: e.dma_start(out=T["qT_scr"][:, :, t * 128:(t + 1) * 128], in_=qTt[:]),
                  reads=[qTtag], writes=[("qT_scr", t)], dma_key=("st",) + qTtag)

            import os as _os7
            stages = (s0, s1, s2, s3, s4, s5, s6, s7, s8)
            tmode = _os7.environ.get("TILE_SEQ", "0")
            if tmode.startswith("g"):
                k_int = int(tmode[1:])
                for si in range(k_int):
                    for i in range(2):
                        stages[si](i)
                for i in range(2):
                    for si in range(k_int, 9):
                        stages[si](i)
            elif tmode == "1":
                for i in range(2):
                    for stage in stages:
                        stage(i)
            else:
                for stage in (s0, s1, s2, s3, s4, s5, s6, s7, s8):
                    for i in range(2):
                        stage(i)

        def fm(st):
            bp = st % 2
            F = fmS[bp]
            ycl_t, ycl_tag = ycl_sbs.next()
            for fc in (6, 7, 8, 9, 2, 4, 0, 3, 5, 1):
                bank, btag = fm_pss.next()

                def mm(e, bank=bank, fc=fc, bp=bp):
                    r = None
                    for c in range(8):
                        r = e.matmul(bank[:, 0:256], lhsT=w_sb[:, c, 1536 + fc * 128:1536 + (fc + 1) * 128], rhs=xnT[bp][:, c, :],
                                     start=(c == 0), stop=(c == 7))
                    return r
                A("pe", mm, reads=w_tags + [("xnT", bp, 0), ("xnT", bp, 1)], writes=[btag], cost=1100)
                if fc in (6, 7):
                    lc = fc - 6
                    A("act", lambda e, bank=bank, lc=lc, fc=fc: e.activation(out=lx[:, lc, 3:259], in_=bank[:, 0:256], func=AF.Identity, bias=bcol[:, fc:fc + 1]),
                      reads=[btag, "bcol"], writes=[f"lx{lc}"])
                else:
                    A("dve", lambda e, bank=bank, fc=fc, F=F: e.tensor_scalar(out=F[:, fc, :], in0=bank[:, 0:256], scalar1=bcol[:, fc:fc + 1], scalar2=None, op0=ALU.add),
                      reads=[btag, "bcol"], writes=[("fmS", fc)])
            for lc in range(2):
                xr, ea, sa, ei, uu, gz, yy = lt[lc]
                tg = lambda nm, lc=lc: f"{nm}{lc}"
                lxb, hbb = lx[:, lc, :], hb[:, lc, :]
                cw = [cols_sb[:, l, 62 + lc * 4 + k:62 + lc * 4 + k + 1] for k in range(4)]
                cb = cols_sb[:, l, 70 + lc:71 + lc]
                nba = lruc_sb[:, l, 0 + lc:1 + lc]
                nbx = lruc_sb[:, l, 2 + lc:3 + lc]
                sp8 = lruc_sb[:, l, 4 + lc:5 + lc]
                G = F[:, 8 + lc, :]
                gtag = ("fmS", 8 + lc)
                A("dve", lambda e, lxb=lxb, cw=cw, cb=cb, xr=xr: e.tensor_scalar(out=xr[:], in0=lxb[:, 0:256], scalar1=cw[0], scalar2=cb, op0=ALU.mult, op1=ALU.add),
                  reads=[tg("lx")], writes=[tg("xr")])
                for k in range(1, 4):
                    A("dve", lambda e, lxb=lxb, cw=cw, k=k, xr=xr: e.scalar_tensor_tensor(out=xr[:], in0=lxb[:, k:k + 256], scalar=cw[k], in1=xr[:],
                                                                                      op0=ALU.mult, op1=ALU.add),
                      reads=[tg("lx"), tg("xr")], writes=[tg("xr")])
                A("dve", lambda e, lxb=lxb: e.tensor_copy(out=lxb[:, 0:3], in_=lxb[:, 256:259]), reads=[tg("lx"), tg("xr")], writes=[tg("lx")], n=3)
                A("pool", lambda e, lc=lc, xr=xr: e.tensor_copy(out=xrb[lc][:], in_=xr[:]), reads=[tg("xr")], writes=[tg("xrb")], cost=1500)
                rb, rbtag = fm_pss.next()
                A("pe", lambda e, lc=lc, rb=rb: e.matmul(rb[:, 0:256], lhsT=wabd[:, 0, lc, :], rhs=xrb[lc][:], start=True, stop=True),
                  reads=[tg("xrb")] + wabd_tags, writes=[rbtag], cost=200)
                A("act", lambda e, nba=nba, rb=rb, ea=ea: e.activation(out=ea[:], in_=rb[:, 0:256], func=AF.Exp, scale=-1.0, bias=nba),
                  reads=[rbtag], writes=[tg("ea")])
                ib, ibtag = fm_pss.next()
                A("pe", lambda e, lc=lc, ib=ib: e.matmul(ib[:, 0:256], lhsT=wabd[:, 1, lc, :], rhs=xrb[lc][:], start=True, stop=True),
                  reads=[tg("xrb")] + wabd_tags, writes=[ibtag], cost=200)
                A("act", lambda e, nbx=nbx, ib=ib, ei=ei: e.activation(out=ei[:], in_=ib[:, 0:256], func=AF.Exp, scale=-1.0, bias=nbx),
                  reads=[ibtag], writes=[tg("ei")])
                A("act", lambda e, ea=ea: e.activation(out=ea[:], in_=ea[:], func=AF.Ln, bias=1.0), reads=[tg("ea")], writes=[tg("ea")])
                A("act", lambda e, ea=ea: e.activation(out=ea[:], in_=ea[:], func=AF.Exp, scale=-1.0), reads=[tg("ea")], writes=[tg("ea")])
                A("act", lambda e, ea=ea, sp8=sp8: e.activation(out=ea[:], in_=ea[:], func=AF.Exp, scale=sp8), reads=[tg("ea")], writes=[tg("ea")])
                A("act", lambda e, ea=ea, sa=sa: e.activation(out=sa[:], in_=ea[:], func=AF.Square), reads=[tg("ea")], writes=[tg("sa")])
                A("act", lambda e, sa=sa: e.activation(out=sa[:], in_=sa[:], func=AF.Ln, scale=-1.0, bias=1.0), reads=[tg("sa")], writes=[tg("sa")])
                A("act", lambda e, sa=sa: e.activation(out=sa[:], in_=sa[:], func=AF.Exp, scale=0.5), reads=[tg("sa")], writes=[tg("sa")])
                A("act", lambda e, ei=ei: e.activation(out=ei[:], in_=ei[:], func=AF.Ln, bias=1.0), reads=[tg("ei")], writes=[tg("ei")])
                A("act", lambda e, ei=ei: e.activation(out=ei[:], in_=ei[:], func=AF.Exp, scale=-1.0), reads=[tg("ei")], writes=[tg("ei")])
                A("dve", lambda e, ei=ei, xr=xr, uu=uu: e.tensor_tensor(out=uu[:], in0=ei[:], in1=xr[:], op=ALU.mult), reads=[tg("ei"), tg("xr")], writes=[tg("uu")])
                A("dve", lambda e, sa=sa, uu=uu: e.tensor_tensor(out=uu[:], in0=uu[:], in1=sa[:], op=ALU.mult), reads=[tg("uu"), tg("sa")], writes=[tg("uu")])
                A("dve", lambda e, hbb=hbb, ea=ea, uu=uu: e.tensor_tensor_scan(out=hbb[:, 1:257], data0=ea[:], data1=uu[:], initial=hbb[:, 0:1],
                                                                              op0=ALU.mult, op1=ALU.add),
                  reads=[tg("ea"), tg("uu"), tg("hb")], writes=[tg("hbh")], n=512)
                A("act", lambda e, G=G, gz=gz: e.activation(out=gz[:], in_=G, func=AF.Square), reads=[gtag], writes=[tg("gz")])
                A("dve", lambda e, gz=gz: e.tensor_scalar(out=gz[:], in0=gz[:], scalar1=0.044715, scalar2=1.0, op0=ALU.mult, op1=ALU.add),
                  reads=[tg("gz")], writes=[tg("gz")])
                A("dve", lambda e, gz=gz, G=G: e.tensor_tensor(out=gz[:], in0=gz[:], in1=G, op=ALU.mult), reads=[tg("gz"), gtag], writes=[tg("gz")])
                A("act", lambda e, gz=gz: e.activation(out=gz[:], in_=gz[:], func=AF.Exp, scale=-1.5957691216057308), reads=[tg("gz")], writes=[tg("gz")])
                A("act", lambda e, gz=gz: e.activation(out=gz[:], in_=gz[:], func=AF.Ln, bias=1.0), reads=[tg("gz")], writes=[tg("gz")])
                A("act", lambda e, gz=gz: e.activation(out=gz[:], in_=gz[:], func=AF.Exp, scale=-1.0), reads=[tg("gz")], writes=[tg("gz")])
                A("dve", lambda e, gz=gz, G=G: e.tensor_tensor(out=gz[:], in0=gz[:], in1=G, op=ALU.mult), reads=[tg("gz"), gtag], writes=[tg("gz")])
                A("dve", lambda e, hbb=hbb, gz=gz, yy=yy: e.tensor_tensor(out=yy[:], in0=hbb[:, 1:257], in1=gz[:], op=ALU.mult),
                  reads=[tg("hbh"), tg("gz")], writes=[tg("yy")])
                A("dve", lambda e, hbb=hbb: e.tensor_copy(out=hbb[:, 0:1], in_=hbb[:, 256:257]), reads=[tg("hbh"), tg("yy")], writes=[tg("hb")], n=1)
                A("pool", lambda e, lc=lc, ycl_t=ycl_t, yy=yy: e.tensor_copy(out=ycl_t[:, 2 + lc, :], in_=yy[:]), reads=[tg("yy")], writes=[ycl_tag + (2 + lc,)], cost=1500)
                A("pool", lambda e, lc=lc, yy=yy, bp=bp: e.tensor_tensor(out=ysq[bp][:, 2 + lc, :], in0=yy[:], in1=yy[:], op=ALU.mult), reads=[tg("yy")], writes=[("ysq", bp, 2 + lc)], cost=1000)
            for cc in range(2):
                cub = cu[:, cc, :]
                tg = lambda nm, cc=cc: f"{nm}{cc}"
                w0 = cols_sb[:, l, 56 + cc * 3 + 0:56 + cc * 3 + 1]
                w1 = cols_sb[:, l, 56 + cc * 3 + 1:56 + cc * 3 + 2]
                w2 = cols_sb[:, l, 56 + cc * 3 + 2:56 + cc * 3 + 3]
                Bt, Ct, Ut = ("fmS", 0 + cc), ("fmS", 2 + cc), ("fmS", 4 + cc)
                A("dve", lambda e, cub=cub, cc=cc, F=F: e.tensor_tensor(out=cub[:, 2:258], in0=F[:, 2 + cc, :], in1=F[:, 4 + cc, :], op=ALU.mult),
                  reads=[Ct, Ut], writes=[tg("cu")])
                A("dve", lambda e, cub=cub, w0=w0, cc=cc: e.tensor_scalar(out=ct[cc][:], in0=cub[:, 0:256], scalar1=w0, scalar2=None, op0=ALU.mult),
                  reads=[tg("cu")], writes=[tg("ct")])
                A("dve", lambda e, cub=cub, w1=w1, cc=cc: e.scalar_tensor_tensor(out=ct[cc][:], in0=cub[:, 1:257], scalar=w1, in1=ct[cc][:], op0=ALU.mult, op1=ALU.add),
                  reads=[tg("cu"), tg("ct")], writes=[tg("ct")])
                A("dve", lambda e, cub=cub, w2=w2, cc=cc: e.scalar_tensor_tensor(out=ct[cc][:], in0=cub[:, 2:258], scalar=w2, in1=ct[cc][:], op0=ALU.mult, op1=ALU.add),
                  reads=[tg("cu"), tg("ct")], writes=[tg("ct")])
                A("dve", lambda e, cub=cub: e.tensor_copy(out=cub[:, 0:2], in_=cub[:, 256:258]), reads=[tg("cu"), tg("ct")], writes=[tg("cu")], n=2)
                A("dve", lambda e, cc=cc, F=F: e.tensor_tensor(out=cy[cc][:], in0=F[:, 0 + cc, :], in1=ct[cc][:], op=ALU.mult),
                  reads=[Bt, tg("ct")], writes=[tg("cy")])
                A("pool", lambda e, cc=cc, ycl_t=ycl_t: e.tensor_copy(out=ycl_t[:, cc, :], in_=cy[cc][:]), reads=[tg("cy")], writes=[ycl_tag + (cc,)], cost=1500)
                A("pool", lambda e, cc=cc, bp=bp: e.tensor_tensor(out=ysq[bp][:, cc, :], in0=cy[cc][:], in1=cy[cc][:], op=ALU.mult), reads=[tg("cy")], writes=[("ysq", bp, cc)], cost=1000)
            for i in range(2):
                t = 2 * st + i

                def mm(e, i=i, bp=bp):
                    r = None
                    for g in range(2):
                        for c2 in range(2):
                            r = e.matmul(ss_ps[:, g:g + 1], lhsT=ysq[bp][:, 2 * g + c2, i * 128:(i + 1) * 128], rhs=ones_bf[:, 0:1],
                                         start=(c2 == 0), stop=(c2 == 1))
                    return r
                A("pe", mm, reads=[("ysq", bp, c) for c in range(4)] + ["ones_bf"], writes=["sm_ps"], cost=500)
                rstd2(grstd[:, t, :], ss_ps, 256, ["sm_ps"], [("grstd", t)], stg[i][:, 0:2], f"stg{i}")
            A("sp", lambda e, ycl_t=ycl_t, st=st: e.dma_start(
                out=T["ycl_scr"].rearrange("(c p) t -> p c t", p=128)[:, :, st * 256:(st + 1) * 256], in_=ycl_t[:]),
              reads=[ycl_tag + (c,) for c in range(4)], writes=[("ycl_scr", st)], dma_key=("st",) + ycl_tag)

        mgen = None
        if T.get("mod_factory") is not None:
            mgen = T["mod_factory"](lambda n_, s_, d_: sb(es, f"{n_}_m1", s_, d_), lambda n_, s_, d_: ps(es, f"{n_}_m1", s_, d_))
        tiles(0)
        for st in range(nblk):
            if st + 1 < nblk:
                tiles(st + 1)
            fm(st)
            if mgen is not None:
                next(mgen, None)
                next(mgen, None)
        if mgen is not None:
            for _ in mgen:
                pass


def phase_B(nc, P, top, sb0, ps0, l, x_src, T, nblk):
    A = P.add
    sb = lambda es, name, shape, dt: sb0(es, f"{name}_B{l}", shape, dt)
    ps = lambda es, name, shape, dt: ps0(es, f"{name}_B{l}", shape, dt)
    cols_sb, identb, ones_bf, ones_f, grstd = T["cols_sb"], T["identb"], T["ones_bf"], T["ones_f"], T["grstd"]
    rstd_from_ssq = T["rstd_from_ssq"]
    with ExitStack() as es:
        kT = sb(es, "kT_all", [128, 8, S], BF16)
        v_all = sb(es, "v_all", [128, NT, 1024], BF16)
        wo = sb(es, "w_out_sb", [128, 8, D], BF16)
        caus = sb(es, "caus", [128, 2, 256], BF16)
        qT_blks = Rot([sb(es, f"qTb{i}", [128, 8, 256], BF16) for i in range(2)], "qTb")
        ycl_blks = Rot([sb(es, f"yclb{i}", [128, 4, 256], BF16) for i in range(2)], "yclb")
        x_sbs = Rot([sb(es, f"xB{i}", [128, D], F32) for i in range(3)], "xB")
        g1bc = x_sbs.t[2]
        p_sbs = Rot([sb(es, f"pB{i}", [128, 2, 256], BF16) for i in range(4)], "pB")
        rdens = Rot([sb(es, f"rdB{i}", [64, 256], F32) for i in range(2)], "rdB")
        yattn = sb(es, "yattn", [128, 4, 256], F32)
        ya_bf = sb(es, "ya_bf", [128, 4, 256], BF16)
        ysq = sb(es, "ysqB", [128, 4, 256], BF16)
        stb = sb(es, "stb", [128, 8], F32)
        s_pss = Rot([ps(es, f"s_ps{i}", [128, 2, 256], F32) for i in range(3)], "s_ps")
        oT_pss = Rot([ps(es, f"oT_ps{i}", [128, 512], F32) for i in range(2)], "oT_ps")
        sm_ps = ps(es, "smB_ps", [128, 512], F32)
        op_pss = Rot([ps(es, f"op_ps{i}", [128, 512], F32) for i in range(2)], "op_ps")

        for c in range(8):
            A("pool", lambda e, c=c: e.dma_start(out=wo[:, c, :], in_=T["w_out"][l, c * 128:(c + 1) * 128, :]),
              writes=[("wo", c)], dma_key=("w", c))
        A("sp", lambda e: e.dma_start(out=g1bc[:], in_=T["gates_scr"][l, 0:1, :].partition_broadcast(128)), writes=[("xB", 2)], dma_key=("xB", 2))
        A("sp", lambda e: e.dma_start(out=caus[:], in_=T["c_caus"][:, :, :]), writes=["caus"], dma_key="a1")
        for c in range(8):
            eng = "dve"
            A(eng, lambda e, c=c: e.scalar_tensor_tensor(out=wo[:, c, :], in0=wo[:, c, :], scalar=cols_sb[:, l, 48 + c:49 + c], in1=g1bc[:],
                                                         op0=ALU.mult, op1=ALU.mult),
              reads=[("xB", 2)], writes=[("wo", c)], n=D)
        wo_tags = [("wo", c) for c in range(8)]
        for h in range(8):
            A("dve", lambda e, h=h: e.memset(kT[64:128, h, :], 0.0), writes=[("kToh", h)], n=2048)
        for k_, qt_ in enumerate(qT_blks.t):
            A("dve", lambda e, qt_=qt_: e.memset(qt_[:], 0.0), writes=[("qTb", k_)], n=1024)
        for h in range(8):
            A("sp", lambda e, h=h: e.dma_start(out=kT[64:80, h, :], in_=T["c_oh"][:, :]), writes=[("kToh", h)], dma_key=("spk", h))
        oh_tags = [("kToh", h) for h in range(8)]
        for j in range(nblk):
            A("sp", lambda e, j=j: e.dma_start(out=kT[0:64, :, j * 256:(j + 1) * 256], in_=T["kT_scr"][:, :, j * 256:(j + 1) * 256]),
              writes=[("kT", j)], dma_key=("kT", j % 4))
            A("sp", lambda e, j=j: e.dma_start(out=v_all[:, 2 * j:2 * j + 2, :], in_=T["v_scr"][2 * j:2 * j + 2, :, :].rearrange("t p f -> p t f")),
              writes=[("v", j)], dma_key=("v", j % 4))

        for qb in range(nblk):
            qTb, qtag = qT_blks.next()
            yclb, ytag = ycl_blks.next()
            A("sp", lambda e, qTb=qTb, qb=qb: e.dma_start(out=qTb[0:80, :, :], in_=T["qT_scr"][:, :, qb * 256:(qb + 1) * 256]), writes=[qtag], dma_key=qtag)
            A("sp", lambda e, yclb=yclb, qb=qb: e.dma_start(
                out=yclb[:], in_=T["ycl_scr"].rearrange("(c p) t -> p c t", p=128)[:, :, qb * 256:(qb + 1) * 256]), writes=[ytag], dma_key=ytag)
            xts = []
            for i in range(2):
                t = 2 * qb + i
                xt, xtag = x_sbs.next()
                A("sp", lambda e, xt=xt, t=t: e.dma_start(out=xt[:], in_=x_src[t * 128:(t + 1) * 128, :]), writes=[xtag], dma_key=xtag)
                xts.append((xt, xtag))
            for h in range(8):
                oT, otag = oT_pss.next()
                for j in range(qb + 1):
                    own = (j == qb)
                    sp_, stag = s_pss.next()
                    if not own:
                        def mm(e, sp_=sp_, j=j, h=h, qTb=qTb):
                            r = None
                            for kk in range(2):
                                r = e.matmul(sp_[:, kk, :], lhsT=kT[:, h, (2 * j + kk) * 128:(2 * j + kk + 1) * 128], rhs=qTb[:, h, :],
                                             start=True, stop=True)
                            return r
                        A("pe", mm, reads=[("kT", j), ("kToh", h), qtag], writes=[stag])
                    else:
                        def mm(e, sp_=sp_, j=j, h=h, qTb=qTb):
                            r = None
                            for kk in range(2):
                                e.matmul(sp_[:, kk, :], lhsT=kT[0:64, h, (2 * j + kk) * 128:(2 * j + kk + 1) * 128], rhs=qTb[0:64, h, :],
                                         start=True, stop=False)
                                r = e.matmul(sp_[:, kk, :], lhsT=identb[:], rhs=caus[:, kk, :], start=False, stop=True)
                            return r
                        A("pe", mm, reads=[("kT", j), qtag, "caus", "identb"], writes=[stag])
                    pt, ptag = p_sbs.next()
                    A("act", lambda e, pt=pt, sp_=sp_: e.activation(out=pt[:], in_=sp_[:], func=AF.Exp), reads=[stag], writes=[ptag], cost=600)

                    def mm(e, pt=pt, oT=oT, j=j, h=h, qb=qb):
                        r = None
                        for kk in range(2):
                            r = e.matmul(oT[:, 0:256], lhsT=v_all[:, 2 * j + kk, h * 128:(h + 1) * 128], rhs=pt[:, kk, :],
                                         start=(j == 0 and kk == 0), stop=(j == qb and kk == 1))
                        return r
                    A("pe", mm, reads=[("v", j), ptag], writes=[otag], cost=280)
                rd, rdtag = rdens.next()
                A("dve", lambda e, rd=rd, oT=oT: e.reciprocal(out=rd[:], in_=oT[64:128, 0:256]), reads=[otag], writes=[rdtag], cost=1900)
                pb = (h % 2) * 64
                A("dve", lambda e, rd=rd, oT=oT, pb=pb, h=h: e.tensor_tensor(out=yattn[pb:pb + 64, h // 2, :], in0=oT[0:64, 0:256], in1=rd[:], op=ALU.mult),
                  reads=[otag, rdtag], writes=[("yattn", h)])
            ya_tags = [("yattn", h) for h in range(8)]
            A("pool", lambda e: e.tensor_copy(out=ya_bf[:], in_=yattn[:]), reads=ya_tags, writes=["ya_bf"])
            A("act", lambda e: e.activation(out=ysq[:], in_=yattn[:], func=AF.Square), reads=ya_tags, writes=["ysqB"])
            for i in range(2):
                t = 2 * qb + i
                xt, xtag = xts[i]

                def mm(e, i=i):
                    r = None
                    for c in range(4):
                        r = e.matmul(sm_ps[:, 0:1], lhsT=ysq[:, c, i * 128:(i + 1) * 128], rhs=ones_bf[:, 0:1], start=(c == 0), stop=(c == 3))
                    return r
                A("pe", mm, reads=["ysqB", "ones_bf"], writes=["smB_ps"])
                rstd_from_ssq(stb[:, 2:3], sm_ps[:, 0:1], 512, ["smB_ps"], ["stbr"], stb[:, 1:2], "stbt")
                for n in range(2):
                    nsl = slice(n * 512, (n + 1) * 512)
                    for g in range(3):
                        op_, optag = op_pss.next()
                        if g == 0:
                            srcs = [(ya_bf, c, c) for c in range(4)]
                            rtags = ["ya_bf"]
                            scal = stb[:, 2:3]
                            stag2 = ["stbr"]
                        else:
                            srcs = [(yclb, 2 * (g - 1) + c2, 4 + 2 * (g - 1) + c2) for c2 in range(2)]
                            rtags = [ytag]
                            scal = grstd[:, t, g - 1:g]
                            stag2 = []

                        def mm(e, op_=op_, srcs=srcs, i=i, nsl=nsl):
                            r = None
                            for k, (src, sc, wc) in enumerate(srcs):
                                r = e.matmul(op_[:], lhsT=src[:, sc, i * 128:(i + 1) * 128], rhs=wo[:, wc, nsl], start=(k == 0), stop=(k == len(srcs) - 1))
                            return r
                        A("pe", mm, reads=rtags + wo_tags, writes=[optag])
                        A("dve", lambda e, op_=op_, xt=xt, scal=scal, nsl=nsl: e.scalar_tensor_tensor(
                            out=xt[:, nsl], in0=op_[:], scalar=scal, in1=xt[:, nsl], op0=ALU.mult, op1=ALU.add),
                          reads=[optag, xtag] + stag2, writes=[xtag])
                A("sp", lambda e, xt=xt, t=t: e.dma_start(out=T["x1_scr"][t * 128:(t + 1) * 128, :], in_=xt[:]),
                  reads=[xtag], writes=[("x1_scr", t)], dma_key=("st",) + xtag)


def phase_C(nc, P, top, sb0, ps0, l, x_dst, T, nblk):
    A = P.add
    sb = lambda es, name, shape, dt: sb0(es, f"{name}_C{l}", shape, dt)
    ps = lambda es, name, shape, dt: ps0(es, f"{name}_C{l}", shape, dt)
    eff_sb, shb_sb, identb = T["eff_sb"], T["shb_sb"], T["identb"]
    rstd_from_ssq = T["rstd_from_ssq"]
    with ExitStack() as es:
        wu = sb(es, "w_up_sb", [128, 8, DFF], BF16)
        wd = sb(es, "w_dn_sb", [128, 32, D], BF16)
        bup = sb(es, "bup", [128, 32], F32)
        x_sbs = Rot([sb(es, f"xC{i}", [128, D], F32) for i in range(3)], "xC")
        junk = sb(es, "junkC", [128, D], BF16)
        xn2 = [sb(es, f"xnC{i}", [128, D], BF16) for i in range(2)]
        xnT2 = [sb(es, f"xnTC{i}", [128, 8, 256], BF16) for i in range(2)]
        hT2 = [sb(es, f"hT{i}", [128, 32, 256], BF16) for i in range(2)]
        rts = Rot([sb(es, f"rt{i}", [128, 256], BF16) for i in range(3)], "rt")
        stc = sb(es, "stc", [128, 8], F32)
        tp_ps = ps(es, "tpC_ps", [128, 8, 128], BF16)
        up_pss = Rot([ps(es, f"up_ps{i}", [128, 512], F32) for i in range(3)], "up_ps")
        dn_pss = Rot([ps(es, f"dn_ps{i}", [128, 512], F32) for i in range(2)], "dn_ps")
        sm_ps = ps(es, "smC_ps", [128, 512], F32)

        for c in range(8):
            A("pool", lambda e, c=c: e.dma_start(out=wu[:, c, :], in_=T["w_up"][l, c * 128:(c + 1) * 128, :], max_dma_last_dim=4096),
              writes=[("wu", c)], dma_key=("w", c))
        for f in range(32):
            A("pool", lambda e, f=f: e.dma_start(out=wd[:, f, :], in_=T["w_down"][l, f * 128:(f + 1) * 128, :]),
              writes=[("wd", f)], dma_key=("wdk", f % 8))
        g2t, g2tag = x_sbs.t[1], ("xC", 1)
        A("sp", lambda e: e.dma_start(out=g2t[:], in_=T["gates_scr"][l, 1:2, :].partition_broadcast(128)), writes=[g2tag], dma_key=g2tag)
        wu_tags = [("wu", c) for c in range(8)]
        wd_tags = [("wd", f) for f in range(32)]

        def mm(e):
            r = None
            for f in range(32):
                for c in range(8):
                    r = e.matmul(sm_ps[:, f:f + 1], lhsT=wu[:, c, f * 128:(f + 1) * 128], rhs=shb_sb[:, l, 1, c:c + 1], start=(c == 0), stop=(c == 7))
            return r
        A("pe", mm, reads=wu_tags, writes=["smC_ps"])
        A("dve", lambda e: e.tensor_copy(out=bup[:], in_=sm_ps[:, 0:32]), reads=["smC_ps"], writes=["bup"])
        for c in range(8):
            A("act", lambda e, c=c: e.activation(out=wu[:, c, :], in_=wu[:, c, :], func=AF.Copy, scale=eff_sb[:, l, 16 + c:17 + c]),
              reads=[], writes=[("wu", c)], n=DFF)
        for f in range(32):
            eng = "pool" if f % 4 == 3 else "dve"
            A(eng, lambda e, f=f: e.tensor_tensor(out=wd[:, f, :], in0=wd[:, f, :], in1=g2t[:], op=ALU.mult), reads=[g2tag], writes=[("wd", f)], n=D)

        for st in range(nblk):
            xts = []
            bp = st % 2
            xnT, hT = xnT2[bp], hT2[bp]
            for i in range(2):
                xn = xn2[i]
                t = 2 * st + i
                xt, xtag = x_sbs.next()
                xts.append((xt, xtag))
                A("sp", lambda e, xt=xt, t=t: e.dma_start(out=xt[:], in_=T["x1_scr"][t * 128:(t + 1) * 128, :]), writes=[xtag], dma_key=xtag)
                A("act", lambda e, xt=xt: e.activation(out=junk[:], in_=xt[:], func=AF.Square, accum_out=stc[:, 0:1]),
                  reads=[xtag], writes=["junkC", "stc"])
                rstd_from_ssq(stc[:, 2:3], stc[:, 0:1], D, ["stc"], ["stcr"], stc[:, 1:2], "stct")
                A("dve", lambda e, xt=xt, xn=xn: e.tensor_scalar(out=xn[:], in0=xt[:], scalar1=stc[:, 2:3], scalar2=None, op0=ALU.mult),
                  reads=[xtag, "stcr"], writes=[f"xnC{i}"], n=D)

                def tr(e, xn=xn):
                    r = None
                    for c in range(8):
                        r = e.transpose(out=tp_ps[:, c, :], in_=xn[:, c * 128:(c + 1) * 128], identity=identb[:])
                    return r
                A("pe", tr, reads=[f"xnC{i}", "identb"], writes=["tpC_ps"])
                A("act", lambda e, i=i, xnT=xnT: e.copy(out=xnT[:, :, i * 128:(i + 1) * 128], in_=tp_ps[:]), reads=["tpC_ps"], writes=[("xnTC", bp, i)], n=D)
            for f in range(32):
                up, uptag = up_pss.next()

                def mm(e, up=up, f=f, xnT=xnT):
                    r = None
                    for c in range(8):
                        r = e.matmul(up[:, 0:256], lhsT=wu[:, c, f * 128:(f + 1) * 128], rhs=xnT[:, c, :], start=(c == 0), stop=(c == 7))
                    return r
                A("pe", mm, reads=wu_tags + [("xnTC", bp, 0), ("xnTC", bp, 1)], writes=[uptag], cost=1000)
                rt, rttag = rts.next()
                A("dve", lambda e, up=up, rt=rt, f=f: e.tensor_scalar(out=rt[:], in0=up[:, 0:256], scalar1=bup[:, f:f + 1], scalar2=0.0, op0=ALU.add, op1=ALU.max),
                  reads=[uptag, "bup"], writes=[rttag])
                A("act", lambda e, rt=rt, f=f, hT=hT: e.activation(out=hT[:, f, :], in_=rt[:], func=AF.Square), reads=[rttag], writes=[("hT", bp, f)])
            hT_tags = [("hT", bp, f) for f in range(32)]
            for i in range(2):
                t = 2 * st + i
                xt, xtag = xts[i]
                for n in range(2):
                    nsl = slice(n * 512, (n + 1) * 512)
                    dn, dntag = dn_pss.next()

                    def mm(e, dn=dn, i=i, nsl=nsl, hT=hT):
                        r = None
                        for f in range(32):
                            r = e.matmul(dn[:], lhsT=hT[:, f, i * 128:(i + 1) * 128], rhs=wd[:, f, nsl], start=(f == 0), stop=(f == 31))
                        return r
                    A("pe", mm, reads=hT_tags + wd_tags, writes=[dntag], cost=7200)
                    A("dve", lambda e, dn=dn, xt=xt, nsl=nsl: e.tensor_tensor(out=xt[:, nsl], in0=dn[:], in1=xt[:, nsl], op=ALU.add),
                      reads=[dntag, xtag], writes=[xtag])
                A("sp", lambda e, xt=xt, t=t: e.dma_start(out=x_dst[t * 128:(t + 1) * 128, :], in_=xt[:]),
                  reads=[xtag], writes=[("x_dst", t)], dma_key=("st",) + xtag)


def _consts():
    bf = ml_dtypes.bfloat16
    identb = np.eye(128, dtype=np.float32).astype(bf)
    identf = np.eye(128, dtype=np.float32)
    oh = np.zeros((16, S), np.float32)
    for j in range(16):
        oh[j, j * 256:(j + 1) * 256] = 1.0
    negmask = np.zeros((16, 16), np.float32)
    for own in range(16):
        negmask[own, own:] = -1e30
    kk = np.arange(128)[:, None]
    qq = np.arange(128)[None, :]
    tri = np.where(kk <= qq, 0.0, -BIG).astype(np.float32)
    caus = np.zeros((128, 2, 256), np.float32)
    caus[:, 0, 0:128] = tri
    caus[:, 1, 0:128] = -BIG
    caus[:, 1, 128:256] = tri
    return dict(c_identb=identb, c_identf=identf, c_oh=oh.astype(bf), c_negmask=negmask.reshape(1, 256),
                c_caus=caus.astype(bf))


def _col(v):
    v = np.asarray(v, np.float32)
    return np.ascontiguousarray(v.reshape(-1, 128).T)


def make_in_maps(inputs, n_cores=8):
    f = lambda k: np.ascontiguousarray(np.asarray(inputs[k], np.float32))
    cols = np.zeros((2, 128, NCOLS), np.float32)
    for l in range(2):
        b = f("b_ada")[l]
        parts = [_col(f("ln1_g")[l]), _col(f("ln2_g")[l]),
                 _col(b[0:1024]), _col(b[1024:2048]), _col(b[3072:4096]), _col(b[4096:5120]),
                 _col(f("mix_norm_g")[l])]
        scw = f("sc_w")[l]
        parts.append(np.concatenate([np.stack([scw[k, cc * 128:(cc + 1) * 128] for k in range(3)], 1) for cc in range(2)], 1))
        lcw = f("lru_conv_w")[l]
        parts.append(np.concatenate([np.stack([lcw[k, cc * 128:(cc + 1) * 128] for k in range(4)], 1) for cc in range(2)], 1))
        parts += [_col(f("lru_conv_b")[l]), _col(f("lru_ba")[l]), _col(f("lru_bx")[l]), _col(f("lru_lambda")[l])]
        cols[l] = np.concatenate(parts, 1)
    shared = dict(cols=cols, b_ada=f("b_ada"), w_ada=f("w_ada"), w_in=f("w_in"), q_norm_g=f("q_norm_g"), k_norm_g=f("k_norm_g"),
                  lru_wa=f("lru_wa"), lru_wx=f("lru_wx"), w_out=f("w_out"), w_up=f("w_up"), w_down=f("w_down"))
    shared.update(_consts())
    x = f("x")
    c = f("c")
    maps = []
    for b in range(n_cores):
        m = dict(shared)
        m["x"] = x[b]
        m["ccol"] = _col(c[b])
        maps.append(m)
    return maps


_NC = None


def kernel(**inputs):
    global _NC
    if _NC is None:
        _NC = build_program()[0]
    maps = make_in_maps(inputs)
    res = run_bass_kernel_spmd(_NC, maps, core_ids=list(range(8)))
    return np.stack([np.asarray(r["out"], np.float32) for r in res.results], 0)
```

```python
import numpy as np
import ml_dtypes
from contextlib import ExitStack
import concourse.bass as bass
import concourse.mybir as mybir
from concourse.bass_utils import run_bass_kernel_spmd

F32 = mybir.dt.float32
BF16 = mybir.dt.bfloat16
AF = mybir.ActivationFunctionType
ALU = mybir.AluOpType
AX = mybir.AxisListType

S = 4096
D = 1024
NT = 32
NB = 16
DIN = 2816
DFF = 4096
BIG = 30000.0
EPS = 1e-6
NCOLS = 78
ENGS = ("pe", "act", "dve", "pool", "sp")
import os as _osg
FP32_GUARD = _osg.environ.get("FP32_GUARD", "1") == "1"


class Prog:
    def __init__(self, nc, strict=True):
        self.nc = nc
        self.strict = strict
        self.ops = []
        self.last_w = {}
        self.readers = {}
        self.last_dma = {}
        self.keymap = {}
        self.nosched = False
        self.phase = 0

    def add(self, eng, fn, reads=(), writes=(), dma_key=None, n=256, cost=None):
        if cost is None:
            if dma_key is not None:
                cost = 3000.0
            elif eng == "pe":
                cost = 400.0
            elif eng == "act":
                cost = 320.0 + n / 1.4
            elif eng == "dve":
                cost = 250.0 + n / 0.96
            else:
                cost = 300.0 + n / 0.5
        if dma_key is not None:
            cls = "W" if eng == "pool" else "H"
            kk = (cls, dma_key)
            if kk not in self.keymap:
                self.keymap[kk] = (cls, sum(1 for q in self.keymap if q[0] == cls))
            dma_key = self.keymap[kk]
        i = len(self.ops)
        deps = set()
        for t in reads:
            if t in self.last_w:
                deps.add(self.last_w[t])
        for t in writes:
            if t in self.last_w:
                deps.add(self.last_w[t])
            for r in self.readers.get(t, ()):
                deps.add(r)
        if dma_key is not None:
            if dma_key in self.last_dma:
                deps.add(self.last_dma[dma_key])
            self.last_dma[dma_key] = i
        deps.discard(i)
        for t in reads:
            self.readers.setdefault(t, []).append(i)
        for t in writes:
            self.last_w[t] = i
            self.readers[t] = []
        self.ops.append(dict(eng=eng, fn=fn, deps=sorted(deps), dma_key=dma_key,
                             phase=self.phase, barrier=False, cost=float(cost), nosched=self.nosched))
        return i

    def barrier(self):
        self.ops.append(dict(eng=None, fn=None, deps=[], dma_key=None,
                             phase=self.phase, barrier=True))
        self.phase += 1
        self.last_w = {}
        self.readers = {}
        self.last_dma = {}
        self.keymap = {}

    def schedule(self, window=48):
        ops = self.ops
        n = len(ops)
        order = []
        start = 0
        while start < n:
            end = start
            while end < n and not ops[end]["barrier"]:
                end += 1
            ids = list(range(start, end))
            import os as _os2
            sp_ = _os2.environ.get("SCHED_PHASES")
            if ids and sp_ is not None and str(ops[ids[0]]["phase"]) not in sp_.split(","):
                order.extend(ids)
            elif ids:
                fz = set(_os2.environ.get("SCHED_FREEZE", "").split(","))
                if ops[ids[0]].get("nosched"):
                    fz |= set(_os2.environ.get("PHASEA_FREEZE", "").split(","))
                order.extend(self._sched_phase(ids, window, fz))
            if end < n:
                order.append(end)
            start = end + 1
        remap = {old: new for new, old in enumerate(order)}
        newops = []
        for old in order:
            o = ops[old]
            o["deps"] = sorted(remap[d] for d in o["deps"])
            newops.append(o)
        self.ops = newops

    def _sched_phase(self, ids, window, freeze=()):
        ops = self.ops
        self._freeze = set(freeze)
        idset = set(ids)
        queues = {e: [i for i in ids if ops[i]["eng"] == e] for e in ENGS}
        qpos = {e: 0 for e in ENGS}
        scheduled = {}
        eng_free = {e: 0.0 for e in ENGS}
        out = []
        remaining = len(ids)
        taken = set()
        while remaining:
            best = None
            for e in ENGS:
                q = queues[e]
                p = qpos[e]
                while p < len(q) and q[p] in taken:
                    p += 1
                qpos[e] = p
                cnt = 0
                k = p
                win = 1 if e in self._freeze else window
                while k < len(q) and cnt < win:
                    i = q[k]
                    k += 1
                    if i in taken:
                        continue
                    cnt += 1
                    ok = True
                    rt = 0.0
                    for d in ops[i]["deps"]:
                        if d in idset:
                            if d not in scheduled:
                                ok = False
                                break
                            if scheduled[d] > rt:
                                rt = scheduled[d]
                    if not ok:
                        continue
                    st = max(eng_free[e], rt)
                    if best is None or st < best[0] - 1e-9 or (abs(st - best[0]) <= 1e-9 and i < best[1]):
                        best = (st, i, e)
                    if rt <= eng_free[e]:
                        break
            assert best is not None, "scheduler deadlock"
            st, i, e = best
            o = ops[i]
            if o["dma_key"] is not None:
                eng_free[e] = st + 120.0
                scheduled[i] = st + o["cost"]
            else:
                eng_free[e] = st + o["cost"]
                scheduled[i] = st + o["cost"] + 150.0
            taken.add(i)
            out.append(i)
            remaining -= 1
        self.est_ns = getattr(self, "est_ns", 0.0) + max(scheduled.values())
        self.est_phase = getattr(self, "est_phase", []) + [(ops[ids[0]]["phase"], max(scheduled.values()), dict(eng_free))]
        return out

    def emit(self):
        nc = self.nc
        import os as _os1
        if _os1.environ.get("SCHED", "1") == "1":
            self.schedule()
        ops = self.ops
        nph = self.phase + 1
        strict = self.strict

        def same_eng_free(od, o):
            return od["eng"] == o["eng"] and o["dma_key"] is None and (od["eng"] == "pe" or not strict)

        need = [False] * len(ops)
        for i, o in enumerate(ops):
            if o["barrier"]:
                continue
            for d in o["deps"]:
                od = ops[d]
                if od["dma_key"] is not None:
                    continue
                if same_eng_free(od, o):
                    continue
                need[d] = True
        last_in_phase = {}
        for i, o in enumerate(ops):
            if o["barrier"] or o["dma_key"] is not None:
                continue
            last_in_phase[(o["eng"], o["phase"])] = i
        for i in last_in_phase.values():
            need[i] = True
        cnt = {}
        dcnt = {}
        val = [None] * len(ops)
        dma_keys = []
        for i, o in enumerate(ops):
            if o["barrier"]:
                continue
            if o["dma_key"] is not None:
                k = o["dma_key"]
                if k not in dcnt:
                    dcnt[k] = 0
                    dma_keys.append(k)
                dcnt[k] += 16
                val[i] = dcnt[k]
            elif need[i]:
                k = (o["eng"], o["phase"])
                cnt[k] = cnt.get(k, 0) + 1
                val[i] = cnt[k]
        esem = {}
        for (e, ph) in sorted(cnt.keys(), key=lambda t: (t[1], t[0])):
            esem[(e, ph)] = nc.alloc_semaphore(f"s_{e}_{ph}")
        dsem = {k: nc.alloc_semaphore(f"d_{j}") for j, k in enumerate(dma_keys)}
        self.n_sems = len(esem) + len(dsem)
        final_cnt = dict(cnt)
        per = {e: [] for e in ENGS}
        for i, o in enumerate(ops):
            if o["barrier"]:
                for e in ENGS:
                    per[e].append(i)
            else:
                per[o["eng"]].append(i)
        dma_upto = {}
        run = {}
        for i, o in enumerate(ops):
            if o["barrier"]:
                dma_upto[i] = dict(run)
            elif o["dma_key"] is not None:
                run[o["dma_key"]] = val[i]
        dma_final = dict(run)

        def gen(e):
            def body(eng):
                waited = {}

                def w(sem, name, v):
                    if waited.get(name, 0) >= v:
                        return
                    waited[name] = v
                    eng.wait_ge(sem, v)

                for i in per[e]:
                    o = ops[i]
                    if o["barrier"]:
                        ph = o["phase"]
                        for e2 in ("pe", "act", "dve", "pool"):
                            v = final_cnt.get((e2, ph), 0)
                            if v:
                                w(esem[(e2, ph)], (e2, ph), v)
                        for k, v in dma_upto[i].items():
                            w(dsem[k], k, v)
                        continue
                    for d in o["deps"]:
                        od = ops[d]
                        if od["dma_key"] is not None:
                            w(dsem[od["dma_key"]], od["dma_key"], val[d])
                        else:
                            if same_eng_free(od, o):
                                continue
                            k = (od["eng"], od["phase"])
                            w(esem[k], k, val[d])
                    inst = o["fn"](eng)
                    if o["dma_key"] is not None:
                        inst.then_inc(dsem[o["dma_key"]], 16)
                    elif need[i]:
                        inst.then_inc(esem[(e, o["phase"])], 1)
                if e == "sp":
                    for k, v in dma_final.items():
                        w(dsem[k], k, v)
                    for (e2, ph), v in final_cnt.items():
                        w(esem[(e2, ph)], (e2, ph), v)
            return body

        with nc.Block() as block:
            block.tensor(gen("pe"))
            block.scalar(gen("act"))
            block.vector(gen("dve"))
            block.gpsimd(gen("pool"))
            block.sync(gen("sp"))


class Rot:
    def __init__(self, tensors, name):
        self.t = tensors
        self.name = name
        self.i = 0

    def next(self):
        k = self.i % len(self.t)
        self.i += 1
        return self.t[k], (self.name, k)


def build_program(n_layers=2, phases="ABC", debug=False, nblk=NB):
    nc = bass.Bass("TRN2", target_bir_lowering=False)
    dbg_kind = "ExternalOutput" if debug else "Internal"

    def din(name, shape, dt=F32):
        return nc.dram_tensor(name, list(shape), dt, kind="ExternalInput").ap()

    def dscr(name, shape, dt=F32):
        return nc.dram_tensor(name, list(shape), dt, kind=dbg_kind).ap()

    x_in = din("x", [S, D])
    ccol = din("ccol", [128, 8])
    cols = din("cols", [2, 128, NCOLS])
    bada = din("b_ada", [2, 6 * D])
    w_ada = din("w_ada", [2, D, 6 * D])
    w_in = din("w_in", [2, D, DIN])
    qng = din("q_norm_g", [2, 64])
    kng = din("k_norm_g", [2, 64])
    lru_wa = din("lru_wa", [2, 4, 64, 64])
    lru_wx = din("lru_wx", [2, 4, 64, 64])
    w_out = din("w_out", [2, D, D])
    w_up = din("w_up", [2, D, DFF])
    w_down = din("w_down", [2, DFF, D])
    c_identb = din("c_identb", [128, 128], BF16)
    c_identf = din("c_identf", [128, 128], F32)
    c_oh = din("c_oh", [16, S], BF16)
    c_negmask = din("c_negmask", [1, 256], F32)
    c_caus = din("c_caus", [128, 2, 256], BF16)
    out = nc.dram_tensor("out", [S, D], F32, kind="ExternalOutput").ap()

    gates_scr = dscr("gates_scr", [2, 2, D])
    qT_scr = dscr("qT_scr", [80, 8, S], BF16)
    kT_scr = dscr("kT_scr", [64, 8, S], BF16)
    v_scr = dscr("v_scr", [NT, 128, 1024], BF16)
    ycl_scr = dscr("ycl_scr", [512, S], BF16)
    x1_scr = dscr("x1_scr", [S, D])
    xmid_scr = dscr("xmid_scr", [S, D])
    dbg_grstd = dscr("dbg_grstd", [128, NT * 2]) if debug else None

    import os as _os0
    P = Prog(nc, strict=(_os0.environ.get('STRICT', '1') == '1'))
    A = P.add

    with ExitStack() as top:
        def sb(es, name, shape, dt):
            return es.enter_context(nc.sbuf_tensor(name, list(shape), dt))

        def ps(es, name, shape, dt):
            return es.enter_context(nc.psum_tensor(name, list(shape), dt))

        cols_sb = sb(top, "cols_sb", [128, 2, NCOLS], F32)
        eff_sb = sb(top, "eff_sb", [128, 2, 32], F32)
        shb_sb = sb(top, "shb_sb", [128, 2, 2, 8], BF16)
        lruc_sb = sb(top, "lruc_sb", [128, 2, 8], F32)
        identb = sb(top, "identb", [128, 128], BF16)
        identf = sb(top, "identf", [128, 128], F32)
        ones_bf = sb(top, "ones_bf", [128, 128], BF16)
        ones_f = sb(top, "ones_f", [128, 128], F32)
        grstd = sb(top, "grstd", [128, NT, 2], F32)
        epsc = sb(top, "epsc", [128, 1], F32)

        A("sp", lambda e: e.dma_start(out=identb[:], in_=c_identb[:, :]), writes=["identb"], dma_key="c0")
        A("sp", lambda e: e.dma_start(out=identf[:], in_=c_identf[:, :]), writes=["identf"], dma_key="c1")
        A("sp", lambda e: e.dma_start(out=cols_sb[:], in_=cols.rearrange("l p n -> p l n")), writes=["cols"], dma_key="c2")
        A("pool", lambda e: e.memset(ones_bf[:], 1.0), writes=["ones_bf"])
        A("pool", lambda e: e.memset(ones_f[:], 1.0), writes=["ones_f"])
        A("pool", lambda e: e.memset(epsc[:], EPS), writes=["epsc"])
        if debug:
            A("pool", lambda e: e.memset(grstd[:], 0.0), writes=["grstd_init"])

        def rstd_from_ssq(dst, ssq, n, tags_r, tags_w, tmp, tmptag):
            A("dve", lambda e: e.tensor_scalar(out=tmp, in0=ssq, scalar1=1.0 / n, scalar2=EPS, op0=ALU.mult, op1=ALU.add),
              reads=tags_r, writes=[tmptag])
            A("act", lambda e: e.activation(out=tmp, in_=tmp, func=AF.Ln), reads=[tmptag], writes=[tmptag])
            A("act", lambda e: e.activation(out=dst, in_=tmp, func=AF.Exp, scale=-0.5), reads=[tmptag], writes=tags_w)

        cact = sb(top, "cact", [128, 8], BF16)

        def make_mod_gen(sbf, psf, l, BW=256):
            wa = [sbf(f"wa{i}", [128, 8, BW], BF16) for i in range(2)]
            modc = sbf("modc", [128, 32], F32)
            grow = sbf("grow", [1, 2, D], F32)
            brow = sbf("brow", [1, 2, D], F32)
            lam = sbf("lam", [128, 2], F32)
            bank = psf("mod_ps", [128, 512], F32)
            mod_ps = bank[:, 0:32]
            g_ps = bank[0:1, 256:256 + BW]
            per = D // BW

            def gen():
                for g_ in range(2):
                    A("sp", lambda e, g_=g_: e.dma_start(out=brow[:, g_, :], in_=bada[l:l + 1, (2 + 3 * g_) * D:(3 + 3 * g_) * D]),
                      writes=[("brow", g_)], dma_key=("c4", g_))
                for blk in range(6 * per):
                    sec, off = blk // per, (blk % per) * BW
                    wt, wtag = wa[blk % 2], ("wa", blk % 2)
                    A("pool", lambda e, wt=wt, blk=blk: e.dma_start(
                        out=wt[:], in_=w_ada[l, :, blk * BW:(blk + 1) * BW].rearrange("(c p) n -> p c n", p=128)),
                      writes=[wtag], dma_key=("wa", blk % 2), cost=6000)
                    if sec in (2, 5):
                        g = 0 if sec == 2 else 1

                        def mm(e, wt=wt):
                            r = None
                            for c in range(8):
                                r = e.matmul(g_ps, lhsT=cact[:, c:c + 1], rhs=wt[:, c, :], start=(c == 0), stop=(c == 7))
                            return r
                        A("pe", mm, reads=[wtag, "cact"], writes=["modbank"], cost=1000)
                        A("dve", lambda e, g=g, off=off: e.tensor_tensor(
                            out=grow[:, g, off:off + BW], in0=g_ps, in1=brow[:, g, off:off + BW], op=ALU.add),
                          reads=["modbank", ("brow", g)], writes=[("grow", g, blk % per)])
                        if blk % per == per - 1:
                            A("sp", lambda e, g=g: e.dma_start(out=gates_scr[l, g:g + 1, :], in_=grow[:, g, :]),
                              reads=[("grow", g, q) for q in range(per)], dma_key=("grow", g))
                    else:
                        base = {0: 0, 1: 8, 3: 16, 4: 24}[sec] + off // 128

                        def mm(e, wt=wt, base=base):
                            r = None
                            for sub in range(BW // 128):
                                for c in range(8):
                                    r = e.matmul(mod_ps[:, base + sub:base + sub + 1], lhsT=wt[:, c, sub * 128:(sub + 1) * 128],
                                                 rhs=cact[:, c:c + 1], start=(c == 0), stop=(c == 7))
                            return r
                        A("pe", mm, reads=[wtag, "cact"], writes=["modbank"], cost=1000)
                    yield
                A("dve", lambda e: e.tensor_tensor(out=modc[:], in0=mod_ps, in1=cols_sb[:, l, 16:48], op=ALU.add),
                  reads=["modbank", "cols"], writes=["modc"])
                for k2 in range(2):
                    A("dve", lambda e, k2=k2: e.scalar_tensor_tensor(
                        out=eff_sb[:, l, 16 * k2:16 * k2 + 8], in0=modc[:, 16 * k2 + 8:16 * k2 + 16], scalar=1.0,
                        in1=cols_sb[:, l, 8 * k2:8 * k2 + 8], op0=ALU.add, op1=ALU.mult),
                      reads=["modc", "cols"], writes=[("eff", l, k2, 0)])
                    A("dve", lambda e, k2=k2: e.tensor_copy(out=eff_sb[:, l, 16 * k2 + 8:16 * k2 + 16], in_=modc[:, 16 * k2:16 * k2 + 8]),
                      reads=["modc"], writes=[("eff", l, k2, 1)])
                    A("dve", lambda e, k2=k2: e.tensor_copy(out=shb_sb[:, l, k2, :], in_=modc[:, 16 * k2:16 * k2 + 8]),
                      reads=["modc"], writes=[("shb", l, k2)])
                A("dve", lambda e: e.tensor_scalar(out=lruc_sb[:, l, 0:4], in0=cols_sb[:, l, 72:76], scalar1=-1.0, scalar2=None, op0=ALU.mult),
                  reads=["cols"], writes=[("lruc", l, 0)])
                A("act", lambda e: e.activation(out=lam[:], in_=cols_sb[:, l, 76:78], func=AF.Exp, scale=-1.0),
                  reads=["cols"], writes=[("lam", l)])
                A("act", lambda e: e.activation(out=lam[:], in_=lam[:], func=AF.Ln, bias=1.0),
                  reads=[("lam", l)], writes=[("lam", l)])
                A("dve", lambda e: e.tensor_scalar(out=lruc_sb[:, l, 4:6], in0=lam[:], scalar1=-8.0, scalar2=None, op0=ALU.mult),
                  reads=[("lam", l)], writes=[("lruc", l, 1)])
                yield
            return gen()

        with ExitStack() as es:
            cc = sb(es, "cc", [128, 8], F32)
            ce = sb(es, "ce", [128, 8], F32)
            A("sp", lambda e: e.dma_start(out=cc[:], in_=ccol[:, :]), writes=["cc"], dma_key="c3")
            A("act", lambda e: e.activation(out=ce[:], in_=cc[:], func=AF.Exp, scale=-1.0), reads=["cc"], writes=["ce"])
            A("dve", lambda e: e.tensor_scalar(out=ce[:], in0=ce[:], scalar1=1.0, scalar2=None, op0=ALU.add), reads=["ce"], writes=["ce"])
            A("dve", lambda e: e.reciprocal(out=ce[:], in_=ce[:]), reads=["ce"], writes=["ce"])
            A("dve", lambda e: e.tensor_tensor(out=cact[:], in0=ce[:], in1=cc[:], op=ALU.mult), reads=["ce", "cc"], writes=["cact"])
            overlap_mod = (n_layers == 2 and "A" in phases)
            for l0 in range(1 if overlap_mod else n_layers):
                for _ in make_mod_gen(lambda n_, s_, d_: sb(es, f"{n_}_m{l0}", s_, d_), lambda n_, s_, d_: ps(es, f"{n_}_m{l0}", s_, d_), l0):
                    pass
            P.barrier()
        mod_factory = (lambda sbf, psf: make_mod_gen(sbf, psf, 1)) if overlap_mod else None

        for l in range(n_layers):
            x_src = x_in if l == 0 else xmid_scr
            x_dst = xmid_scr if l < n_layers - 1 else out
            if l >= 2:
                x_dst = out
            if "A" in phases:
                P.nosched = True
                phase_A(nc, P, top, sb, ps, l, x_src, dict(
                    cols_sb=cols_sb, eff_sb=eff_sb, shb_sb=shb_sb, lruc_sb=lruc_sb, identb=identb, identf=identf,
                    ones_bf=ones_bf, ones_f=ones_f, grstd=grstd, w_in=w_in, qng=qng, kng=kng, lru_wa=lru_wa,
                    lru_wx=lru_wx, c_negmask=c_negmask, qT_scr=qT_scr, kT_scr=kT_scr, v_scr=v_scr, ycl_scr=ycl_scr,
                    rstd_from_ssq=rstd_from_ssq, dbg_grstd=dbg_grstd, epsc=epsc, mod_factory=(mod_factory if l == 0 else None)), nblk)
                P.barrier()
                P.nosched = False
            TT = dict(cols_sb=cols_sb, eff_sb=eff_sb, shb_sb=shb_sb, identb=identb, identf=identf, ones_bf=ones_bf, ones_f=ones_f,
                      grstd=grstd, w_out=w_out, w_up=w_up, w_down=w_down, gates_scr=gates_scr, c_oh=c_oh, c_caus=c_caus,
                      qT_scr=qT_scr, kT_scr=kT_scr, v_scr=v_scr, ycl_scr=ycl_scr, x1_scr=x1_scr, rstd_from_ssq=rstd_from_ssq)
            if "B" in phases:
                phase_B(nc, P, top, sb, ps, l, x_src, TT, nblk)
                P.barrier()
            if "C" in phases:
                phase_C(nc, P, top, sb, ps, l, x_dst, TT, nblk)
                P.barrier()
        P.emit()
    return nc, P


def phase_A(nc, P, top, sb0, ps0, l, x_src, T, nblk):
    A = P.add
    sb = lambda es, name, shape, dt: sb0(es, f"{name}_A{l}", shape, dt)
    ps = lambda es, name, shape, dt: ps0(es, f"{name}_A{l}", shape, dt)
    cols_sb, eff_sb, shb_sb, lruc_sb = T["cols_sb"], T["eff_sb"], T["shb_sb"], T["lruc_sb"]
    identb, identf, ones_bf, ones_f, grstd, epsc = T["identb"], T["identf"], T["ones_bf"], T["ones_f"], T["grstd"], T["epsc"]

    def rstd2(dst, ssq, n, rtags, wtags, tmp, tmptag):
        A("act", lambda e: e.activation(out=tmp, in_=ssq, func=AF.Ln, scale=1.0 / n, bias=epsc[:, 0:1]), reads=rtags, writes=[tmptag], n=8)
        A("act", lambda e: e.activation(out=dst, in_=tmp, func=AF.Exp, scale=-0.5), reads=[tmptag], writes=wtags, n=8)

    with ExitStack() as es:
        dbl = lambda name, shape, dt: [sb(es, f"{name}{i}", shape, dt) for i in range(2)]
        w_sb = sb(es, "w_in_sb", [128, 8, DIN], BF16)
        brow = sb(es, "b_in_row", [1, 1536], BF16)
        bcol = sb(es, "b_in_col", [128, 10], F32)
        gq = sb(es, "gq", [128, 64], F32)
        gk = sb(es, "gk", [128, 64], F32)
        negm = sb(es, "negm", [128, 16, 16], F32)
        kmeanT = sb(es, "kmeanT", [64, 8, 16], F32)
        km_hi = sb(es, "km_hi", [64, 8, 16], BF16)
        km_lo = sb(es, "km_lo", [64, 8, 16], BF16)
        km_tmp = sb(es, "km_tmp", [64, 8], F32)
        wabd = sb(es, "wabd", [128, 2, 2, 128], BF16)
        wtmp = sb(es, "wtmp", [128, 2, 2, 64], F32)
        x_sbs = Rot([sb(es, f"xA{i}", [128, D], F32) for i in range(2)], "xA")
        junkx = sb(es, "junkx", [128, D], BF16)
        junkq = dbl("junkq", [128, 512], BF16)
        junkk = dbl("junkk", [128, 512], BF16)
        qraw = dbl("qraw", [128, 512], F32)
        kraw = dbl("kraw", [128, 512], F32)
        xn = dbl("xn", [128, D], BF16)
        xnT = dbl("xnT", [128, 8, 256], BF16)
        stx = dbl("stx", [128, 4], F32)
        stq = dbl("stq", [128, 3, 8], F32)
        stk = dbl("stk", [128, 3, 8], F32)
        stg = dbl("stg", [128, 4], F32)
        kf = dbl("kf", [128, 512], F32)
        kb = dbl("kb", [128, 512], BF16)
        qaug = dbl("qaug", [128, 8, 80], BF16)
        qT8 = dbl("qT8", [64, 8, 128], BF16)
        gm = dbl("gm", [128, 8, 16], F32)
        m8 = dbl("m8", [128, 8, 8], F32)
        msk = dbl("msk", [128, 8, 16], F32)
        v_sbs = Rot([sb(es, f"vA{i}", [128, 8, 128], BF16) for i in range(2)], "vA")
        kT_sbs = Rot([sb(es, f"kTA{i}", [64, 8, 128], BF16) for i in range(2)], "kTA")
        qT_sbs = Rot([sb(es, f"qTA{i}", [80, 8, 128], BF16) for i in range(2)], "qTA")
        fmS = [sb(es, "fmS0", [128, 10, 256], F32)] * 2
        cu = sb(es, "cu", [128, 2, 258], F32)
        lx = sb(es, "lx", [128, 2, 259], F32)
        hb = sb(es, "hb", [128, 2, 257], F32)
        ct = dbl("ct", [128, 256], F32)
        cy = dbl("cy", [128, 256], F32)
        lt = [[sb(es, f"lt{lc}_{k}", [128, 256], F32) for k in range(7)] for lc in range(2)]
        xrb = dbl("xrb", [128, 256], BF16)
        ysq = dbl("ysq", [128, 4, 256], BF16)
        ycl_sbs = Rot([sb(es, f"ycl{i}", [128, 4, 256], BF16) for i in range(2)], "ycl")
        tp_ps = ps(es, "tp_ps", [128, 8, 128], BF16)
        qkv_ps = [ps(es, f"qkv_ps{i}", [128, 512], F32) for i in range(3)]
        fm_pss = Rot([ps(es, f"fm_ps{i}", [128, 512], F32) for i in range(2)], "fm_ps")
        sm_ps = ps(es, "sm_ps", [128, 512], F32)
        gate_ps = sm_ps[:, 0:128].rearrange("p (h n) -> p h n", h=8)
        km_ps = sm_ps[0:64, 128:136]
        ss_ps = sm_ps[:, 136:138]
        bc_ps = sm_ps[:, 144:154]

        for c in range(8):
            A("pool", lambda e, c=c: e.dma_start(out=w_sb[:, c, :], in_=T["w_in"][l, c * 128:(c + 1) * 128, :], max_dma_last_dim=4096),
              writes=[("w_in", c)], dma_key=("w", c), cost=12000)
        A("sp", lambda e: e.dma_start(out=gq[:], in_=T["qng"][l:l + 1, :].partition_broadcast(128)), writes=["gq"], dma_key="a0")
        A("sp", lambda e: e.dma_start(out=gk[:], in_=T["kng"][l:l + 1, :].partition_broadcast(128)), writes=["gk"], dma_key="a1")
        A("sp", lambda e: e.dma_start(out=negm[:].rearrange("p a b -> p (a b)"), in_=T["c_negmask"][0:1, :].partition_broadcast(128)),
          writes=["negm"], dma_key="a2")
        A("dve", lambda e: e.scalar_tensor_tensor(out=gk[:], in0=gk[:], scalar=0.125, in1=gq[:], op0=ALU.mult, op1=ALU.mult),
          reads=["gq", "gk"], writes=["gk"])
        A("pool", lambda e: e.memset(kmeanT[:], 0.0), writes=["kmeanT"])
        A("pool", lambda e: e.memset(km_hi[:], 0.0), writes=["km_hi"])
        A("pool", lambda e: e.memset(km_lo[:], 0.0), writes=["km_lo"])
        A("pool", lambda e: e.memset(cu[:], 0.0), writes=["cu0", "cu1"])
        A("pool", lambda e: e.memset(lx[:], 0.0), writes=["lx0", "lx1"])
        A("pool", lambda e: e.memset(hb[:], 0.0), writes=["hb0", "hb1"])
        for i in range(2):
            A("pool", lambda e, i=i: e.memset(qaug[i][:], 0.0), writes=[f"qaug_b{i}", f"qaug_q{i}"])
        for k, vt in enumerate(v_sbs.t):
            A("pool", lambda e, vt=vt: e.memset(vt[:], 1.0), writes=[("vA", k)])
        A("pool", lambda e: e.memset(wabd[:], 0.0), writes=["wabd"])
        for g, wsrc in enumerate((T["lru_wa"], T["lru_wx"])):
            for hh in range(4):
                ch, hf = hh // 2, hh % 2
                A("sp", lambda e, g=g, wsrc=wsrc, hh=hh, ch=ch, hf=hf: e.dma_start(
                    out=wtmp[hf * 64:(hf + 1) * 64, g, ch, :], in_=wsrc[l, hh, :, :]), writes=[("wtmp", g, hh)], dma_key=("spk", g * 4 + hh))
                A("dve", lambda e, g=g, ch=ch, hf=hf: e.tensor_copy(out=wabd[hf * 64:(hf + 1) * 64, g, ch, hf * 64:(hf + 1) * 64],
                                                                   in_=wtmp[hf * 64:(hf + 1) * 64, g, ch, :]),
                  reads=[("wtmp", g, hh), "wabd"], writes=[("wabd", g, hh)])
        wabd_tags = [("wabd", g, hh) for g in range(2) for hh in range(4)]
        w_tags = [("w_in", c) for c in range(8)]

        for j in range(3):
            def mm(e, j=j):
                r = None
                for c in range(8):
                    r = e.matmul(qkv_ps[j][0:1, :], lhsT=shb_sb[:, l, 0, c:c + 1], rhs=w_sb[:, c, j * 512:(j + 1) * 512],
                                 start=(c == 0), stop=(c == 7))
                return r
            A("pe", mm, reads=w_tags + [("shb", l, 0)], writes=[("qkv_ps", j)])
            A("act", lambda e, j=j: e.copy(out=brow[:, j * 512:(j + 1) * 512], in_=qkv_ps[j][0:1, :]), reads=[("qkv_ps", j)], writes=["brow"], n=512)

        def mm(e):
            r = None
            for fc in range(10):
                for c in range(8):
                    r = e.matmul(bc_ps[:, fc:fc + 1], lhsT=w_sb[:, c, 1536 + fc * 128:1536 + (fc + 1) * 128],
                                 rhs=shb_sb[:, l, 0, c:c + 1], start=(c == 0), stop=(c == 7))
            return r
        A("pe", mm, reads=w_tags + [("shb", l, 0)], writes=["sm_ps"])
        A("dve", lambda e: e.tensor_copy(out=bcol[:], in_=bc_ps), reads=["sm_ps"], writes=["bcol"])
        for c in range(8):
            if c % 2 == 0:
                A("dve", lambda e, c=c: e.tensor_scalar(out=w_sb[:, c, :], in0=w_sb[:, c, :], scalar1=eff_sb[:, l, c:c + 1], scalar2=None, op0=ALU.mult),
                  reads=[("eff", l, 0, 0)], writes=[("w_in", c)], n=DIN)
            else:
                A("act", lambda e, c=c: e.activation(out=w_sb[:, c, :], in_=w_sb[:, c, :], func=AF.Copy, scale=eff_sb[:, l, c:c + 1]),
                  reads=[("eff", l, 0, 0)], writes=[("w_in", c)], n=DIN)

        def h3(ap):
            return ap.rearrange("p (h d) -> p h d", h=8)

        def tiles(st):
            bp = st % 2
            ctx = [dict(), dict()]

            def s0(i):
                t = 2 * st + i
                xt, xtag = x_sbs.next()
                ctx[i].update(t=t, xt=xt, xtag=xtag)
                A("sp", lambda e, xt=xt, t=t: e.dma_start(out=xt[:], in_=x_src[t * 128:(t + 1) * 128, :]), writes=[xtag], dma_key=xtag)
                A("act", lambda e, xt=xt, i=i: e.activation(out=junkx[:], in_=xt[:], func=AF.Square, accum_out=stx[i][:, 0:1]),
                  reads=[xtag], writes=["junkx", f"stx{i}"], n=D)
                rstd2(stx[i][:, 2:3], stx[i][:, 0:1], D, [f"stx{i}"], [f"stxr{i}"], stx[i][:, 1:2], f"stxt{i}")
                A("dve", lambda e, xt=xt, i=i: e.tensor_scalar(out=xn[i][:], in0=xt[:], scalar1=stx[i][:, 2:3], scalar2=None, op0=ALU.mult),
                  reads=[xtag, f"stxr{i}"], writes=[f"xn{i}"], n=D)

            def s1(i):
                def tr(e, i=i):
                    r = None
                    for c in range(8):
                        r = e.transpose(out=tp_ps[:, c, :], in_=xn[i][:, c * 128:(c + 1) * 128], identity=identb[:])
                    return r
                A("pe", tr, reads=[f"xn{i}", "identb"], writes=["tp_ps"])
                A("act", lambda e, i=i, bp=bp: e.copy(out=xnT[bp][:, :, i * 128:(i + 1) * 128], in_=tp_ps[:]), reads=["tp_ps"], writes=[("xnT", bp, i)], n=D)

            def s2(i):
                t = ctx[i]["t"]
                for j in range(3):
                    def mm(e, j=j, i=i, bp=bp):
                        for c in range(8):
                            e.matmul(qkv_ps[j][:], lhsT=xnT[bp][:, c, i * 128:(i + 1) * 128], rhs=w_sb[:, c, j * 512:(j + 1) * 512],
                                     start=(c == 0), stop=False)
                        return e.matmul(qkv_ps[j][:], lhsT=ones_bf[0:1, :], rhs=brow[0:1, j * 512:(j + 1) * 512], start=False, stop=True)
                    A("pe", mm, reads=w_tags + [("xnT", bp, i), "brow", "ones_bf"], writes=[("qkv_ps", j)], cost=2200)
                A("act", lambda e, i=i: e.copy(out=qraw[i][:], in_=qkv_ps[0][:]), reads=[("qkv_ps", 0)], writes=[f"qraw{i}"], n=512)
                A("act", lambda e, i=i: e.copy(out=kraw[i][:], in_=qkv_ps[1][:]), reads=[("qkv_ps", 1)], writes=[f"kraw{i}"], n=512)
                vt, vtag = v_sbs.next()
                A("act", lambda e, vt=vt: e.copy(out=vt[:, :, 0:64], in_=h3(qkv_ps[2][:])), reads=[("qkv_ps", 2)], writes=[vtag], n=512)
                A("sp", lambda e, vt=vt, t=t: e.dma_start(out=T["v_scr"][t, :, :], in_=vt[:].rearrange("p h d -> p (h d)")),
                  reads=[vtag], writes=[("v_scr", t)], dma_key=("st",) + vtag)

            def s3(i):
                A("act", lambda e, i=i: e.activation(out=junkq[i][:], in_=qraw[i][:], func=AF.Square), reads=[f"qraw{i}"], writes=[f"junkq{i}"], n=512)
                A("dve", lambda e, i=i: e.tensor_reduce(out=stq[i][:, 0, :], in_=h3(junkq[i][:]), axis=AX.X, op=ALU.add),
                  reads=[f"junkq{i}"], writes=[f"stq{i}"], n=512)
                rstd2(stq[i][:, 2, :], stq[i][:, 0, :], 64, [f"stq{i}"], [f"stqr{i}"], stq[i][:, 1, :], f"stqt{i}")
                A("dve", lambda e, i=i: e.tensor_tensor(out=qaug[i][:, :, 0:64], in0=h3(qraw[i][:]),
                                                        in1=stq[i][:, 2, :].unsqueeze(2).to_broadcast([128, 8, 64]), op=ALU.mult),
                  reads=[f"qraw{i}", f"stqr{i}"], writes=[f"qaug_q{i}"], n=512)
                A("act", lambda e, i=i: e.activation(out=junkk[i][:], in_=kraw[i][:], func=AF.Square), reads=[f"kraw{i}"], writes=[f"junkk{i}"], n=512)
                A("dve", lambda e, i=i: e.tensor_reduce(out=stk[i][:, 0, :], in_=h3(junkk[i][:]), axis=AX.X, op=ALU.add),
                  reads=[f"junkk{i}"], writes=[f"stk{i}"], n=512)
                rstd2(stk[i][:, 2, :], stk[i][:, 0, :], 64, [f"stk{i}"], [f"stkr{i}"], stk[i][:, 1, :], f"stkt{i}")
                A("dve", lambda e, i=i: e.tensor_tensor(out=h3(kf[i][:]), in0=h3(kraw[i][:]),
                                                        in1=stk[i][:, 2, :].unsqueeze(2).to_broadcast([128, 8, 64]), op=ALU.mult),
                  reads=[f"kraw{i}", f"stkr{i}"], writes=[f"kf{i}"], n=512)
                A("dve", lambda e, i=i: e.tensor_tensor(out=h3(kb[i][:]), in0=h3(kf[i][:]),
                                                        in1=gk[:].unsqueeze(1).to_broadcast([128, 8, 64]), op=ALU.mult),
                  reads=[f"kf{i}", "gk"], writes=[f"kb{i}"], n=512)

            def s4(i):
                def mm(e, i=i):
                    r = None
                    for h in range(8):
                        r = e.matmul(km_ps[:, h:h + 1], lhsT=kb[i][:, h * 64:(h + 1) * 64], rhs=ones_bf[:, 0:1], start=True, stop=True)
                    return r
                A("pe", mm, reads=[f"kb{i}", "ones_bf"], writes=["sm_ps"], cost=600)
                if i == 0:
                    A("dve", lambda e: e.tensor_scalar(out=kmeanT[:, :, st], in0=km_ps, scalar1=1.0 / 256, scalar2=None, op0=ALU.mult),
                      reads=["sm_ps"], writes=["kmeanT"], n=8)
                else:
                    A("dve", lambda e: e.scalar_tensor_tensor(out=kmeanT[:, :, st], in0=km_ps, scalar=1.0 / 256, in1=kmeanT[:, :, st],
                                                              op0=ALU.mult, op1=ALU.add),
                      reads=["sm_ps"], writes=["kmeanT"], n=8)
                    A("dve", lambda e: e.tensor_copy(out=km_hi[:, :, st], in_=kmeanT[:, :, st]), reads=["kmeanT"], writes=["km_hi"], n=8)
                    A("dve", lambda e: e.tensor_tensor(out=km_tmp[:], in0=kmeanT[:, :, st], in1=km_hi[:, :, st], op=ALU.subtract),
                      reads=["kmeanT", "km_hi"], writes=["km_tmp"], n=8)
                    A("dve", lambda e: e.tensor_copy(out=km_lo[:, :, st], in_=km_tmp[:]), reads=["km_tmp"], writes=["km_lo"], n=8)

            def s5(i):
                t = ctx[i]["t"]
                kTt, kTtag = kT_sbs.next()

                def tr(e, i=i):
                    r = None
                    for h in range(8):
                        r = e.transpose(out=tp_ps[0:64, h, :], in_=kb[i][:, h * 64:(h + 1) * 64], identity=identb[:])
                    return r
                A("pe", tr, reads=[f"kb{i}", "identb"], writes=["tp_ps"])
                A("act", lambda e, kTt=kTt: e.copy(out=kTt[:], in_=tp_ps[0:64, :, :]), reads=["tp_ps"], writes=[kTtag], n=D)
                A("sp", lambda e, kTt=kTt, t=t: e.dma_start(out=T["kT_scr"][:, :, t * 128:(t + 1) * 128], in_=kTt[:]),
                  reads=[kTtag], writes=[("kT_scr", t)], dma_key=("st",) + kTtag)

            def s6(i):
                if st < 1:
                    return

                def tr(e, i=i):
                    r = None
                    for h in range(8):
                        r = e.transpose(out=tp_ps[0:64, h, :], in_=qaug[i][:, h, 0:64], identity=identb[:])
                    return r
                A("pe", tr, reads=[f"qaug_q{i}", "identb"], writes=["tp_ps"])
                A("act", lambda e, i=i: e.copy(out=qT8[i][:], in_=tp_ps[0:64, :, :]), reads=["tp_ps"], writes=[f"qT8{i}"], n=D)

            def s7(i):
                if st < 1:
                    return

                def mm(e, i=i):
                    r = None
                    for h in range(8):
                        e.matmul(gate_ps[:, h, :], lhsT=qT8[i][:, h, :], rhs=km_hi[:, h, :], start=True, stop=False)
                        r = e.matmul(gate_ps[:, h, :], lhsT=qT8[i][:, h, :], rhs=km_lo[:, h, :], start=False, stop=True)
                    return r
                A("pe", mm, reads=[f"qT8{i}", "km_hi", "km_lo"], writes=["sm_ps"], cost=1000)
                A("dve", lambda e, i=i: e.tensor_tensor(out=gm[i][:], in0=gate_ps, in1=negm[:, st:st + 1, :].to_broadcast([128, 8, 16]), op=ALU.add),
                  reads=["sm_ps", "negm"], writes=[f"gm{i}"], n=128)
                for h in range(8):
                    A("dve", lambda e, h=h, i=i: e.max(out=m8[i][:, h, :], in_=gm[i][:, h, :]), reads=[f"gm{i}"], writes=[(f"m8{i}", h)], n=16)
                A("dve", lambda e, i=i: e.tensor_tensor(out=msk[i][:], in0=gm[i][:], in1=m8[i][:, :, 2:3].to_broadcast([128, 8, 16]), op=ALU.is_ge),
                  reads=[f"gm{i}"] + [(f"m8{i}", h) for h in range(8)], writes=[f"msk{i}"], n=128)
                A("dve", lambda e, i=i: e.tensor_scalar(out=qaug[i][:, :, 64:80], in0=msk[i][:], scalar1=BIG, scalar2=-BIG, op0=ALU.mult, op1=ALU.add),
                  reads=[f"msk{i}"], writes=[f"qaug_b{i}"], n=128)
                A("dve", lambda e, i=i: e.memset(qaug[i][:, :, 64 + st:65 + st], 0.0), reads=[f"qaug_b{i}"], writes=[f"qaug_b{i}"], n=8)

            def s8(i):
                t = ctx[i]["t"]
                qTt, qTtag = qT_sbs.next()

                def tr(e, i=i):
                    r = None
                    for h in range(8):
                        r = e.transpose(out=tp_ps[0:80, h, :], in_=qaug[i][:, h, :], identity=identb[:])
                    return r
                A("pe", tr, reads=[f"qaug_q{i}", f"qaug_b{i}", "identb"], writes=["tp_ps"])
                A("act", lambda e, qTt=qTt: e.copy(out=qTt[:], in_=tp_ps[0:80, :, :]), reads=["tp_ps"], writes=[qTtag], n=D)
                A("sp", lambda e, qTt=qTt, t=t: e.dma_start(out=T["qT_scr"][:, :, t * 128:(t + 1) * 128], in_=qTt[:]),
                  reads=[qTtag], writes=[("qT_scr", t)], dma_key=("st",) + qTtag)

            import os as _os7
            stages = (s0, s1, s2, s3, s4, s5, s6, s7, s8)
            tmode = _os7.environ.get("TILE_SEQ", "0")
            if tmode.startswith("g"):
                k_int = int(tmode[1:])
                for si in range(k_int):
                    for i in range(2):
                        stages[si](i)
                for i in range(2):
                    for si in range(k_int, 9):
                        stages[si](i)
            elif tmode == "1":
                for i in range(2):
                    for stage in stages:
                        stage(i)
            else:
                for stage in (s0, s1, s2, s3, s4, s5, s6, s7, s8):
                    for i in range(2):
                        stage(i)

        def fm(st):
            bp = st % 2
            F = fmS[bp]
            ycl_t, ycl_tag = ycl_sbs.next()
            for fc in (6, 7, 8, 9, 2, 4, 0, 3, 5, 1):
                bank, btag = fm_pss.next()

                def mm(e, bank=bank, fc=fc, bp=bp):
                    r = None
                    for c in range(8):
                        r = e.matmul(bank[:, 0:256], lhsT=w_sb[:, c, 1536 + fc * 128:1536 + (fc + 1) * 128], rhs=xnT[bp][:, c, :],
                                     start=(c == 0), stop=(c == 7))
                    return r
                A("pe", mm, reads=w_tags + [("xnT", bp, 0), ("xnT", bp, 1)], writes=[btag], cost=1100)
                if fc in (6, 7):
                    lc = fc - 6
                    A("act", lambda e, bank=bank, lc=lc, fc=fc: e.activation(out=lx[:, lc, 3:259], in_=bank[:, 0:256], func=AF.Identity, bias=bcol[:, fc:fc + 1]),
                      reads=[btag, "bcol"], writes=[f"lx{lc}"])
                else:
                    A("dve", lambda e, bank=bank, fc=fc, F=F: e.tensor_scalar(out=F[:, fc, :], in0=bank[:, 0:256], scalar1=bcol[:, fc:fc + 1], scalar2=None, op0=ALU.add),
                      reads=[btag, "bcol"], writes=[("fmS", fc)])
            for lc in range(2):
                xr, ea, sa, ei, uu, gz, yy = lt[lc]
                tg = lambda nm, lc=lc: f"{nm}{lc}"
                lxb, hbb = lx[:, lc, :], hb[:, lc, :]
                cw = [cols_sb[:, l, 62 + lc * 4 + k:62 + lc * 4 + k + 1] for k in range(4)]
                cb = cols_sb[:, l, 70 + lc:71 + lc]
                nba = lruc_sb[:, l, 0 + lc:1 + lc]
                nbx = lruc_sb[:, l, 2 + lc:3 + lc]
                sp8 = lruc_sb[:, l, 4 + lc:5 + lc]
                G = F[:, 8 + lc, :]
                gtag = ("fmS", 8 + lc)
                A("dve", lambda e, lxb=lxb, cw=cw, cb=cb, xr=xr: e.tensor_scalar(out=xr[:], in0=lxb[:, 0:256], scalar1=cw[0], scalar2=cb, op0=ALU.mult, op1=ALU.add),
                  reads=[tg("lx")], writes=[tg("xr")])
                for k in range(1, 4):
                    A("dve", lambda e, lxb=lxb, cw=cw, k=k, xr=xr: e.scalar_tensor_tensor(out=xr[:], in0=lxb[:, k:k + 256], scalar=cw[k], in1=xr[:],
                                                                                      op0=ALU.mult, op1=ALU.add),
                      reads=[tg("lx"), tg("xr")], writes=[tg("xr")])
                A("dve", lambda e, lxb=lxb: e.tensor_copy(out=lxb[:, 0:3], in_=lxb[:, 256:259]), reads=[tg("lx"), tg("xr")], writes=[tg("lx")], n=3)
                A("pool", lambda e, lc=lc, xr=xr: e.tensor_copy(out=xrb[lc][:], in_=xr[:]), reads=[tg("xr")], writes=[tg("xrb")], cost=1500)
                rb, rbtag = fm_pss.next()
                A("pe", lambda e, lc=lc, rb=rb: e.matmul(rb[:, 0:256], lhsT=wabd[:, 0, lc, :], rhs=xrb[lc][:], start=True, stop=True),
                  reads=[tg("xrb")] + wabd_tags, writes=[rbtag], cost=200)
                A("act", lambda e, nba=nba, rb=rb, ea=ea: e.activation(out=ea[:], in_=rb[:, 0:256], func=AF.Exp, scale=-1.0, bias=nba),
                  reads=[rbtag], writes=[tg("ea")])
                ib, ibtag = fm_pss.next()
                A("pe", lambda e, lc=lc, ib=ib: e.matmul(ib[:, 0:256], lhsT=wabd[:, 1, lc, :], rhs=xrb[lc][:], start=True, stop=True),
                  reads=[tg("xrb")] + wabd_tags, writes=[ibtag], cost=200)
                A("act", lambda e, nbx=nbx, ib=ib, ei=ei: e.activation(out=ei[:], in_=ib[:, 0:256], func=AF.Exp, scale=-1.0, bias=nbx),
                  reads=[ibtag], writes=[tg("ei")])
                A("act", lambda e, ea=ea: e.activation(out=ea[:], in_=ea[:], func=AF.Ln, bias=1.0), reads=[tg("ea")], writes=[tg("ea")])
                A("act", lambda e, ea=ea: e.activation(out=ea[:], in_=ea[:], func=AF.Exp, scale=-1.0), reads=[tg("ea")], writes=[tg("ea")])
                A("act", lambda e, ea=ea, sp8=sp8: e.activation(out=ea[:], in_=ea[:], func=AF.Exp, scale=sp8), reads=[tg("ea")], writes=[tg("ea")])
                A("act", lambda e, ea=ea, sa=sa: e.activation(out=sa[:], in_=ea[:], func=AF.Square), reads=[tg("ea")], writes=[tg("sa")])
                A("act", lambda e, sa=sa: e.activation(out=sa[:], in_=sa[:], func=AF.Ln, scale=-1.0, bias=1.0), reads=[tg("sa")], writes=[tg("sa")])
                A("act", lambda e, sa=sa: e.activation(out=sa[:], in_=sa[:], func=AF.Exp, scale=0.5), reads=[tg("sa")], writes=[tg("sa")])
                A("act", lambda e, ei=ei: e.activation(out=ei[:], in_=ei[:], func=AF.Ln, bias=1.0), reads=[tg("ei")], writes=[tg("ei")])
                A("act", lambda e, ei=ei: e.activation(out=ei[:], in_=ei[:], func=AF.Exp, scale=-1.0), reads=[tg("ei")], writes=[tg("ei")])
                A("dve", lambda e, ei=ei, xr=xr, uu=uu: e.tensor_tensor(out=uu[:], in0=ei[:], in1=xr[:], op=ALU.mult), reads=[tg("ei"), tg("xr")], writes=[tg("uu")])
                A("dve", lambda e, sa=sa, uu=uu: e.tensor_tensor(out=uu[:], in0=uu[:], in1=sa[:], op=ALU.mult), reads=[tg("uu"), tg("sa")], writes=[tg("uu")])
                A("dve", lambda e, hbb=hbb, ea=ea, uu=uu: e.tensor_tensor_scan(out=hbb[:, 1:257], data0=ea[:], data1=uu[:], initial=hbb[:, 0:1],
                                                                              op0=ALU.mult, op1=ALU.add),
                  reads=[tg("ea"), tg("uu"), tg("hb")], writes=[tg("hbh")], n=512)
                A("act", lambda e, G=G, gz=gz: e.activation(out=gz[:], in_=G, func=AF.Square), reads=[gtag], writes=[tg("gz")])
                A("dve", lambda e, gz=gz: e.tensor_scalar(out=gz[:], in0=gz[:], scalar1=0.044715, scalar2=1.0, op0=ALU.mult, op1=ALU.add),
                  reads=[tg("gz")], writes=[tg("gz")])
                A("dve", lambda e, gz=gz, G=G: e.tensor_tensor(out=gz[:], in0=gz[:], in1=G, op=ALU.mult), reads=[tg("gz"), gtag], writes=[tg("gz")])
                A("act", lambda e, gz=gz: e.activation(out=gz[:], in_=gz[:], func=AF.Exp, scale=-1.5957691216057308), reads=[tg("gz")], writes=[tg("gz")])
                A("act", lambda e, gz=gz: e.activation(out=gz[:], in_=gz[:], func=AF.Ln, bias=1.0), reads=[tg("gz")], writes=[tg("gz")])
                A("act", lambda e, gz=gz: e.activation(out=gz[:], in_=gz[:], func=AF.Exp, scale=-1.0), reads=[tg("gz")], writes=[tg("gz")])
                A("dve", lambda e, gz=gz, G=G: e.tensor_tensor(out=gz[:], in0=gz[:], in1=G, op=ALU.mult), reads=[tg("gz"), gtag], writes=[tg("gz")])
                A("dve", lambda e, hbb=hbb, gz=gz, yy=yy: e.tensor_tensor(out=yy[:], in0=hbb[:, 1:257], in1=gz[:], op=ALU.mult),
                  reads=[tg("hbh"), tg("gz")], writes=[tg("yy")])
                A("dve", lambda e, hbb=hbb: e.tensor_copy(out=hbb[:, 0:1], in_=hbb[:, 256:257]), reads=[tg("hbh"), tg("yy")], writes=[tg("hb")], n=1)
                A("pool", lambda e, lc=lc, ycl_t=ycl_t, yy=yy: e.tensor_copy(out=ycl_t[:, 2 + lc, :], in_=yy[:]), reads=[tg("yy")], writes=[ycl_tag + (2 + lc,)], cost=1500)
                A("pool", lambda e, lc=lc, yy=yy, bp=bp: e.tensor_tensor(out=ysq[bp][:, 2 + lc, :], in0=yy[:], in1=yy[:], op=ALU.mult), reads=[tg("yy")], writes=[("ysq", bp, 2 + lc)], cost=1000)
            for cc in range(2):
                cub = cu[:, cc, :]
                tg = lambda nm, cc=cc: f"{nm}{cc}"
                w0 = cols_sb[:, l, 56 + cc * 3 + 0:56 + cc * 3 + 1]
                w1 = cols_sb[:, l, 56 + cc * 3 + 1:56 + cc * 3 + 2]
                w2 = cols_sb[:, l, 56 + cc * 3 + 2:56 + cc * 3 + 3]
                Bt, Ct, Ut = ("fmS", 0 + cc), ("fmS", 2 + cc), ("fmS", 4 + cc)
                A("dve", lambda e, cub=cub, cc=cc, F=F: e.tensor_tensor(out=cub[:, 2:258], in0=F[:, 2 + cc, :], in1=F[:, 4 + cc, :], op=ALU.mult),
                  reads=[Ct, Ut], writes=[tg("cu")])
                A("dve", lambda e, cub=cub, w0=w0, cc=cc: e.tensor_scalar(out=ct[cc][:], in0=cub[:, 0:256], scalar1=w0, scalar2=None, op0=ALU.mult),
                  reads=[tg("cu")], writes=[tg("ct")])
                A("dve", lambda e, cub=cub, w1=w1, cc=cc: e.scalar_tensor_tensor(out=ct[cc][:], in0=cub[:, 1:257], scalar=w1, in1=ct[cc][:], op0=ALU.mult, op1=ALU.add),
                  reads=[tg("cu"), tg("ct")], writes=[tg("ct")])
                A("dve", lambda e, cub=cub, w2=w2, cc=cc: e.scalar_tensor_tensor(out=ct[cc][:], in0=cub[:, 2:258], scalar=w2, in1=ct[cc][:], op0=ALU.mult, op1=ALU.add),
                  reads=[tg("cu"), tg("ct")], writes=[tg("ct")])
                A("dve", lambda e, cub=cub: e.tensor_copy(out=cub[:, 0:2], in_=cub[:, 256:258]), reads=[tg("cu"), tg("ct")], writes=[tg("cu")], n=2)
                A("dve", lambda e, cc=cc, F=F: e.tensor_tensor(out=cy[cc][:], in0=F[:, 0 + cc, :], in1=ct[cc][:], op=ALU.mult),
                  reads=[Bt, tg("ct")], writes=[tg("cy")])
                A("pool", lambda e, cc=cc, ycl_t=ycl_t: e.tensor_copy(out=ycl_t[:, cc, :], in_=cy[cc][:]), reads=[tg("cy")], writes=[ycl_tag + (cc,)], cost=1500)
                A("pool", lambda e, cc=cc, bp=bp: e.tensor_tensor(out=ysq[bp][:, cc, :], in0=cy[cc][:], in1=cy[cc][:], op=ALU.mult), reads=[tg("cy")], writes=[("ysq", bp, cc)], cost=1000)
            for i in range(2):
                t = 2 * st + i

                def mm(e, i=i, bp=bp):
                    r = None
                    for g in range(2):
                        for c2 in range(2):
                            r = e.matmul(ss_ps[:, g:g + 1], lhsT=ysq[bp][:, 2 * g + c2, i * 128:(i + 1) * 128], rhs=ones_bf[:, 0:1],
                                         start=(c2 == 0), stop=(c2 == 1))
                    return r
                A("pe", mm, reads=[("ysq", bp, c) for c in range(4)] + ["ones_bf"], writes=["sm_ps"], cost=500)
                rstd2(grstd[:, t, :], ss_ps, 256, ["sm_ps"], [("grstd", t)], stg[i][:, 0:2], f"stg{i}")
            A("sp", lambda e, ycl_t=ycl_t, st=st: e.dma_start(
                out=T["ycl_scr"].rearrange("(c p) t -> p c t", p=128)[:, :, st * 256:(st + 1) * 256], in_=ycl_t[:]),
              reads=[ycl_tag + (c,) for c in range(4)], writes=[("ycl_scr", st)], dma_key=("st",) + ycl_tag)

        mgen = None
        if T.get("mod_factory") is not None:
            mgen = T["mod_factory"](lambda n_, s_, d_: sb(es, f"{n_}_m1", s_, d_), lambda n_, s_, d_: ps(es, f"{n_}_m1", s_, d_))
        tiles(0)
        for st in range(nblk):
            if st + 1 < nblk:
                tiles(st + 1)
            fm(st)
            if mgen is not None:
                next(mgen, None)
                next(mgen, None)
        if mgen is not None:
            for _ in mgen:
                pass


def phase_B(nc, P, top, sb0, ps0, l, x_src, T, nblk):
    A = P.add
    sb = lambda es, name, shape, dt: sb0(es, f"{name}_B{l}", shape, dt)
    ps = lambda es, name, shape, dt: ps0(es, f"{name}_B{l}", shape, dt)
    cols_sb, identb, ones_bf, ones_f, grstd = T["cols_sb"], T["identb"], T["ones_bf"], T["ones_f"], T["grstd"]
    rstd_from_ssq = T["rstd_from_ssq"]
    with ExitStack() as es:
        kT = sb(es, "kT_all", [128, 8, S], BF16)
        v_all = sb(es, "v_all", [128, NT, 1024], BF16)
        wo = sb(es, "w_out_sb", [128, 8, D], BF16)
        caus = sb(es, "caus", [128, 2, 256], BF16)
        qT_blks = Rot([sb(es, f"qTb{i}", [128, 8, 256], BF16) for i in range(2)], "qTb")
        ycl_blks = Rot([sb(es, f"yclb{i}", [128, 4, 256], BF16) for i in range(2)], "yclb")
        x_sbs = Rot([sb(es, f"xB{i}", [128, D], F32) for i in range(3)], "xB")
        g1bc = x_sbs.t[2]
        p_sbs = Rot([sb(es, f"pB{i}", [128, 2, 256], BF16) for i in range(4)], "pB")
        rdens = Rot([sb(es, f"rdB{i}", [64, 256], F32) for i in range(2)], "rdB")
        yattn = sb(es, "yattn", [128, 4, 256], F32)
        ya_bf = sb(es, "ya_bf", [128, 4, 256], BF16)
        ysq = sb(es, "ysqB", [128, 4, 256], BF16)
        stb = sb(es, "stb", [128, 8], F32)
        s_pss = Rot([ps(es, f"s_ps{i}", [128, 2, 256], F32) for i in range(3)], "s_ps")
        oT_pss = Rot([ps(es, f"oT_ps{i}", [128, 512], F32) for i in range(2)], "oT_ps")
        sm_ps = ps(es, "smB_ps", [128, 512], F32)
        op_pss = Rot([ps(es, f"op_ps{i}", [128, 512], F32) for i in range(2)], "op_ps")

        for c in range(8):
            A("pool", lambda e, c=c: e.dma_start(out=wo[:, c, :], in_=T["w_out"][l, c * 128:(c + 1) * 128, :]),
              writes=[("wo", c)], dma_key=("w", c))
        A("sp", lambda e: e.dma_start(out=g1bc[:], in_=T["gates_scr"][l, 0:1, :].partition_broadcast(128)), writes=[("xB", 2)], dma_key=("xB", 2))
        A("sp", lambda e: e.dma_start(out=caus[:], in_=T["c_caus"][:, :, :]), writes=["caus"], dma_key="a1")
        for c in range(8):
            eng = "dve"
            A(eng, lambda e, c=c: e.scalar_tensor_tensor(out=wo[:, c, :], in0=wo[:, c, :], scalar=cols_sb[:, l, 48 + c:49 + c], in1=g1bc[:],
                                                         op0=ALU.mult, op1=ALU.mult),
              reads=[("xB", 2)], writes=[("wo", c)], n=D)
        wo_tags = [("wo", c) for c in range(8)]
        for h in range(8):
            A("dve", lambda e, h=h: e.memset(kT[64:128, h, :], 0.0), writes=[("kToh", h)], n=2048)
        for k_, qt_ in enumerate(qT_blks.t):
            A("dve", lambda e, qt_=qt_: e.memset(qt_[:], 0.0), writes=[("qTb", k_)], n=1024)
        for h in range(8):
            A("sp", lambda e, h=h: e.dma_start(out=kT[64:80, h, :], in_=T["c_oh"][:, :]), writes=[("kToh", h)], dma_key=("spk", h))
        oh_tags = [("kToh", h) for h in range(8)]
        for qb in range(nblk):
            A("sp", lambda e, j=qb: e.dma_start(out=kT[0:64, :, j * 256:(j + 1) * 256], in_=T["kT_scr"][:, :, j * 256:(j + 1) * 256]),
              writes=[("kT", qb)], dma_key=("kT", qb % 4))
            A("sp", lambda e, j=qb: e.dma_start(out=v_all[:, 2 * j:2 * j + 2, :], in_=T["v_scr"][2 * j:2 * j + 2, :, :].rearrange("t p f -> p t f")),
              writes=[("v", qb)], dma_key=("v", qb % 4))
            qTb, qtag = qT_blks.next()
            yclb, ytag = ycl_blks.next()
            A("sp", lambda e, qTb=qTb, qb=qb: e.dma_start(out=qTb[0:80, :, :], in_=T["qT_scr"][:, :, qb * 256:(qb + 1) * 256]), writes=[qtag], dma_key=qtag)
            A("sp", lambda e, yclb=yclb, qb=qb: e.dma_start(
                out=yclb[:], in_=T["ycl_scr"].rearrange("(c p) t -> p c t", p=128)[:, :, qb * 256:(qb + 1) * 256]), writes=[ytag], dma_key=ytag)
            xts = []
            for i in range(2):
                t = 2 * qb + i
                xt, xtag = x_sbs.next()
                A("sp", lambda e, xt=xt, t=t: e.dma_start(out=xt[:], in_=x_src[t * 128:(t + 1) * 128, :]), writes=[xtag], dma_key=xtag)
                xts.append((xt, xtag))
            for h in range(8):
                oT, otag = oT_pss.next()
                for j in range(qb + 1):
                    own = (j == qb)
                    sp_, stag = s_pss.next()
                    if not own:
                        def mm(e, sp_=sp_, j=j, h=h, qTb=qTb):
                            r = None
                            for kk in range(2):
                                r = e.matmul(sp_[:, kk, :], lhsT=kT[:, h, (2 * j + kk) * 128:(2 * j + kk + 1) * 128], rhs=qTb[:, h, :],
                                             start=True, stop=True)
                            return r
                        A("pe", mm, reads=[("kT", j), ("kToh", h), qtag], writes=[stag])
                    else:
                        def mm(e, sp_=sp_, j=j, h=h, qTb=qTb):
                            r = None
                            for kk in range(2):
                                e.matmul(sp_[:, kk, :], lhsT=kT[:, h, (2 * j + kk) * 128:(2 * j + kk + 1) * 128], rhs=qTb[:, h, :],
                                         start=True, stop=False)
                                r = e.matmul(sp_[:, kk, :], lhsT=identb[:], rhs=caus[:, kk, :], start=False, stop=True)
                            return r
                        A("pe", mm, reads=[("kT", j), ("kToh", h), qtag, "caus", "identb"], writes=[stag])
                    pt, ptag = p_sbs.next()
                    A("act", lambda e, pt=pt, sp_=sp_: e.activation(out=pt[:], in_=sp_[:], func=AF.Exp), reads=[stag], writes=[ptag], cost=600)

                    def mm(e, pt=pt, oT=oT, j=j, h=h, qb=qb):
                        r = None
                        for kk in range(2):
                            r = e.matmul(oT[:, 0:256], lhsT=v_all[:, 2 * j + kk, h * 128:(h + 1) * 128], rhs=pt[:, kk, :],
                                         start=(j == 0 and kk == 0), stop=(j == qb and kk == 1))
                        return r
                    A("pe", mm, reads=[("v", j), ptag], writes=[otag], cost=280)
                rd, rdtag = rdens.next()
                A("dve", lambda e, rd=rd, oT=oT: e.reciprocal(out=rd[:], in_=oT[64:128, 0:256]), reads=[otag], writes=[rdtag], cost=1900)
                pb = (h % 2) * 64
                A("dve", lambda e, rd=rd, oT=oT, pb=pb, h=h: e.tensor_tensor(out=yattn[pb:pb + 64, h // 2, :], in0=oT[0:64, 0:256], in1=rd[:], op=ALU.mult),
                  reads=[otag, rdtag], writes=[("yattn", h)])
            ya_tags = [("yattn", h) for h in range(8)]
            A("pool", lambda e: e.tensor_copy(out=ya_bf[:], in_=yattn[:]), reads=ya_tags, writes=["ya_bf"])
            A("act", lambda e: e.activation(out=ysq[:], in_=yattn[:], func=AF.Square), reads=ya_tags, writes=["ysqB"])
            for i in range(2):
                t = 2 * qb + i
                xt, xtag = xts[i]

                def mm(e, i=i):
                    r = None
                    for c in range(4):
                        r = e.matmul(sm_ps[:, 0:1], lhsT=ysq[:, c, i * 128:(i + 1) * 128], rhs=ones_bf[:, 0:1], start=(c == 0), stop=(c == 3))
                    return r
                A("pe", mm, reads=["ysqB", "ones_bf"], writes=["smB_ps"])
                rstd_from_ssq(stb[:, 2:3], sm_ps[:, 0:1], 512, ["smB_ps"], ["stbr"], stb[:, 1:2], "stbt")
                for n in range(2):
                    nsl = slice(n * 512, (n + 1) * 512)
                    for g in range(3):
                        op_, optag = op_pss.next()
                        if g == 0:
                            srcs = [(ya_bf, c, c) for c in range(4)]
                            rtags = ["ya_bf"]
                            scal = stb[:, 2:3]
                            stag2 = ["stbr"]
                        else:
                            srcs = [(yclb, 2 * (g - 1) + c2, 4 + 2 * (g - 1) + c2) for c2 in range(2)]
                            rtags = [ytag]
                            scal = grstd[:, t, g - 1:g]
                            stag2 = []

                        def mm(e, op_=op_, srcs=srcs, i=i, nsl=nsl):
                            r = None
                            for k, (src, sc, wc) in enumerate(srcs):
                                r = e.matmul(op_[:], lhsT=src[:, sc, i * 128:(i + 1) * 128], rhs=wo[:, wc, nsl], start=(k == 0), stop=(k == len(srcs) - 1))
                            return r
                        A("pe", mm, reads=rtags + wo_tags, writes=[optag])
                        A("dve", lambda e, op_=op_, xt=xt, scal=scal, nsl=nsl: e.scalar_tensor_tensor(
                            out=xt[:, nsl], in0=op_[:], scalar=scal, in1=xt[:, nsl], op0=ALU.mult, op1=ALU.add),
                          reads=[optag, xtag] + stag2, writes=[xtag])
                A("sp", lambda e, xt=xt, t=t: e.dma_start(out=T["x1_scr"][t * 128:(t + 1) * 128, :], in_=xt[:]),
                  reads=[xtag], writes=[("x1_scr", t)], dma_key=("st",) + xtag)


def phase_C(nc, P, top, sb0, ps0, l, x_dst, T, nblk):
    A = P.add
    sb = lambda es, name, shape, dt: sb0(es, f"{name}_C{l}", shape, dt)
    ps = lambda es, name, shape, dt: ps0(es, f"{name}_C{l}", shape, dt)
    eff_sb, shb_sb, identb = T["eff_sb"], T["shb_sb"], T["identb"]
    rstd_from_ssq = T["rstd_from_ssq"]
    with ExitStack() as es:
        wu = sb(es, "w_up_sb", [128, 8, DFF], BF16)
        wd = sb(es, "w_dn_sb", [128, 32, D], BF16)
        bup = sb(es, "bup", [128, 32], F32)
        x_sbs = Rot([sb(es, f"xC{i}", [128, D], F32) for i in range(3)], "xC")
        junk = sb(es, "junkC", [128, D], BF16)
        xn2 = [sb(es, f"xnC{i}", [128, D], BF16) for i in range(2)]
        xnT2 = [sb(es, f"xnTC{i}", [128, 8, 256], BF16) for i in range(2)]
        hT2 = [sb(es, f"hT{i}", [128, 32, 256], BF16) for i in range(2)]
        rts = Rot([sb(es, f"rt{i}", [128, 256], BF16) for i in range(3)], "rt")
        stc = sb(es, "stc", [128, 8], F32)
        tp_ps = ps(es, "tpC_ps", [128, 8, 128], BF16)
        up_pss = Rot([ps(es, f"up_ps{i}", [128, 512], F32) for i in range(3)], "up_ps")
        dn_pss = Rot([ps(es, f"dn_ps{i}", [128, 512], F32) for i in range(2)], "dn_ps")
        sm_ps = ps(es, "smC_ps", [128, 512], F32)

        for c in range(8):
            A("pool", lambda e, c=c: e.dma_start(out=wu[:, c, :], in_=T["w_up"][l, c * 128:(c + 1) * 128, :], max_dma_last_dim=4096),
              writes=[("wu", c)], dma_key=("w", c))
        for f in range(32):
            A("pool", lambda e, f=f: e.dma_start(out=wd[:, f, :], in_=T["w_down"][l, f * 128:(f + 1) * 128, :]),
              writes=[("wd", f)], dma_key=("wdk", f % 8))
        g2t, g2tag = x_sbs.t[1], ("xC", 1)
        A("sp", lambda e: e.dma_start(out=g2t[:], in_=T["gates_scr"][l, 1:2, :].partition_broadcast(128)), writes=[g2tag], dma_key=g2tag)
        wu_tags = [("wu", c) for c in range(8)]
        wd_tags = [("wd", f) for f in range(32)]

        def mm(e):
            r = None
            for f in range(32):
                for c in range(8):
                    r = e.matmul(sm_ps[:, f:f + 1], lhsT=wu[:, c, f * 128:(f + 1) * 128], rhs=shb_sb[:, l, 1, c:c + 1], start=(c == 0), stop=(c == 7))
            return r
        A("pe", mm, reads=wu_tags, writes=["smC_ps"])
        A("dve", lambda e: e.tensor_copy(out=bup[:], in_=sm_ps[:, 0:32]), reads=["smC_ps"], writes=["bup"])
        for c in range(8):
            A("act", lambda e, c=c: e.activation(out=wu[:, c, :], in_=wu[:, c, :], func=AF.Copy, scale=eff_sb[:, l, 16 + c:17 + c]),
              reads=[], writes=[("wu", c)], n=DFF)
        for f in range(32):
            eng = "pool" if f % 4 == 3 else "dve"
            A(eng, lambda e, f=f: e.tensor_tensor(out=wd[:, f, :], in0=wd[:, f, :], in1=g2t[:], op=ALU.mult), reads=[g2tag], writes=[("wd", f)], n=D)

        for st in range(nblk):
            xts = []
            bp = st % 2
            xnT, hT = xnT2[bp], hT2[bp]
            for i in range(2):
                xn = xn2[i]
                t = 2 * st + i
                xt, xtag = x_sbs.next()
                xts.append((xt, xtag))
                A("sp", lambda e, xt=xt, t=t: e.dma_start(out=xt[:], in_=T["x1_scr"][t * 128:(t + 1) * 128, :]), writes=[xtag], dma_key=xtag)
                A("act", lambda e, xt=xt: e.activation(out=junk[:], in_=xt[:], func=AF.Square, accum_out=stc[:, 0:1]),
                  reads=[xtag], writes=["junkC", "stc"])
                rstd_from_ssq(stc[:, 2:3], stc[:, 0:1], D, ["stc"], ["stcr"], stc[:, 1:2], "stct")
                A("dve", lambda e, xt=xt, xn=xn: e.tensor_scalar(out=xn[:], in0=xt[:], scalar1=stc[:, 2:3], scalar2=None, op0=ALU.mult),
                  reads=[xtag, "stcr"], writes=[f"xnC{i}"], n=D)

                def tr(e, xn=xn):
                    r = None
                    for c in range(8):
                        r = e.transpose(out=tp_ps[:, c, :], in_=xn[:, c * 128:(c + 1) * 128], identity=identb[:])
                    return r
                A("pe", tr, reads=[f"xnC{i}", "identb"], writes=["tpC_ps"])
                A("act", lambda e, i=i, xnT=xnT: e.copy(out=xnT[:, :, i * 128:(i + 1) * 128], in_=tp_ps[:]), reads=["tpC_ps"], writes=[("xnTC", bp, i)], n=D)
            for f in range(32):
                up, uptag = up_pss.next()

                def mm(e, up=up, f=f, xnT=xnT):
                    r = None
                    for c in range(8):
                        r = e.matmul(up[:, 0:256], lhsT=wu[:, c, f * 128:(f + 1) * 128], rhs=xnT[:, c, :], start=(c == 0), stop=(c == 7))
                    return r
                A("pe", mm, reads=wu_tags + [("xnTC", bp, 0), ("xnTC", bp, 1)], writes=[uptag], cost=1000)
                rt, rttag = rts.next()
                A("dve", lambda e, up=up, rt=rt, f=f: e.tensor_scalar(out=rt[:], in0=up[:, 0:256], scalar1=bup[:, f:f + 1], scalar2=0.0, op0=ALU.add, op1=ALU.max),
                  reads=[uptag, "bup"], writes=[rttag])
                A("act", lambda e, rt=rt, f=f, hT=hT: e.activation(out=hT[:, f, :], in_=rt[:], func=AF.Square), reads=[rttag], writes=[("hT", bp, f)])
            hT_tags = [("hT", bp, f) for f in range(32)]
            for i in range(2):
                t = 2 * st + i
                xt, xtag = xts[i]
                for n in range(2):
                    nsl = slice(n * 512, (n + 1) * 512)
                    dn, dntag = dn_pss.next()

                    def mm(e, dn=dn, i=i, nsl=nsl, hT=hT):
                        r = None
                        for f in range(32):
                            r = e.matmul(dn[:], lhsT=hT[:, f, i * 128:(i + 1) * 128], rhs=wd[:, f, nsl], start=(f == 0), stop=(f == 31))
                        return r
                    A("pe", mm, reads=hT_tags + wd_tags, writes=[dntag], cost=7200)
                    A("dve", lambda e, dn=dn, xt=xt, nsl=nsl: e.tensor_tensor(out=xt[:, nsl], in0=dn[:], in1=xt[:, nsl], op=ALU.add),
                      reads=[dntag, xtag], writes=[xtag])
                A("sp", lambda e, xt=xt, t=t: e.dma_start(out=x_dst[t * 128:(t + 1) * 128, :], in_=xt[:]),
                  reads=[xtag], writes=[("x_dst", t)], dma_key=("st",) + xtag)


def _consts():
    bf = ml_dtypes.bfloat16
    identb = np.eye(128, dtype=np.float32).astype(bf)
    identf = np.eye(128, dtype=np.float32)
    oh = np.zeros((16, S), np.float32)
    for j in range(16):
        oh[j, j * 256:(j + 1) * 256] = 1.0
    negmask = np.zeros((16, 16), np.float32)
    for own in range(16):
        negmask[own, own:] = -1e30
    kk = np.arange(128)[:, None]
    qq = np.arange(128)[None, :]
    tri = np.where(kk <= qq, 0.0, -BIG).astype(np.float32)
    caus = np.zeros((128, 2, 256), np.float32)
    caus[:, 0, 0:128] = tri
    caus[:, 1, 0:128] = -BIG
    caus[:, 1, 128:256] = tri
    return dict(c_identb=identb, c_identf=identf, c_oh=oh.astype(bf), c_negmask=negmask.reshape(1, 256),
                c_caus=caus.astype(bf))


def _col(v):
    v = np.asarray(v, np.float32)
    return np.ascontiguousarray(v.reshape(-1, 128).T)


def make_in_maps(inputs, n_cores=8):
    f = lambda k: np.ascontiguousarray(np.asarray(inputs[k], np.float32))
    cols = np.zeros((2, 128, NCOLS), np.float32)
    for l in range(2):
        b = f("b_ada")[l]
        parts = [_col(f("ln1_g")[l]), _col(f("ln2_g")[l]),
                 _col(b[0:1024]), _col(b[1024:2048]), _col(b[3072:4096]), _col(b[4096:5120]),
                 _col(f("mix_norm_g")[l])]
        scw = f("sc_w")[l]
        parts.append(np.concatenate([np.stack([scw[k, cc * 128:(cc + 1) * 128] for k in range(3)], 1) for cc in range(2)], 1))
        lcw = f("lru_conv_w")[l]
        parts.append(np.concatenate([np.stack([lcw[k, cc * 128:(cc + 1) * 128] for k in range(4)], 1) for cc in range(2)], 1))
        parts += [_col(f("lru_conv_b")[l]), _col(f("lru_ba")[l]), _col(f("lru_bx")[l]), _col(f("lru_lambda")[l])]
        cols[l] = np.concatenate(parts, 1)
    shared = dict(cols=cols, b_ada=f("b_ada"), w_ada=f("w_ada"), w_in=f("w_in"), q_norm_g=f("q_norm_g"), k_norm_g=f("k_norm_g"),
                  lru_wa=f("lru_wa"), lru_wx=f("lru_wx"), w_out=f("w_out"), w_up=f("w_up"), w_down=f("w_down"))
    shared.update(_consts())
    x = f("x")
    c = f("c")
    maps = []
    for b in range(n_cores):
        m = dict(shared)
        m["x"] = x[b]
        m["ccol"] = _col(c[b])
        maps.append(m)
    return maps


_NC = None


def kernel(**inputs):
    global _NC
    if _NC is None:
        _NC = build_program()[0]
    maps = make_in_maps(inputs)
    res = run_bass_kernel_spmd(_NC, maps, core_ids=list(range(8)))
    return np.stack([np.asarray(r["out"], np.float32) for r in res.results], 0)
```

```python
import numpy as np
import ml_dtypes
from contextlib import ExitStack
import concourse.bass as bass
import concourse.mybir as mybir
from concourse.bass_utils import run_bass_kernel_spmd

F32 = mybir.dt.float32
BF16 = mybir.dt.bfloat16
AF = mybir.ActivationFunctionType
ALU = mybir.AluOpType
AX = mybir.AxisListType

S = 4096
D = 1024
NT = 32
NB = 16
DIN = 2816
DFF = 4096
BIG = 30000.0
EPS = 1e-6
NCOLS = 78
ENGS = ("pe", "act", "dve", "pool", "sp")
import os as _osg
FP32_GUARD = _osg.environ.get("FP32_GUARD", "1") == "1"


class Prog:
    def __init__(self, nc, strict=True):
        self.nc = nc
        self.strict = strict
        self.ops = []
        self.last_w = {}
        self.readers = {}
        self.last_dma = {}
        self.keymap = {}
        self.nosched = False
        self.phase = 0

    def add(self, eng, fn, reads=(), writes=(), dma_key=None, n=256, cost=None):
        if cost is None:
            if dma_key is not None:
                cost = 3000.0
            elif eng == "pe":
                cost = 400.0
            elif eng == "act":
                cost = 320.0 + n / 1.4
            elif eng == "dve":
                cost = 250.0 + n / 0.96
            else:
                cost = 300.0 + n / 0.5
        if dma_key is not None:
            cls = "W" if eng == "pool" else "H"
            kk = (cls, dma_key)
            if kk not in self.keymap:
                self.keymap[kk] = (cls, sum(1 for q in self.keymap if q[0] == cls))
            dma_key = self.keymap[kk]
        i = len(self.ops)
        deps = set()
        for t in reads:
            if t in self.last_w:
                deps.add(self.last_w[t])
        for t in writes:
            if t in self.last_w:
                deps.add(self.last_w[t])
            for r in self.readers.get(t, ()):
                deps.add(r)
        if dma_key is not None:
            if dma_key in self.last_dma:
                deps.add(self.last_dma[dma_key])
            self.last_dma[dma_key] = i
        deps.discard(i)
        for t in reads:
            self.readers.setdefault(t, []).append(i)
        for t in writes:
            self.last_w[t] = i
            self.readers[t] = []
        self.ops.append(dict(eng=eng, fn=fn, deps=sorted(deps), dma_key=dma_key,
                             phase=self.phase, barrier=False, cost=float(cost), nosched=self.nosched))
        return i

    def barrier(self):
        self.ops.append(dict(eng=None, fn=None, deps=[], dma_key=None,
                             phase=self.phase, barrier=True))
        self.phase += 1
        self.last_w = {}
        self.readers = {}
        self.last_dma = {}
        self.keymap = {}

    def schedule(self, window=48):
        ops = self.ops
        n = len(ops)
        order = []
        start = 0
        while start < n:
            end = start
            while end < n and not ops[end]["barrier"]:
                end += 1
            ids = list(range(start, end))
            import os as _os2
            sp_ = _os2.environ.get("SCHED_PHASES")
            if ids and sp_ is not None and str(ops[ids[0]]["phase"]) not in sp_.split(","):
                order.extend(ids)
            elif ids:
                fz = set(_os2.environ.get("SCHED_FREEZE", "").split(","))
                if ops[ids[0]].get("nosched"):
                    fz |= set(_os2.environ.get("PHASEA_FREEZE", "").split(","))
                order.extend(self._sched_phase(ids, window, fz))
            if end < n:
                order.append(end)
            start = end + 1
        remap = {old: new for new, old in enumerate(order)}
        newops = []
        for old in order:
            o = ops[old]
            o["deps"] = sorted(remap[d] for d in o["deps"])
            newops.append(o)
        self.ops = newops

    def _sched_phase(self, ids, window, freeze=()):
        ops = self.ops
        self._freeze = set(freeze)
        idset = set(ids)
        queues = {e: [i for i in ids if ops[i]["eng"] == e] for e in ENGS}
        qpos = {e: 0 for e in ENGS}
        scheduled = {}
        eng_free = {e: 0.0 for e in ENGS}
        out = []
        remaining = len(ids)
        taken = set()
        while remaining:
            best = None
            for e in ENGS:
                q = queues[e]
                p = qpos[e]
                while p < len(q) and q[p] in taken:
                    p += 1
                qpos[e] = p
                cnt = 0
                k = p
                win = 1 if e in self._freeze else window
                while k < len(q) and cnt < win:
                    i = q[k]
                    k += 1
                    if i in taken:
                        continue
                    cnt += 1
                    ok = True
                    rt = 0.0
                    for d in ops[i]["deps"]:
                        if d in idset:
                            if d not in scheduled:
                                ok = False
                                break
                            if scheduled[d] > rt:
                                rt = scheduled[d]
                    if not ok:
                        continue
                    st = max(eng_free[e], rt)
                    if best is None or st < best[0] - 1e-9 or (abs(st - best[0]) <= 1e-9 and i < best[1]):
                        best = (st, i, e)
                    if rt <= eng_free[e]:
                        break
            assert best is not None, "scheduler deadlock"
            st, i, e = best
            o = ops[i]
            if o["dma_key"] is not None:
                eng_free[e] = st + 120.0
                scheduled[i] = st + o["cost"]
            else:
                eng_free[e] = st + o["cost"]
                scheduled[i] = st + o["cost"] + 150.0
            taken.add(i)
            out.append(i)
            remaining -= 1
        self.est_ns = getattr(self, "est_ns", 0.0) + max(scheduled.values())
        self.est_phase = getattr(self, "est_phase", []) + [(ops[ids[0]]["phase"], max(scheduled.values()), dict(eng_free))]
        return out

    def emit(self):
        nc = self.nc
        import os as _os1
        if _os1.environ.get("SCHED", "1") == "1":
            self.schedule()
        ops = self.ops
        nph = self.phase + 1
        strict = self.strict

        def same_eng_free(od, o):
            return od["eng"] == o["eng"] and o["dma_key"] is None and (od["eng"] == "pe" or not strict)

        need = [False] * len(ops)
        for i, o in enumerate(ops):
            if o["barrier"]:
                continue
            for d in o["deps"]:
                od = ops[d]
                if od["dma_key"] is not None:
                    continue
                if same_eng_free(od, o):
                    continue
                need[d] = True
        last_in_phase = {}
        for i, o in enumerate(ops):
            if o["barrier"] or o["dma_key"] is not None:
                continue
            last_in_phase[(o["eng"], o["phase"])] = i
        for i in last_in_phase.values():
            need[i] = True
        cnt = {}
        dcnt = {}
        val = [None] * len(ops)
        dma_keys = []
        for i, o in enumerate(ops):
            if o["barrier"]:
                continue
            if o["dma_key"] is not None:
                k = o["dma_key"]
                if k not in dcnt:
                    dcnt[k] = 0
                    dma_keys.append(k)
                dcnt[k] += 16
                val[i] = dcnt[k]
            elif need[i]:
                k = (o["eng"], o["phase"])
                cnt[k] = cnt.get(k, 0) + 1
                val[i] = cnt[k]
        esem = {}
        for (e, ph) in sorted(cnt.keys(), key=lambda t: (t[1], t[0])):
            esem[(e, ph)] = nc.alloc_semaphore(f"s_{e}_{ph}")
        dsem = {k: nc.alloc_semaphore(f"d_{j}") for j, k in enumerate(dma_keys)}
        self.n_sems = len(esem) + len(dsem)
        final_cnt = dict(cnt)
        per = {e: [] for e in ENGS}
        for i, o in enumerate(ops):
            if o["barrier"]:
                for e in ENGS:
                    per[e].append(i)
            else:
                per[o["eng"]].append(i)
        dma_upto = {}
        run = {}
        for i, o in enumerate(ops):
            if o["barrier"]:
                dma_upto[i] = dict(run)
            elif o["dma_key"] is not None:
                run[o["dma_key"]] = val[i]
        dma_final = dict(run)

        def gen(e):
            def body(eng):
                waited = {}

                def w(sem, name, v):
                    if waited.get(name, 0) >= v:
                        return
                    waited[name] = v
                    eng.wait_ge(sem, v)

                for i in per[e]:
                    o = ops[i]
                    if o["barrier"]:
                        ph = o["phase"]
                        for e2 in ("pe", "act", "dve", "pool"):
                            v = final_cnt.get((e2, ph), 0)
                            if v:
                                w(esem[(e2, ph)], (e2, ph), v)
                        for k, v in dma_upto[i].items():
                            w(dsem[k], k, v)
                        continue
                    for d in o["deps"]:
                        od = ops[d]
                        if od["dma_key"] is not None:
                            w(dsem[od["dma_key"]], od["dma_key"], val[d])
                        else:
                            if same_eng_free(od, o):
                                continue
                            k = (od["eng"], od["phase"])
                            w(esem[k], k, val[d])
                    inst = o["fn"](eng)
                    if o["dma_key"] is not None:
                        inst.then_inc(dsem[o["dma_key"]], 16)
                    elif need[i]:
                        inst.then_inc(esem[(e, o["phase"])], 1)
                if e == "sp":
                    for k, v in dma_final.items():
                        w(dsem[k], k, v)
                    for (e2, ph), v in final_cnt.items():
                        w(esem[(e2, ph)], (e2, ph), v)
            return body

        with nc.Block() as block:
            block.tensor(gen("pe"))
            block.scalar(gen("act"))
            block.vector(gen("dve"))
            block.gpsimd(gen("pool"))
            block.sync(gen("sp"))


class Rot:
    def __init__(self, tensors, name):
        self.t = tensors
        self.name = name
        self.i = 0

    def next(self):
        k = self.i % len(self.t)
        self.i += 1
        return self.t[k], (self.name, k)


def build_program(n_layers=2, phases="ABC", debug=False, nblk=NB):
    nc = bass.Bass("TRN2", target_bir_lowering=False)
    dbg_kind = "ExternalOutput" if debug else "Internal"

    def din(name, shape, dt=F32):
        return nc.dram_tensor(name, list(shape), dt, kind="ExternalInput").ap()

    def dscr(name, shape, dt=F32):
        return nc.dram_tensor(name, list(shape), dt, kind=dbg_kind).ap()

    x_in = din("x", [S, D])
    ccol = din("ccol", [128, 8])
    cols = din("cols", [2, 128, NCOLS])
    bada = din("b_ada", [2, 6 * D])
    w_ada = din("w_ada", [2, D, 6 * D])
    w_in = din("w_in", [2, D, DIN])
    qng = din("q_norm_g", [2, 64])
    kng = din("k_norm_g", [2, 64])
    lru_wa = din("lru_wa", [2, 4, 64, 64])
    lru_wx = din("lru_wx", [2, 4, 64, 64])
    w_out = din("w_out", [2, D, D])
    w_up = din("w_up", [2, D, DFF])
    w_down = din("w_down", [2, DFF, D])
    c_identb = din("c_identb", [128, 128], BF16)
    c_identf = din("c_identf", [128, 128], F32)
    c_oh = din("c_oh", [16, S], BF16)
    c_negmask = din("c_negmask", [1, 256], F32)
    c_caus = din("c_caus", [128, 2, 256], BF16)
    out = nc.dram_tensor("out", [S, D], F32, kind="ExternalOutput").ap()

    gates_scr = dscr("gates_scr", [2, 2, D])
    qT_scr = dscr("qT_scr", [80, 8, S], BF16)
    kT_scr = dscr("kT_scr", [64, 8, S], BF16)
    v_scr = dscr("v_scr", [NT, 128, 1024], BF16)
    ycl_scr = dscr("ycl_scr", [512, S], BF16)
    x1_scr = dscr("x1_scr", [S, D])
    xmid_scr = dscr("xmid_scr", [S, D])
    dbg_grstd = dscr("dbg_grstd", [128, NT * 2]) if debug else None

    import os as _os0
    P = Prog(nc, strict=(_os0.environ.get('STRICT', '1') == '1'))
    A = P.add

    with ExitStack() as top:
        def sb(es, name, shape, dt):
            return es.enter_context(nc.sbuf_tensor(name, list(shape), dt))

        def ps(es, name, shape, dt):
            return es.enter_context(nc.psum_tensor(name, list(shape), dt))

        cols_sb = sb(top, "cols_sb", [128, 2, NCOLS], F32)
        eff_sb = sb(top, "eff_sb", [128, 2, 32], F32)
        shb_sb = sb(top, "shb_sb", [128, 2, 2, 8], BF16)
        lruc_sb = sb(top, "lruc_sb", [128, 2, 8], F32)
        identb = sb(top, "identb", [128, 128], BF16)
        identf = sb(top, "identf", [128, 128], F32)
        ones_bf = sb(top, "ones_bf", [128, 128], BF16)
        ones_f = sb(top, "ones_f", [128, 128], F32)
        grstd = sb(top, "grstd", [128, NT, 2], F32)
        epsc = sb(top, "epsc", [128, 1], F32)

        A("sp", lambda e: e.dma_start(out=identb[:], in_=c_identb[:, :]), writes=["identb"], dma_key="c0")
        A("sp", lambda e: e.dma_start(out=identf[:], in_=c_identf[:, :]), writes=["identf"], dma_key="c1")
        A("sp", lambda e: e.dma_start(out=cols_sb[:], in_=cols.rearrange("l p n -> p l n")), writes=["cols"], dma_key="c2")
        A("pool", lambda e: e.memset(ones_bf[:], 1.0), writes=["ones_bf"])
        A("pool", lambda e: e.memset(ones_f[:], 1.0), writes=["ones_f"])
        A("pool", lambda e: e.memset(epsc[:], EPS), writes=["epsc"])
        if debug:
            A("pool", lambda e: e.memset(grstd[:], 0.0), writes=["grstd_init"])

        def rstd_from_ssq(dst, ssq, n, tags_r, tags_w, tmp, tmptag):
            A("dve", lambda e: e.tensor_scalar(out=tmp, in0=ssq, scalar1=1.0 / n, scalar2=EPS, op0=ALU.mult, op1=ALU.add),
              reads=tags_r, writes=[tmptag])
            A("act", lambda e: e.activation(out=tmp, in_=tmp, func=AF.Ln), reads=[tmptag], writes=[tmptag])
            A("act", lambda e: e.activation(out=dst, in_=tmp, func=AF.Exp, scale=-0.5), reads=[tmptag], writes=tags_w)

        cact = sb(top, "cact", [128, 8], BF16)

        def make_mod_gen(sbf, psf, layers, BW=256):
            wa = [sbf(f"wa{i}", [128, 8, BW], BF16) for i in range(2)]
            modc = sbf("modc", [128, 32], F32)
            grow = sbf("grow", [1, 2, D], F32)
            brow = sbf("brow", [1, 2, D], F32)
            lam = sbf("lam", [128, 2], F32)
            bank = psf("mod_ps", [128, 512], F32)
            mod_ps = bank[:, 0:32]
            g_ps = bank[0:1, 256:256 + BW]
            per = D // BW

            def assemble(l, k2):
                A("dve", lambda e: e.tensor_tensor(out=modc[:, 16 * k2:16 * k2 + 16], in0=mod_ps[:, 16 * k2:16 * k2 + 16],
                                                   in1=cols_sb[:, l, 16 + 16 * k2:32 + 16 * k2], op=ALU.add),
                  reads=["modbank", "cols"], writes=[("modc", k2)], n=16)
                A("dve", lambda e: e.scalar_tensor_tensor(
                    out=eff_sb[:, l, 16 * k2:16 * k2 + 8], in0=modc[:, 16 * k2 + 8:16 * k2 + 16], scalar=1.0,
                    in1=cols_sb[:, l, 8 * k2:8 * k2 + 8], op0=ALU.add, op1=ALU.mult),
                  reads=[("modc", k2), "cols"], writes=[("eff", l, k2, 0)], n=8)
                A("dve", lambda e: e.tensor_copy(out=eff_sb[:, l, 16 * k2 + 8:16 * k2 + 16], in_=modc[:, 16 * k2:16 * k2 + 8]),
                  reads=[("modc", k2)], writes=[("eff", l, k2, 1)], n=8)
                A("dve", lambda e: e.tensor_copy(out=shb_sb[:, l, k2, :], in_=modc[:, 16 * k2:16 * k2 + 8]),
                  reads=[("modc", k2)], writes=[("shb", l, k2)], n=8)

            def gen():
                for l in layers:
                    A("dve", lambda e, l=l: e.tensor_scalar(out=lruc_sb[:, l, 0:4], in0=cols_sb[:, l, 72:76], scalar1=-1.0, scalar2=None, op0=ALU.mult),
                      reads=["cols"], writes=[("lruc", l, 0)], n=4)
                    A("act", lambda e, l=l: e.activation(out=lam[:], in_=cols_sb[:, l, 76:78], func=AF.Exp, scale=-1.0),
                      reads=["cols"], writes=["lam"], n=2)
                    A("act", lambda e: e.activation(out=lam[:], in_=lam[:], func=AF.Ln, bias=1.0), reads=["lam"], writes=["lam"], n=2)
                    A("dve", lambda e, l=l: e.tensor_scalar(out=lruc_sb[:, l, 4:6], in0=lam[:], scalar1=-8.0, scalar2=None, op0=ALU.mult),
                      reads=["lam"], writes=[("lruc", l, 1)], n=2)
                    for g_ in range(2):
                        A("sp", lambda e, g_=g_, l=l: e.dma_start(out=brow[:, g_, :], in_=bada[l:l + 1, (2 + 3 * g_) * D:(3 + 3 * g_) * D]),
                          writes=[("brow", g_)], dma_key=("c4", g_))
                    for blk in range(6 * per):
                        sec, off = blk // per, (blk % per) * BW
                        wt, wtag = wa[blk % 2], ("wa", blk % 2)
                        A("pool", lambda e, wt=wt, blk=blk, l=l: e.dma_start(
                            out=wt[:], in_=w_ada[l, :, blk * BW:(blk + 1) * BW].rearrange("(c p) n -> p c n", p=128)),
                          writes=[wtag], dma_key=("wa", blk % 2), cost=6000)
                        if sec in (2, 5):
                            g = 0 if sec == 2 else 1

                            def mm(e, wt=wt):
                                r = None
                                for c in range(8):
                                    r = e.matmul(g_ps, lhsT=cact[:, c:c + 1], rhs=wt[:, c, :], start=(c == 0), stop=(c == 7))
                                return r
                            A("pe", mm, reads=[wtag, "cact"], writes=["modbank"], cost=1000)
                            A("dve", lambda e, g=g, off=off: e.tensor_tensor(
                                out=grow[:, g, off:off + BW], in0=g_ps, in1=brow[:, g, off:off + BW], op=ALU.add),
                              reads=["modbank", ("brow", g)], writes=[("grow", g, blk % per)])
                            if blk % per == per - 1:
                                A("sp", lambda e, g=g, l=l: e.dma_start(out=gates_scr[l, g:g + 1, :], in_=grow[:, g, :]),
                                  reads=[("grow", g, q) for q in range(per)], dma_key=("grow", g))
                        else:
                            base = {0: 0, 1: 8, 3: 16, 4: 24}[sec] + off // 128

                            def mm(e, wt=wt, base=base):
                                r = None
                                for sub in range(BW // 128):
                                    for c in range(8):
                                        r = e.matmul(mod_ps[:, base + sub:base + sub + 1], lhsT=wt[:, c, sub * 128:(sub + 1) * 128],
                                                     rhs=cact[:, c:c + 1], start=(c == 0), stop=(c == 7))
                                return r
                            A("pe", mm, reads=[wtag, "cact"], writes=["modbank"], cost=1000)
                            if sec in (1, 4) and blk % per == per - 1:
                                assemble(l, 0 if sec == 1 else 1)
                        yield
            return gen()

        with ExitStack() as es:
            cc = sb(es, "cc", [128, 8], F32)
            ce = sb(es, "ce", [128, 8], F32)
            A("sp", lambda e: e.dma_start(out=cc[:], in_=ccol[:, :]), writes=["cc"], dma_key="c3")
            A("act", lambda e: e.activation(out=ce[:], in_=cc[:], func=AF.Exp, scale=-1.0), reads=["cc"], writes=["ce"])
            A("dve", lambda e: e.tensor_scalar(out=ce[:], in0=ce[:], scalar1=1.0, scalar2=None, op0=ALU.add), reads=["ce"], writes=["ce"])
            A("dve", lambda e: e.reciprocal(out=ce[:], in_=ce[:]), reads=["ce"], writes=["ce"])
            A("dve", lambda e: e.tensor_tensor(out=cact[:], in0=ce[:], in1=cc[:], op=ALU.mult), reads=["ce", "cc"], writes=["cact"])
            overlap_mod = ("A" in phases)
            if not overlap_mod:
                for _ in make_mod_gen(lambda n_, s_, d_: sb(es, f"{n_}_m", s_, d_), lambda n_, s_, d_: ps(es, f"{n_}_m", s_, d_), list(range(n_layers))):
                    pass
            P.barrier()
        mod_factory = (lambda sbf, psf: make_mod_gen(sbf, psf, list(range(n_layers)))) if overlap_mod else None

        for l in range(n_layers):
            x_src = x_in if l == 0 else xmid_scr
            x_dst = xmid_scr if l < n_layers - 1 else out
            if l >= 2:
                x_dst = out
            if "A" in phases:
                P.nosched = True
                phase_A(nc, P, top, sb, ps, l, x_src, dict(
                    cols_sb=cols_sb, eff_sb=eff_sb, shb_sb=shb_sb, lruc_sb=lruc_sb, identb=identb, identf=identf,
                    ones_bf=ones_bf, ones_f=ones_f, grstd=grstd, w_in=w_in, qng=qng, kng=kng, lru_wa=lru_wa,
                    lru_wx=lru_wx, c_negmask=c_negmask, qT_scr=qT_scr, kT_scr=kT_scr, v_scr=v_scr, ycl_scr=ycl_scr,
                    rstd_from_ssq=rstd_from_ssq, dbg_grstd=dbg_grstd, epsc=epsc, mod_factory=(mod_factory if l == 0 else None)), nblk)
                P.barrier()
                P.nosched = False
            TT = dict(cols_sb=cols_sb, eff_sb=eff_sb, shb_sb=shb_sb, identb=identb, identf=identf, ones_bf=ones_bf, ones_f=ones_f,
                      grstd=grstd, w_out=w_out, w_up=w_up, w_down=w_down, gates_scr=gates_scr, c_oh=c_oh, c_caus=c_caus,
                      qT_scr=qT_scr, kT_scr=kT_scr, v_scr=v_scr, ycl_scr=ycl_scr, x1_scr=x1_scr, rstd_from_ssq=rstd_from_ssq)
            if "B" in phases:
                phase_B(nc, P, top, sb, ps, l, x_src, TT, nblk)
                P.barrier()
            if "C" in phases:
                phase_C(nc, P, top, sb, ps, l, x_dst, TT, nblk)
                P.barrier()
        P.emit()
    return nc, P


def phase_A(nc, P, top, sb0, ps0, l, x_src, T, nblk):
    A = P.add
    sb = lambda es, name, shape, dt: sb0(es, f"{name}_A{l}", shape, dt)
    ps = lambda es, name, shape, dt: ps0(es, f"{name}_A{l}", shape, dt)
    cols_sb, eff_sb, shb_sb, lruc_sb = T["cols_sb"], T["eff_sb"], T["shb_sb"], T["lruc_sb"]
    identb, identf, ones_bf, ones_f, grstd, epsc = T["identb"], T["identf"], T["ones_bf"], T["ones_f"], T["grstd"], T["epsc"]

    def rstd2(dst, ssq, n, rtags, wtags, tmp, tmptag):
        A("act", lambda e: e.activation(out=tmp, in_=ssq, func=AF.Ln, scale=1.0 / n, bias=epsc[:, 0:1]), reads=rtags, writes=[tmptag], n=8)
        A("act", lambda e: e.activation(out=dst, in_=tmp, func=AF.Exp, scale=-0.5), reads=[tmptag], writes=wtags, n=8)

    with ExitStack() as es:
        dbl = lambda name, shape, dt: [sb(es, f"{name}{i}", shape, dt) for i in range(2)]
        w_sb = sb(es, "w_in_sb", [128, 8, DIN], BF16)
        brow = sb(es, "b_in_row", [1, 1536], BF16)
        bcol = sb(es, "b_in_col", [128, 10], F32)
        gq = sb(es, "gq", [128, 64], F32)
        gk = sb(es, "gk", [128, 64], F32)
        negm = sb(es, "negm", [128, 16, 16], F32)
        kmeanT = sb(es, "kmeanT", [64, 8, 16], F32)
        km_hi = sb(es, "km_hi", [64, 8, 16], BF16)
        km_lo = sb(es, "km_lo", [64, 8, 16], BF16)
        km_tmp = sb(es, "km_tmp", [64, 8], F32)
        wabd = sb(es, "wabd", [128, 2, 2, 128], BF16)
        wtmp = sb(es, "wtmp", [128, 2, 2, 64], F32)
        x_sbs = Rot([sb(es, f"xA{i}", [128, D], F32) for i in range(2)], "xA")
        junkx = sb(es, "junkx", [128, D], BF16)
        junkq = dbl("junkq", [128, 512], BF16)
        junkk = dbl("junkk", [128, 512], BF16)
        qraw = dbl("qraw", [128, 512], F32)
        kraw = dbl("kraw", [128, 512], F32)
        xn = dbl("xn", [128, D], BF16)
        xnT = dbl("xnT", [128, 8, 256], BF16)
        stx = dbl("stx", [128, 4], F32)
        stq = dbl("stq", [128, 3, 8], F32)
        stk = dbl("stk", [128, 3, 8], F32)
        stg = dbl("stg", [128, 4], F32)
        kf = dbl("kf", [128, 512], F32)
        kb = dbl("kb", [128, 512], BF16)
        qaug = dbl("qaug", [128, 8, 80], BF16)
        qT8 = dbl("qT8", [64, 8, 128], BF16)
        gm = dbl("gm", [128, 8, 16], F32)
        m8 = dbl("m8", [128, 8, 8], F32)
        msk = dbl("msk", [128, 8, 16], F32)
        v_sbs = Rot([sb(es, f"vA{i}", [128, 8, 128], BF16) for i in range(2)], "vA")
        kT_sbs = Rot([sb(es, f"kTA{i}", [64, 8, 128], BF16) for i in range(2)], "kTA")
        qT_sbs = Rot([sb(es, f"qTA{i}", [80, 8, 128], BF16) for i in range(2)], "qTA")
        fmS = [sb(es, "fmS0", [128, 10, 256], F32)] * 2
        cu = sb(es, "cu", [128, 2, 258], F32)
        lx = sb(es, "lx", [128, 2, 259], F32)
        hb = sb(es, "hb", [128, 2, 257], F32)
        ct = dbl("ct", [128, 256], F32)
        cy = dbl("cy", [128, 256], F32)
        lt = [[sb(es, f"lt{lc}_{k}", [128, 256], F32) for k in range(7)] for lc in range(2)]
        xrb = dbl("xrb", [128, 256], BF16)
        ysq = dbl("ysq", [128, 4, 256], BF16)
        ycl_sbs = Rot([sb(es, f"ycl{i}", [128, 4, 256], BF16) for i in range(2)], "ycl")
        tp_ps = ps(es, "tp_ps", [128, 8, 128], BF16)
        qkv_ps = [ps(es, f"qkv_ps{i}", [128, 512], F32) for i in range(3)]
        fm_pss = Rot([ps(es, f"fm_ps{i}", [128, 512], F32) for i in range(2)], "fm_ps")
        sm_ps = ps(es, "sm_ps", [128, 512], F32)
        gate_ps = sm_ps[:, 0:128].rearrange("p (h n) -> p h n", h=8)
        km_ps = sm_ps[0:64, 128:136]
        ss_ps = sm_ps[:, 136:138]
        bc_ps = sm_ps[:, 144:154]

        for c in range(8):
            A("pool", lambda e, c=c: e.dma_start(out=w_sb[:, c, :], in_=T["w_in"][l, c * 128:(c + 1) * 128, :], max_dma_last_dim=4096),
              writes=[("w_in", c)], dma_key=("w", c), cost=12000)
        A("sp", lambda e: e.dma_start(out=gq[:], in_=T["qng"][l:l + 1, :].partition_broadcast(128)), writes=["gq"], dma_key="a0")
        A("sp", lambda e: e.dma_start(out=gk[:], in_=T["kng"][l:l + 1, :].partition_broadcast(128)), writes=["gk"], dma_key="a1")
        A("sp", lambda e: e.dma_start(out=negm[:].rearrange("p a b -> p (a b)"), in_=T["c_negmask"][0:1, :].partition_broadcast(128)),
          writes=["negm"], dma_key="a2")
        A("dve", lambda e: e.scalar_tensor_tensor(out=gk[:], in0=gk[:], scalar=0.125, in1=gq[:], op0=ALU.mult, op1=ALU.mult),
          reads=["gq", "gk"], writes=["gk"])
        A("pool", lambda e: e.memset(kmeanT[:], 0.0), writes=["kmeanT"])
        A("pool", lambda e: e.memset(km_hi[:], 0.0), writes=["km_hi"])
        A("pool", lambda e: e.memset(km_lo[:], 0.0), writes=["km_lo"])
        A("pool", lambda e: e.memset(cu[:], 0.0), writes=["cu0", "cu1"])
        A("pool", lambda e: e.memset(lx[:], 0.0), writes=["lx0", "lx1"])
        A("pool", lambda e: e.memset(hb[:], 0.0), writes=["hb0", "hb1"])
        for i in range(2):
            A("pool", lambda e, i=i: e.memset(qaug[i][:], 0.0), writes=[f"qaug_b{i}", f"qaug_q{i}"])
        for k, vt in enumerate(v_sbs.t):
            A("pool", lambda e, vt=vt: e.memset(vt[:], 1.0), writes=[("vA", k)])
        A("pool", lambda e: e.memset(wabd[:], 0.0), writes=["wabd"])
        for g, wsrc in enumerate((T["lru_wa"], T["lru_wx"])):
            for hh in range(4):
                ch, hf = hh // 2, hh % 2
                A("sp", lambda e, g=g, wsrc=wsrc, hh=hh, ch=ch, hf=hf: e.dma_start(
                    out=wtmp[hf * 64:(hf + 1) * 64, g, ch, :], in_=wsrc[l, hh, :, :]), writes=[("wtmp", g, hh)], dma_key=("spk", g * 4 + hh))
                A("dve", lambda e, g=g, ch=ch, hf=hf: e.tensor_copy(out=wabd[hf * 64:(hf + 1) * 64, g, ch, hf * 64:(hf + 1) * 64],
                                                                   in_=wtmp[hf * 64:(hf + 1) * 64, g, ch, :]),
                  reads=[("wtmp", g, hh), "wabd"], writes=[("wabd", g, hh)])
        wabd_tags = [("wabd", g, hh) for g in range(2) for hh in range(4)]
        w_tags = [("w_in", c) for c in range(8)]

        mgen = None
        if T.get("mod_factory") is not None:
            mgen = T["mod_factory"](lambda n_, s_, d_: sb(es, f"{n_}_m", s_, d_), lambda n_, s_, d_: ps(es, f"{n_}_m", s_, d_))
            for _ in range(8):
                next(mgen, None)
        for j in range(3):
            def mm(e, j=j):
                r = None
                for c in range(8):
                    r = e.matmul(qkv_ps[j][0:1, :], lhsT=shb_sb[:, l, 0, c:c + 1], rhs=w_sb[:, c, j * 512:(j + 1) * 512],
                                 start=(c == 0), stop=(c == 7))
                return r
            A("pe", mm, reads=w_tags + [("shb", l, 0)], writes=[("qkv_ps", j)])
            A("act", lambda e, j=j: e.copy(out=brow[:, j * 512:(j + 1) * 512], in_=qkv_ps[j][0:1, :]), reads=[("qkv_ps", j)], writes=["brow"], n=512)

        def mm(e):
            r = None
            for fc in range(10):
                for c in range(8):
                    r = e.matmul(bc_ps[:, fc:fc + 1], lhsT=w_sb[:, c, 1536 + fc * 128:1536 + (fc + 1) * 128],
                                 rhs=shb_sb[:, l, 0, c:c + 1], start=(c == 0), stop=(c == 7))
            return r
        A("pe", mm, reads=w_tags + [("shb", l, 0)], writes=["sm_ps"])
        A("dve", lambda e: e.tensor_copy(out=bcol[:], in_=bc_ps), reads=["sm_ps"], writes=["bcol"])
        for c in range(8):
            if c % 2 == 0:
                A("dve", lambda e, c=c: e.tensor_scalar(out=w_sb[:, c, :], in0=w_sb[:, c, :], scalar1=eff_sb[:, l, c:c + 1], scalar2=None, op0=ALU.mult),
                  reads=[("eff", l, 0, 0)], writes=[("w_in", c)], n=DIN)
            else:
                A("act", lambda e, c=c: e.activation(out=w_sb[:, c, :], in_=w_sb[:, c, :], func=AF.Copy, scale=eff_sb[:, l, c:c + 1]),
                  reads=[("eff", l, 0, 0)], writes=[("w_in", c)], n=DIN)

        def h3(ap):
            return ap.rearrange("p (h d) -> p h d", h=8)

        def tiles(st):
            bp = st % 2
            ctx = [dict(), dict()]

            def s0(i):
                t = 2 * st + i
                xt, xtag = x_sbs.next()
                ctx[i].update(t=t, xt=xt, xtag=xtag)
                A("sp", lambda e, xt=xt, t=t: e.dma_start(out=xt[:], in_=x_src[t * 128:(t + 1) * 128, :]), writes=[xtag], dma_key=xtag)
                A("act", lambda e, xt=xt, i=i: e.activation(out=junkx[:], in_=xt[:], func=AF.Square, accum_out=stx[i][:, 0:1]),
                  reads=[xtag], writes=["junkx", f"stx{i}"], n=D)
                rstd2(stx[i][:, 2:3], stx[i][:, 0:1], D, [f"stx{i}"], [f"stxr{i}"], stx[i][:, 1:2], f"stxt{i}")
                A("dve", lambda e, xt=xt, i=i: e.tensor_scalar(out=xn[i][:], in0=xt[:], scalar1=stx[i][:, 2:3], scalar2=None, op0=ALU.mult),
                  reads=[xtag, f"stxr{i}"], writes=[f"xn{i}"], n=D)

            def s1(i):
                def tr(e, i=i):
                    r = None
                    for c in range(8):
                        r = e.transpose(out=tp_ps[:, c, :], in_=xn[i][:, c * 128:(c + 1) * 128], identity=identb[:])
                    return r
                A("pe", tr, reads=[f"xn{i}", "identb"], writes=["tp_ps"])
                A("act", lambda e, i=i, bp=bp: e.copy(out=xnT[bp][:, :, i * 128:(i + 1) * 128], in_=tp_ps[:]), reads=["tp_ps"], writes=[("xnT", bp, i)], n=D)

            def s2(i):
                t = ctx[i]["t"]
                for j in range(3):
                    def mm(e, j=j, i=i, bp=bp):
                        for c in range(8):
                            e.matmul(qkv_ps[j][:], lhsT=xnT[bp][:, c, i * 128:(i + 1) * 128], rhs=w_sb[:, c, j * 512:(j + 1) * 512],
                                     start=(c == 0), stop=False)
                        return e.matmul(qkv_ps[j][:], lhsT=ones_bf[0:1, :], rhs=brow[0:1, j * 512:(j + 1) * 512], start=False, stop=True)
                    A("pe", mm, reads=w_tags + [("xnT", bp, i), "brow", "ones_bf"], writes=[("qkv_ps", j)], cost=2200)
                A("act", lambda e, i=i: e.copy(out=qraw[i][:], in_=qkv_ps[0][:]), reads=[("qkv_ps", 0)], writes=[f"qraw{i}"], n=512)
                A("act", lambda e, i=i: e.copy(out=kraw[i][:], in_=qkv_ps[1][:]), reads=[("qkv_ps", 1)], writes=[f"kraw{i}"], n=512)
                vt, vtag = v_sbs.next()
                A("act", lambda e, vt=vt: e.copy(out=vt[:, :, 0:64], in_=h3(qkv_ps[2][:])), reads=[("qkv_ps", 2)], writes=[vtag], n=512)
                A("sp", lambda e, vt=vt, t=t: e.dma_start(out=T["v_scr"][t, :, :], in_=vt[:].rearrange("p h d -> p (h d)")),
                  reads=[vtag], writes=[("v_scr", t)], dma_key=("st",) + vtag)

            def s3(i):
                A("act", lambda e, i=i: e.activation(out=junkq[i][:], in_=qraw[i][:], func=AF.Square), reads=[f"qraw{i}"], writes=[f"junkq{i}"], n=512)
                A("dve", lambda e, i=i: e.tensor_reduce(out=stq[i][:, 0, :], in_=h3(junkq[i][:]), axis=AX.X, op=ALU.add),
                  reads=[f"junkq{i}"], writes=[f"stq{i}"], n=512)
                rstd2(stq[i][:, 2, :], stq[i][:, 0, :], 64, [f"stq{i}"], [f"stqr{i}"], stq[i][:, 1, :], f"stqt{i}")
                A("dve", lambda e, i=i: e.tensor_tensor(out=qaug[i][:, :, 0:64], in0=h3(qraw[i][:]),
                                                        in1=stq[i][:, 2, :].unsqueeze(2).to_broadcast([128, 8, 64]), op=ALU.mult),
                  reads=[f"qraw{i}", f"stqr{i}"], writes=[f"qaug_q{i}"], n=512)
                A("act", lambda e, i=i: e.activation(out=junkk[i][:], in_=kraw[i][:], func=AF.Square), reads=[f"kraw{i}"], writes=[f"junkk{i}"], n=512)
                A("dve", lambda e, i=i: e.tensor_reduce(out=stk[i][:, 0, :], in_=h3(junkk[i][:]), axis=AX.X, op=ALU.add),
                  reads=[f"junkk{i}"], writes=[f"stk{i}"], n=512)
                rstd2(stk[i][:, 2, :], stk[i][:, 0, :], 64, [f"stk{i}"], [f"stkr{i}"], stk[i][:, 1, :], f"stkt{i}")
                A("dve", lambda e, i=i: e.tensor_tensor(out=h3(kf[i][:]), in0=h3(kraw[i][:]),
                                                        in1=stk[i][:, 2, :].unsqueeze(2).to_broadcast([128, 8, 64]), op=ALU.mult),
                  reads=[f"kraw{i}", f"stkr{i}"], writes=[f"kf{i}"], n=512)
                A("dve", lambda e, i=i: e.tensor_tensor(out=h3(kb[i][:]), in0=h3(kf[i][:]),
                                                        in1=gk[:].unsqueeze(1).to_broadcast([128, 8, 64]), op=ALU.mult),
                  reads=[f"kf{i}", "gk"], writes=[f"kb{i}"], n=512)

            def s4(i):
                def mm(e, i=i):
                    r = None
                    for h in range(8):
                        r = e.matmul(km_ps[:, h:h + 1], lhsT=kb[i][:, h * 64:(h + 1) * 64], rhs=ones_bf[:, 0:1], start=True, stop=True)
                    return r
                A("pe", mm, reads=[f"kb{i}", "ones_bf"], writes=["sm_ps"], cost=600)
                if i == 0:
                    A("dve", lambda e: e.tensor_scalar(out=kmeanT[:, :, st], in0=km_ps, scalar1=1.0 / 256, scalar2=None, op0=ALU.mult),
                      reads=["sm_ps"], writes=["kmeanT"], n=8)
                else:
                    A("dve", lambda e: e.scalar_tensor_tensor(out=kmeanT[:, :, st], in0=km_ps, scalar=1.0 / 256, in1=kmeanT[:, :, st],
                                                              op0=ALU.mult, op1=ALU.add),
                      reads=["sm_ps"], writes=["kmeanT"], n=8)
                    A("dve", lambda e: e.tensor_copy(out=km_hi[:, :, st], in_=kmeanT[:, :, st]), reads=["kmeanT"], writes=["km_hi"], n=8)
                    A("dve", lambda e: e.tensor_tensor(out=km_tmp[:], in0=kmeanT[:, :, st], in1=km_hi[:, :, st], op=ALU.subtract),
                      reads=["kmeanT", "km_hi"], writes=["km_tmp"], n=8)
                    A("dve", lambda e: e.tensor_copy(out=km_lo[:, :, st], in_=km_tmp[:]), reads=["km_tmp"], writes=["km_lo"], n=8)

            def s5(i):
                t = ctx[i]["t"]
                kTt, kTtag = kT_sbs.next()

                def tr(e, i=i):
                    r = None
                    for h in range(8):
                        r = e.transpose(out=tp_ps[0:64, h, :], in_=kb[i][:, h * 64:(h + 1) * 64], identity=identb[:])
                    return r
                A("pe", tr, reads=[f"kb{i}", "identb"], writes=["tp_ps"])
                A("act", lambda e, kTt=kTt: e.copy(out=kTt[:], in_=tp_ps[0:64, :, :]), reads=["tp_ps"], writes=[kTtag], n=D)
                A("sp", lambda e, kTt=kTt, t=t: e.dma_start(out=T["kT_scr"][:, :, t * 128:(t + 1) * 128], in_=kTt[:]),
                  reads=[kTtag], writes=[("kT_scr", t)], dma_key=("st",) + kTtag)

            def s6(i):
                if st < 1:
                    return

                def tr(e, i=i):
                    r = None
                    for h in range(8):
                        r = e.transpose(out=tp_ps[0:64, h, :], in_=qaug[i][:, h, 0:64], identity=identb[:])
                    return r
                A("pe", tr, reads=[f"qaug_q{i}", "identb"], writes=["tp_ps"])
                A("act", lambda e, i=i: e.copy(out=qT8[i][:], in_=tp_ps[0:64, :, :]), reads=["tp_ps"], writes=[f"qT8{i}"], n=D)

            def s7(i):
                if st < 1:
                    return

                def mm(e, i=i):
                    r = None
                    for h in range(8):
                        e.matmul(gate_ps[:, h, :], lhsT=qT8[i][:, h, :], rhs=km_hi[:, h, :], start=True, stop=False)
                        r = e.matmul(gate_ps[:, h, :], lhsT=qT8[i][:, h, :], rhs=km_lo[:, h, :], start=False, stop=True)
                    return r
                A("pe", mm, reads=[f"qT8{i}", "km_hi", "km_lo"], writes=["sm_ps"], cost=1000)
                A("dve", lambda e, i=i: e.tensor_tensor(out=gm[i][:], in0=gate_ps, in1=negm[:, st:st + 1, :].to_broadcast([128, 8, 16]), op=ALU.add),
                  reads=["sm_ps", "negm"], writes=[f"gm{i}"], n=128)
                for h in range(8):
                    A("dve", lambda e, h=h, i=i: e.max(out=m8[i][:, h, :], in_=gm[i][:, h, :]), reads=[f"gm{i}"], writes=[(f"m8{i}", h)], n=16)
                A("dve", lambda e, i=i: e.tensor_tensor(out=msk[i][:], in0=gm[i][:], in1=m8[i][:, :, 2:3].to_broadcast([128, 8, 16]), op=ALU.is_ge),
                  reads=[f"gm{i}"] + [(f"m8{i}", h) for h in range(8)], writes=[f"msk{i}"], n=128)
                A("dve", lambda e, i=i: e.tensor_scalar(out=qaug[i][:, :, 64:80], in0=msk[i][:], scalar1=BIG, scalar2=-BIG, op0=ALU.mult, op1=ALU.add),
                  reads=[f"msk{i}"], writes=[f"qaug_b{i}"], n=128)
                A("dve", lambda e, i=i: e.memset(qaug[i][:, :, 64 + st:65 + st], 0.0), reads=[f"qaug_b{i}"], writes=[f"qaug_b{i}"], n=8)

            def s8(i):
                t = ctx[i]["t"]
                qTt, qTtag = qT_sbs.next()

                def tr(e, i=i):
                    r = None
                    for h in range(8):
                        r = e.transpose(out=tp_ps[0:80, h, :], in_=qaug[i][:, h, :], identity=identb[:])
                    return r
                A("pe", tr, reads=[f"qaug_q{i}", f"qaug_b{i}", "identb"], writes=["tp_ps"])
                A("act", lambda e, qTt=qTt: e.copy(out=qTt[:], in_=tp_ps[0:80, :, :]), reads=["tp_ps"], writes=[qTtag], n=D)
                A("sp", lambda e, qTt=qTt, t=t: e.dma_start(out=T["qT_scr"][:, :, t * 128:(t + 1) * 128], in_=qTt[:]),
                  reads=[qTtag], writes=[("qT_scr", t)], dma_key=("st",) + qTtag)

            import os as _os7
            stages = (s0, s1, s2, s3, s4, s5, s6, s7, s8)
            tmode = _os7.environ.get("TILE_SEQ", "0")
            if tmode.startswith("g"):
                k_int = int(tmode[1:])
                for si in range(k_int):
                    for i in range(2):
                        stages[si](i)
                for i in range(2):
                    for si in range(k_int, 9):
                        stages[si](i)
            elif tmode == "1":
                for i in range(2):
                    for stage in stages:
                        stage(i)
            else:
                for stage in (s0, s1, s2, s3, s4, s5, s6, s7, s8):
                    for i in range(2):
                        stage(i)

        def fm(st):
            bp = st % 2
            F = fmS[bp]
            ycl_t, ycl_tag = ycl_sbs.next()
            for fc in (6, 7, 8, 9, 2, 4, 0, 3, 5, 1):
                bank, btag = fm_pss.next()

                def mm(e, bank=bank, fc=fc, bp=bp):
                    r = None
                    for c in range(8):
                        r = e.matmul(bank[:, 0:256], lhsT=w_sb[:, c, 1536 + fc * 128:1536 + (fc + 1) * 128], rhs=xnT[bp][:, c, :],
                                     start=(c == 0), stop=(c == 7))
                    return r
                A("pe", mm, reads=w_tags + [("xnT", bp, 0), ("xnT", bp, 1)], writes=[btag], cost=1100)
                if fc in (6, 7):
                    lc = fc - 6
                    A("act", lambda e, bank=bank, lc=lc, fc=fc: e.activation(out=lx[:, lc, 3:259], in_=bank[:, 0:256], func=AF.Identity, bias=bcol[:, fc:fc + 1]),
                      reads=[btag, "bcol"], writes=[f"lx{lc}"])
                else:
                    A("dve", lambda e, bank=bank, fc=fc, F=F: e.tensor_scalar(out=F[:, fc, :], in0=bank[:, 0:256], scalar1=bcol[:, fc:fc + 1], scalar2=None, op0=ALU.add),
                      reads=[btag, "bcol"], writes=[("fmS", fc)])
            for lc in range(2):
                xr, ea, sa, ei, uu, gz, yy = lt[lc]
                tg = lambda nm, lc=lc: f"{nm}{lc}"
                lxb, hbb = lx[:, lc, :], hb[:, lc, :]
                cw = [cols_sb[:, l, 62 + lc * 4 + k:62 + lc * 4 + k + 1] for k in range(4)]
                cb = cols_sb[:, l, 70 + lc:71 + lc]
                nba = lruc_sb[:, l, 0 + lc:1 + lc]
                nbx = lruc_sb[:, l, 2 + lc:3 + lc]
                sp8 = lruc_sb[:, l, 4 + lc:5 + lc]
                G = F[:, 8 + lc, :]
                gtag = ("fmS", 8 + lc)
                A("dve", lambda e, lxb=lxb, cw=cw, cb=cb, xr=xr: e.tensor_scalar(out=xr[:], in0=lxb[:, 0:256], scalar1=cw[0], scalar2=cb, op0=ALU.mult, op1=ALU.add),
                  reads=[tg("lx")], writes=[tg("xr")])
                for k in range(1, 4):
                    A("dve", lambda e, lxb=lxb, cw=cw, k=k, xr=xr: e.scalar_tensor_tensor(out=xr[:], in0=lxb[:, k:k + 256], scalar=cw[k], in1=xr[:],
                                                                                      op0=ALU.mult, op1=ALU.add),
                      reads=[tg("lx"), tg("xr")], writes=[tg("xr")])
                A("dve", lambda e, lxb=lxb: e.tensor_copy(out=lxb[:, 0:3], in_=lxb[:, 256:259]), reads=[tg("lx"), tg("xr")], writes=[tg("lx")], n=3)
                A("pool", lambda e, lc=lc, xr=xr: e.tensor_copy(out=xrb[lc][:], in_=xr[:]), reads=[tg("xr")], writes=[tg("xrb")], cost=1500)
                rb, rbtag = fm_pss.next()
                A("pe", lambda e, lc=lc, rb=rb: e.matmul(rb[:, 0:256], lhsT=wabd[:, 0, lc, :], rhs=xrb[lc][:], start=True, stop=True),
                  reads=[tg("xrb")] + wabd_tags, writes=[rbtag], cost=200)
                A("act", lambda e, nba=nba, rb=rb, ea=ea: e.activation(out=ea[:], in_=rb[:, 0:256], func=AF.Exp, scale=-1.0, bias=nba),
                  reads=[rbtag], writes=[tg("ea")])
                ib, ibtag = fm_pss.next()
                A("pe", lambda e, lc=lc, ib=ib: e.matmul(ib[:, 0:256], lhsT=wabd[:, 1, lc, :], rhs=xrb[lc][:], start=True, stop=True),
                  reads=[tg("xrb")] + wabd_tags, writes=[ibtag], cost=200)
                A("act", lambda e, nbx=nbx, ib=ib, ei=ei: e.activation(out=ei[:], in_=ib[:, 0:256], func=AF.Exp, scale=-1.0, bias=nbx),
                  reads=[ibtag], writes=[tg("ei")])
                A("act", lambda e, ea=ea: e.activation(out=ea[:], in_=ea[:], func=AF.Ln, bias=1.0), reads=[tg("ea")], writes=[tg("ea")])
                A("act", lambda e, ea=ea: e.activation(out=ea[:], in_=ea[:], func=AF.Exp, scale=-1.0), reads=[tg("ea")], writes=[tg("ea")])
                A("act", lambda e, ea=ea, sp8=sp8: e.activation(out=ea[:], in_=ea[:], func=AF.Exp, scale=sp8), reads=[tg("ea")], writes=[tg("ea")])
                A("act", lambda e, ea=ea, sa=sa: e.activation(out=sa[:], in_=ea[:], func=AF.Square), reads=[tg("ea")], writes=[tg("sa")])
                A("act", lambda e, sa=sa: e.activation(out=sa[:], in_=sa[:], func=AF.Ln, scale=-1.0, bias=1.0), reads=[tg("sa")], writes=[tg("sa")])
                A("act", lambda e, sa=sa: e.activation(out=sa[:], in_=sa[:], func=AF.Exp, scale=0.5), reads=[tg("sa")], writes=[tg("sa")])
                A("act", lambda e, ei=ei: e.activation(out=ei[:], in_=ei[:], func=AF.Ln, bias=1.0), reads=[tg("ei")], writes=[tg("ei")])
                A("act", lambda e, ei=ei: e.activation(out=ei[:], in_=ei[:], func=AF.Exp, scale=-1.0), reads=[tg("ei")], writes=[tg("ei")])
                A("dve", lambda e, ei=ei, xr=xr, uu=uu: e.tensor_tensor(out=uu[:], in0=ei[:], in1=xr[:], op=ALU.mult), reads=[tg("ei"), tg("xr")], writes=[tg("uu")])
                A("dve", lambda e, sa=sa, uu=uu: e.tensor_tensor(out=uu[:], in0=uu[:], in1=sa[:], op=ALU.mult), reads=[tg("uu"), tg("sa")], writes=[tg("uu")])
                A("dve", lambda e, hbb=hbb, ea=ea, uu=uu: e.tensor_tensor_scan(out=hbb[:, 1:257], data0=ea[:], data1=uu[:], initial=hbb[:, 0:1],
                                                                              op0=ALU.mult, op1=ALU.add),
                  reads=[tg("ea"), tg("uu"), tg("hb")], writes=[tg("hbh")], n=512)
                A("act", lambda e, G=G, gz=gz: e.activation(out=gz[:], in_=G, func=AF.Square), reads=[gtag], writes=[tg("gz")])
                A("dve", lambda e, gz=gz: e.tensor_scalar(out=gz[:], in0=gz[:], scalar1=0.044715, scalar2=1.0, op0=ALU.mult, op1=ALU.add),
                  reads=[tg("gz")], writes=[tg("gz")])
                A("dve", lambda e, gz=gz, G=G: e.tensor_tensor(out=gz[:], in0=gz[:], in1=G, op=ALU.mult), reads=[tg("gz"), gtag], writes=[tg("gz")])
                A("act", lambda e, gz=gz: e.activation(out=gz[:], in_=gz[:], func=AF.Exp, scale=-1.5957691216057308), reads=[tg("gz")], writes=[tg("gz")])
                A("act", lambda e, gz=gz: e.activation(out=gz[:], in_=gz[:], func=AF.Ln, bias=1.0), reads=[tg("gz")], writes=[tg("gz")])
                A("act", lambda e, gz=gz: e.activation(out=gz[:], in_=gz[:], func=AF.Exp, scale=-1.0), reads=[tg("gz")], writes=[tg("gz")])
                A("dve", lambda e, gz=gz, G=G: e.tensor_tensor(out=gz[:], in0=gz[:], in1=G, op=ALU.mult), reads=[tg("gz"), gtag], writes=[tg("gz")])
                A("dve", lambda e, hbb=hbb, gz=gz, yy=yy: e.tensor_tensor(out=yy[:], in0=hbb[:, 1:257], in1=gz[:], op=ALU.mult),
                  reads=[tg("hbh"), tg("gz")], writes=[tg("yy")])
                A("dve", lambda e, hbb=hbb: e.tensor_copy(out=hbb[:, 0:1], in_=hbb[:, 256:257]), reads=[tg("hbh"), tg("yy")], writes=[tg("hb")], n=1)
                A("pool", lambda e, lc=lc, ycl_t=ycl_t, yy=yy: e.tensor_copy(out=ycl_t[:, 2 + lc, :], in_=yy[:]), reads=[tg("yy")], writes=[ycl_tag + (2 + lc,)], cost=1500)
                A("pool", lambda e, lc=lc, yy=yy, bp=bp: e.tensor_tensor(out=ysq[bp][:, 2 + lc, :], in0=yy[:], in1=yy[:], op=ALU.mult), reads=[tg("yy")], writes=[("ysq", bp, 2 + lc)], cost=1000)
            for cc in range(2):
                cub = cu[:, cc, :]
                tg = lambda nm, cc=cc: f"{nm}{cc}"
                w0 = cols_sb[:, l, 56 + cc * 3 + 0:56 + cc * 3 + 1]
                w1 = cols_sb[:, l, 56 + cc * 3 + 1:56 + cc * 3 + 2]
                w2 = cols_sb[:, l, 56 + cc * 3 + 2:56 + cc * 3 + 3]
                Bt, Ct, Ut = ("fmS", 0 + cc), ("fmS", 2 + cc), ("fmS", 4 + cc)
                A("dve", lambda e, cub=cub, cc=cc, F=F: e.tensor_tensor(out=cub[:, 2:258], in0=F[:, 2 + cc, :], in1=F[:, 4 + cc, :], op=ALU.mult),
                  reads=[Ct, Ut], writes=[tg("cu")])
                A("dve", lambda e, cub=cub, w0=w0, cc=cc: e.tensor_scalar(out=ct[cc][:], in0=cub[:, 0:256], scalar1=w0, scalar2=None, op0=ALU.mult),
                  reads=[tg("cu")], writes=[tg("ct")])
                A("dve", lambda e, cub=cub, w1=w1, cc=cc: e.scalar_tensor_tensor(out=ct[cc][:], in0=cub[:, 1:257], scalar=w1, in1=ct[cc][:], op0=ALU.mult, op1=ALU.add),
                  reads=[tg("cu"), tg("ct")], writes=[tg("ct")])
                A("dve", lambda e, cub=cub, w2=w2, cc=cc: e.scalar_tensor_tensor(out=ct[cc][:], in0=cub[:, 2:258], scalar=w2, in1=ct[cc][:], op0=ALU.mult, op1=ALU.add),
                  reads=[tg("cu"), tg("ct")], writes=[tg("ct")])
                A("dve", lambda e, cub=cub: e.tensor_copy(out=cub[:, 0:2], in_=cub[:, 256:258]), reads=[tg("cu"), tg("ct")], writes=[tg("cu")], n=2)
                A("dve", lambda e, cc=cc, F=F: e.tensor_tensor(out=cy[cc][:], in0=F[:, 0 + cc, :], in1=ct[cc][:], op=ALU.mult),
                  reads=[Bt, tg("ct")], writes=[tg("cy")])
                A("pool", lambda e, cc=cc, ycl_t=ycl_t: e.tensor_copy(out=ycl_t[:, cc, :], in_=cy[cc][:]), reads=[tg("cy")], writes=[ycl_tag + (cc,)], cost=1500)
                A("pool", lambda e, cc=cc, bp=bp: e.tensor_tensor(out=ysq[bp][:, cc, :], in0=cy[cc][:], in1=cy[cc][:], op=ALU.mult), reads=[tg("cy")], writes=[("ysq", bp, cc)], cost=1000)
            for i in range(2):
                t = 2 * st + i

                def mm(e, i=i, bp=bp):
                    r = None
                    for g in range(2):
                        for c2 in range(2):
                            r = e.matmul(ss_ps[:, g:g + 1], lhsT=ysq[bp][:, 2 * g + c2, i * 128:(i + 1) * 128], rhs=ones_bf[:, 0:1],
                                         start=(c2 == 0), stop=(c2 == 1))
                    return r
                A("pe", mm, reads=[("ysq", bp, c) for c in range(4)] + ["ones_bf"], writes=["sm_ps"], cost=500)
                rstd2(grstd[:, t, :], ss_ps, 256, ["sm_ps"], [("grstd", t)], stg[i][:, 0:2], f"stg{i}")
            A("sp", lambda e, ycl_t=ycl_t, st=st: e.dma_start(
                out=T["ycl_scr"].rearrange("(c p) t -> p c t", p=128)[:, :, st * 256:(st + 1) * 256], in_=ycl_t[:]),
              reads=[ycl_tag + (c,) for c in range(4)], writes=[("ycl_scr", st)], dma_key=("st",) + ycl_tag)

        tiles(0)
        for st in range(nblk):
            if st + 1 < nblk:
                tiles(st + 1)
            fm(st)
            if mgen is not None:
                for _ in range(3):
                    next(mgen, None)
        if mgen is not None:
            for _ in mgen:
                pass


def phase_B(nc, P, top, sb0, ps0, l, x_src, T, nblk):
    A = P.add
    sb = lambda es, name, shape, dt: sb0(es, f"{name}_B{l}", shape, dt)
    ps = lambda es, name, shape, dt: ps0(es, f"{name}_B{l}", shape, dt)
    cols_sb, identb, ones_bf, ones_f, grstd = T["cols_sb"], T["identb"], T["ones_bf"], T["ones_f"], T["grstd"]
    rstd_from_ssq = T["rstd_from_ssq"]
    with ExitStack() as es:
        kT = sb(es, "kT_all", [128, 8, S], BF16)
        v_all = sb(es, "v_all", [128, NT, 1024], BF16)
        wo = sb(es, "w_out_sb", [128, 8, D], BF16)
        caus = sb(es, "caus", [128, 2, 256], BF16)
        qT_blks = Rot([sb(es, f"qTb{i}", [128, 8, 256], BF16) for i in range(2)], "qTb")
        ycl_blks = Rot([sb(es, f"yclb{i}", [128, 4, 256], BF16) for i in range(2)], "yclb")
        x_sbs = Rot([sb(es, f"xB{i}", [128, D], F32) for i in range(3)], "xB")
        g1bc = x_sbs.t[2]
        p_sbs = Rot([sb(es, f"pB{i}", [128, 2, 256], BF16) for i in range(4)], "pB")
        rdens = Rot([sb(es, f"rdB{i}", [64, 256], F32) for i in range(2)], "rdB")
        yattn = sb(es, "yattn", [128, 4, 256], F32)
        ya_bf = sb(es, "ya_bf", [128, 4, 256], BF16)
        ysq = sb(es, "ysqB", [128, 4, 256], BF16)
        stb = sb(es, "stb", [128, 8], F32)
        s_pss = Rot([ps(es, f"s_ps{i}", [128, 2, 256], F32) for i in range(3)], "s_ps")
        oT_pss = Rot([ps(es, f"oT_ps{i}", [128, 512], F32) for i in range(2)], "oT_ps")
        sm_ps = ps(es, "smB_ps", [128, 512], F32)
        op_pss = Rot([ps(es, f"op_ps{i}", [128, 512], F32) for i in range(2)], "op_ps")

        for c in range(8):
            A("pool", lambda e, c=c: e.dma_start(out=wo[:, c, :], in_=T["w_out"][l, c * 128:(c + 1) * 128, :]),
              writes=[("wo", c)], dma_key=("w", c))
        A("sp", lambda e: e.dma_start(out=g1bc[:], in_=T["gates_scr"][l, 0:1, :].partition_broadcast(128)), writes=[("xB", 2)], dma_key=("xB", 2))
        A("sp", lambda e: e.dma_start(out=caus[:], in_=T["c_caus"][:, :, :]), writes=["caus"], dma_key="a1")
        for c in range(8):
            eng = "dve"
            A(eng, lambda e, c=c: e.scalar_tensor_tensor(out=wo[:, c, :], in0=wo[:, c, :], scalar=cols_sb[:, l, 48 + c:49 + c], in1=g1bc[:],
                                                         op0=ALU.mult, op1=ALU.mult),
              reads=[("xB", 2)], writes=[("wo", c)], n=D)
        wo_tags = [("wo", c) for c in range(8)]
        for h in range(8):
            A("dve", lambda e, h=h: e.memset(kT[64:128, h, :], 0.0), writes=[("kToh", h)], n=2048)
        for k_, qt_ in enumerate(qT_blks.t):
            A("dve", lambda e, qt_=qt_: e.memset(qt_[:], 0.0), writes=[("qTb", k_)], n=1024)
        for h in range(8):
            A("sp", lambda e, h=h: e.dma_start(out=kT[64:80, h, :], in_=T["c_oh"][:, :]), writes=[("kToh", h)], dma_key=("spk", h))
        oh_tags = [("kToh", h) for h in range(8)]
        for qb in range(nblk):
            A("sp", lambda e, j=qb: e.dma_start(out=kT[0:64, :, j * 256:(j + 1) * 256], in_=T["kT_scr"][:, :, j * 256:(j + 1) * 256]),
              writes=[("kT", qb)], dma_key=("kT", qb % 4))
            A("sp", lambda e, j=qb: e.dma_start(out=v_all[:, 2 * j:2 * j + 2, :], in_=T["v_scr"][2 * j:2 * j + 2, :, :].rearrange("t p f -> p t f")),
              writes=[("v", qb)], dma_key=("v", qb % 4))
            qTb, qtag = qT_blks.next()
            yclb, ytag = ycl_blks.next()
            A("sp", lambda e, qTb=qTb, qb=qb: e.dma_start(out=qTb[0:80, :, :], in_=T["qT_scr"][:, :, qb * 256:(qb + 1) * 256]), writes=[qtag], dma_key=qtag)
            A("sp", lambda e, yclb=yclb, qb=qb: e.dma_start(
                out=yclb[:], in_=T["ycl_scr"].rearrange("(c p) t -> p c t", p=128)[:, :, qb * 256:(qb + 1) * 256]), writes=[ytag], dma_key=ytag)
            xts = []
            for i in range(2):
                t = 2 * qb + i
                xt, xtag = x_sbs.next()
                A("sp", lambda e, xt=xt, t=t: e.dma_start(out=xt[:], in_=x_src[t * 128:(t + 1) * 128, :]), writes=[xtag], dma_key=xtag)
                xts.append((xt, xtag))
            for h in range(8):
                oT, otag = oT_pss.next()
                for j in range(qb + 1):
                    own = (j == qb)
                    sp_, stag = s_pss.next()
                    if not own:
                        def mm(e, sp_=sp_, j=j, h=h, qTb=qTb):
                            r = None
                            for kk in range(2):
                                r = e.matmul(sp_[:, kk, :], lhsT=kT[:, h, (2 * j + kk) * 128:(2 * j + kk + 1) * 128], rhs=qTb[:, h, :],
                                             start=True, stop=True)
                            return r
                        A("pe", mm, reads=[("kT", j), ("kToh", h), qtag], writes=[stag])
                    else:
                        def mm(e, sp_=sp_, j=j, h=h, qTb=qTb):
                            r = None
                            for kk in range(2):
                                e.matmul(sp_[:, kk, :], lhsT=kT[:, h, (2 * j + kk) * 128:(2 * j + kk + 1) * 128], rhs=qTb[:, h, :],
                                         start=True, stop=False)
                                r = e.matmul(sp_[:, kk, :], lhsT=identb[:], rhs=caus[:, kk, :], start=False, stop=True)
                            return r
                        A("pe", mm, reads=[("kT", j), ("kToh", h), qtag, "caus", "identb"], writes=[stag])
                    pt, ptag = p_sbs.next()
                    A("act", lambda e, pt=pt, sp_=sp_: e.activation(out=pt[:], in_=sp_[:], func=AF.Exp), reads=[stag], writes=[ptag], cost=600)

                    def mm(e, pt=pt, oT=oT, j=j, h=h, qb=qb):
                        r = None
                        for kk in range(2):
                            r = e.matmul(oT[:, 0:256], lhsT=v_all[:, 2 * j + kk, h * 128:(h + 1) * 128], rhs=pt[:, kk, :],
                                         start=(j == 0 and kk == 0), stop=(j == qb and kk == 1))
                        return r
                    A("pe", mm, reads=[("v", j), ptag], writes=[otag], cost=280)
                rd, rdtag = rdens.next()
                A("dve", lambda e, rd=rd, oT=oT: e.reciprocal(out=rd[:], in_=oT[64:128, 0:256]), reads=[otag], writes=[rdtag], cost=1900)
                pb = (h % 2) * 64
                A("dve", lambda e, rd=rd, oT=oT, pb=pb, h=h: e.tensor_tensor(out=yattn[pb:pb + 64, h // 2, :], in0=oT[0:64, 0:256], in1=rd[:], op=ALU.mult),
                  reads=[otag, rdtag], writes=[("yattn", h)])
            ya_tags = [("yattn", h) for h in range(8)]
            A("pool", lambda e: e.tensor_copy(out=ya_bf[:], in_=yattn[:]), reads=ya_tags, writes=["ya_bf"])
            A("act", lambda e: e.activation(out=ysq[:], in_=yattn[:], func=AF.Square), reads=ya_tags, writes=["ysqB"])
            for i in range(2):
                t = 2 * qb + i
                xt, xtag = xts[i]

                def mm(e, i=i):
                    r = None
                    for c in range(4):
                        r = e.matmul(sm_ps[:, 0:1], lhsT=ysq[:, c, i * 128:(i + 1) * 128], rhs=ones_bf[:, 0:1], start=(c == 0), stop=(c == 3))
                    return r
                A("pe", mm, reads=["ysqB", "ones_bf"], writes=["smB_ps"])
                rstd_from_ssq(stb[:, 2:3], sm_ps[:, 0:1], 512, ["smB_ps"], ["stbr"], stb[:, 1:2], "stbt")
                for n in range(2):
                    nsl = slice(n * 512, (n + 1) * 512)
                    for g in range(3):
                        op_, optag = op_pss.next()
                        if g == 0:
                            srcs = [(ya_bf, c, c) for c in range(4)]
                            rtags = ["ya_bf"]
                            scal = stb[:, 2:3]
                            stag2 = ["stbr"]
                        else:
                            srcs = [(yclb, 2 * (g - 1) + c2, 4 + 2 * (g - 1) + c2) for c2 in range(2)]
                            rtags = [ytag]
                            scal = grstd[:, t, g - 1:g]
                            stag2 = []

                        def mm(e, op_=op_, srcs=srcs, i=i, nsl=nsl):
                            r = None
                            for k, (src, sc, wc) in enumerate(srcs):
                                r = e.matmul(op_[:], lhsT=src[:, sc, i * 128:(i + 1) * 128], rhs=wo[:, wc, nsl], start=(k == 0), stop=(k == len(srcs) - 1))
                            return r
                        A("pe", mm, reads=rtags + wo_tags, writes=[optag])
                        A("dve", lambda e, op_=op_, xt=xt, scal=scal, nsl=nsl: e.scalar_tensor_tensor(
                            out=xt[:, nsl], in0=op_[:], scalar=scal, in1=xt[:, nsl], op0=ALU.mult, op1=ALU.add),
                          reads=[optag, xtag] + stag2, writes=[xtag])
                A("sp", lambda e, xt=xt, t=t: e.dma_start(out=T["x1_scr"][t * 128:(t + 1) * 128, :], in_=xt[:]),
                  reads=[xtag], writes=[("x1_scr", t)], dma_key=("st",) + xtag)


def phase_C(nc, P, top, sb0, ps0, l, x_dst, T, nblk):
    A = P.add
    sb = lambda es, name, shape, dt: sb0(es, f"{name}_C{l}", shape, dt)
    ps = lambda es, name, shape, dt: ps0(es, f"{name}_C{l}", shape, dt)
    eff_sb, shb_sb, identb = T["eff_sb"], T["shb_sb"], T["identb"]
    rstd_from_ssq = T["rstd_from_ssq"]
    with ExitStack() as es:
        wu = sb(es, "w_up_sb", [128, 8, DFF], BF16)
        wd = sb(es, "w_dn_sb", [128, 32, D], BF16)
        bup = sb(es, "bup", [128, 32], F32)
        x_sbs = Rot([sb(es, f"xC{i}", [128, D], F32) for i in range(3)], "xC")
        junk = sb(es, "junkC", [128, D], BF16)
        xn2 = [sb(es, f"xnC{i}", [128, D], BF16) for i in range(2)]
        xnT2 = [sb(es, f"xnTC{i}", [128, 8, 256], BF16) for i in range(2)]
        hT2 = [sb(es, f"hT{i}", [128, 32, 256], BF16) for i in range(2)]
        rts = Rot([sb(es, f"rt{i}", [128, 256], BF16) for i in range(3)], "rt")
        stc = sb(es, "stc", [128, 8], F32)
        tp_ps = ps(es, "tpC_ps", [128, 8, 128], BF16)
        up_pss = Rot([ps(es, f"up_ps{i}", [128, 512], F32) for i in range(3)], "up_ps")
        dn_pss = Rot([ps(es, f"dn_ps{i}", [128, 512], F32) for i in range(2)], "dn_ps")
        sm_ps = ps(es, "smC_ps", [128, 512], F32)

        for c in range(8):
            A("pool", lambda e, c=c: e.dma_start(out=wu[:, c, :], in_=T["w_up"][l, c * 128:(c + 1) * 128, :], max_dma_last_dim=4096),
              writes=[("wu", c)], dma_key=("w", c))
        for f in range(32):
            A("pool", lambda e, f=f: e.dma_start(out=wd[:, f, :], in_=T["w_down"][l, f * 128:(f + 1) * 128, :]),
              writes=[("wd", f)], dma_key=("wdk", f % 8))
        g2t, g2tag = x_sbs.t[1], ("xC", 1)
        A("sp", lambda e: e.dma_start(out=g2t[:], in_=T["gates_scr"][l, 1:2, :].partition_broadcast(128)), writes=[g2tag], dma_key=g2tag)
        wu_tags = [("wu", c) for c in range(8)]
        wd_tags = [("wd", f) for f in range(32)]

        def mm(e):
            r = None
            for f in range(32):
                for c in range(8):
                    r = e.matmul(sm_ps[:, f:f + 1], lhsT=wu[:, c, f * 128:(f + 1) * 128], rhs=shb_sb[:, l, 1, c:c + 1], start=(c == 0), stop=(c == 7))
            return r
        A("pe", mm, reads=wu_tags, writes=["smC_ps"])
        A("dve", lambda e: e.tensor_copy(out=bup[:], in_=sm_ps[:, 0:32]), reads=["smC_ps"], writes=["bup"])
        for c in range(8):
            A("act", lambda e, c=c: e.activation(out=wu[:, c, :], in_=wu[:, c, :], func=AF.Copy, scale=eff_sb[:, l, 16 + c:17 + c]),
              reads=[], writes=[("wu", c)], n=DFF)
        for f in range(32):
            eng = "pool" if f % 4 == 3 else "dve"
            A(eng, lambda e, f=f: e.tensor_tensor(out=wd[:, f, :], in0=wd[:, f, :], in1=g2t[:], op=ALU.mult), reads=[g2tag], writes=[("wd", f)], n=D)

        for st in range(nblk):
            xts = []
            bp = st % 2
            xnT, hT = xnT2[bp], hT2[bp]
            for i in range(2):
                xn = xn2[i]
                t = 2 * st + i
                xt, xtag = x_sbs.next()
                xts.append((xt, xtag))
                A("sp", lambda e, xt=xt, t=t: e.dma_start(out=xt[:], in_=T["x1_scr"][t * 128:(t + 1) * 128, :]), writes=[xtag], dma_key=xtag)
                A("act", lambda e, xt=xt: e.activation(out=junk[:], in_=xt[:], func=AF.Square, accum_out=stc[:, 0:1]),
                  reads=[xtag], writes=["junkC", "stc"])
                rstd_from_ssq(stc[:, 2:3], stc[:, 0:1], D, ["stc"], ["stcr"], stc[:, 1:2], "stct")
                A("dve", lambda e, xt=xt, xn=xn: e.tensor_scalar(out=xn[:], in0=xt[:], scalar1=stc[:, 2:3], scalar2=None, op0=ALU.mult),
                  reads=[xtag, "stcr"], writes=[f"xnC{i}"], n=D)

                def tr(e, xn=xn):
                    r = None
                    for c in range(8):
                        r = e.transpose(out=tp_ps[:, c, :], in_=xn[:, c * 128:(c + 1) * 128], identity=identb[:])
                    return r
                A("pe", tr, reads=[f"xnC{i}", "identb"], writes=["tpC_ps"])
                A("act", lambda e, i=i, xnT=xnT: e.copy(out=xnT[:, :, i * 128:(i + 1) * 128], in_=tp_ps[:]), reads=["tpC_ps"], writes=[("xnTC", bp, i)], n=D)
            for f in range(32):
                up, uptag = up_pss.next()

                def mm(e, up=up, f=f, xnT=xnT):
                    r = None
                    for c in range(8):
                        r = e.matmul(up[:, 0:256], lhsT=wu[:, c, f * 128:(f + 1) * 128], rhs=xnT[:, c, :], start=(c == 0), stop=(c == 7))
                    return r
                A("pe", mm, reads=wu_tags + [("xnTC", bp, 0), ("xnTC", bp, 1)], writes=[uptag], cost=1000)
                rt, rttag = rts.next()
                A("dve", lambda e, up=up, rt=rt, f=f: e.tensor_scalar(out=rt[:], in0=up[:, 0:256], scalar1=bup[:, f:f + 1], scalar2=0.0, op0=ALU.add, op1=ALU.max),
                  reads=[uptag, "bup"], writes=[rttag])
                A("act", lambda e, rt=rt, f=f, hT=hT: e.activation(out=hT[:, f, :], in_=rt[:], func=AF.Square), reads=[rttag], writes=[("hT", bp, f)])
            hT_tags = [("hT", bp, f) for f in range(32)]
            for i in range(2):
                t = 2 * st + i
                xt, xtag = xts[i]
                for n in range(2):
                    nsl = slice(n * 512, (n + 1) * 512)
                    dn, dntag = dn_pss.next()

                    def mm(e, dn=dn, i=i, nsl=nsl, hT=hT):
                        r = None
                        for f in range(32):
                            r = e.matmul(dn[:], lhsT=hT[:, f, i * 128:(i + 1) * 128], rhs=wd[:, f, nsl], start=(f == 0), stop=(f == 31))
                        return r
                    A("pe", mm, reads=hT_tags + wd_tags, writes=[dntag], cost=7200)
                    A("dve", lambda e, dn=dn, xt=xt, nsl=nsl: e.tensor_tensor(out=xt[:, nsl], in0=dn[:], in1=xt[:, nsl], op=ALU.add),
                      reads=[dntag, xtag], writes=[xtag])
                A("sp", lambda e, xt=xt, t=t: e.dma_start(out=x_dst[t * 128:(t + 1) * 128, :], in_=xt[:]),
                  reads=[xtag], writes=[("x_dst", t)], dma_key=("st",) + xtag)


def _consts():
    bf = ml_dtypes.bfloat16
    identb = np.eye(128, dtype=np.float32).astype(bf)
    identf = np.eye(128, dtype=np.float32)
    oh = np.zeros((16, S), np.float32)
    for j in range(16):
        oh[j, j * 256:(j + 1) * 256] = 1.0
    negmask = np.zeros((16, 16), np.float32)
    for own in range(16):
        negmask[own, own:] = -1e30
    kk = np.arange(128)[:, None]
    qq = np.arange(128)[None, :]
    tri = np.where(kk <= qq, 0.0, -BIG).astype(np.float32)
    caus = np.zeros((128, 2, 256), np.float32)
    caus[:, 0, 0:128] = tri
    caus[:, 1, 0:128] = -BIG
    caus[:, 1, 128:256] = tri
    return dict(c_identb=identb, c_identf=identf, c_oh=oh.astype(bf), c_negmask=negmask.reshape(1, 256),
                c_caus=caus.astype(bf))


def _col(v):
    v = np.asarray(v, np.float32)
    return np.ascontiguousarray(v.reshape(-1, 128).T)


def make_in_maps(inputs, n_cores=8):
    f = lambda k: np.ascontiguousarray(np.asarray(inputs[k], np.float32))
    cols = np.zeros((2, 128, NCOLS), np.float32)
    for l in range(2):
        b = f("b_ada")[l]
        parts = [_col(f("ln1_g")[l]), _col(f("ln2_g")[l]),
                 _col(b[0:1024]), _col(b[1024:2048]), _col(b[3072:4096]), _col(b[4096:5120]),
                 _col(f("mix_norm_g")[l])]
        scw = f("sc_w")[l]
        parts.append(np.concatenate([np.stack([scw[k, cc * 128:(cc + 1) * 128] for k in range(3)], 1) for cc in range(2)], 1))
        lcw = f("lru_conv_w")[l]
        parts.append(np.concatenate([np.stack([lcw[k, cc * 128:(cc + 1) * 128] for k in range(4)], 1) for cc in range(2)], 1))
        parts += [_col(f("lru_conv_b")[l]), _col(f("lru_ba")[l]), _col(f("lru_bx")[l]), _col(f("lru_lambda")[l])]
        cols[l] = np.concatenate(parts, 1)
    shared = dict(cols=cols, b_ada=f("b_ada"), w_ada=f("w_ada"), w_in=f("w_in"), q_norm_g=f("q_norm_g"), k_norm_g=f("k_norm_g"),
                  lru_wa=f("lru_wa"), lru_wx=f("lru_wx"), w_out=f("w_out"), w_up=f("w_up"), w_down=f("w_down"))
    shared.update(_consts())
    x = f("x")
    c = f("c")
    maps = []
    for b in range(n_cores):
        m = dict(shared)
        m["x"] = x[b]
        m["ccol"] = _col(c[b])
        maps.append(m)
    return maps


_NC = None


def kernel(**inputs):
    global _NC
    if _NC is None:
        _NC = build_program()[0]
    maps = make_in_maps(inputs)
    res = run_bass_kernel_spmd(_NC, maps, core_ids=list(range(8)))
    return np.stack([np.asarray(r["out"], np.float32) for r in res.results], 0)
```
